# Optimizing a Trainium2 kernel written in Bass

```python
import math
import jax
import jax.numpy as jnp
from jax import lax
import numpy as np

D_MODEL = 1024
BATCH = 32
SEQ = 2048
DEPTH = 2

F32 = jnp.float32
CTX_LEN = 256
GRID_W = 64
ROPE_THETA = 10000.0
NORM_EPS = 1e-6
HEAD_DIM = 64
Q_BLOCK = 128

MLA_HEADS = 8
MLA_NOPE = 64
MLA_ROPE = 32
MLA_V = 64
MLA_Q_RANK = 384
MLA_KV_RANK = 256
RWKV_HEADS = 8
RWKV_N = 64
RWKV_W = RWKV_HEADS * RWKV_N
DECAY_LORA = 64
ICLR_LORA = 64
GATE_LORA = 128
GN_EPS = 64e-5
RWKV_COLS = 3 * RWKV_W + DECAY_LORA + ICLR_LORA + GATE_LORA
SSM_HEADS = 16
SSM_P = 64
SSM_G = 2
SSM_N = 128
SSM_INNER = SSM_HEADS * SSM_P
SSM_XBC = SSM_INNER + 2 * SSM_G * SSM_N
SSM_CONV = 5
SSD_CHUNK = 128
SWA_HEADS = 8
SWA_KV_HEADS = 2
WINDOW = 128
D_FF = 2816
FFN_CONV = 3

AB_SPLITS = (MLA_Q_RANK, MLA_KV_RANK, MLA_ROPE, RWKV_COLS)
AB_IN = sum(AB_SPLITS)
AB_MIX = MLA_HEADS * MLA_V + RWKV_W
CD_SPLITS = (SSM_INNER, SSM_XBC, 2 * SSM_HEADS,
             SWA_HEADS * HEAD_DIM, SWA_KV_HEADS * HEAD_DIM, SWA_KV_HEADS * HEAD_DIM)
CD_IN = sum(CD_SPLITS)
CD_MIX = SSM_INNER + SWA_HEADS * HEAD_DIM
N_EVEN = (DEPTH + 1) // 2
N_ODD = DEPTH // 2

kernel_name = 'hybrid_mla_rwkv7_ssd_swa_prefix_dit'


def split_cols(t, sizes):
    return jnp.split(t, [int(s) for s in np.cumsum(sizes)[:-1]], axis=-1)


def rms_norm(x, g):
    x32 = x.astype(F32)
    y = x32 * lax.rsqrt(jnp.mean(x32 * x32, axis=-1, keepdims=True) + NORM_EPS)
    return y.astype(x.dtype) * g


def modulate(x, g, shift, scale):
    return rms_norm(x, g) * (1 + scale) + shift


def dwconv_centred(x, w, b):
    k = w.shape[0]
    y = lax.conv_general_dilated(x, w[:, None, :].astype(x.dtype), window_strides=(1,),
                                 padding=((k // 2, k // 2),),
                                 dimension_numbers=('NWC', 'WIO', 'NWC'),
                                 feature_group_count=x.shape[-1])
    return y + b


def token_shift(x, mu_prev, mu_next):
    x_prev = jnp.pad(x, ((0, 0), (1, 0), (0, 0)))[:, :-1]
    x_next = jnp.pad(x, ((0, 0), (0, 1), (0, 0)))[:, 1:]
    return x + mu_prev * (x_prev - x) + mu_next * (x_next - x)


def axial_rope(n_tok, rot_dim):
    rows = n_tok // GRID_W
    row_pos, col_pos = jnp.meshgrid(jnp.arange(rows, dtype=F32), jnp.arange(GRID_W, dtype=F32), indexing='ij')
    n_freq = rot_dim // 4
    inv_freq = ROPE_THETA ** (-jnp.arange(n_freq, dtype=F32) / n_freq)
    ang = jnp.concatenate([row_pos.reshape(-1, 1) * inv_freq, col_pos.reshape(-1, 1) * inv_freq], axis=-1)
    return jnp.cos(ang), jnp.sin(ang)


def apply_rope(t, cos, sin):
    t1, t2 = jnp.split(t, 2, axis=-1)
    cos = cos[None, :, None, :].astype(t.dtype)
    sin = sin[None, :, None, :].astype(t.dtype)
    return jnp.concatenate([t1 * cos - t2 * sin, t1 * sin + t2 * cos], axis=-1)


def attend(q, k, v):
    s = jnp.einsum('bqhd,bkhd->bhqk', q, k).astype(F32) * (q.shape[-1] ** -0.5)
    p = jax.nn.softmax(s, axis=-1).astype(v.dtype)
    return jnp.einsum('bhqk,bkhd->bqhd', p, v)


def attend_in_query_blocks(q, k, v):
    b, l, h, d = q.shape
    q_blocks = jnp.moveaxis(q.reshape(b, l // Q_BLOCK, Q_BLOCK, h, d), 1, 0)
    out = lax.map(lambda qb: attend(qb, k, v), q_blocks)
    return jnp.moveaxis(out, 0, 1).reshape(b, l, h * v.shape[-1])


def mla_heads(cq, ckv, k_rot, p, rope):
    b, l, _ = cq.shape
    q = (rms_norm(cq, p['mla_q_norm_g']) @ p['mla_w_q_up']).reshape(b, l, MLA_HEADS, MLA_NOPE + MLA_ROPE)
    kv = (rms_norm(ckv, p['mla_kv_norm_g']) @ p['mla_w_kv_up']).reshape(b, l, MLA_HEADS, MLA_NOPE + MLA_V)
    q_nope, q_rot = jnp.split(q, [MLA_NOPE], axis=-1)
    k_nope, v = jnp.split(kv, [MLA_NOPE], axis=-1)
    k_rot = k_rot[:, :, None, :]
    if rope is not None:
        q_rot = apply_rope(q_rot, *rope)
        k_rot = apply_rope(k_rot, *rope)
    k = jnp.concatenate([k_nope, jnp.broadcast_to(k_rot, (b, l, MLA_HEADS, MLA_ROPE))], axis=-1)
    return jnp.concatenate([q_nope, q_rot], axis=-1), k, v


def rwkv_features(cols, p):
    b, l, _ = cols.shape
    heads = lambda t: t.reshape(b, l, RWKV_HEADS, RWKV_N)
    xs = token_shift(cols, p['rwkv_mu_prev'], p['rwkv_mu_next'])
    r, k, v, xw, xa, xg = split_cols(xs, (RWKV_W, RWKV_W, RWKV_W, DECAY_LORA, ICLR_LORA, GATE_LORA))
    g = jax.nn.sigmoid(xg) @ p['rwkv_g2']
    kk = heads(k * p['rwkv_k_k']).astype(F32)
    kk = kk * lax.rsqrt(jnp.sum(kk * kk, axis=-1, keepdims=True) + 1e-12)
    dirs = []
    for d in range(2):
        w_log = -jax.nn.softplus(-(p['rwkv_w0'][d] + jnp.tanh(xw) @ p['rwkv_w2'][d])) - 0.5
        a = jax.nn.sigmoid(p['rwkv_a0'][d] + xa @ p['rwkv_a2'][d])
        k_eff = k * (1 + (a - 1) * p['rwkv_k_a'])
        dirs.append((heads(jnp.exp(-jnp.exp(w_log.astype(F32)))), heads(k_eff), heads(a)))
    return heads(r), heads(k), heads(v), g, kk, dirs


def rwkv_scan(r, w, k, v, kk, a, state0, reverse):
    def step(s, inp):
        r_t, w_t, k_t, v_t, kk_t, a_t = inp
        sa = jnp.einsum('bhij,bhj->bhi', s, -kk_t)
        s = s * w_t[:, :, None, :] + sa[..., None] * (kk_t * a_t)[:, :, None, :] + v_t[..., None] * k_t[:, :, None, :]
        return s, jnp.einsum('bhij,bhj->bhi', s, r_t)
    xs = tuple(jnp.moveaxis(t.astype(F32), 1, 0) for t in (r, w, k, v, kk, a))
    s_final, ys = lax.scan(step, state0, xs, reverse=reverse)
    return jnp.moveaxis(ys, 0, 1), s_final


def rwkv_bidirectional(f_ctx, f_lat):
    r_c, _, v_c, _, kk_c, dirs_c = f_ctx
    r_l, _, v_l, _, kk_l, dirs_l = f_lat
    s0 = jnp.zeros((r_c.shape[0], RWKV_HEADS, RWKV_N, RWKV_N), F32)
    ys_c, ys_l = [], []
    for (w_c, k_c, a_c), (w_l, k_l, a_l), rev in zip(dirs_c, dirs_l, (False, True)):
        y_c, s_ctx = rwkv_scan(r_c, w_c, k_c, v_c, kk_c, a_c, s0, rev)
        y_l, _ = rwkv_scan(r_l, w_l, k_l, v_l, kk_l, a_l, s_ctx, rev)
        ys_c.append(y_c)
        ys_l.append(y_l)
    return ys_c[0] + ys_c[1], ys_l[0] + ys_l[1]


def rwkv_output(y, feats, p):
    r, k, v, g = feats[:4]
    b, l = r.shape[:2]
    mu = jnp.mean(y, axis=-1, keepdims=True)
    var = jnp.mean(jnp.square(y - mu), axis=-1, keepdims=True)
    yn = ((y - mu) * lax.rsqrt(var + GN_EPS)).reshape(b, l, RWKV_W).astype(g.dtype)
    yn = yn * p['rwkv_ln_g'] + p['rwkv_ln_b']
    bonus = (jnp.sum(r * k * p['rwkv_r_k'], axis=-1, keepdims=True) * v).reshape(b, l, RWKV_W)
    return (yn + bonus) * g


def mixer_mla_rwkv(h_ctx, h_lat, p, rope, need_ctx):
    cq_c, ckv_c, kr_c, rw_c = split_cols(h_ctx @ p['w_in'], AB_SPLITS)
    cq_l, ckv_l, kr_l, rw_l = split_cols(h_lat @ p['w_in'], AB_SPLITS)
    q_c, k_c, v_c = mla_heads(cq_c, ckv_c, kr_c, p, None)
    q_l, k_l, v_l = mla_heads(cq_l, ckv_l, kr_l, p, rope)
    att_l = attend_in_query_blocks(q_l, jnp.concatenate([k_c, k_l], axis=1),
                                   jnp.concatenate([v_c, v_l], axis=1))
    f_c, f_l = rwkv_features(rw_c, p), rwkv_features(rw_l, p)
    y_c, y_l = rwkv_bidirectional(f_c, f_l)
    out_l = jnp.concatenate([att_l, rwkv_output(y_l, f_l, p)], axis=-1) @ p['w_out']
    if not need_ctx:
        return None, out_l
    b, lc, _ = h_ctx.shape
    att_c = attend(q_c, k_c, v_c).reshape(b, lc, MLA_HEADS * MLA_V)
    out_c = jnp.concatenate([att_c, rwkv_output(y_c, f_c, p)], axis=-1) @ p['w_out']
    return out_c, out_l


def ssm_features(xbc, dt_raw, p):
    b, l, _ = xbc.shape
    xbc = jax.nn.silu(dwconv_centred(xbc, p['ssm_conv_w'], p['ssm_conv_b']))
    xs, bm, cm = split_cols(xbc, (SSM_INNER, SSM_G * SSM_N, SSM_G * SSM_N))
    dt = jax.nn.softplus(dt_raw.reshape(b, l, 2, SSM_HEADS) + p['ssm_dt_bias'])
    return (xs.reshape(b, l, SSM_HEADS, SSM_P), dt,
            bm.reshape(b, l, SSM_G, SSM_N), cm.reshape(b, l, SSM_G, SSM_N))


def ssd_scan(x, dt, a_log, bm, cm, h0):
    b, l = x.shape[:2]
    nc, hg = l // SSD_CHUNK, SSM_HEADS // SSM_G
    xc = x.astype(F32).reshape(b, nc, SSD_CHUNK, SSM_G, hg, SSM_P)
    dtc = dt.astype(F32).reshape(b, nc, SSD_CHUNK, SSM_G, hg)
    bc = bm.astype(F32).reshape(b, nc, SSD_CHUNK, SSM_G, SSM_N)
    cc = cm.astype(F32).reshape(b, nc, SSD_CHUNK, SSM_G, SSM_N)
    a = -jnp.exp(a_log.astype(F32)).reshape(SSM_G, hg)
    a_cum = jnp.cumsum(dtc * a, axis=2)
    tri = jnp.tril(jnp.ones((SSD_CHUNK, SSD_CHUNK), bool))
    seg = a_cum[:, :, :, None] - a_cum[:, :, None, :]
    decay_qk = jnp.exp(jnp.where(tri[:, :, None, None], seg, -jnp.inf))
    cb = jnp.einsum('bcqgn,bckgn->bcqkg', cc, bc)
    y_diag = jnp.einsum('bcqkgh,bckghp->bcqghp', cb[..., None] * decay_qk * dtc[:, :, None], xc)
    decay_end = jnp.exp(a_cum[:, :, -1:] - a_cum)
    chunk_states = jnp.einsum('bcqgn,bcqgh,bcqghp->bcghpn', bc, decay_end * dtc, xc)
    chunk_decay = jnp.exp(a_cum[:, :, -1])

    def carry(h, inp):
        dec, st = inp
        return h * dec[..., None, None] + st, h
    h_last, h_in = lax.scan(carry, h0.astype(F32),
                            (jnp.moveaxis(chunk_decay, 1, 0), jnp.moveaxis(chunk_states, 1, 0)))
    h_in = jnp.moveaxis(h_in, 0, 1)
    y_off = jnp.einsum('bcqgn,bcghpn,bcqgh->bcqghp', cc, h_in, jnp.exp(a_cum))
    y = (y_diag + y_off).reshape(b, l, SSM_HEADS, SSM_P)
    return y.astype(x.dtype), h_last


def ssd_bidirectional(f_ctx, f_lat, p):
    flip = lambda t: jnp.flip(t, axis=1)
    x_c, dt_c, b_c, c_c = f_ctx
    x_l, dt_l, b_l, c_l = f_lat
    h0 = jnp.zeros((x_c.shape[0], SSM_G, SSM_HEADS // SSM_G, SSM_P, SSM_N), F32)
    a_log = p['ssm_a_log']
    yf_c, hf = ssd_scan(x_c, dt_c[:, :, 0], a_log[0], b_c, c_c, h0)
    yf_l, _ = ssd_scan(x_l, dt_l[:, :, 0], a_log[0], b_l, c_l, hf)
    yb_c, hb = ssd_scan(flip(x_c), flip(dt_c[:, :, 1]), a_log[1], flip(b_c), flip(c_c), h0)
    yb_l, _ = ssd_scan(flip(x_l), flip(dt_l[:, :, 1]), a_log[1], flip(b_l), flip(c_l), hb)
    d_skip = p['ssm_d'][:, None]
    return yf_c + flip(yb_c) + d_skip * x_c, yf_l + flip(yb_l) + d_skip * x_l


def gated_rms_norm(y, z, g):
    b, l, _ = z.shape
    u = (y.reshape(b, l, SSM_INNER) * jax.nn.silu(z)).reshape(b, l, SSM_G, SSM_INNER // SSM_G)
    return rms_norm(u, g.reshape(SSM_G, -1)).reshape(b, l, SSM_INNER)


def swa_heads(q, k, v, rope):
    b, l, _ = q.shape
    q = q.reshape(b, l, SWA_HEADS, HEAD_DIM)
    k = k.reshape(b, l, SWA_KV_HEADS, HEAD_DIM)
    v = v.reshape(b, l, SWA_KV_HEADS, HEAD_DIM)
    if rope is not None:
        q, k = apply_rope(q, *rope), apply_rope(k, *rope)
    return q.reshape(b, l, SWA_KV_HEADS, SWA_HEADS // SWA_KV_HEADS, HEAD_DIM), k, v


def sink_softmax(scores, sink):
    s_sink = jnp.broadcast_to(sink[:, :, None, None].astype(F32), scores.shape[:-1] + (1,))
    return jax.nn.softmax(jnp.concatenate([scores, s_sink], axis=-1), axis=-1)[..., :-1]


def swa_context(q, k, v, sink):
    s = jnp.einsum('bqghd,bkgd->bghqk', q, k).astype(F32) * (HEAD_DIM ** -0.5)
    p = sink_softmax(s, sink).astype(v.dtype)
    return jnp.einsum('bghqk,bkgd->bqghd', p, v)


def swa_latent(q, k, v, k_ctx, v_ctx, sink):
    b, l = q.shape[:2]
    nb, span, n_ctx = l // Q_BLOCK, Q_BLOCK + 2 * WINDOW, k_ctx.shape[1]
    pad = ((0, 0), (WINDOW, WINDOW), (0, 0), (0, 0))
    kp, vp = jnp.pad(k, pad), jnp.pad(v, pad)
    rel = jnp.arange(span)[None, :] - WINDOW - jnp.arange(Q_BLOCK)[:, None]
    band = jnp.abs(rel) <= WINDOW
    q_blocks = jnp.moveaxis(q.reshape(b, nb, Q_BLOCK, *q.shape[2:]), 1, 0)
    scale = HEAD_DIM ** -0.5

    def block(args):
        i, qb = args
        start = i * Q_BLOCK
        kw = lax.dynamic_slice_in_dim(kp, start, span, axis=1)
        vw = lax.dynamic_slice_in_dim(vp, start, span, axis=1)
        key_pos = start - WINDOW + jnp.arange(span)
        mask = band & ((key_pos >= 0) & (key_pos < l))[None, :]
        s_ctx = jnp.einsum('bqghd,bkgd->bghqk', qb, k_ctx).astype(F32) * scale
        s_win = jnp.einsum('bqghd,bkgd->bghqk', qb, kw).astype(F32) * scale
        s_win = jnp.where(mask, s_win, -jnp.inf)
        p = sink_softmax(jnp.concatenate([s_ctx, s_win], axis=-1), sink).astype(v.dtype)
        return (jnp.einsum('bghqk,bkgd->bqghd', p[..., :n_ctx], v_ctx)
                + jnp.einsum('bghqk,bkgd->bqghd', p[..., n_ctx:], vw))

    out = lax.map(block, (jnp.arange(nb), q_blocks))
    return jnp.moveaxis(out, 0, 1).reshape(b, l, SWA_HEADS * HEAD_DIM)


def mixer_ssd_swa(h_ctx, h_lat, p, rope, need_ctx):
    z_c, xbc_c, dt_c, q_c, k_c, v_c = split_cols(h_ctx @ p['w_in'], CD_SPLITS)
    z_l, xbc_l, dt_l, q_l, k_l, v_l = split_cols(h_lat @ p['w_in'], CD_SPLITS)
    y_c, y_l = ssd_bidirectional(ssm_features(xbc_c, dt_c, p), ssm_features(xbc_l, dt_l, p), p)
    sink = p['swa_sink'].reshape(SWA_KV_HEADS, SWA_HEADS // SWA_KV_HEADS)
    q_c, k_c, v_c = swa_heads(q_c, k_c, v_c, None)
    q_l, k_l, v_l = swa_heads(q_l, k_l, v_l, rope)
    att_l = swa_latent(q_l, k_l, v_l, k_c, v_c, sink)
    out_l = jnp.concatenate([gated_rms_norm(y_l, z_l, p['ssm_norm_g']), att_l], axis=-1) @ p['w_out']
    if not need_ctx:
        return None, out_l
    b, lc, _ = h_ctx.shape
    att_c = swa_context(q_c, k_c, v_c, sink).reshape(b, lc, SWA_HEADS * HEAD_DIM)
    out_c = jnp.concatenate([gated_rms_norm(y_c, z_c, p['ssm_norm_g']), att_c], axis=-1) @ p['w_out']
    return out_c, out_l


def conv_ffn(h, p):
    u = dwconv_centred(h @ p['ffn_w_up'], p['ffn_conv_w'], p['ffn_conv_b'])
    gate, val = jnp.split(u, 2, axis=-1)
    return (jax.nn.silu(gate) * val) @ p['ffn_w_down']


def adaln_mods(cond, p):
    return jnp.split(jax.nn.silu(cond) @ p['ada_w'] + p['ada_b'], 6, axis=-1)


def trunk_layer(x_lat, x_ctx, c, c_ctx, p, mixer, rope, need_ctx):
    m_lat = [m[:, None, :] for m in adaln_mods(c, p)]
    m_ctx = adaln_mods(c_ctx, p)
    o_ctx, o_lat = mixer(modulate(x_ctx, p['norm_mix_g'], m_ctx[0], m_ctx[1]),
                         modulate(x_lat, p['norm_mix_g'], m_lat[0], m_lat[1]), p, rope, need_ctx)
    x_lat = x_lat + m_lat[2] * o_lat
    x_lat = x_lat + m_lat[5] * conv_ffn(modulate(x_lat, p['norm_ffn_g'], m_lat[3], m_lat[4]), p)
    if need_ctx:
        x_ctx = x_ctx + m_ctx[2] * o_ctx
        x_ctx = x_ctx + m_ctx[5] * conv_ffn(modulate(x_ctx, p['norm_ffn_g'], m_ctx[3], m_ctx[4]), p)
    return x_lat, x_ctx


def setup_inputs(seed: int = 0) -> dict:
    key = jax.random.key(seed)
    keys = iter(jax.random.split(key, 64))

    def nrm(shape, scale):
        return jax.random.normal(next(keys), shape, F32) * scale

    def gain(shape):
        return 1.0 + nrm(shape, 0.02)

    def unif(shape, lo, hi):
        return jax.random.uniform(next(keys), shape, F32, lo, hi)

    ne, no = N_EVEN, N_ODD
    dt0 = jnp.exp(unif((no, 2, SSM_HEADS), math.log(1e-3), math.log(1e-1)))
    return {
        'x': nrm((BATCH, SEQ, D_MODEL), 1.0),
        'c': nrm((BATCH, D_MODEL), 1.0),
        'ctx': nrm((BATCH, CTX_LEN, D_MODEL), 1.0),
        'c_ctx': nrm((D_MODEL,), 1.0),
        'ada_w': nrm((DEPTH, D_MODEL, 6 * D_MODEL), 0.5 * D_MODEL ** -0.5),
        'ada_b': nrm((DEPTH, 6 * D_MODEL), 0.01),
        'norm_mix_g': gain((DEPTH, D_MODEL)),
        'norm_ffn_g': gain((DEPTH, D_MODEL)),
        'ffn_w_up': nrm((DEPTH, D_MODEL, 2 * D_FF), D_MODEL ** -0.5),
        'ffn_conv_w': nrm((DEPTH, FFN_CONV, 2 * D_FF), FFN_CONV ** -0.5),
        'ffn_conv_b': nrm((DEPTH, 2 * D_FF), 0.01),
        'ffn_w_down': nrm((DEPTH, D_FF, D_MODEL), D_FF ** -0.5),
        'final_norm_g': gain((D_MODEL,)),
        'ab_w_in': nrm((ne, D_MODEL, AB_IN), D_MODEL ** -0.5),
        'ab_w_out': nrm((ne, AB_MIX, D_MODEL), AB_MIX ** -0.5),
        'mla_q_norm_g': gain((ne, MLA_Q_RANK)),
        'mla_w_q_up': nrm((ne, MLA_Q_RANK, MLA_HEADS * (MLA_NOPE + MLA_ROPE)), MLA_Q_RANK ** -0.5),
        'mla_kv_norm_g': gain((ne, MLA_KV_RANK)),
        'mla_w_kv_up': nrm((ne, MLA_KV_RANK, MLA_HEADS * (MLA_NOPE + MLA_V)), MLA_KV_RANK ** -0.5),
        'rwkv_mu_prev': unif((ne, RWKV_COLS), 0.0, 0.5),
        'rwkv_mu_next': unif((ne, RWKV_COLS), 0.0, 0.5),
        'rwkv_w0': unif((ne, 2, RWKV_W), -6.0, 0.0),
        'rwkv_w2': nrm((ne, 2, DECAY_LORA, RWKV_W), 0.5 * DECAY_LORA ** -0.5),
        'rwkv_a0': nrm((ne, 2, RWKV_W), 0.1),
        'rwkv_a2': nrm((ne, 2, ICLR_LORA, RWKV_W), 0.5 * ICLR_LORA ** -0.5),
        'rwkv_g2': nrm((ne, GATE_LORA, RWKV_W), GATE_LORA ** -0.5),
        'rwkv_k_k': 0.85 + nrm((ne, RWKV_W), 0.02),
        'rwkv_k_a': gain((ne, RWKV_W)),
        'rwkv_r_k': nrm((ne, RWKV_HEADS, RWKV_N), 0.1),
        'rwkv_ln_g': gain((ne, RWKV_W)),
        'rwkv_ln_b': nrm((ne, RWKV_W), 0.01),
        'cd_w_in': nrm((no, D_MODEL, CD_IN), D_MODEL ** -0.5),
        'cd_w_out': nrm((no, CD_MIX, D_MODEL), CD_MIX ** -0.5),
        'ssm_conv_w': nrm((no, SSM_CONV, SSM_XBC), SSM_CONV ** -0.5),
        'ssm_conv_b': nrm((no, SSM_XBC), 0.01),
        'ssm_dt_bias': dt0 + jnp.log(-jnp.expm1(-dt0)),
        'ssm_a_log': jnp.log(unif((no, 2, SSM_HEADS), 1.0, 16.0)),
        'ssm_d': gain((no, SSM_HEADS)),
        'ssm_norm_g': gain((no, SSM_INNER)),
        'swa_sink': nrm((no, SWA_HEADS), 0.5),
    }


def reference(x, c, ctx, c_ctx, ada_w, ada_b, norm_mix_g, norm_ffn_g, ffn_w_up, ffn_conv_w, ffn_conv_b,
              ffn_w_down, final_norm_g, ab_w_in, ab_w_out, mla_q_norm_g, mla_w_q_up, mla_kv_norm_g,
              mla_w_kv_up, rwkv_mu_prev, rwkv_mu_next, rwkv_w0, rwkv_w2, rwkv_a0, rwkv_a2, rwkv_g2,
              rwkv_k_k, rwkv_k_a, rwkv_r_k, rwkv_ln_g, rwkv_ln_b, cd_w_in, cd_w_out, ssm_conv_w,
              ssm_conv_b, ssm_dt_bias, ssm_a_log, ssm_d, ssm_norm_g, swa_sink):
    n_tok = x.shape[1]
    rope_mla = axial_rope(n_tok, MLA_ROPE)
    rope_swa = axial_rope(n_tok, HEAD_DIM)
    x_lat, x_ctx = x, ctx
    for i in range(DEPTH):
        j = i // 2
        p = {'ada_w': ada_w[i], 'ada_b': ada_b[i], 'norm_mix_g': norm_mix_g[i], 'norm_ffn_g': norm_ffn_g[i],
             'ffn_w_up': ffn_w_up[i], 'ffn_conv_w': ffn_conv_w[i], 'ffn_conv_b': ffn_conv_b[i],
             'ffn_w_down': ffn_w_down[i]}
        if i % 2 == 0:
            p.update({'w_in': ab_w_in[j], 'w_out': ab_w_out[j], 'mla_q_norm_g': mla_q_norm_g[j],
                      'mla_w_q_up': mla_w_q_up[j], 'mla_kv_norm_g': mla_kv_norm_g[j],
                      'mla_w_kv_up': mla_w_kv_up[j], 'rwkv_mu_prev': rwkv_mu_prev[j],
                      'rwkv_mu_next': rwkv_mu_next[j], 'rwkv_w0': rwkv_w0[j], 'rwkv_w2': rwkv_w2[j],
                      'rwkv_a0': rwkv_a0[j], 'rwkv_a2': rwkv_a2[j], 'rwkv_g2': rwkv_g2[j],
                      'rwkv_k_k': rwkv_k_k[j], 'rwkv_k_a': rwkv_k_a[j], 'rwkv_r_k': rwkv_r_k[j],
                      'rwkv_ln_g': rwkv_ln_g[j], 'rwkv_ln_b': rwkv_ln_b[j]})
            mixer, rope = mixer_mla_rwkv, rope_mla
        else:
            p.update({'w_in': cd_w_in[j], 'w_out': cd_w_out[j], 'ssm_conv_w': ssm_conv_w[j],
                      'ssm_conv_b': ssm_conv_b[j], 'ssm_dt_bias': ssm_dt_bias[j], 'ssm_a_log': ssm_a_log[j],
                      'ssm_d': ssm_d[j], 'ssm_norm_g': ssm_norm_g[j], 'swa_sink': swa_sink[j]})
            mixer, rope = mixer_ssd_swa, rope_swa
        x_lat, x_ctx = trunk_layer(x_lat, x_ctx, c, c_ctx, p, mixer, rope, i < DEPTH - 1)
    return rms_norm(x_lat, final_norm_g)
```

```python
import contextlib
import math
import numpy as np
import concourse.bass as bass
import concourse.mybir as mybir
from concourse.bass_utils import run_bass_kernel_spmd

F32 = mybir.dt.float32
BF16 = mybir.dt.bfloat16
AF = mybir.ActivationFunctionType
ALU = mybir.AluOpType
AX = mybir.AxisListType

NCORES = 8
BPC = 4
D = 1024
KC = 8
LC = 256
LL = 2048
T = LC + LL
DFF = 2816
EPS = 1e-6
BLOCKS = [(0, 256), (256, 768), (768, 1280), (1280, 1792), (1792, 2304)]
SEM_EPOCH = 50000


class Buf:
    def __init__(self, t, name):
        self.t = t
        self.name = name
        self.lw = []
        self.rd = []
        self.ds = None

    def __getitem__(self, idx):
        return self.t[idx]


class Eng:
    def __init__(self, name, e, is_pe=False):
        self.name = name
        self.e = e
        self.is_pe = is_pe
        self.sems = []
        self.count = 0
        self.epoch = 0
        self.seen = {}


class Kern:
    def __init__(self, nc):
        self.nc = nc
        self.es = contextlib.ExitStack()
        self.engs = {}
        for name, e, ispe in (("pe", nc.tensor, True), ("act", nc.scalar, False),
                              ("dve", nc.vector, False), ("pool", nc.gpsimd, False),
                              ("sp", nc.sync, False)):
            en = Eng(name, e, ispe)
            en.sems.append(self.es.enter_context(nc.semaphore("s_%s_0" % name)))
            self.engs[name] = en
        self.ndsem = 40
        self.dsem = [self.es.enter_context(nc.semaphore("s_d%d" % i)) for i in range(self.ndsem)]
        self.dtot = [0] * self.ndsem
        self.drr = 0
        self.nbuf = 0
        self.n_ops = 0
        self.eps_bufs = {}
        self.in_names = []

    def sb(self, shape, dtype, name=None, stack=None):
        self.nbuf += 1
        name = "%s_%d" % (name or "sb", self.nbuf)
        t = (stack or self.es).enter_context(self.nc.sbuf_tensor(name, list(shape), dtype))
        return Buf(t, name)

    def ps(self, name=None):
        self.nbuf += 1
        name = "%s_%d" % (name or "ps", self.nbuf)
        t = self.es.enter_context(self.nc.psum_tensor(name, [128, 512], F32))
        return Buf(t, name)

    def dram(self, name, shape, dtype, kind="Internal"):
        t = self.nc.dram_tensor(name, list(shape), dtype, kind=kind)
        if kind == "ExternalInput":
            self.in_names.append(name)
        return Buf(t, name)

    def _wait(self, eng, ev):
        if ev[0] == "E":
            src = self.engs[ev[1]]
            ep, n = ev[2], ev[3]
            if src is eng and eng.is_pe:
                return
            key = ("E", ev[1], ep)
            if eng.seen.get(key, 0) >= n:
                return
            eng.e.wait_ge(src.sems[ep], n)
            eng.seen[key] = n
        else:
            i = ev[1]
            tot = self.dtot[i]
            key = ("D", i)
            if eng.seen.get(key, 0) >= tot:
                return
            eng.e.wait_ge(self.dsem[i], tot)
            eng.seen[key] = tot

    def _deps(self, eng, reads, writes):
        for b in reads:
            for ev in b.lw:
                self._wait(eng, ev)
        for b in writes:
            for ev in b.lw:
                self._wait(eng, ev)
            for ev in b.rd:
                self._wait(eng, ev)

    def _record(self, ev, reads, writes):
        for b in reads:
            if ev[0] == "E":
                b.rd = [r for r in b.rd if not (r[0] == "E" and r[1] == ev[1])]
            else:
                b.rd = [r for r in b.rd if r != ev]
            b.rd.append(ev)
        for b in writes:
            b.lw = [ev]
            b.rd = []

    def op(self, engname, fn, reads=(), writes=()):
        eng = self.engs[engname]
        self._deps(eng, reads, writes)
        if eng.count >= SEM_EPOCH:
            eng.epoch += 1
            eng.count = 0
            eng.sems.append(self.es.enter_context(self.nc.semaphore("s_%s_%d" % (engname, eng.epoch))))
        ins = fn(eng.e)
        eng.count += 1
        ins.then_inc(eng.sems[eng.epoch], 1)
        ev = ("E", engname, eng.epoch, eng.count)
        self._record(ev, reads, writes)
        self.n_ops += 1

    def dma(self, engname, out, in_, reads=(), writes=(), **kw):
        eng = self.engs[engname]
        self._deps(eng, reads, writes)
        b = None
        for cand in list(writes) + list(reads):
            if cand.ds is not None:
                b = cand
                break
        if b is None:
            b = (list(writes) + list(reads))[0]
            b.ds = self.drr
            self.drr = (self.drr + 1) % self.ndsem
        i = b.ds
        ins = eng.e.dma_start(out=out, in_=in_, **kw)
        ins.then_inc(self.dsem[i], 16)
        self.dtot[i] += 16
        ev = ("D", i)
        self._record(ev, reads, writes)
        self.n_ops += 1

    def barrier(self):
        for eng in self.engs.values():
            for other in self.engs.values():
                if other is eng:
                    continue
                if other.count > 0:
                    self._wait(eng, ("E", other.name, other.epoch, other.count))
            for i in range(self.ndsem):
                if self.dtot[i] > 0:
                    self._wait(eng, ("D", i))

    def mm(self, out, lhsT, rhs, start, stop, reads, writes):
        self.op("pe", lambda e: e.matmul(out, lhsT=lhsT, rhs=rhs, start=start, stop=stop), reads, writes)

    def transpose(self, out, in_, ident, reads, writes):
        self.op("pe", lambda e: e.transpose(out, in_, ident), reads, writes)

    def act(self, out, in_, func, reads, writes, bias=None, scale=None, eng="act"):
        kw = {}
        if bias is not None:
            kw["bias"] = bias
        if scale is not None:
            kw["scale"] = scale
        self.op(eng, lambda e: e.activation(out=out, in_=in_, func=func, **kw), reads, writes)

    def ts(self, eng, out, in0, s1, s2, op0, op1, reads, writes):
        if op1 is None:
            self.op(eng, lambda e: e.tensor_scalar(out=out, in0=in0, scalar1=s1, scalar2=None, op0=op0), reads, writes)
        else:
            self.op(eng, lambda e: e.tensor_scalar(out=out, in0=in0, scalar1=s1, scalar2=s2, op0=op0, op1=op1), reads, writes)

    def tt(self, eng, out, in0, in1, op, reads, writes):
        self.op(eng, lambda e: e.tensor_tensor(out=out, in0=in0, in1=in1, op=op), reads, writes)

    def stt(self, eng, out, in0, scalar, in1, op0, op1, reads, writes):
        self.op(eng, lambda e: e.scalar_tensor_tensor(out=out, in0=in0, scalar=scalar, in1=in1, op0=op0, op1=op1), reads, writes)

    def copy(self, eng, out, in_, reads, writes):
        if eng == "act":
            self.op(eng, lambda e: e.activation(out=out, in_=in_, func=AF.Copy), reads, writes)
        else:
            self.op(eng, lambda e: e.tensor_copy(out=out, in_=in_), reads, writes)

    def rsqrt(self, out, in_, scale, eps, reads, writes):
        self.op("act", lambda e: e.activation(out=out, in_=in_, func=AF.Sqrt, bias=self.eps_ap(eps), scale=scale), reads, writes)
        self.op("dve", lambda e: e.reciprocal(out=out, in_=out), writes, writes)

    def eps_ap(self, eps):
        if eps not in self.eps_bufs:
            b = self.sb([128, 1], F32, "eps")
            self.memset("dve", b[:, :], float(eps), [b])
            self.eps_bufs[eps] = b
        return self.eps_bufs[eps][:, 0:1]

    def memset(self, eng, ap, val, writes):
        self.op(eng, lambda e: e.memset(ap, val), (), writes)


def colvec(ap1d, n):
    return ap1d.rearrange("(c p) -> p c", p=128)


def stg_view(s, shape):
    n = 1
    for d_ in shape[1:]:
        n *= d_
    v = s[0:shape[0], 0:n]
    if len(shape) == 3:
        v = v.rearrange("p (a b) -> p a b", a=shape[1])
    return v


HD = 64
CD_Z, CD_XBC, CD_DT, CD_Q, CD_K, CD_V = 0, 1024, 2560, 2592, 3104, 3232
CD_IN = 3360


def build_program(cfg):
    nc = bass.Bass("TRN2", target_bir_lowering=False)
    K = Kern(nc)
    nb = cfg.get("nb", BPC)
    layers = cfg.get("layers", [0, 1])
    mixers = cfg.get("mixers", True)
    ES = contextlib.ExitStack

    def din(name, shape):
        return K.dram(name, shape, F32, kind="ExternalInput")

    x_d = din("x", [BPC, LL, D])
    c_d = din("c", [BPC, D])
    ctx_d = din("ctx", [BPC, LC, D])
    cctx_d = din("c_ctx", [D])
    ada_w_d = din("ada_w", [2, D, 6 * D])
    ada_b_d = din("ada_b", [2, 6 * D])
    nmix_d = din("norm_mix_g", [2, D])
    nffn_d = din("norm_ffn_g", [2, D])
    wup_d = din("ffn_w_up", [2, D, 2 * DFF])
    cw_d = din("ffn_conv_w", [2, 3, 2 * DFF])
    cb_d = din("ffn_conv_b", [2, 2 * DFF])
    wdn_d = din("ffn_w_down", [2, DFF, D])
    fng_d = din("final_norm_g", [D])
    if mixers and 1 in layers:
        cdin_d = din("cd_w_in", [1, D, CD_IN])
        cdout_d = din("cd_w_out", [1, 1536, D])
        scw_d = din("ssm_conv_w", [1, 5, 1536])
        scb_d = din("ssm_conv_b", [1, 1536])
        sdtb_d = din("ssm_dt_bias", [1, 2, 16])
        salog_d = din("ssm_a_log", [1, 2, 16])
        sd_d = din("ssm_d", [1, 16])
        sng_d = din("ssm_norm_g", [1, 1024])
        sink_d = din("swa_sink", [1, 8])
        swacos_d = din("swa_cos", [64, LL])
        swasin_d = din("swa_sin", [64, LL])
        triF_d = din("triF", [128, 128])
        triB_d = din("triB", [128, 128])
        strF_d = din("strF", [128, 128])
        strB_d = din("strB", [128, 128])
    if mixers and 0 in layers:
        abin_d = din("ab_w_in", [1, D, 2464])
        about_d = din("ab_w_out", [1, D, D])
        mqg_d = din("mla_q_norm_g", [1, 384])
        mqu_d = din("mla_w_q_up", [1, 384, 768])
        mkg_d = din("mla_kv_norm_g", [1, 256])
        mkvu_d = din("mla_w_kv_up", [1, 256, 1024])
        mlacos_d = din("mla_cos", [32, LL])
        mlasin_d = din("mla_sin", [32, LL])
        rmp_d = din("rwkv_mu_prev", [1, 1792])
        rmn_d = din("rwkv_mu_next", [1, 1792])
        rw0_d = din("rwkv_w0", [1, 2, 512])
        rw2_d = din("rwkv_w2", [1, 2, 64, 512])
        ra0_d = din("rwkv_a0", [1, 2, 512])
        ra2_d = din("rwkv_a2", [1, 2, 64, 512])
        rg2_d = din("rwkv_g2", [1, 128, 512])
        rkk_d = din("rwkv_k_k", [1, 512])
        rka_d = din("rwkv_k_a", [1, 512])
        rrk_d = din("rwkv_r_k", [1, 8, 64])
        rlg_d = din("rwkv_ln_g", [1, 512])
        rlb_d = din("rwkv_ln_b", [1, 512])
        rwlc_d = din("rw_lc", [2, 128, 128])
        rwlexc_d = din("rw_lexc", [2, 128, 128])
        rwmcol_d = din("rw_mcol", [2, 128, 2])
        rwm1_d = din("rw_m1", [2, 128, 128])
        rwm3_d = din("rw_m3", [2, 128, 384])
        rwm1t_d = din("rw_m1t", [2, 128, 128])
        blk64_d = din("blk64", [128, 128])
        gb_d = K.dram("gb_scr", [2, 4, 128, T], BF16)
        y_d = K.dram("y_scr", [18, 128, 512], F32)
    ident_d = din("ident", [128, 128])
    out_d = K.dram("out", [BPC, LL, D], F32, kind="ExternalOutput")
    if cfg.get("dbg_x", False):
        dbgx_d = K.dram("dbgx", [KC, 128, T], F32, kind="ExternalOutput")
    aT_d = K.dram("aT_scr", [DFF, T], BF16)
    xs_d = K.dram("x_scr", [KC, 128, T], F32)
    hT_d = K.dram("hT_scr", [KC, 128, T], BF16)
    sz_d = K.dram("sz_scr", [16, 128, 1024], BF16)
    hin_d = K.dram("hin_scr", [16, 128, 1024], BF16)
    xview = xs_d[:, :, :].rearrange("c p t -> p c t")
    hview = hT_d[:, :, :].rearrange("c p t -> p c t")
    xblk = [Buf(xs_d.t, "xblk%d" % j) for j in range(T // 256)]

    def xdeps(t0, t1):
        return xblk[t0 // 256:(t1 + 255) // 256]

    ident = K.sb([128, 128], F32, "ident")
    identb = K.sb([128, 128], BF16, "identb")
    ones_bf = K.sb([128, 128], BF16, "ones")
    ones_f = K.sb([128, 128], F32, "onesf")
    class LayerBuf(Buf):
        def __init__(self, b_):
            Buf.__init__(self, b_.t, b_.name)
            self.l = 0

        def __getitem__(self, idx):
            return self.t[(idx[0], self.l) + tuple(idx[1:])]

    modsT = LayerBuf(K.sb([128, 2, KC, 6, 5], F32, "modsT"))
    Amod = LayerBuf(K.sb([128, 2, KC, 2, 5], F32, "Amod"))
    gcols = K.sb([128, 5, KC], F32, "gcols")
    cwT = K.sb([128, 2, 3, 44], F32, "cwT")
    cbT = K.sb([128, 2, 44], F32, "cbT")
    PS = [K.ps("ps%d" % i) for i in range(8)]
    for e_ in (EPS, 64e-5, 1e-12, 1.0, 0.0):
        K.eps_ap(e_)

    K.dma("sp", ident[:, :], ident_d[:, :], reads=[ident_d], writes=[ident])
    K.copy("dve", identb[:, :], ident[:, :], [ident], [identb])
    K.memset("dve", ones_bf[:, :], 1.0, [ones_bf])
    K.memset("dve", ones_f[:, :], 1.0, [ones_f])
    for l in range(2):
        K.dma("sp", gcols[:, l, :], colvec(nmix_d[l, :], KC), reads=[nmix_d], writes=[gcols], allow_slow_non_contiguous=True)
        K.dma("sp", gcols[:, 2 + l, :], colvec(nffn_d[l, :], KC), reads=[nffn_d], writes=[gcols], allow_slow_non_contiguous=True)
        for tap in range(3):
            K.dma("sp", cwT[:, l, tap, :], colvec(cw_d[l, tap, :], 44), reads=[cw_d], writes=[cwT], allow_slow_non_contiguous=True)
        K.dma("sp", cbT[:, l, :], colvec(cb_d[l, :], 44), reads=[cb_d], writes=[cbT], allow_slow_non_contiguous=True)
    K.dma("sp", gcols[:, 4, :], colvec(fng_d[:], KC), reads=[fng_d], writes=[gcols], allow_slow_non_contiguous=True)

    def stage_mods(l):
        modsT.l = l
        Amod.l = l
        with ES() as st:
            condT = K.sb([128, KC, 5], F32, "condT", st)
            scond = K.sb([128, KC, 5], F32, "scond", st)
            abT = K.sb([128, 48], F32, "abT", st)
            wb = [K.sb([128, KC, 128], F32, "adaw", st) for _ in range(3)]
            for r in range(4):
                K.dma("sp", condT[:, :, r], colvec(c_d[r, :], KC), reads=[c_d], writes=[condT], allow_slow_non_contiguous=True)
            K.dma("sp", condT[:, :, 4], colvec(cctx_d[:], KC), reads=[cctx_d], writes=[condT], allow_slow_non_contiguous=True)
            K.dma("sp", abT[:, :], colvec(ada_b_d[l, :], 48), reads=[ada_b_d], writes=[abT], allow_slow_non_contiguous=True)
            K.act(scond[:, :, :], condT[:, :, :], AF.Silu, [condT], [scond])
            wview = ada_w_d[l, :, :].rearrange("(kc p) n -> p kc n", p=128)
            for j in range(48):
                w = wb[j % 3]
                K.dma("sp", w[:, :, :], wview[:, :, j * 128:(j + 1) * 128], reads=[ada_w_d], writes=[w])
                pb = PS[j % 2]
                for kc in range(KC):
                    K.mm(pb[:, 0:5], w[:, kc, :], scond[:, kc, :], kc == 0, kc == KC - 1, [w, scond], [pb])
                kind, cc = j // 8, j % 8
                K.ts("dve", modsT[:, cc, kind, :], pb[:, 0:5], abT[:, j:j + 1], None, ALU.add, None, [pb, abT], [modsT])
            for which, kind, gi in ((0, 1, l), (1, 4, 2 + l)):
                for cc in range(KC):
                    K.ts("dve", Amod[:, cc, which, :], modsT[:, cc, kind, :], 1.0, gcols[:, gi, cc:cc + 1],
                         ALU.add, ALU.mult, [modsT, gcols], [Amod])
        K.barrier()

    def stage_load(b):
        with ES() as st:
            xin = [K.sb([128, D], F32, "xin", st) for _ in range(3)]
            xo = [K.sb([128, KC, 128], F32, "xo", st) for _ in range(3)]
            for ti in range(T // 128):
                xb = xin[ti % 3]
                if ti < 2:
                    src, sb_ = ctx_d[b, ti * 128:(ti + 1) * 128, :], ctx_d
                else:
                    src, sb_ = x_d[b, (ti - 2) * 128:(ti - 1) * 128, :], x_d
                K.dma("sp", xb[:, :], src, reads=[sb_], writes=[xb])
                o = xo[ti % 3]
                for half in range(2):
                    pb = PS[(2 * ti + half) % 4]
                    for q in range(4):
                        cc = half * 4 + q
                        K.transpose(pb[:, q * 128:(q + 1) * 128], xb[:, cc * 128:(cc + 1) * 128], ident[:, :], [xb, ident], [pb])
                    K.copy("act" if half == 0 else "dve", o[:, half * 4:half * 4 + 4, :],
                           pb[:, :].rearrange("p (q t) -> p q t", q=4), [pb], [o])
                K.dma("pool", xview[:, :, ti * 128:(ti + 1) * 128], o[:, :, :], reads=[o], writes=xdeps(ti * 128, ti * 128 + 128))
        K.barrier()

    def stage_norm(b, which, hT, to_dram=False, lo=0):
        shift_kind = 0 if which == 0 else 3
        with ES() as st:
            xb = [K.sb([128, KC, 512], F32, "nxb", st) for _ in range(2)]
            sq = [K.sb([128, 512], BF16, "sq", st) for _ in range(3)]
            rstd = [K.sb([128, 512], F32, "rstd", st) for _ in range(2)]
            tmp = [K.sb([128, 512], F32, "ntmp", st) for _ in range(3)]
            for bi, (t0, t1) in enumerate(BLOCKS):
                if t1 <= lo:
                    continue
                n = t1 - t0
                row = 4 if bi == 0 else b
                x_ = xb[bi % 2]
                K.dma("sp", x_[:, :, 0:n], xview[:, :, t0:t1], reads=xdeps(t0, t1), writes=[x_])
                pb = PS[bi % 2]
                for cc in range(KC):
                    s = sq[cc % 3]
                    K.act(s[:, 0:n], x_[:, cc, 0:n], AF.Square, [x_], [s])
                    K.mm(pb[:, 0:n], ones_bf[:, :], s[:, 0:n], cc == 0, cc == KC - 1, [ones_bf, s], [pb])
                r = rstd[bi % 2]
                K.rsqrt(r[:, 0:n], pb[:, 0:n], 1.0 / D, EPS, [pb], [r])
                for cc in range(KC):
                    tm = tmp[cc % 3]
                    K.tt("dve" if cc % 2 == 0 else "pool", tm[:, 0:n], x_[:, cc, 0:n], r[:, 0:n], ALU.mult, [x_, r], [tm])
                    K.act(hT[:, cc, t0:t1], tm[:, 0:n], AF.Identity, [tm, Amod, modsT], [hT],
                          bias=modsT[:, cc, shift_kind, row:row + 1], scale=Amod[:, cc, which, row:row + 1])
            if to_dram:
                for cc in range(KC):
                    K.dma("pool", hT_d[cc, :, :], hT[:, cc, :], reads=[hT], writes=[hT_d])
        K.barrier()

    def load_hT(hT):
        for cc in range(KC):
            K.dma("sp", hT[:, cc, :], hT_d[cc, :, :], reads=[hT_d], writes=[hT])

    def apply_out(b, pieces, gate_kind, tok_lo):
        with ES() as st:
            xb = [K.sb([128, KC, 256], F32, "uxb", st) for _ in range(2)]
            for bi, t0 in enumerate(range(tok_lo, T, 256)):
                t1 = t0 + 256
                row = 4 if t0 < LC else b
                x_ = xb[bi % 2]
                K.dma("sp", x_[:, :, :], xview[:, :, t0:t1], reads=xdeps(t0, t1), writes=[x_])
                for oc in range(KC):
                    pb = PS[oc % 4]
                    for pi, (lf, rf, rd) in enumerate(pieces):
                        K.mm(pb[:, 0:256], lf(oc), rf(t0, t1), pi == 0, pi == len(pieces) - 1, rd, [pb])
                    K.stt("dve", x_[:, oc, :], pb[:, 0:256], modsT[:, oc, gate_kind, row:row + 1], x_[:, oc, :],
                          ALU.mult, ALU.add, [pb, modsT, x_], [x_])
                K.dma("pool", xview[:, :, t0:t1], x_[:, :, :], reads=[x_], writes=xdeps(t0, t1))

    def load_w_bf16(dst_ap, src_ap, shape, src_buf, dst_buf, st_bufs, idx, eng="pool"):
        s = st_bufs[idx % len(st_bufs)]
        v = stg_view(s, shape)
        K.dma("sp", v, src_ap, reads=[src_buf], writes=[s])
        K.copy(eng, dst_ap, v, [s], [dst_buf])

    def stage_ffn(b, l, do_ctx, hT):
        wupv = wup_d[l, :, :].rearrange("(kc p) n -> p kc n", p=128)
        segs = [(LC, LL)] + ([(0, LC)] if do_ctx else [])
        blocks = [bl for bl in BLOCKS if (do_ctx or bl[0] >= LC)]
        PADW = T + 4
        with ES() as st:
            wst = [K.sb([128, KC, 256], F32, "wst", st) for _ in range(2)]
            wbf = [K.sb([128, KC, 256], BF16, "wbf", st) for _ in range(2)]
            ug = [K.sb([128, PADW], F32, "ug", st) for _ in range(2)]
            uv = [K.sb([128, PADW], F32, "uv", st) for _ in range(2)]
            cg = K.sb([128, T], F32, "cg", st)
            cv = K.sb([128, T], F32, "cv", st)
            ao = [K.sb([128, T], BF16, "ao", st) for _ in range(2)]
            for u in ug + uv:
                K.memset("pool", u[:, :], 0.0, [u])

            def pad_off(t):
                return t + 1 if t < LC else t + 3

            for fc in range(22):
                ws, wb_ = wst[fc % 2], wbf[fc % 2]
                K.dma("sp", ws[:, :, 0:128], wupv[:, :, fc * 128:(fc + 1) * 128], reads=[wup_d], writes=[ws])
                K.dma("sp", ws[:, :, 128:256], wupv[:, :, DFF + fc * 128:DFF + (fc + 1) * 128], reads=[wup_d], writes=[ws])
                K.copy("pool", wb_[:, :, :], ws[:, :, :], [ws], [wb_])
                g_, v_ = ug[fc % 2], uv[fc % 2]
                for bi, (t0, t1) in enumerate(blocks):
                    n = t1 - t0
                    for half, dst in ((0, g_), (1, v_)):
                        pb = PS[(bi * 2 + half) % 4]
                        for kc in range(KC):
                            K.mm(pb[:, 0:n], wb_[:, kc, half * 128:(half + 1) * 128], hT[:, kc, t0:t1],
                                 kc == 0, kc == KC - 1, [wb_, hT], [pb])
                        K.copy("act", dst[:, pad_off(t0):pad_off(t0) + n], pb[:, 0:n], [pb], [dst])
                ao_ = ao[fc % 2]
                for half, src, dst, ch in ((0, g_, cg, fc), (1, v_, cv, 22 + fc)):
                    for (s0, ln) in segs:
                        p0 = pad_off(s0)
                        e1 = "dve" if half == 0 else "pool"
                        K.ts(e1, dst[:, s0:s0 + ln], src[:, p0:p0 + ln], cwT[:, l, 1, ch:ch + 1], cbT[:, l, ch:ch + 1],
                             ALU.mult, ALU.add, [src, cwT, cbT], [dst])
                        K.stt("dve", dst[:, s0:s0 + ln], src[:, p0 - 1:p0 - 1 + ln], cwT[:, l, 0, ch:ch + 1], dst[:, s0:s0 + ln],
                              ALU.mult, ALU.add, [src, cwT, dst], [dst])
                        K.stt("dve", dst[:, s0:s0 + ln], src[:, p0 + 1:p0 + 1 + ln], cwT[:, l, 2, ch:ch + 1], dst[:, s0:s0 + ln],
                              ALU.mult, ALU.add, [src, cwT, dst], [dst])
                for (s0, ln) in segs:
                    K.act(cg[:, s0:s0 + ln], cg[:, s0:s0 + ln], AF.Silu, [cg], [cg])
                    K.tt("pool", ao_[:, s0:s0 + ln], cg[:, s0:s0 + ln], cv[:, s0:s0 + ln], ALU.mult, [cg, cv], [ao_])
                lo = 0 if do_ctx else LC
                K.dma("pool", aT_d[fc * 128:(fc + 1) * 128, lo:T], ao_[:, lo:T], reads=[ao_], writes=[aT_d])
        K.barrier()
        wdv = wdn_d[l, :, :].rearrange("(kc p) n -> p kc n", p=128)
        aTv = aT_d[:, :].rearrange("(kc p) t -> p kc t", p=128)
        with ES() as st:
            wd = K.sb([128, 22, D], BF16, "wd", st)
            wds = [K.sb([128, 2 * D], F32, "wds", st) for _ in range(2)]
            ab = [K.sb([128, 22, 256], BF16, "ab", st) for _ in range(2)]
            for j in range(11):
                load_w_bf16(wd[:, 2 * j:2 * j + 2, :], wdv[:, 2 * j:2 * j + 2, :], (128, 2, D), wdn_d, wd, wds, j)
            cnt = [0]

            def rhs_fn(t0, t1):
                return ab[cnt[0] % 2]

            lo = 0 if do_ctx else LC
            with ES() as st2:
                xb = [K.sb([128, KC, 256], F32, "uxb", st2) for _ in range(2)]
                for bi, t0 in enumerate(range(lo, T, 256)):
                    t1 = t0 + 256
                    row = 4 if t0 < LC else b
                    a_ = ab[bi % 2]
                    x_ = xb[bi % 2]
                    K.dma("sp", a_[:, :, :], aTv[:, :, t0:t1], reads=[aT_d], writes=[a_])
                    K.dma("sp", x_[:, :, :], xview[:, :, t0:t1], reads=xdeps(t0, t1), writes=[x_])
                    for oc in range(KC):
                        pb = PS[oc % 4]
                        for kc in range(22):
                            K.mm(pb[:, 0:256], wd[:, kc, oc * 128:(oc + 1) * 128], a_[:, kc, :], kc == 0, kc == 21, [wd, a_], [pb])
                        K.stt("dve", x_[:, oc, :], pb[:, 0:256], modsT[:, oc, 5, row:row + 1], x_[:, oc, :],
                              ALU.mult, ALU.add, [pb, modsT, x_], [x_])
                    K.dma("pool", xview[:, :, t0:t1], x_[:, :, :], reads=[x_], writes=xdeps(t0, t1))
        K.barrier()

    def stage_out(b):
        with ES() as st:
            xb = [K.sb([128, KC, 512], F32, "oxb", st) for _ in range(2)]
            sq = [K.sb([128, 512], BF16, "sq", st) for _ in range(3)]
            rstd = [K.sb([128, 512], F32, "rstd", st) for _ in range(2)]
            yT = [K.sb([128, KC, 512], F32, "yT", st) for _ in range(2)]
            ob = [K.sb([128, D], F32, "ob", st) for _ in range(3)]
            for bi, (t0, t1) in enumerate(BLOCKS[1:]):
                n = t1 - t0
                x_ = xb[bi % 2]
                K.dma("sp", x_[:, :, 0:n], xview[:, :, t0:t1], reads=xdeps(t0, t1), writes=[x_])
                pb = PS[bi % 2]
                for cc in range(KC):
                    s = sq[cc % 3]
                    K.act(s[:, 0:n], x_[:, cc, 0:n], AF.Square, [x_], [s])
                    K.mm(pb[:, 0:n], ones_bf[:, :], s[:, 0:n], cc == 0, cc == KC - 1, [ones_bf, s], [pb])
                r = rstd[bi % 2]
                K.rsqrt(r[:, 0:n], pb[:, 0:n], 1.0 / D, EPS, [pb], [r])
                y = yT[bi % 2]
                for cc in range(KC):
                    K.stt("dve", y[:, cc, 0:n], x_[:, cc, 0:n], gcols[:, 4, cc:cc + 1], r[:, 0:n],
                          ALU.mult, ALU.mult, [x_, gcols, r], [y])
                for ti in range(n // 128):
                    o = ob[ti % 3]
                    for half in range(2):
                        pb2 = PS[2 + (2 * ti + half) % 4]
                        for q in range(4):
                            cc = half * 4 + q
                            K.transpose(pb2[:, q * 128:(q + 1) * 128], y[:, cc, ti * 128:(ti + 1) * 128], ident[:, :], [y, ident], [pb2])
                        K.copy("act", o[:, half * 512:(half + 1) * 512], pb2[:, :], [pb2], [o])
                    tok = t0 - LC + ti * 128
                    K.dma("pool", out_d[b, tok:tok + 128, :], o[:, :], reads=[o], writes=[out_d])
        K.barrier()

    def stage_swa(b, hT):
        win = cdin_d[0, :, :].rearrange("(kc p) n -> p kc n", p=128)
        with ES() as st:
            attT = K.sb([64, 8, LL], BF16, "attT", st)
            wo = K.sb([64, 8, D], BF16, "wo_att", st)
            with ES() as st1:
                wstg = [K.sb([128, 4096], F32, "wstg", st1)]
                wq = K.sb([128, KC, 512], BF16, "wq", st1)
                wqs = K.sb([128, KC, 512], BF16, "wqs", st1)
                wk = K.sb([128, KC, 128], BF16, "wk", st1)
                wks = K.sb([128, KC, 128], BF16, "wks", st1)
                wv = K.sb([128, KC, 128], BF16, "wv", st1)
                cosT = K.sb([64, LL], F32, "cosT", st1)
                sinT = K.sb([64, LL], F32, "sinT", st1)
                qT = K.sb([64, 8, LL], BF16, "qT", st1)
                kT = K.sb([64, 2, T], BF16, "kT", st1)
                vtok = K.sb([128, 18, 128], BF16, "vtok", st1)
                esink = K.sb([64, 8], F32, "esink", st1)
                maskP = K.sb([128, 128], F32, "maskP", st1)
                maskN = K.sb([128, 128], F32, "maskN", st1)
                t1b = [K.sb([64, 512], F32, "rt1", st1) for _ in range(2)]
                t2b = [K.sb([64, 512], F32, "rt2", st1) for _ in range(2)]
                pT = [K.sb([128, 512], BF16, "pT", st1) for _ in range(3)]
                dsum = [K.sb([64, 512], F32, "dsum", st1) for _ in range(2)]
                K.dma("sp", cosT[:, :], swacos_d[:, :], reads=[swacos_d], writes=[cosT])
                K.dma("sp", sinT[:, :], swasin_d[:, :], reads=[swasin_d], writes=[sinT])
                K.dma("sp", maskP[:, :], triB_d[:, :], reads=[triB_d], writes=[maskP])
                K.dma("sp", maskN[:, :], triF_d[:, :], reads=[triF_d], writes=[maskN])
                K.dma("sp", esink[:, :], sink_d[0, :].partition_broadcast(64), reads=[sink_d], writes=[esink])
                K.act(esink[:, :], esink[:, :], AF.Exp, [esink], [esink])
                load_w_bf16(wq[:, :, :], win[:, :, CD_Q:CD_Q + 512], (128, KC, 512), cdin_d, wq, wstg, 0)
                load_w_bf16(wk[:, :, :], win[:, :, CD_K:CD_K + 128], (128, KC, 128), cdin_d, wk, wstg, 0)
                load_w_bf16(wv[:, :, :], win[:, :, CD_V:CD_V + 128], (128, KC, 128), cdin_d, wv, wstg, 0)
                for (w_, ws_, nh) in ((wq, wqs, 64), (wk, wks, 16)):
                    wv4 = w_[:, :, :].rearrange("p k (h two d) -> p (k h) two d", two=2, d=32)
                    ws4 = ws_[:, :, :].rearrange("p k (h two d) -> p (k h) two d", two=2, d=32)
                    K.ts("pool", ws4[:, :, 0, :], wv4[:, :, 1, :], -1.0, None, ALU.mult, None, [w_], [ws_])
                    K.copy("pool", ws4[:, :, 1, :], wv4[:, :, 0, :], [w_], [ws_])
                wov = cdout_d[0, 1024:1536, :].rearrange("(h d) n -> d h n", d=64)
                for j in range(2):
                    load_w_bf16(wo[:, j * 4:(j + 1) * 4, :], wov[:, j * 4:(j + 1) * 4, :], (64, 4, D), cdout_d, wo, wstg, 0)
                cnt = 0
                for h in range(8):
                    for j in range(4):
                        t0 = LC + j * 512
                        pa, pb = PS[(cnt * 2) % 4], PS[(cnt * 2 + 1) % 4]
                        for kc in range(KC):
                            K.mm(pa[0:64, :], wq[:, kc, h * 64:(h + 1) * 64], hT[:, kc, t0:t0 + 512], kc == 0, kc == KC - 1, [wq, hT], [pa])
                        for kc in range(KC):
                            K.mm(pb[0:64, :], wqs[:, kc, h * 64:(h + 1) * 64], hT[:, kc, t0:t0 + 512], kc == 0, kc == KC - 1, [wqs, hT], [pb])
                        a_, b_ = t1b[cnt % 2], t2b[cnt % 2]
                        K.tt("dve", a_[:, :], pa[0:64, :], cosT[:, j * 512:(j + 1) * 512], ALU.mult, [pa, cosT], [a_])
                        K.tt("dve", b_[:, :], pb[0:64, :], sinT[:, j * 512:(j + 1) * 512], ALU.mult, [pb, sinT], [b_])
                        K.tt("pool", qT[:, h, j * 512:(j + 1) * 512], a_[:, :], b_[:, :], ALU.add, [a_, b_], [qT])
                        cnt += 1
                for g in range(2):
                    pa = PS[cnt % 4]
                    for kc in range(KC):
                        K.mm(pa[0:64, 0:LC], wk[:, kc, g * 64:(g + 1) * 64], hT[:, kc, 0:LC], kc == 0, kc == KC - 1, [wk, hT], [pa])
                    K.copy("act", kT[:, g, 0:LC], pa[0:64, 0:LC], [pa], [kT])
                    cnt += 1
                    for j in range(4):
                        t0 = LC + j * 512
                        pa, pb = PS[(cnt * 2) % 4], PS[(cnt * 2 + 1) % 4]
                        for kc in range(KC):
                            K.mm(pa[0:64, :], wk[:, kc, g * 64:(g + 1) * 64], hT[:, kc, t0:t0 + 512], kc == 0, kc == KC - 1, [wk, hT], [pa])
                        for kc in range(KC):
                            K.mm(pb[0:64, :], wks[:, kc, g * 64:(g + 1) * 64], hT[:, kc, t0:t0 + 512], kc == 0, kc == KC - 1, [wks, hT], [pb])
                        a_, b_ = t1b[cnt % 2], t2b[cnt % 2]
                        K.tt("dve", a_[:, :], pa[0:64, :], cosT[:, j * 512:(j + 1) * 512], ALU.mult, [pa, cosT], [a_])
                        K.tt("dve", b_[:, :], pb[0:64, :], sinT[:, j * 512:(j + 1) * 512], ALU.mult, [pb, sinT], [b_])
                        K.tt("pool", kT[:, g, t0:t0 + 512], a_[:, :], b_[:, :], ALU.add, [a_, b_], [kT])
                        cnt += 1
                for ti in range(18):
                    pa = PS[ti % 4]
                    for kc in range(KC):
                        K.mm(pa[:, 0:128], hT[:, kc, ti * 128:(ti + 1) * 128], wv[:, kc, :], kc == 0, kc == KC - 1, [hT, wv], [pa])
                    K.copy("act", vtok[:, ti, :], pa[:, 0:128], [pa], [vtok])
                u = 0
                for i in range(16):
                    for g in range(2):
                        keys = [(0, None), (1, None)]
                        if i > 0:
                            keys.append((2 + i - 1, maskP))
                        keys.append((2 + i, None))
                        if i < 15:
                            keys.append((2 + i + 1, maskN))
                        pnum, pden = PS[4 + (u % 2) * 2], PS[5 + (u % 2) * 2]
                        for ki, (kt, mask) in enumerate(keys):
                            psc = PS[(u * 5 + ki) % 4]
                            K.mm(psc[:, :].rearrange("p (h q) -> p h q", h=4), kT[:, g, kt * 128:(kt + 1) * 128],
                                 qT[:, g * 4:(g + 1) * 4, i * 128:(i + 1) * 128], True, True, [kT, qT], [psc])
                            p_ = pT[(u * 5 + ki) % 3]
                            K.act(p_[:, :], psc[:, :], AF.Exp, [psc], [p_], scale=0.125)
                            if mask is not None:
                                K.tt("pool", p_[:, :].rearrange("p (h q) -> p h q", h=4), p_[:, :].rearrange("p (h q) -> p h q", h=4),
                                     mask[:, :].unsqueeze(1).to_broadcast([128, 4, 128]), ALU.mult, [p_, mask], [p_])
                            K.mm(pnum[0:64, :], vtok[:, kt, g * 64:(g + 1) * 64], p_[:, :], ki == 0, ki == len(keys) - 1, [vtok, p_], [pnum])
                            K.mm(pden[0:64, :], ones_bf[:, 0:64], p_[:, :], ki == 0, ki == len(keys) - 1, [ones_bf, p_], [pden])
                        d_ = dsum[u % 2]
                        K.tt("dve", d_[:, :].rearrange("p (h q) -> p h q", h=4), pden[0:64, :].rearrange("p (h q) -> p h q", h=4),
                             esink[:, g * 4:(g + 1) * 4].unsqueeze(2).to_broadcast([64, 4, 128]), ALU.add, [pden, esink], [d_])
                        K.op("dve", lambda e, d_=d_: e.reciprocal(out=d_[:, :], in_=d_[:, :]), [d_], [d_])
                        K.tt("dve", attT[:, g * 4:(g + 1) * 4, i * 128:(i + 1) * 128], pnum[0:64, :].rearrange("p (h q) -> p h q", h=4),
                             d_[:, :].rearrange("p (h q) -> p h q", h=4), ALU.mult, [pnum, d_], [attT])
                        u += 1
            K.barrier()
            pieces = [((lambda oc, h=h: wo[:, h, oc * 128:(oc + 1) * 128]),
                       (lambda t0, t1, h=h: attT[:, h, t0 - LC:t1 - LC]), [wo, attT]) for h in range(8)]
            apply_out(b, pieces, 2, LC)
        K.barrier()

    def stage_ssd(b, hT_scope_fn):
        win = cdin_d[0, :, :].rearrange("(kc p) n -> p kc n", p=128)
        with ES() as st:
            uT = K.sb([128, 8, LL], BF16, "uT", st)
            with ES() as st1:
                xs_tok = K.sb([128, 18, 1024], BF16, "xs_tok", st1)
                B_tok = K.sb([128, 18, 256], BF16, "B_tok", st1)
                BCT = K.sb([128, 4, T], BF16, "BCT", st1)
                dtv = K.sb([128, 18, 32], F32, "dtv", st1)
                dtA = K.sb([128, 18, 32], F32, "dtA", st1)
                a_bc = K.sb([128, 32], F32, "a_bc", st1)
                dtb_bc = K.sb([128, 32], F32, "dtb_bc", st1)
                D_bc = K.sb([128, 16], F32, "D_bc", st1)
                sng = K.sb([128, 8], F32, "sng", st1)
                scw = K.sb([128, 5, 12], F32, "scw", st1)
                scb = K.sb([128, 12], F32, "scb", st1)
                triF = K.sb([128, 128], F32, "triF", st1)
                triB = K.sb([128, 128], F32, "triB", st1)
                strF = K.sb([128, 128], F32, "strF", st1)
                strB = K.sb([128, 128], F32, "strB", st1)
                for (dst, src) in ((triF, triF_d), (triB, triB_d), (strF, strF_d), (strB, strB_d)):
                    K.dma("sp", dst[:, :], src[:, :], reads=[src], writes=[dst])
                K.dma("sp", a_bc[:, :], salog_d[0, :, :].rearrange("a b -> (a b)").partition_broadcast(128), reads=[salog_d], writes=[a_bc])
                K.act(a_bc[:, :], a_bc[:, :], AF.Exp, [a_bc], [a_bc])
                K.ts("dve", a_bc[:, :], a_bc[:, :], -1.0, None, ALU.mult, None, [a_bc], [a_bc])
                K.dma("sp", dtb_bc[:, :], sdtb_d[0, :, :].rearrange("a b -> (a b)").partition_broadcast(128), reads=[sdtb_d], writes=[dtb_bc])
                K.dma("sp", D_bc[:, :], sd_d[0, :].partition_broadcast(128), reads=[sd_d], writes=[D_bc])
                K.dma("sp", sng[:, :], colvec(sng_d[0, :], 8), reads=[sng_d], writes=[sng], allow_slow_non_contiguous=True)
                for tap in range(5):
                    K.dma("sp", scw[:, tap, :], colvec(scw_d[0, tap, :], 12), reads=[scw_d], writes=[scw], allow_slow_non_contiguous=True)
                K.dma("sp", scb[:, :], colvec(scb_d[0, :], 12), reads=[scb_d], writes=[scb], allow_slow_non_contiguous=True)
                with ES() as st2:
                    hT = K.sb([128, KC, T], BF16, "hT", st2)
                    load_hT(hT)
                    wstg = [K.sb([128, 4096], F32, "wstg", st2)]
                    st3 = ES()
                    wch = [K.sb([128, KC, 128], BF16, "wch", st3) for _ in range(2)]
                    PW = T + 8
                    upad = [K.sb([128, PW], F32, "upad", st3) for _ in range(2)]
                    cvb = [K.sb([128, T], F32, "cvb", st3) for _ in range(1)]
                    xsT = [K.sb([128, T], BF16, "xsT", st3) for _ in range(2)]
                    for u_ in upad:
                        K.memset("pool", u_[:, :], 0.0, [u_])

                    def poff(t):
                        return t + 2 if t < LC else t + 6

                    for fc in range(12):
                        w_ = wch[fc % 2]
                        load_w_bf16(w_[:, :, :], win[:, :, CD_XBC + fc * 128:CD_XBC + (fc + 1) * 128], (128, KC, 128), cdin_d, w_, wstg, fc)
                        up = upad[fc % 2]
                        for bi, (t0, t1) in enumerate(BLOCKS):
                            n = t1 - t0
                            pb = PS[bi % 4]
                            for kc in range(KC):
                                K.mm(pb[:, 0:n], w_[:, kc, :], hT[:, kc, t0:t1], kc == 0, kc == KC - 1, [w_, hT], [pb])
                            K.copy("act", up[:, poff(t0):poff(t0) + n], pb[:, 0:n], [pb], [up])
                        cv_ = cvb[0]
                        for (s0, ln) in ((0, LC), (LC, LL)):
                            p0 = poff(s0)
                            K.ts("pool", cv_[:, s0:s0 + ln], up[:, p0:p0 + ln], scw[:, 2, fc:fc + 1], scb[:, fc:fc + 1],
                                 ALU.mult, ALU.add, [up, scw, scb], [cv_])
                            for tap in (0, 1, 3, 4):
                                K.stt("dve", cv_[:, s0:s0 + ln], up[:, p0 + tap - 2:p0 + tap - 2 + ln], scw[:, tap, fc:fc + 1], cv_[:, s0:s0 + ln],
                                      ALU.mult, ALU.add, [up, scw, cv_], [cv_])
                        if fc < 8:
                            dstT, dst_ap = xsT[fc % 2], xsT[fc % 2][:, :]
                        else:
                            dstT, dst_ap = BCT, BCT[:, fc - 8, :]
                        K.act(dst_ap, cv_[:, :], AF.Silu, [cv_], [dstT])
                        if fc < 10:
                            for grp in range(3):
                                tis = list(range(grp * 8, min(18, grp * 8 + 8)))
                                pb = PS[4 + grp % 2]
                                pbv = pb[:, :].bitcast(BF16)
                                for qi, ti in enumerate(tis):
                                    K.transpose(pbv[:, qi * 128:(qi + 1) * 128], dst_ap[:, ti * 128:(ti + 1) * 128], identb[:, :], [dstT, identb], [pb])
                                nt = len(tis)
                                src_v = pbv[:, 0:nt * 128].rearrange("p (a f) -> p a f", a=nt)
                                if fc < 8:
                                    K.copy("act", xs_tok[:, tis[0]:tis[0] + nt, fc * 128:(fc + 1) * 128], src_v, [pb], [xs_tok])
                                else:
                                    K.copy("act", B_tok[:, tis[0]:tis[0] + nt, (fc - 8) * 128:(fc - 7) * 128], src_v, [pb], [B_tok])
                    K.barrier()
                    st3.close()
                    wz = K.sb([128, KC, 1024], BF16, "wz", st2)
                    wdt = K.sb([128, KC, 32], BF16, "wdt", st2)
                    szb = [K.sb([128, 1024], BF16, "szb", st2) for _ in range(2)]
                    dtt = [K.sb([128, 32], F32, "dtt", st2) for _ in range(2)]
                    for j in range(2):
                        load_w_bf16(wz[:, :, j * 512:(j + 1) * 512], win[:, :, CD_Z + j * 512:CD_Z + (j + 1) * 512], (128, KC, 512), cdin_d, wz, wstg, j)
                    load_w_bf16(wdt[:, :, :], win[:, :, CD_DT:CD_DT + 32], (128, KC, 32), cdin_d, wdt, wstg, 0)
                    for ti in range(18):
                        pb = PS[ti % 4]
                        for kc in range(KC):
                            K.mm(pb[:, 0:32], hT[:, kc, ti * 128:(ti + 1) * 128], wdt[:, kc, :], kc == 0, kc == KC - 1, [hT, wdt], [pb])
                        d_ = dtt[ti % 2]
                        K.tt("dve", d_[:, :], pb[:, 0:32], dtb_bc[:, :], ALU.add, [pb, dtb_bc], [d_])
                        K.act(d_[:, :], d_[:, :], AF.Exp, [d_], [d_])
                        K.act(dtv[:, ti, :], d_[:, :], AF.Ln, [d_], [dtv], bias=K.eps_ap(1.0))
                    K.tt("dve", dtA[:, :, :], dtv[:, :, :], a_bc[:, :].unsqueeze(1).to_broadcast([128, 18, 32]), ALU.mult, [dtv, a_bc], [dtA])
                    for li in range(16):
                        ti = li + 2
                        s_ = szb[li % 2]
                        for j in range(2):
                            pb = PS[(li * 2 + j) % 4]
                            for kc in range(KC):
                                K.mm(pb[:, :], hT[:, kc, ti * 128:(ti + 1) * 128], wz[:, kc, j * 512:(j + 1) * 512], kc == 0, kc == KC - 1, [hT, wz], [pb])
                            K.act(s_[:, j * 512:(j + 1) * 512], pb[:, :], AF.Silu, [pb], [s_])
                        K.dma("pool", sz_d[li, :, :], s_[:, :], reads=[s_], writes=[sz_d])
                K.barrier()
                with ES() as st2:
                    Hf = K.sb([128, 2, 512], F32, "Hf", st2)
                    Hb = K.sb([128, 2, 512], F32, "Hb", st2)
                    hbf = [K.sb([128, 1024], BF16, "hbf", st2) for _ in range(2)]
                    hinf = [K.sb([128, 1024], BF16, "hinf", st2) for _ in range(2)]
                    szt = [K.sb([128, 1024], BF16, "szt", st2) for _ in range(2)]
                    prep = {}
                    for nm in ("acs0", "eac0", "dend0", "cdec0", "wgt0", "acs1", "eac1", "dend1", "cdec1", "wgt1"):
                        prep[nm] = K.sb([128, 16], F32, nm, st2)
                    xte = K.sb([128, 1024], BF16, "xte", st2)
                    rseg = K.sb([128, 16, 128], F32, "rseg", st2)
                    cbm = [K.sb([128, 2, 128], F32, "cbm", st2) for _ in range(2)]
                    eseg = [K.sb([128, 512], F32, "eseg", st2) for _ in range(2)]
                    Lt = [K.sb([128, 16, 128], BF16, "Lt", st2) for _ in range(2)]
                    xdt = [K.sb([128, 1024], BF16, "xdt", st2) for _ in range(2)]
                    yacc = K.sb([128, 1024], F32, "yacc", st2)
                    ytmp = K.sb([128, 512], F32, "ytmp", st2)
                    ub = K.sb([128, 1024], F32, "ub", st2)
                    ubf = K.sb([128, 1024], BF16, "ubf", st2)
                    ssq = K.sb([128, 2], F32, "ssq", st2)
                    junk = K.sb([128, 512], BF16, "junk", st2)
                    K.memset("dve", Hf[:, :, :], 0.0, [Hf])
                    K.memset("dve", Hb[:, :, :], 0.0, [Hb])
                    tri = (triF, triB)
                    strm = (strF, strB)

                    def do_prep(c, d):
                        pp = PS[0]
                        K.mm(pp[:, 0:16], tri[d][:, :], dtA[:, c, d * 16:(d + 1) * 16], True, True, [tri[d], dtA], [pp])
                        K.mm(pp[:, 16:32], ones_f[:, :], dtA[:, c, d * 16:(d + 1) * 16], True, True, [ones_f, dtA], [pp])
                        acs, eac, dend, cdec, wgt = (prep[n_ + str(d)] for n_ in ("acs", "eac", "dend", "cdec", "wgt"))
                        K.copy("act", acs[:, :], pp[:, 0:16], [pp], [acs])
                        K.act(eac[:, :], pp[:, 0:16], AF.Exp, [pp], [eac])
                        K.act(cdec[:, :], pp[:, 16:32], AF.Exp, [pp], [cdec])
                        K.tt("dve", dend[:, :], pp[:, 16:32], acs[:, :], ALU.subtract, [pp, acs], [dend])
                        K.act(dend[:, :], dend[:, :], AF.Exp, [dend], [dend])
                        K.tt("dve", wgt[:, :], dend[:, :], dtv[:, c, d * 16:(d + 1) * 16], ALU.mult, [dend, dtv], [wgt])

                    def state_update(c, d, H):
                        wgt, cdec = prep["wgt" + str(d)], prep["cdec" + str(d)]
                        K.tt("pool", xte[:, :].rearrange("p (h e) -> p h e", h=16), xs_tok[:, c, :].rearrange("p (h e) -> p h e", h=16),
                             wgt[:, :].unsqueeze(2).to_broadcast([128, 16, 64]), ALU.mult, [xs_tok, wgt], [xte])
                        for g in range(2):
                            pb = PS[6 + g]
                            K.mm(pb[:, :], B_tok[:, c, g * 128:(g + 1) * 128], xte[:, g * 512:(g + 1) * 512], True, True, [B_tok, xte], [pb])
                            hv = H[:, g, :].rearrange("p (h e) -> p h e", h=8)
                            K.tt("pool", hv, hv, cdec[:, g * 8:(g + 1) * 8].unsqueeze(2).to_broadcast([128, 8, 64]), ALU.mult, [H, cdec], [H])
                            K.tt("dve", H[:, g, :], H[:, g, :], pb[:, :], ALU.add, [H, pb], [H])

                    for c in range(18):
                        if c >= 2:
                            hb_ = hbf[c % 2]
                            K.copy("act", hb_[:, :], Hf[:, :, :].rearrange("p g e -> p (g e)"), [Hf], [hb_])
                            K.dma("pool", hin_d[c - 2, :, :], hb_[:, :], reads=[hb_], writes=[hin_d])
                        if c == 17:
                            break
                        do_prep(c, 0)
                        state_update(c, 0, Hf)
                    K.barrier()
                    for c in [1, 0] + list(range(17, 1, -1)):
                        do_prep(c, 1)
                        if c >= 2:
                            li = c - 2
                            do_prep(c, 0)
                            hi_ = hinf[li % 2]
                            sz_ = szt[li % 2]
                            K.dma("sp", hi_[:, :], hin_d[li, :, :], reads=[hin_d], writes=[hi_])
                            K.dma("sp", sz_[:, :], sz_d[li, :, :], reads=[sz_d], writes=[sz_])
                            hb_ = hbf[li % 2]
                            K.copy("act", hb_[:, :], Hb[:, :, :].rearrange("p g e -> p (g e)"), [Hb], [hb_])
                            tsl = slice(c * 128, (c + 1) * 128)
                            pcb = PS[1]
                            for g in range(2):
                                K.mm(pcb[:, g * 128:(g + 1) * 128], BCT[:, g, tsl], BCT[:, 2 + g, tsl], True, True, [BCT], [pcb])
                            for d in range(2):
                                K.tt("dve", cbm[d][:, :, :], pcb[:, 0:256].rearrange("p (g q) -> p g q", g=2),
                                     tri[d][:, :].unsqueeze(1).to_broadcast([128, 2, 128]), ALU.mult, [pcb, tri[d]], [cbm[d]])
                            for d in range(2):
                                K.tt("pool", rseg[:, :, :], tri[d][:, :].unsqueeze(1).to_broadcast([128, 16, 128]),
                                     dtA[:, c, d * 16:(d + 1) * 16].unsqueeze(2).to_broadcast([128, 16, 128]), ALU.mult, [tri[d], dtA], [rseg])
                                for hb4 in range(4):
                                    pseg = PS[2 + hb4 % 2]
                                    K.mm(pseg[:, :], strm[d][:, :], rseg[:, hb4 * 4:(hb4 + 1) * 4, :], True, True, [strm[d], rseg], [pseg])
                                    es_ = eseg[hb4 % 2]
                                    K.act(es_[:, :], pseg[:, :], AF.Exp, [pseg], [es_])
                                    g = hb4 // 2
                                    K.tt("pool", Lt[d][:, hb4 * 4:(hb4 + 1) * 4, :], es_[:, :].rearrange("p (h q) -> p h q", h=4),
                                         cbm[d][:, g, :].unsqueeze(1).to_broadcast([128, 4, 128]), ALU.mult, [es_, cbm[d]], [Lt[d]])
                                K.tt("dve", xdt[d][:, :].rearrange("p (h e) -> p h e", h=16), xs_tok[:, c, :].rearrange("p (h e) -> p h e", h=16),
                                     dtv[:, c, d * 16:(d + 1) * 16].unsqueeze(2).to_broadcast([128, 16, 64]), ALU.mult, [xs_tok, dtv], [xdt[d]])
                            for h in range(16):
                                py = PS[4 + h // 8]
                                col = (h % 8) * 64
                                for d in range(2):
                                    K.mm(py[:, col:col + 64], Lt[d][:, h, :], xdt[d][:, h * 64:(h + 1) * 64], d == 0, d == 1, [Lt[d], xdt[d]], [py])
                            K.tt("pool", yacc[:, :].rearrange("p (h e) -> p h e", h=16), xs_tok[:, c, :].rearrange("p (h e) -> p h e", h=16),
                                 D_bc[:, :].unsqueeze(2).to_broadcast([128, 16, 64]), ALU.mult, [xs_tok, D_bc], [yacc])
                            for g in range(2):
                                K.tt("dve", yacc[:, g * 512:(g + 1) * 512], yacc[:, g * 512:(g + 1) * 512], PS[4 + g][:, :], ALU.add, [yacc, PS[4 + g]], [yacc])
                            for d in range(2):
                                hsrc = hi_ if d == 0 else hb_
                                eac = prep["eac" + str(d)]
                                for g in range(2):
                                    po = PS[6 + g]
                                    K.mm(po[:, :], BCT[:, 2 + g, tsl], hsrc[:, g * 512:(g + 1) * 512], True, True, [BCT, hsrc], [po])
                                    K.tt("dve", ytmp[:, :].rearrange("p (h e) -> p h e", h=8), po[:, :].rearrange("p (h e) -> p h e", h=8),
                                         eac[:, g * 8:(g + 1) * 8].unsqueeze(2).to_broadcast([128, 8, 64]), ALU.mult, [po, eac], [ytmp])
                                    K.tt("pool", yacc[:, g * 512:(g + 1) * 512], yacc[:, g * 512:(g + 1) * 512], ytmp[:, :], ALU.add, [yacc, ytmp], [yacc])
                            K.tt("dve", ub[:, :], yacc[:, :], sz_[:, :], ALU.mult, [yacc, sz_], [ub])
                            K.memset("pool", ssq[:, :], 0.0, [ssq])
                            for g in range(2):
                                K.op("act", lambda e, g=g: e.activation(out=junk[:, :], in_=ub[:, g * 512:(g + 1) * 512], func=AF.Square,
                                                                         accum_out=ssq[:, g:g + 1]), [ub], [junk, ssq])
                            K.rsqrt(ssq[:, :], ssq[:, :], 1.0 / 512, EPS, [ssq], [ssq])
                            for g in range(2):
                                K.ts("dve", ubf[:, g * 512:(g + 1) * 512], ub[:, g * 512:(g + 1) * 512], ssq[:, g:g + 1], None, ALU.mult, None, [ub, ssq], [ubf])
                            pt = PS[1]
                            ptv = pt[:, :].bitcast(BF16)
                            for cc in range(8):
                                K.transpose(ptv[:, cc * 128:(cc + 1) * 128], ubf[:, cc * 128:(cc + 1) * 128], identb[:, :], [ubf, identb], [pt])
                            for cc in range(8):
                                K.act(uT[:, cc, li * 128:(li + 1) * 128], ptv[:, cc * 128:(cc + 1) * 128], AF.Identity, [pt, sng], [uT], scale=sng[:, cc:cc + 1])
                        if c != 2:
                            state_update(c, 1, Hb)
            K.barrier()
            wo = K.sb([128, 8, D], BF16, "wo_ssd", st)
            wostg = [K.sb([128, 4096], F32, "wostg", st)]
            for j in range(2):
                load_w_bf16(wo[:, j * 4:(j + 1) * 4, :], cdout_d[0, 0:1024, :].rearrange("(kc p) n -> p kc n", p=128)[:, j * 4:(j + 1) * 4, :],
                            (128, 4, D), cdout_d, wo, wostg, 0)
            pieces = [((lambda oc, cc=cc: wo[:, cc, oc * 128:(oc + 1) * 128]),
                       (lambda t0, t1, cc=cc: uT[:, cc, t0 - LC:t1 - LC]), [wo, uT]) for cc in range(8)]
            apply_out(b, pieces, 2, LC)
        K.barrier()


    AB_CQ, AB_CKV, AB_KR, AB_RW = 0, 384, 640, 672

    def stage_mla(b, hT):
        win = abin_d[0, :, :].rearrange("(kc p) n -> p kc n", p=128)
        sc = 96.0 ** -0.5
        with ES() as st:
            attT = K.sb([64, 8, T], BF16, "mattT", st)
            wo = K.sb([64, 8, D], BF16, "wo_mla", st)
            with ES() as st1:
                wcq = K.sb([128, KC, 384], BF16, "wcq", st1)
                wckv = K.sb([128, KC, 256], BF16, "wckv", st1)
                wkr = K.sb([128, KC, 32], BF16, "wkr", st1)
                wkrs = K.sb([128, KC, 32], BF16, "wkrs", st1)
                wqu = K.sb([128, 3, 768], BF16, "wqu", st1)
                wqus = K.sb([128, 24, 32], BF16, "wqus", st1)
                wkvu = K.sb([128, 2, 1024], BF16, "wkvu", st1)
                cqn = K.sb([128, 3, T], BF16, "cqn", st1)
                ckvn = K.sb([128, 2, T], BF16, "ckvn", st1)
                krT = K.sb([32, T], BF16, "krT", st1)
                vtok = K.sb([128, 18, 64], BF16, "mvtok", st1)
                cosT = K.sb([32, LL], F32, "mcos", st1)
                sinT = K.sb([32, LL], F32, "msin", st1)
                gq = K.sb([128, 3], F32, "gq", st1)
                gkv = K.sb([128, 2], F32, "gkv", st1)
                qn = K.sb([64, T], BF16, "qn", st1)
                qr = K.sb([32, T], BF16, "qr", st1)
                kn = K.sb([64, T], BF16, "kn", st1)
                sq = [K.sb([128, 512], BF16, "msq", st1) for _ in range(2)]
                rstd = [K.sb([128, 512], F32, "mrstd", st1) for _ in range(2)]
                t1b = [K.sb([32, 512], F32, "mt1", st1) for _ in range(1)]
                t2b = [K.sb([32, 512], F32, "mt2", st1) for _ in range(1)]
                pT = [K.sb([128, 512], BF16, "mpT", st1) for _ in range(3)]
                rec = [K.sb([64, 512], F32, "mrec", st1) for _ in range(1)]
                stw = ES()
                wstg = [K.sb([128, 4096], F32, "wstg", stw)]
                K.dma("sp", cosT[:, :], mlacos_d[:, :], reads=[mlacos_d], writes=[cosT])
                K.dma("sp", sinT[:, :], mlasin_d[:, :], reads=[mlasin_d], writes=[sinT])
                K.dma("sp", gq[:, :], colvec(mqg_d[0, :], 3), reads=[mqg_d], writes=[gq], allow_slow_non_contiguous=True)
                K.dma("sp", gkv[:, :], colvec(mkg_d[0, :], 2), reads=[mkg_d], writes=[gkv], allow_slow_non_contiguous=True)
                load_w_bf16(wcq[:, :, :], win[:, :, AB_CQ:AB_CQ + 384], (128, KC, 384), abin_d, wcq, wstg, 0)
                load_w_bf16(wckv[:, :, :], win[:, :, AB_CKV:AB_CKV + 256], (128, KC, 256), abin_d, wckv, wstg, 0)
                load_w_bf16(wkr[:, :, :], win[:, :, AB_KR:AB_KR + 32], (128, KC, 32), abin_d, wkr, wstg, 0)
                load_w_bf16(wqu[:, :, :], mqu_d[0, :, :].rearrange("(c p) n -> p c n", p=128), (128, 3, 768), mqu_d, wqu, wstg, 0)
                load_w_bf16(wkvu[:, :, :], mkvu_d[0, :, :].rearrange("(c p) n -> p c n", p=128), (128, 2, 1024), mkvu_d, wkvu, wstg, 0)
                wov = about_d[0, 0:512, :].rearrange("(h d) n -> d h n", d=64)
                for j in range(2):
                    load_w_bf16(wo[:, j * 4:(j + 1) * 4, :], wov[:, j * 4:(j + 1) * 4, :], (64, 4, D), about_d, wo, wstg, 0)
                K.ts("pool", wkrs[:, :, 0:16], wkr[:, :, 16:32], -1.0, None, ALU.mult, None, [wkr], [wkrs])
                K.copy("pool", wkrs[:, :, 16:32], wkr[:, :, 0:16], [wkr], [wkrs])
                wq24 = wqu[:, :, :].rearrange("p c (h e) -> p (c h) e", e=96)
                K.ts("pool", wqus[:, :, 0:16], wq24[:, :, 80:96], -1.0, None, ALU.mult, None, [wqu], [wqus])
                K.copy("pool", wqus[:, :, 16:32], wq24[:, :, 64:80], [wqu], [wqus])
                K.barrier()
                stw.close()
                for bi, (t0, t1) in enumerate(BLOCKS):
                    n = t1 - t0
                    for (w_, nch, g_, dst, pbase) in ((wcq, 3, gq, cqn, 0), (wckv, 2, gkv, ckvn, 4)):
                        for c3 in range(nch):
                            pb = PS[pbase + c3]
                            for kc in range(KC):
                                K.mm(pb[:, 0:n], w_[:, kc, c3 * 128:(c3 + 1) * 128], hT[:, kc, t0:t1], kc == 0, kc == KC - 1, [w_, hT], [pb])
                        pst = PS[pbase + 3] if pbase == 0 else PS[pbase + 2]
                        for c3 in range(nch):
                            s_ = sq[c3 % 2]
                            K.act(s_[:, 0:n], PS[pbase + c3][:, 0:n], AF.Square, [PS[pbase + c3]], [s_])
                            K.mm(pst[:, 0:n], ones_bf[:, :], s_[:, 0:n], c3 == 0, c3 == nch - 1, [ones_bf, s_], [pst])
                        r_ = rstd[0 if pbase == 0 else 1]
                        K.rsqrt(r_[:, 0:n], pst[:, 0:n], 1.0 / (nch * 128), EPS, [pst], [r_])
                        for c3 in range(nch):
                            K.stt("dve", dst[:, c3, t0:t1], PS[pbase + c3][:, 0:n], g_[:, c3:c3 + 1], r_[:, 0:n], ALU.mult, ALU.mult,
                                  [PS[pbase + c3], g_, r_], [dst])
                for bi, (t0, t1) in enumerate(BLOCKS):
                    n = t1 - t0
                    pa, pb = PS[0 + 2 * (bi % 2)], PS[1 + 2 * (bi % 2)]
                    for kc in range(KC):
                        K.mm(pa[0:32, 0:n], wkr[:, kc, :], hT[:, kc, t0:t1], kc == 0, kc == KC - 1, [wkr, hT], [pa])
                    if bi == 0:
                        K.copy("dve", krT[:, t0:t1], pa[0:32, 0:n], [pa], [krT])
                    else:
                        for kc in range(KC):
                            K.mm(pb[0:32, 0:n], wkrs[:, kc, :], hT[:, kc, t0:t1], kc == 0, kc == KC - 1, [wkrs, hT], [pb])
                        a_, b_ = t1b[0], t2b[0]
                        K.tt("dve", a_[:, 0:n], pa[0:32, 0:n], cosT[:, t0 - LC:t1 - LC], ALU.mult, [pa, cosT], [a_])
                        K.tt("dve", b_[:, 0:n], pb[0:32, 0:n], sinT[:, t0 - LC:t1 - LC], ALU.mult, [pb, sinT], [b_])
                        K.tt("pool", krT[:, t0:t1], a_[:, 0:n], b_[:, 0:n], ALU.add, [a_, b_], [krT])
                u = 0
                for h in range(8):
                    for ti in range(18):
                        pa = PS[4 + ti % 2]
                        for c3 in range(2):
                            K.mm(pa[:, 0:64], ckvn[:, c3, ti * 128:(ti + 1) * 128], wkvu[:, c3, h * 128 + 64:h * 128 + 128],
                                 c3 == 0, c3 == 1, [ckvn, wkvu], [pa])
                        K.copy("dve", vtok[:, ti, :], pa[:, 0:64], [pa], [vtok])
                    for bi, (t0, t1) in enumerate(BLOCKS):
                        n = t1 - t0
                        pa, pb, pc, pd = PS[0], PS[1], PS[2], PS[3]
                        for c3 in range(3):
                            K.mm(pa[0:64, 0:n], wqu[:, c3, h * 96:h * 96 + 64], cqn[:, c3, t0:t1], c3 == 0, c3 == 2, [wqu, cqn], [pa])
                        K.copy("dve", qn[:, t0:t1], pa[0:64, 0:n], [pa], [qn])
                        for c3 in range(3):
                            K.mm(pb[0:32, 0:n], wqu[:, c3, h * 96 + 64:h * 96 + 96], cqn[:, c3, t0:t1], c3 == 0, c3 == 2, [wqu, cqn], [pb])
                        if bi == 0:
                            K.copy("dve", qr[:, t0:t1], pb[0:32, 0:n], [pb], [qr])
                        else:
                            for c3 in range(3):
                                K.mm(pc[0:32, 0:n], wqus[:, c3 * 8 + h, :], cqn[:, c3, t0:t1], c3 == 0, c3 == 2, [wqus, cqn], [pc])
                            a_, b_ = t1b[0], t2b[0]
                            K.tt("dve", a_[:, 0:n], pb[0:32, 0:n], cosT[:, t0 - LC:t1 - LC], ALU.mult, [pb, cosT], [a_])
                            K.tt("dve", b_[:, 0:n], pc[0:32, 0:n], sinT[:, t0 - LC:t1 - LC], ALU.mult, [pc, sinT], [b_])
                            K.tt("pool", qr[:, t0:t1], a_[:, 0:n], b_[:, 0:n], ALU.add, [a_, b_], [qr])
                        for c3 in range(2):
                            K.mm(pd[0:64, 0:n], wkvu[:, c3, h * 128:h * 128 + 64], ckvn[:, c3, t0:t1], c3 == 0, c3 == 1, [wkvu, ckvn], [pd])
                        K.copy("dve", kn[:, t0:t1], pd[0:64, 0:n], [pd], [kn])
                    for bi, (t0, t1) in enumerate(BLOCKS):
                        n = t1 - t0
                        keys = [0, 1] if bi == 0 else list(range(18))
                        pnum, pden = PS[4 + (u % 2) * 2], PS[5 + (u % 2) * 2]
                        for ki, kt in enumerate(keys):
                            psc = PS[ki % 4]
                            ks = slice(kt * 128, (kt + 1) * 128)
                            K.mm(psc[:, 0:n], kn[:, ks], qn[:, t0:t1], True, False, [kn, qn], [psc])
                            K.mm(psc[:, 0:n], krT[:, ks], qr[:, t0:t1], False, True, [krT, qr], [psc])
                            p_ = pT[ki % 3]
                            K.act(p_[:, 0:n], psc[:, 0:n], AF.Exp, [psc], [p_], scale=sc)
                            K.mm(pnum[0:64, 0:n], vtok[:, kt, :], p_[:, 0:n], ki == 0, ki == len(keys) - 1, [vtok, p_], [pnum])
                            K.mm(pden[0:64, 0:n], ones_bf[:, 0:64], p_[:, 0:n], ki == 0, ki == len(keys) - 1, [ones_bf, p_], [pden])
                        r_ = rec[0]
                        K.op("dve", lambda e, r_=r_, pden=pden, n=n: e.reciprocal(out=r_[:, 0:n], in_=pden[0:64, 0:n]), [pden], [r_])
                        K.tt("dve", attT[:, h, t0:t1], pnum[0:64, 0:n], r_[:, 0:n], ALU.mult, [pnum, r_], [attT])
                        u += 1
            K.barrier()
            pieces = [((lambda oc, h=h: wo[:, h, oc * 128:(oc + 1) * 128]),
                       (lambda t0, t1, h=h: attT[:, h, t0:t1]), [wo, attT]) for h in range(8)]
            apply_out(b, pieces, 2, 0)
        K.barrier()

    CW = -math.exp(-0.5)

    def stage_rwkv(b):
        win = abin_d[0, :, :].rearrange("(kc p) n -> p kc n", p=128)
        with ES() as st:
            rwo = K.sb([128, 4, T], BF16, "rwo", st)
            with ES() as st1:
                rT = K.sb([128, 4, T], BF16, "rT", st1)
                kT = K.sb([128, 4, T], BF16, "kT", st1)
                kknT = K.sb([128, 4, T], BF16, "kknT", st1)
                vtok = K.sb([128, 18, 512], BF16, "rvtok", st1)
                xwaT = K.sb([128, T], BF16, "xwaT", st1)
                cols = K.sb([128, 10, 4], F32, "rcols", st1)
                mu = K.sb([128, 3, 14], F32, "mu", st1)
                blk64 = K.sb([128, 128], BF16, "blk64", st1)
                blk64f = K.sb([128, 128], F32, "blk64f", st1)
                K.dma("sp", blk64f[:, :], blk64_d[:, :], reads=[blk64_d], writes=[blk64f])
                K.copy("dve", blk64[:, :], blk64f[:, :], [blk64f], [blk64])
                for i_, src in enumerate((rkk_d[0, :], rka_d[0, :], None, rrk_d[0, :, :].rearrange("a b -> (a b)"), rlg_d[0, :], rlb_d[0, :],
                                          None, ra0_d[0, 0, :], ra0_d[0, 1, :])):
                    if src is not None:
                        K.dma("sp", cols[:, i_, :], colvec(src, 4), reads=[rkk_d], writes=[cols], allow_slow_non_contiguous=True)
                K.ts("dve", cols[:, 2, :], cols[:, 1, :], -1.0, 1.0, ALU.mult, ALU.add, [cols], [cols])
                K.dma("sp", mu[:, 0, :], colvec(rmp_d[0, :], 14), reads=[rmp_d], writes=[mu], allow_slow_non_contiguous=True)
                K.dma("sp", mu[:, 1, :], colvec(rmn_d[0, :], 14), reads=[rmn_d], writes=[mu], allow_slow_non_contiguous=True)
                K.tt("dve", mu[:, 2, :], mu[:, 0, :], mu[:, 1, :], ALU.add, [mu], [mu])
                K.ts("dve", mu[:, 2, :], mu[:, 2, :], -1.0, 1.0, ALU.mult, ALU.add, [mu], [mu])
                with ES() as st2:
                    hT = K.sb([128, KC, T], BF16, "hT", st2)
                    load_hT(hT)
                    wstg = [K.sb([128, 4096], F32, "wstg", st2)]
                    wch = [K.sb([128, KC, 128], BF16, "rwch", st2) for _ in range(2)]
                    g2b = K.sb([128, 512], BF16, "g2b", st2)
                    upad = [K.sb([128, T + 4], F32, "rupad", st2) for _ in range(2)]
                    xs = [K.sb([128, T], F32, "rxs", st2) for _ in range(1)]
                    xsb = [K.sb([128, T], BF16, "rxsb", st2) for _ in range(1)]
                    t32 = [K.sb([128, 512], F32, "rt32", st2) for _ in range(2)]
                    t16 = [K.sb([128, 512], BF16, "rt16", st2) for _ in range(2)]
                    gbo = [K.sb([128, T], BF16, "gbo", st2) for _ in range(1)]
                    rk32 = [K.sb([128, 512], F32, "rk32", st2) for _ in range(2)]
                    for u_ in upad:
                        K.memset("pool", u_[:, :], 0.0, [u_])
                    load_w_bf16(g2b[:, :], rg2_d[0, :, :], (128, 512), rg2_d, g2b, wstg, 0)

                    def poff(t):
                        return t + 1 if t < LC else t + 3

                    order = [4, 5, 6, 7, 0, 1, 2, 3, 8, 9, 10, 11, 12, 13]
                    for oi, fc in enumerate(order):
                        w_ = wch[oi % 2]
                        load_w_bf16(w_[:, :, :], win[:, :, AB_RW + fc * 128:AB_RW + (fc + 1) * 128], (128, KC, 128), abin_d, w_, wstg, 0)
                        up = upad[oi % 2]
                        for bi, (t0, t1) in enumerate(BLOCKS):
                            n = t1 - t0
                            pb = PS[bi % 4]
                            for kc in range(KC):
                                K.mm(pb[:, 0:n], w_[:, kc, :], hT[:, kc, t0:t1], kc == 0, kc == KC - 1, [w_, hT], [pb])
                            K.copy("act", up[:, poff(t0):poff(t0) + n], pb[:, 0:n], [pb], [up])
                        x_ = xs[0]
                        for (s0, ln) in ((0, LC), (LC, LL)):
                            p0 = poff(s0)
                            K.ts("pool", x_[:, s0:s0 + ln], up[:, p0:p0 + ln], mu[:, 2, fc:fc + 1], None, ALU.mult, None, [up, mu], [x_])
                            K.stt("dve", x_[:, s0:s0 + ln], up[:, p0 - 1:p0 - 1 + ln], mu[:, 0, fc:fc + 1], x_[:, s0:s0 + ln], ALU.mult, ALU.add, [up, mu, x_], [x_])
                            K.stt("dve", x_[:, s0:s0 + ln], up[:, p0 + 1:p0 + 1 + ln], mu[:, 1, fc:fc + 1], x_[:, s0:s0 + ln], ALU.mult, ALU.add, [up, mu, x_], [x_])
                        if fc < 4:
                            c4 = fc
                            K.copy("act", rT[:, c4, :], x_[:, :], [x_], [rT])
                            for bi, (t0, t1) in enumerate(BLOCKS):
                                n = t1 - t0
                                a_, b_ = t32[bi % 2], t16[bi % 2]
                                K.stt("dve", b_[:, 0:n], x_[:, t0:t1], cols[:, 3, c4:c4 + 1], kT[:, c4, t0:t1], ALU.mult, ALU.mult, [x_, cols, kT], [b_])
                                pb = PS[4 + bi % 2]
                                K.mm(pb[:, 0:n], blk64[:, :], b_[:, 0:n], True, True, [blk64, b_], [pb])
                                K.copy("act", gbo[0][:, t0:t1], pb[:, 0:n], [pb], [gbo[0]])
                            K.dma("pool", gb_d[1, c4, :, :], gbo[0][:, :], reads=[gbo[0]], writes=[gb_d])
                        elif fc < 8:
                            c4 = fc - 4
                            K.copy("act", kT[:, c4, :], x_[:, :], [x_], [kT])
                            for bi, (t0, t1) in enumerate(BLOCKS):
                                n = t1 - t0
                                a_, b_ = t32[bi % 2], t16[bi % 2]
                                K.ts("dve", a_[:, 0:n], x_[:, t0:t1], cols[:, 0, c4:c4 + 1], None, ALU.mult, None, [x_, cols], [a_])
                                K.act(b_[:, 0:n], a_[:, 0:n], AF.Square, [a_], [b_])
                                pb = PS[4 + bi % 2]
                                K.mm(pb[:, 0:n], blk64[:, :], b_[:, 0:n], True, True, [blk64, b_], [pb])
                                r_ = rk32[bi % 2]
                                K.rsqrt(r_[:, 0:n], pb[:, 0:n], 1.0, 1e-12, [pb], [r_])
                                K.tt("pool", kknT[:, c4, t0:t1], a_[:, 0:n], r_[:, 0:n], ALU.mult, [a_, r_], [kknT])
                        elif fc < 12:
                            c4 = fc - 8
                            xb_ = xsb[0]
                            K.copy("act", xb_[:, :], x_[:, :], [x_], [xb_])
                            for grp in range(3):
                                tis = list(range(grp * 8, min(18, grp * 8 + 8)))
                                pb = PS[4 + grp % 2]
                                pbv = pb[:, :].bitcast(BF16)
                                for qi, ti in enumerate(tis):
                                    K.transpose(pbv[:, qi * 128:(qi + 1) * 128], xb_[:, ti * 128:(ti + 1) * 128], identb[:, :], [xb_, identb], [pb])
                                nt = len(tis)
                                K.copy("dve", vtok[:, tis[0]:tis[0] + nt, c4 * 128:(c4 + 1) * 128],
                                       pbv[:, 0:nt * 128].rearrange("p (a f) -> p a f", a=nt), [pb], [vtok])
                            K.dma("pool", gb_d[0, c4, :, :], xb_[:, :], reads=[xb_], writes=[gb_d])
                        elif fc == 12:
                            K.act(xwaT[0:64, :], x_[0:64, :], AF.Tanh, [x_], [xwaT])
                            K.copy("act", xwaT[64:128, :], x_[64:128, :], [x_], [xwaT])
                        else:
                            xb_ = xsb[0]
                            K.act(xb_[:, :], x_[:, :], AF.Sigmoid, [x_], [xb_])
                            for c4 in range(4):
                                for bi, (t0, t1) in enumerate(BLOCKS):
                                    n = t1 - t0
                                    pb = PS[bi % 4]
                                    K.mm(pb[:, 0:n], g2b[:, c4 * 128:(c4 + 1) * 128], xb_[:, t0:t1], True, True, [g2b, xb_], [pb])
                                    K.copy("act", rwo[:, c4, t0:t1], pb[:, 0:n], [pb], [rwo])
                K.barrier()
                with ES() as st2:
                    w2b = K.sb([64, 2, 512], BF16, "w2b", st2)
                    a2b = K.sb([128, 2, 512], BF16, "a2b", st2)
                    w0bc = K.sb([128, 2, 512], F32, "w0bc", st2)
                    wstg = [K.sb([128, 4096], F32, "wstg", st2)]
                    lcm = K.sb([128, 2, 128], F32, "lcm", st2)
                    lexcm = K.sb([128, 2, 128], F32, "lexcm", st2)
                    mcol = K.sb([128, 2, 2], F32, "mcol", st2)
                    m1 = K.sb([128, 2, 128], F32, "m1", st2)
                    m3 = K.sb([128, 2, 384], F32, "m3", st2)
                    m1t = K.sb([128, 2, 128], F32, "m1t", st2)
                    for dst, src in ((lcm, rwlc_d), (lexcm, rwlexc_d), (mcol, rwmcol_d), (m1, rwm1_d), (m3, rwm3_d), (m1t, rwm1t_d)):
                        for d in range(2):
                            K.dma("sp", dst[:, d, :], src[d, :, :], reads=[src], writes=[dst])
                    for d in range(2):
                        load_w_bf16(w2b[:, d, :], rw2_d[0, d, :, :], (64, 512), rw2_d, w2b, wstg, 0)
                        s_ = wstg[0]
                        K.dma("sp", s_[64:128, 0:512], ra2_d[0, d, :, :], reads=[ra2_d], writes=[s_])
                        K.copy("pool", a2b[64:128, d, :], s_[64:128, 0:512], [s_], [a2b])
                        K.dma("sp", w0bc[:, d, :], rw0_d[0, d, :].partition_broadcast(128), reads=[rw0_d], writes=[w0bc])
                    sg = K.sb([128, 512], F32, "sg", st2)
                    aT = K.sb([128, 128], F32, "aT", st2)
                    tmpa = K.sb([128, 128], F32, "tmpa", st2)
                    eL = [K.sb([128, 128], F32, "eL", st2) for _ in range(2)]
                    enL = [K.sb([128, 128], F32, "enL", st2) for _ in range(2)]
                    eLex = [K.sb([128, 128], F32, "eLex", st2) for _ in range(2)]
                    pm_sb = K.sb([128, 4, 2], F32, "pm_sb", st2)
                    gm = K.sb([128, 4, 2], F32, "gm", st2)
                    AR = K.sb([128, 4, 256], BF16, "AR", st2)
                    BH = K.sb([128, 4, 128], BF16, "BH", st2)
                    KH = K.sb([128, 4, 128], BF16, "KH", st2)
                    BKtok = K.sb([128, 2, 512], BF16, "BKtok", st2)
                    Q = [K.sb([128, 8, 128], F32, "Qa", st2), K.sb([128, 8, 128], F32, "Qb", st2)]
                    QT = [K.sb([128, 8, 128], F32, "QTa", st2), K.sb([128, 8, 128], F32, "QTb", st2)]
                    Nm = K.sb([128, 8, 128], F32, "Nm", st2)
                    S3 = K.sb([128, 8, 384], BF16, "S3", st2)
                    H = K.sb([128, 4, 64], F32, "H", st2)
                    H0 = K.sb([128, 4, 64], F32, "H0", st2)
                    H0b = K.sb([128, 4, 64], BF16, "H0b", st2)
                    W_sb = K.sb([128, 512], F32, "W_sb", st2)
                    U_sb = K.sb([128, 512], BF16, "U_sb", st2)
                    ybuf = [K.sb([128, 512], F32, "ybuf", st2) for _ in range(2)]
                    yold = [K.sb([128, 512], F32, "yold", st2) for _ in range(2)]
                    for d in range(2):
                        K.memset("dve", H[:, :, :], 0.0, [H])
                        tiles = list(range(18)) if d == 0 else [1, 0] + list(range(17, 1, -1))
                        for ci, c in enumerate(tiles):
                            tsl = slice(c * 128, (c + 1) * 128)
                            pz = PS[0]
                            K.mm(pz[:, :], xwaT[0:64, tsl], w2b[:, d, :], True, True, [xwaT, w2b], [pz])
                            K.tt("dve", sg[:, :], pz[:, :], w0bc[:, d, :], ALU.add, [pz, w0bc], [sg])
                            K.act(sg[:, :], sg[:, :], AF.Sigmoid, [sg], [sg])
                            for f4 in range(4):
                                fs = slice(f4 * 128, (f4 + 1) * 128)
                                pa = PS[1]
                                K.mm(pa[:, 0:128], a2b[64:128, d, fs], xwaT[64:128, tsl], True, True, [a2b, xwaT], [pa])
                                K.act(aT[:, :], pa[:, 0:128], AF.Sigmoid, [pa, cols], [aT], bias=cols[:, 7 + d, f4:f4 + 1])
                                pl = PS[2]
                                K.mm(pl[:, 0:128], sg[:, fs], lcm[:, d, :], True, True, [sg, lcm], [pl])
                                K.mm(pl[:, 128:256], sg[:, fs], lexcm[:, d, :], True, True, [sg, lexcm], [pl])
                                K.mm(pl[:, 256:258], sg[:, fs], mcol[:, d, :], True, True, [sg, mcol], [pl])
                                e1, e2, e3 = eL[f4 % 2], enL[f4 % 2], eLex[f4 % 2]
                                K.act(e1[:, :], pl[:, 0:128], AF.Exp, [pl], [e1], scale=CW)
                                K.act(e2[:, :], pl[:, 0:128], AF.Exp, [pl], [e2], scale=-CW)
                                K.act(e3[:, :], pl[:, 128:256], AF.Exp, [pl], [e3], scale=CW)
                                K.copy("act", pm_sb[:, f4, :], pl[:, 256:258], [pl], [pm_sb])
                                K.tt("dve", AR[:, f4, 128:256], rT[:, f4, tsl], e1[:, :], ALU.mult, [rT, e1], [AR])
                                K.stt("dve", AR[:, f4, 0:128], kknT[:, f4, tsl], -1.0, e3[:, :], ALU.mult, ALU.mult, [kknT, e3], [AR])
                                K.tt("pool", tmpa[:, :], kknT[:, f4, tsl], aT[:, :], ALU.mult, [kknT, aT], [tmpa])
                                K.tt("pool", BH[:, f4, :], tmpa[:, :], e2[:, :], ALU.mult, [tmpa, e2], [BH])
                                K.ts("dve", tmpa[:, :], aT[:, :], cols[:, 1, f4:f4 + 1], cols[:, 2, f4:f4 + 1], ALU.mult, ALU.add, [aT, cols], [tmpa])
                                K.tt("pool", tmpa[:, :], tmpa[:, :], kT[:, f4, tsl], ALU.mult, [tmpa, kT], [tmpa])
                                K.tt("pool", KH[:, f4, :], tmpa[:, :], e2[:, :], ALU.mult, [tmpa, e2], [KH])
                            K.tt("dve", pm_sb[:, :, 1], pm_sb[:, :, 1], pm_sb[:, :, 0], ALU.subtract, [pm_sb], [pm_sb])
                            K.act(gm[:, :, :], pm_sb[:, :, :], AF.Exp, [pm_sb], [gm], scale=CW)
                            pt = PS[3]
                            ptv = pt[:, :].bitcast(BF16)
                            for f4 in range(4):
                                K.transpose(ptv[:, f4 * 128:(f4 + 1) * 128], BH[:, f4, :], identb[:, :], [BH, identb], [pt])
                                K.transpose(ptv[:, 512 + f4 * 128:512 + (f4 + 1) * 128], KH[:, f4, :], identb[:, :], [KH, identb], [pt])
                            K.copy("dve", BKtok[:, :, :], ptv[:, :].rearrange("p (a f) -> p a f", a=2), [pt], [BKtok])
                            for h in range(8):
                                f4, hr = h // 2, slice((h % 2) * 64, (h % 2) * 64 + 64)
                                ps_ = PS[4 + h % 2]
                                K.mm(ps_[:, 0:256], BH[hr, f4, :], AR[hr, f4, :], True, True, [BH, AR], [ps_])
                                K.mm(ps_[:, 256:512], KH[hr, f4, :], AR[hr, f4, :], True, True, [KH, AR], [ps_])
                                K.tt("dve", Q[0][:, h, :], ps_[:, 0:128], m1[:, d, :], ALU.mult, [ps_, m1], [Q[0]])
                                K.tt("dve", S3[:, h, :], ps_[:, 128:512], m3[:, d, :], ALU.mult, [ps_, m3], [S3])
                                pq = PS[6 + h % 2]
                                K.mm(pq[:, 0:128], AR[hr, f4, 0:128], BH[hr, f4, :], True, True, [AR, BH], [pq])
                                K.tt("dve", QT[0][:, h, :], pq[:, 0:128], m1t[:, d, :], ALU.mult, [pq, m1t], [QT[0]])
                            K.tt("pool", Nm[:, :, :], Q[0][:, :, :], ident[:, :].unsqueeze(1).to_broadcast([128, 8, 128]), ALU.add, [Q[0], ident], [Nm])
                            cur = 0
                            for lev in range(1, 7):
                                nxt = 1 - cur
                                for half in range(2):
                                    pqa, pqb = PS[4 + half], PS[6 + half]
                                    for hh in range(4):
                                        h = half * 4 + hh
                                        cs = slice(hh * 128, (hh + 1) * 128)
                                        K.mm(pqb[:, cs], Q[cur][:, h, :], QT[cur][:, h, :], True, True, [Q[cur], QT[cur]], [pqb])
                                        if lev < 6:
                                            K.mm(pqa[:, cs], QT[cur][:, h, :], Q[cur][:, h, :], True, True, [Q[cur], QT[cur]], [pqa])
                                    hs = slice(half * 4, half * 4 + 4)
                                    K.copy("act", QT[nxt][:, hs, :], pqb[:, :].rearrange("p (a f) -> p a f", a=4), [pqb], [QT[nxt]])
                                    if lev < 6:
                                        K.copy("dve", Q[nxt][:, hs, :], pqa[:, :].rearrange("p (a f) -> p a f", a=4), [pqa], [Q[nxt]])
                                cur = nxt
                                for half in range(2):
                                    pn = PS[1 + half]
                                    for hh in range(4):
                                        h = half * 4 + hh
                                        K.mm(pn[:, hh * 128:(hh + 1) * 128], QT[cur][:, h, :], Nm[:, h, :], True, True, [QT[cur], Nm], [pn])
                                    hs = slice(half * 4, half * 4 + 4)
                                    K.tt("dve", Nm[:, hs, :], Nm[:, hs, :], pn[:, :].rearrange("p (a f) -> p a f", a=4), ALU.add, [Nm, pn], [Nm])
                            K.tt("dve", H0[:, :, :], H[:, :, :], gm[:, :, 0:1].to_broadcast([128, 4, 64]), ALU.mult, [H, gm], [H0])
                            K.copy("act", H0b[:, :, :], H0[:, :, :], [H0], [H0b])
                            pw = PS[0]
                            for h in range(8):
                                f4, hr = h // 2, slice((h % 2) * 64, (h % 2) * 64 + 64)
                                cs = slice(h * 64, (h + 1) * 64)
                                K.mm(pw[:, cs], AR[hr, f4, 0:128], H0b[hr, f4, :], True, False, [AR, H0b], [pw])
                                K.mm(pw[:, cs], S3[:, h, 128:256], vtok[:, c, cs], False, True, [S3, vtok], [pw])
                            K.copy("act", W_sb[:, :], pw[:, :], [pw], [W_sb])
                            pu = PS[3]
                            for h in range(8):
                                cs = slice(h * 64, (h + 1) * 64)
                                K.mm(pu[:, cs], Nm[:, h, :], W_sb[:, cs], True, True, [Nm, W_sb], [pu])
                            K.copy("act", U_sb[:, :], pu[:, :], [pu], [U_sb])
                            py = PS[0]
                            for h in range(8):
                                f4, hr = h // 2, slice((h % 2) * 64, (h % 2) * 64 + 64)
                                cs = slice(h * 64, (h + 1) * 64)
                                K.mm(py[:, cs], AR[hr, f4, 128:256], H0b[hr, f4, :], True, False, [AR, H0b], [py])
                                K.mm(py[:, cs], S3[:, h, 0:128], U_sb[:, cs], False, False, [S3, U_sb], [py])
                                K.mm(py[:, cs], S3[:, h, 256:384], vtok[:, c, cs], False, True, [S3, vtok], [py])
                            yb_ = ybuf[ci % 2]
                            if d == 0:
                                K.copy("dve", yb_[:, :], py[:, :], [py], [yb_])
                            else:
                                yo_ = yold[ci % 2]
                                K.dma("sp", yo_[:, :], y_d[c, :, :], reads=[y_d], writes=[yo_])
                                K.tt("dve", yb_[:, :], py[:, :], yo_[:, :], ALU.add, [py, yo_], [yb_])
                            K.dma("pool", y_d[c, :, :], yb_[:, :], reads=[yb_], writes=[y_d])
                            if ci < len(tiles) - 1:
                                ph = PS[3]
                                for f4 in range(4):
                                    fs = slice(f4 * 128, (f4 + 1) * 128)
                                    K.mm(ph[:, fs], BKtok[:, 0, fs], U_sb[:, fs], True, False, [BKtok, U_sb], [ph])
                                    K.mm(ph[:, fs], BKtok[:, 1, fs], vtok[:, c, fs], False, True, [BKtok, vtok], [ph])
                                phv = ph[:, :].rearrange("p (f x) -> p f x", f=4)
                                for e2_ in range(2):
                                    rs = slice(e2_ * 64, e2_ * 64 + 64)
                                    K.tt("dve", H[rs, :, :], H0[rs, :, :], phv[rs, :, e2_ * 64:(e2_ + 1) * 64], ALU.add, [H0, ph], [H])
                                    K.tt("pool", H[rs, :, :], H[rs, :, :], gm[rs, :, 1:2].to_broadcast([64, 4, 64]), ALU.mult, [H, gm], [H])
                K.barrier()
                with ES() as st2:
                    yt = [K.sb([128, 512], F32, "yt", st2) for _ in range(2)]
                    ysq = K.sb([128, 512], F32, "ysq", st2)
                    s1 = K.sb([128, 8], F32, "s1", st2)
                    s2 = K.sb([128, 8], F32, "s2", st2)
                    ynb = K.sb([128, 512], BF16, "ynb", st2)
                    vT_ = [K.sb([128, 4, 128], BF16, "vT_", st2) for _ in range(2)]
                    sc_ = [K.sb([128, 4, 128], BF16, "sc_", st2) for _ in range(2)]
                    yn32 = K.sb([128, 4, 128], F32, "yn32", st2)
                    bon = K.sb([128, 4, 128], F32, "bon", st2)
                    gbv = gb_d[:, :, :, :].rearrange("a c p t -> a p c t")
                    for c in range(18):
                        tsl = slice(c * 128, (c + 1) * 128)
                        y_ = yt[c % 2]
                        K.dma("sp", y_[:, :], y_d[c, :, :], reads=[y_d], writes=[y_])
                        K.dma("sp", vT_[c % 2][:, :, :], gbv[0, :, :, tsl], reads=[gb_d], writes=[vT_[c % 2]])
                        K.dma("sp", sc_[c % 2][:, :, :], gbv[1, :, :, tsl], reads=[gb_d], writes=[sc_[c % 2]])
                        yv = y_[:, :].rearrange("p (h e) -> p h e", h=8)
                        K.op("dve", lambda e, yv=yv: e.tensor_reduce(out=s1[:, :], in_=yv, axis=AX.X, op=ALU.add), [y_], [s1])
                        K.act(ysq[:, :], y_[:, :], AF.Square, [y_], [ysq])
                        K.op("dve", lambda e: e.tensor_reduce(out=s2[:, :], in_=ysq[:, :].rearrange("p (h e) -> p h e", h=8), axis=AX.X, op=ALU.add), [ysq], [s2])
                        K.ts("dve", s1[:, :], s1[:, :], 1.0 / 64, None, ALU.mult, None, [s1], [s1])
                        K.tt("dve", ysq[:, 0:8], s1[:, :], s1[:, :], ALU.mult, [s1], [ysq])
                        K.stt("dve", s2[:, :], s2[:, :], 1.0 / 64, ysq[:, 0:8], ALU.mult, ALU.subtract, [s2, ysq], [s2])
                        K.rsqrt(s2[:, :], s2[:, :], 1.0, 64e-5, [s2], [s2])
                        K.tt("dve", yv, yv, s1[:, :].unsqueeze(2).to_broadcast([128, 8, 64]), ALU.subtract, [y_, s1], [y_])
                        K.tt("dve", ynb[:, :].rearrange("p (h e) -> p h e", h=8), yv, s2[:, :].unsqueeze(2).to_broadcast([128, 8, 64]), ALU.mult, [y_, s2], [ynb])
                        pt = PS[c % 2]
                        ptv = pt[:, :].bitcast(BF16)
                        for c4 in range(4):
                            K.transpose(ptv[:, c4 * 128:(c4 + 1) * 128], ynb[:, c4 * 128:(c4 + 1) * 128], identb[:, :], [ynb, identb], [pt])
                        for c4 in range(4):
                            K.act(yn32[:, c4, :], ptv[:, c4 * 128:(c4 + 1) * 128], AF.Identity, [pt, cols], [yn32],
                                  bias=cols[:, 5, c4:c4 + 1], scale=cols[:, 4, c4:c4 + 1])
                        K.tt("pool", bon[:, :, :], vT_[c % 2][:, :, :], sc_[c % 2][:, :, :], ALU.mult, [vT_[c % 2], sc_[c % 2]], [bon])
                        K.tt("pool", yn32[:, :, :], yn32[:, :, :], bon[:, :, :], ALU.add, [yn32, bon], [yn32])
                        K.tt("dve", rwo[:, :, tsl], yn32[:, :, :], rwo[:, :, tsl], ALU.mult, [yn32, rwo], [rwo])
            K.barrier()
            wo = K.sb([128, 4, D], BF16, "wo_rw", st)
            wostg = [K.sb([128, 4096], F32, "wostg", st)]
            load_w_bf16(wo[:, :, :], about_d[0, 512:1024, :].rearrange("(kc p) n -> p kc n", p=128), (128, 4, D), about_d, wo, wostg, 0)
            pieces = [((lambda oc, cc=cc: wo[:, cc, oc * 128:(oc + 1) * 128]),
                       (lambda t0, t1, cc=cc: rwo[:, cc, t0:t1]), [wo, rwo]) for cc in range(4)]
            apply_out(b, pieces, 2, 0)
        K.barrier()

    K.barrier()
    for l in layers:
        stage_mods(l)
    for b in range(nb):
        stage_load(b)
        for li, l in enumerate(layers):
            last = (li == len(layers) - 1) and not cfg.get("force_ctx", False)
            modsT.l = l
            Amod.l = l
            if mixers:
                if l == 1:
                    with ES() as sth:
                        hT = K.sb([128, KC, T], BF16, "hT", sth)
                        stage_norm(b, 0, hT, to_dram=True)
                        if cfg.get("swa", True):
                            stage_swa(b, hT)
                    if cfg.get("ssd", True):
                        stage_ssd(b, None)
                else:
                    with ES() as sth:
                        hT = K.sb([128, KC, T], BF16, "hT", sth)
                        stage_norm(b, 0, hT, to_dram=True)
                        if cfg.get("mla", True):
                            stage_mla(b, hT)
                    if cfg.get("rwkv", True):
                        stage_rwkv(b)
            with ES() as sth:
                hT = K.sb([128, KC, T], BF16, "hT", sth)
                stage_norm(b, 1, hT, lo=0 if not last else LC)
                stage_ffn(b, l, do_ctx=not last, hT=hT)
        stage_out(b)
        if cfg.get("dbg_x", False) and b == 0:
            for cc in range(KC):
                K.dma("sp", dbgx_d[cc, :, :], xs_d[cc, :, :], reads=xblk, writes=[dbgx_d])
    K.barrier()
    K.es.close()
    return nc, K


CONST_INPUTS = None


def _rope_tables(rot_dim):
    n_freq = rot_dim // 4
    rows = np.arange(LL, dtype=np.float32) // 64
    cols = np.arange(LL, dtype=np.float32) % 64
    inv = (np.float32(10000.0) ** (-np.arange(n_freq, dtype=np.float32) / np.float32(n_freq))).astype(np.float32)
    ang = np.concatenate([rows[:, None] * inv[None, :], cols[:, None] * inv[None, :]], axis=-1).astype(np.float32)
    cos, sin = np.cos(ang).astype(np.float32), np.sin(ang).astype(np.float32)
    cosT = np.concatenate([cos.T, cos.T], axis=0)
    sinT = np.concatenate([sin.T, sin.T], axis=0)
    return np.ascontiguousarray(cosT), np.ascontiguousarray(sinT)


def const_inputs():
    global CONST_INPUTS
    if CONST_INPUTS is None:
        s = np.arange(128)
        triF = (s[:, None] <= s[None, :]).astype(np.float32)
        c = {"ident": np.eye(128, dtype=np.float32), "triF": triF, "triB": np.ascontiguousarray(triF.T),
             "strF": (s[:, None] > s[None, :]).astype(np.float32), "strB": (s[:, None] < s[None, :]).astype(np.float32)}
        c["swa_cos"], c["swa_sin"] = _rope_tables(64)
        c["mla_cos"], c["mla_sin"] = _rope_tables(32)
        triB = triF.T
        incl = [triF, triB]
        strict = [c["strB"], c["strF"]]
        m = 63
        c["rw_lc"] = np.stack([incl[d] - incl[d][:, m:m + 1] for d in range(2)]).astype(np.float32)
        c["rw_lexc"] = np.stack([strict[d] - incl[d][:, m:m + 1] for d in range(2)]).astype(np.float32)
        c["rw_mcol"] = np.stack([np.stack([incl[d][:, m], np.ones(128, np.float32)], axis=1) for d in range(2)]).astype(np.float32)
        c["rw_m1"] = np.stack([strict[d] for d in range(2)]).astype(np.float32)
        c["rw_m3"] = np.stack([np.concatenate([incl[d], strict[d], incl[d]], axis=1) for d in range(2)]).astype(np.float32)
        c["rw_m1t"] = np.stack([np.ascontiguousarray(strict[d].T) for d in range(2)]).astype(np.float32)
        blk = np.zeros((128, 128), np.float32)
        blk[:64, :64] = 1.0
        blk[64:, 64:] = 1.0
        c["blk64"] = blk
        CONST_INPUTS = c
    return CONST_INPUTS


def make_in_maps(nc_names, inputs, ncores=NCORES):
    consts = const_inputs()
    in_maps = []
    for core in range(ncores):
        m = {}
        sl = slice(core * BPC, (core + 1) * BPC)
        for k in nc_names:
            if k in consts:
                m[k] = consts[k]
            else:
                v = np.asarray(inputs[k])
                m[k] = np.ascontiguousarray(v[sl] if k in ("x", "c", "ctx") else v)
        in_maps.append(m)
    return in_maps


def kernel(**inputs):
    cfg = {}
    nc, K = build_program(cfg)
    in_maps = make_in_maps(K.in_names, inputs)
    res = run_bass_kernel_spmd(nc, in_maps, core_ids=list(range(NCORES)))
    return np.concatenate([r["out"] for r in res.results], axis=0)
```

```python
import contextlib
import math
import numpy as np
import concourse.bass as bass
import concourse.mybir as mybir
from concourse.bass_utils import run_bass_kernel_spmd

F32 = mybir.dt.float32
BF16 = mybir.dt.bfloat16
AF = mybir.ActivationFunctionType
ALU = mybir.AluOpType
AX = mybir.AxisListType

NCORES = 8
BPC = 4
D = 1024
KC = 8
LC = 256
LL = 2048
T = LC + LL
DFF = 2816
EPS = 1e-6
BLOCKS = [(0, 256), (256, 768), (768, 1280), (1280, 1792), (1792, 2304)]
SEM_EPOCH = 50000


class Buf:
    def __init__(self, t, name):
        self.t = t
        self.name = name
        self.lw = []
        self.rd = []
        self.ds = None

    def __getitem__(self, idx):
        return self.t[idx]


class Eng:
    def __init__(self, name, e, is_pe=False):
        self.name = name
        self.e = e
        self.is_pe = is_pe
        self.sems = []
        self.count = 0
        self.epoch = 0
        self.seen = {}


class Kern:
    def __init__(self, nc):
        self.nc = nc
        self.es = contextlib.ExitStack()
        self.engs = {}
        for name, e, ispe in (("pe", nc.tensor, True), ("act", nc.scalar, False),
                              ("dve", nc.vector, False), ("pool", nc.gpsimd, False),
                              ("sp", nc.sync, False)):
            en = Eng(name, e, ispe)
            en.sems.append(self.es.enter_context(nc.semaphore("s_%s_0" % name)))
            self.engs[name] = en
        self.ndsem = 40
        self.dsem = [self.es.enter_context(nc.semaphore("s_d%d" % i)) for i in range(self.ndsem)]
        self.dtot = [0] * self.ndsem
        self.drr = 0
        self.nbuf = 0
        self.n_ops = 0
        self.eps_bufs = {}
        self.in_names = []

    def sb(self, shape, dtype, name=None, stack=None):
        self.nbuf += 1
        name = "%s_%d" % (name or "sb", self.nbuf)
        t = (stack or self.es).enter_context(self.nc.sbuf_tensor(name, list(shape), dtype))
        return Buf(t, name)

    def ps(self, name=None):
        self.nbuf += 1
        name = "%s_%d" % (name or "ps", self.nbuf)
        t = self.es.enter_context(self.nc.psum_tensor(name, [128, 512], F32))
        return Buf(t, name)

    def dram(self, name, shape, dtype, kind="Internal"):
        t = self.nc.dram_tensor(name, list(shape), dtype, kind=kind)
        if kind == "ExternalInput":
            self.in_names.append(name)
        return Buf(t, name)

    def _wait(self, eng, ev):
        if ev[0] == "E":
            src = self.engs[ev[1]]
            ep, n = ev[2], ev[3]
            if src is eng and eng.is_pe:
                return
            key = ("E", ev[1], ep)
            if eng.seen.get(key, 0) >= n:
                return
            eng.e.wait_ge(src.sems[ep], n)
            eng.seen[key] = n
        else:
            i = ev[1]
            tot = self.dtot[i]
            key = ("D", i)
            if eng.seen.get(key, 0) >= tot:
                return
            eng.e.wait_ge(self.dsem[i], tot)
            eng.seen[key] = tot

    def _deps(self, eng, reads, writes):
        for b in reads:
            for ev in b.lw:
                self._wait(eng, ev)
        for b in writes:
            for ev in b.lw:
                self._wait(eng, ev)
            for ev in b.rd:
                self._wait(eng, ev)

    def _record(self, ev, reads, writes):
        for b in reads:
            if ev[0] == "E":
                b.rd = [r for r in b.rd if not (r[0] == "E" and r[1] == ev[1])]
            else:
                b.rd = [r for r in b.rd if r != ev]
            b.rd.append(ev)
        for b in writes:
            b.lw = [ev]
            b.rd = []

    def op(self, engname, fn, reads=(), writes=()):
        eng = self.engs[engname]
        self._deps(eng, reads, writes)
        if eng.count >= SEM_EPOCH:
            eng.epoch += 1
            eng.count = 0
            eng.sems.append(self.es.enter_context(self.nc.semaphore("s_%s_%d" % (engname, eng.epoch))))
        ins = fn(eng.e)
        eng.count += 1
        ins.then_inc(eng.sems[eng.epoch], 1)
        ev = ("E", engname, eng.epoch, eng.count)
        self._record(ev, reads, writes)
        self.n_ops += 1

    def dma(self, engname, out, in_, reads=(), writes=(), **kw):
        eng = self.engs[engname]
        self._deps(eng, reads, writes)
        b = None
        for cand in list(writes) + list(reads):
            if cand.ds is not None:
                b = cand
                break
        if b is None:
            b = (list(writes) + list(reads))[0]
            b.ds = self.drr
            self.drr = (self.drr + 1) % self.ndsem
        i = b.ds
        ins = eng.e.dma_start(out=out, in_=in_, **kw)
        ins.then_inc(self.dsem[i], 16)
        self.dtot[i] += 16
        ev = ("D", i)
        self._record(ev, reads, writes)
        self.n_ops += 1

    def barrier(self):
        for eng in self.engs.values():
            for other in self.engs.values():
                if other is eng:
                    continue
                if other.count > 0:
                    self._wait(eng, ("E", other.name, other.epoch, other.count))
            for i in range(self.ndsem):
                if self.dtot[i] > 0:
                    self._wait(eng, ("D", i))

    def mm(self, out, lhsT, rhs, start, stop, reads, writes):
        self.op("pe", lambda e: e.matmul(out, lhsT=lhsT, rhs=rhs, start=start, stop=stop), reads, writes)

    def transpose(self, out, in_, ident, reads, writes):
        self.op("pe", lambda e: e.transpose(out, in_, ident), reads, writes)

    def act(self, out, in_, func, reads, writes, bias=None, scale=None, eng="act"):
        kw = {}
        if bias is not None:
            kw["bias"] = bias
        if scale is not None:
            kw["scale"] = scale
        self.op(eng, lambda e: e.activation(out=out, in_=in_, func=func, **kw), reads, writes)

    def ts(self, eng, out, in0, s1, s2, op0, op1, reads, writes):
        if op1 is None:
            self.op(eng, lambda e: e.tensor_scalar(out=out, in0=in0, scalar1=s1, scalar2=None, op0=op0), reads, writes)
        else:
            self.op(eng, lambda e: e.tensor_scalar(out=out, in0=in0, scalar1=s1, scalar2=s2, op0=op0, op1=op1), reads, writes)

    def tt(self, eng, out, in0, in1, op, reads, writes):
        self.op(eng, lambda e: e.tensor_tensor(out=out, in0=in0, in1=in1, op=op), reads, writes)

    def stt(self, eng, out, in0, scalar, in1, op0, op1, reads, writes):
        self.op(eng, lambda e: e.scalar_tensor_tensor(out=out, in0=in0, scalar=scalar, in1=in1, op0=op0, op1=op1), reads, writes)

    def copy(self, eng, out, in_, reads, writes):
        if eng == "act":
            self.op(eng, lambda e: e.activation(out=out, in_=in_, func=AF.Copy), reads, writes)
        else:
            self.op(eng, lambda e: e.tensor_copy(out=out, in_=in_), reads, writes)

    def rsqrt(self, out, in_, scale, eps, reads, writes):
        self.op("act", lambda e: e.activation(out=out, in_=in_, func=AF.Sqrt, bias=self.eps_ap(eps), scale=scale), reads, writes)
        self.op("dve", lambda e: e.reciprocal(out=out, in_=out), writes, writes)

    def eps_ap(self, eps):
        if eps not in self.eps_bufs:
            b = self.sb([128, 1], F32, "eps")
            self.memset("dve", b[:, :], float(eps), [b])
            self.eps_bufs[eps] = b
        return self.eps_bufs[eps][:, 0:1]

    def memset(self, eng, ap, val, writes):
        self.op(eng, lambda e: e.memset(ap, val), (), writes)


def colvec(ap1d, n):
    return ap1d.rearrange("(c p) -> p c", p=128)


def stg_view(s, shape):
    n = 1
    for d_ in shape[1:]:
        n *= d_
    v = s[0:shape[0], 0:n]
    if len(shape) == 3:
        v = v.rearrange("p (a b) -> p a b", a=shape[1])
    return v


HD = 64
CD_Z, CD_XBC, CD_DT, CD_Q, CD_K, CD_V = 0, 1024, 2560, 2592, 3104, 3232
CD_IN = 3360


def build_program(cfg):
    nc = bass.Bass("TRN2", target_bir_lowering=False)
    K = Kern(nc)
    nb = cfg.get("nb", BPC)
    layers = cfg.get("layers", [0, 1])
    mixers = cfg.get("mixers", True)
    ES = contextlib.ExitStack

    def din(name, shape):
        return K.dram(name, shape, F32, kind="ExternalInput")

    x_d = din("x", [BPC, LL, D])
    c_d = din("c", [BPC, D])
    ctx_d = din("ctx", [BPC, LC, D])
    cctx_d = din("c_ctx", [D])
    ada_w_d = din("ada_w", [2, D, 6 * D])
    ada_b_d = din("ada_b", [2, 6 * D])
    nmix_d = din("norm_mix_g", [2, D])
    nffn_d = din("norm_ffn_g", [2, D])
    wup_d = din("ffn_w_up", [2, D, 2 * DFF])
    cw_d = din("ffn_conv_w", [2, 3, 2 * DFF])
    cb_d = din("ffn_conv_b", [2, 2 * DFF])
    wdn_d = din("ffn_w_down", [2, DFF, D])
    fng_d = din("final_norm_g", [D])
    if mixers and 1 in layers:
        cdin_d = din("cd_w_in", [1, D, CD_IN])
        cdout_d = din("cd_w_out", [1, 1536, D])
        scw_d = din("ssm_conv_w", [1, 5, 1536])
        scb_d = din("ssm_conv_b", [1, 1536])
        sdtb_d = din("ssm_dt_bias", [1, 2, 16])
        salog_d = din("ssm_a_log", [1, 2, 16])
        sd_d = din("ssm_d", [1, 16])
        sng_d = din("ssm_norm_g", [1, 1024])
        sink_d = din("swa_sink", [1, 8])
        swacos_d = din("swa_cos", [64, LL])
        swasin_d = din("swa_sin", [64, LL])
        triF_d = din("triF", [128, 128])
        triB_d = din("triB", [128, 128])
        strF_d = din("strF", [128, 128])
        strB_d = din("strB", [128, 128])
    if mixers and 0 in layers:
        abin_d = din("ab_w_in", [1, D, 2464])
        about_d = din("ab_w_out", [1, D, D])
        mqg_d = din("mla_q_norm_g", [1, 384])
        mqu_d = din("mla_w_q_up", [1, 384, 768])
        mkg_d = din("mla_kv_norm_g", [1, 256])
        mkvu_d = din("mla_w_kv_up", [1, 256, 1024])
        mlacos_d = din("mla_cos", [32, LL])
        mlasin_d = din("mla_sin", [32, LL])
        rmp_d = din("rwkv_mu_prev", [1, 1792])
        rmn_d = din("rwkv_mu_next", [1, 1792])
        rw0_d = din("rwkv_w0", [1, 2, 512])
        rw2_d = din("rwkv_w2", [1, 2, 64, 512])
        ra0_d = din("rwkv_a0", [1, 2, 512])
        ra2_d = din("rwkv_a2", [1, 2, 64, 512])
        rg2_d = din("rwkv_g2", [1, 128, 512])
        rkk_d = din("rwkv_k_k", [1, 512])
        rka_d = din("rwkv_k_a", [1, 512])
        rrk_d = din("rwkv_r_k", [1, 8, 64])
        rlg_d = din("rwkv_ln_g", [1, 512])
        rlb_d = din("rwkv_ln_b", [1, 512])
        rwlc_d = din("rw_lc", [2, 128, 128])
        rwlexc_d = din("rw_lexc", [2, 128, 128])
        rwmcol_d = din("rw_mcol", [2, 128, 2])
        rwm1_d = din("rw_m1", [2, 128, 128])
        rwm3_d = din("rw_m3", [2, 128, 384])
        rwm1t_d = din("rw_m1t", [2, 128, 128])
        blk64_d = din("blk64", [128, 128])
        gb_d = K.dram("gb_scr", [3, 4, 128, T], BF16)
        y_d = K.dram("y_scr", [18, 128, 512], F32)
        yf_d = K.dram("yf_scr", [18, 128, 512], F32)
    ident_d = din("ident", [128, 128])
    out_d = K.dram("out", [BPC, LL, D], F32, kind="ExternalOutput")
    if cfg.get("dbg_x", False):
        dbgx_d = K.dram("dbgx", [KC, 128, T], F32, kind="ExternalOutput")
    aT_d = K.dram("aT_scr", [DFF, T], BF16)
    xs_d = K.dram("x_scr", [KC, 128, T], F32)
    hT_d = K.dram("hT_scr", [KC, 128, T], BF16)
    sz_d = K.dram("sz_scr", [16, 128, 1024], BF16)
    hin_d = K.dram("hin_scr", [16, 128, 1024], BF16)
    xview = xs_d[:, :, :].rearrange("c p t -> p c t")
    hview = hT_d[:, :, :].rearrange("c p t -> p c t")
    xblk = [Buf(xs_d.t, "xblk%d" % j) for j in range(T // 256)]

    def xdeps(t0, t1):
        return xblk[t0 // 256:(t1 + 255) // 256]

    ident = K.sb([128, 128], F32, "ident")
    identb = K.sb([128, 128], BF16, "identb")
    ones_bf = K.sb([128, 128], BF16, "ones")
    ones_f = K.sb([128, 128], F32, "onesf")
    class LayerBuf(Buf):
        def __init__(self, b_):
            Buf.__init__(self, b_.t, b_.name)
            self.l = 0

        def __getitem__(self, idx):
            return self.t[(idx[0], self.l) + tuple(idx[1:])]

    modsT = LayerBuf(K.sb([128, 2, KC, 6, 5], F32, "modsT"))
    Amod = LayerBuf(K.sb([128, 2, KC, 2, 5], F32, "Amod"))
    gcols = K.sb([128, 5, KC], F32, "gcols")
    cwT = K.sb([128, 2, 3, 44], F32, "cwT")
    cbT = K.sb([128, 2, 44], F32, "cbT")
    PS = [K.ps("ps%d" % i) for i in range(8)]
    for e_ in (EPS, 64e-5, 1e-12, 1.0, 0.0):
        K.eps_ap(e_)

    K.dma("sp", ident[:, :], ident_d[:, :], reads=[ident_d], writes=[ident])
    K.copy("dve", identb[:, :], ident[:, :], [ident], [identb])
    K.memset("dve", ones_bf[:, :], 1.0, [ones_bf])
    K.memset("dve", ones_f[:, :], 1.0, [ones_f])
    for l in range(2):
        K.dma("sp", gcols[:, l, :], colvec(nmix_d[l, :], KC), reads=[nmix_d], writes=[gcols], allow_slow_non_contiguous=True)
        K.dma("sp", gcols[:, 2 + l, :], colvec(nffn_d[l, :], KC), reads=[nffn_d], writes=[gcols], allow_slow_non_contiguous=True)
        for tap in range(3):
            K.dma("sp", cwT[:, l, tap, :], colvec(cw_d[l, tap, :], 44), reads=[cw_d], writes=[cwT], allow_slow_non_contiguous=True)
        K.dma("sp", cbT[:, l, :], colvec(cb_d[l, :], 44), reads=[cb_d], writes=[cbT], allow_slow_non_contiguous=True)
    K.dma("sp", gcols[:, 4, :], colvec(fng_d[:], KC), reads=[fng_d], writes=[gcols], allow_slow_non_contiguous=True)

    def stage_mods(l):
        modsT.l = l
        Amod.l = l
        with ES() as st:
            condT = K.sb([128, KC, 5], F32, "condT", st)
            scond = K.sb([128, KC, 5], F32, "scond", st)
            abT = K.sb([128, 48], F32, "abT", st)
            wb = [K.sb([128, KC, 128], F32, "adaw", st) for _ in range(3)]
            for r in range(4):
                K.dma("sp", condT[:, :, r], colvec(c_d[r, :], KC), reads=[c_d], writes=[condT], allow_slow_non_contiguous=True)
            K.dma("sp", condT[:, :, 4], colvec(cctx_d[:], KC), reads=[cctx_d], writes=[condT], allow_slow_non_contiguous=True)
            K.dma("sp", abT[:, :], colvec(ada_b_d[l, :], 48), reads=[ada_b_d], writes=[abT], allow_slow_non_contiguous=True)
            K.act(scond[:, :, :], condT[:, :, :], AF.Silu, [condT], [scond])
            wview = ada_w_d[l, :, :].rearrange("(kc p) n -> p kc n", p=128)
            for j in range(48):
                w = wb[j % 3]
                K.dma("sp", w[:, :, :], wview[:, :, j * 128:(j + 1) * 128], reads=[ada_w_d], writes=[w])
                pb = PS[j % 2]
                for kc in range(KC):
                    K.mm(pb[:, 0:5], w[:, kc, :], scond[:, kc, :], kc == 0, kc == KC - 1, [w, scond], [pb])
                kind, cc = j // 8, j % 8
                K.ts("dve", modsT[:, cc, kind, :], pb[:, 0:5], abT[:, j:j + 1], None, ALU.add, None, [pb, abT], [modsT])
            for which, kind, gi in ((0, 1, l), (1, 4, 2 + l)):
                for cc in range(KC):
                    K.ts("dve", Amod[:, cc, which, :], modsT[:, cc, kind, :], 1.0, gcols[:, gi, cc:cc + 1],
                         ALU.add, ALU.mult, [modsT, gcols], [Amod])
        K.barrier()

    def stage_load(b):
        with ES() as st:
            xin = [K.sb([128, D], F32, "xin", st) for _ in range(3)]
            xo = [K.sb([128, KC, 128], F32, "xo", st) for _ in range(3)]
            for ti in range(T // 128):
                xb = xin[ti % 3]
                if ti < 2:
                    src, sb_ = ctx_d[b, ti * 128:(ti + 1) * 128, :], ctx_d
                else:
                    src, sb_ = x_d[b, (ti - 2) * 128:(ti - 1) * 128, :], x_d
                K.dma("sp", xb[:, :], src, reads=[sb_], writes=[xb])
                o = xo[ti % 3]
                for half in range(2):
                    pb = PS[(2 * ti + half) % 4]
                    for q in range(4):
                        cc = half * 4 + q
                        K.transpose(pb[:, q * 128:(q + 1) * 128], xb[:, cc * 128:(cc + 1) * 128], ident[:, :], [xb, ident], [pb])
                    K.copy("act" if half == 0 else "dve", o[:, half * 4:half * 4 + 4, :],
                           pb[:, :].rearrange("p (q t) -> p q t", q=4), [pb], [o])
                K.dma("pool", xview[:, :, ti * 128:(ti + 1) * 128], o[:, :, :], reads=[o], writes=xdeps(ti * 128, ti * 128 + 128))
        K.barrier()

    def stage_norm(b, which, hT, to_dram=False, lo=0):
        shift_kind = 0 if which == 0 else 3
        with ES() as st:
            xb = [K.sb([128, KC, 512], F32, "nxb", st) for _ in range(2)]
            sq = [K.sb([128, 512], BF16, "sq", st) for _ in range(3)]
            rstd = [K.sb([128, 512], F32, "rstd", st) for _ in range(2)]
            tmp = [K.sb([128, 512], F32, "ntmp", st) for _ in range(3)]
            for bi, (t0, t1) in enumerate(BLOCKS):
                if t1 <= lo:
                    continue
                n = t1 - t0
                row = 4 if bi == 0 else b
                x_ = xb[bi % 2]
                K.dma("sp", x_[:, :, 0:n], xview[:, :, t0:t1], reads=xdeps(t0, t1), writes=[x_])
                pb = PS[bi % 2]
                for cc in range(KC):
                    s = sq[cc % 3]
                    K.act(s[:, 0:n], x_[:, cc, 0:n], AF.Square, [x_], [s])
                    K.mm(pb[:, 0:n], ones_bf[:, :], s[:, 0:n], cc == 0, cc == KC - 1, [ones_bf, s], [pb])
                r = rstd[bi % 2]
                K.rsqrt(r[:, 0:n], pb[:, 0:n], 1.0 / D, EPS, [pb], [r])
                for cc in range(KC):
                    tm = tmp[cc % 3]
                    K.tt("dve" if cc % 2 == 0 else "pool", tm[:, 0:n], x_[:, cc, 0:n], r[:, 0:n], ALU.mult, [x_, r], [tm])
                    K.act(hT[:, cc, t0:t1], tm[:, 0:n], AF.Identity, [tm, Amod, modsT], [hT],
                          bias=modsT[:, cc, shift_kind, row:row + 1], scale=Amod[:, cc, which, row:row + 1])
            if to_dram:
                for cc in range(KC):
                    K.dma("pool", hT_d[cc, :, :], hT[:, cc, :], reads=[hT], writes=[hT_d])
        K.barrier()

    def load_hT(hT):
        for cc in range(KC):
            K.dma("sp", hT[:, cc, :], hT_d[cc, :, :], reads=[hT_d], writes=[hT])

    def apply_out(b, pieces, gate_kind, tok_lo):
        with ES() as st:
            xb = [K.sb([128, KC, 256], F32, "uxb", st) for _ in range(2)]
            for bi, t0 in enumerate(range(tok_lo, T, 256)):
                t1 = t0 + 256
                row = 4 if t0 < LC else b
                x_ = xb[bi % 2]
                K.dma("sp", x_[:, :, :], xview[:, :, t0:t1], reads=xdeps(t0, t1), writes=[x_])
                for oc in range(KC):
                    pb = PS[oc % 4]
                    for pi, (lf, rf, rd) in enumerate(pieces):
                        K.mm(pb[:, 0:256], lf(oc), rf(t0, t1), pi == 0, pi == len(pieces) - 1, rd, [pb])
                    K.stt("dve", x_[:, oc, :], pb[:, 0:256], modsT[:, oc, gate_kind, row:row + 1], x_[:, oc, :],
                          ALU.mult, ALU.add, [pb, modsT, x_], [x_])
                K.dma("pool", xview[:, :, t0:t1], x_[:, :, :], reads=[x_], writes=xdeps(t0, t1))

    cast_rr = [0]

    def load_w_bf16(dst_ap, src_ap, shape, src_buf, dst_buf, st_bufs, idx, eng=None):
        s = st_bufs[idx % len(st_bufs)]
        v = stg_view(s, shape)
        K.dma("sp", v, src_ap, reads=[src_buf], writes=[s])
        if eng is None:
            cast_rr[0] += 1
            eng = "dve" if cast_rr[0] % 2 == 0 else "act"
        K.copy(eng, dst_ap, v, [s], [dst_buf])

    def stage_ffn(b, l, do_ctx, hT):
        wupv = wup_d[l, :, :].rearrange("(kc p) n -> p kc n", p=128)
        segs = [(LC, LL)] + ([(0, LC)] if do_ctx else [])
        blocks = [bl for bl in BLOCKS if (do_ctx or bl[0] >= LC)]
        PADW = T + 4
        with ES() as st:
            wst = [K.sb([128, KC, 256], F32, "wst", st) for _ in range(2)]
            wbf = [K.sb([128, KC, 256], BF16, "wbf", st) for _ in range(2)]
            ug = [K.sb([128, PADW], F32, "ug", st) for _ in range(2)]
            uv = [K.sb([128, PADW], F32, "uv", st) for _ in range(2)]
            cg = K.sb([128, T], F32, "cg", st)
            cv = K.sb([128, T], F32, "cv", st)
            ao = [K.sb([128, T], BF16, "ao", st) for _ in range(2)]
            for u in ug + uv:
                K.memset("pool", u[:, :], 0.0, [u])

            def pad_off(t):
                return t + 1 if t < LC else t + 3

            for fc in range(22):
                ws, wb_ = wst[fc % 2], wbf[fc % 2]
                K.dma("sp", ws[:, :, 0:128], wupv[:, :, fc * 128:(fc + 1) * 128], reads=[wup_d], writes=[ws])
                K.dma("sp", ws[:, :, 128:256], wupv[:, :, DFF + fc * 128:DFF + (fc + 1) * 128], reads=[wup_d], writes=[ws])
                K.copy("dve", wb_[:, :, 0:128], ws[:, :, 0:128], [ws], [wb_])
                K.copy("act", wb_[:, :, 128:256], ws[:, :, 128:256], [ws], [wb_])
                g_, v_ = ug[fc % 2], uv[fc % 2]
                for bi, (t0, t1) in enumerate(blocks):
                    n = t1 - t0
                    for half, dst in ((0, g_), (1, v_)):
                        pb = PS[(bi * 2 + half) % 4]
                        for kc in range(KC):
                            K.mm(pb[:, 0:n], wb_[:, kc, half * 128:(half + 1) * 128], hT[:, kc, t0:t1],
                                 kc == 0, kc == KC - 1, [wb_, hT], [pb])
                        K.copy("act", dst[:, pad_off(t0):pad_off(t0) + n], pb[:, 0:n], [pb], [dst])
                ao_ = ao[fc % 2]
                for half, src, dst, ch in ((0, g_, cg, fc), (1, v_, cv, 22 + fc)):
                    for (s0, ln) in segs:
                        p0 = pad_off(s0)
                        K.act(dst[:, s0:s0 + ln], src[:, p0:p0 + ln], AF.Identity, [src, cwT, cbT], [dst],
                              bias=cbT[:, l, ch:ch + 1], scale=cwT[:, l, 1, ch:ch + 1])
                        K.stt("dve", dst[:, s0:s0 + ln], src[:, p0 - 1:p0 - 1 + ln], cwT[:, l, 0, ch:ch + 1], dst[:, s0:s0 + ln],
                              ALU.mult, ALU.add, [src, cwT, dst], [dst])
                        K.stt("dve", dst[:, s0:s0 + ln], src[:, p0 + 1:p0 + 1 + ln], cwT[:, l, 2, ch:ch + 1], dst[:, s0:s0 + ln],
                              ALU.mult, ALU.add, [src, cwT, dst], [dst])
                for (s0, ln) in segs:
                    K.act(cg[:, s0:s0 + ln], cg[:, s0:s0 + ln], AF.Silu, [cg], [cg])
                    K.tt("dve" if ln > 1024 else "pool", ao_[:, s0:s0 + ln], cg[:, s0:s0 + ln], cv[:, s0:s0 + ln], ALU.mult, [cg, cv], [ao_])
                lo = 0 if do_ctx else LC
                K.dma("pool", aT_d[fc * 128:(fc + 1) * 128, lo:T], ao_[:, lo:T], reads=[ao_], writes=[aT_d])
        K.barrier()
        wdv = wdn_d[l, :, :].rearrange("(kc p) n -> p kc n", p=128)
        aTv = aT_d[:, :].rearrange("(kc p) t -> p kc t", p=128)
        with ES() as st:
            wd = K.sb([128, 22, D], BF16, "wd", st)
            wds = [K.sb([128, 2 * D], F32, "wds", st) for _ in range(2)]
            ab = [K.sb([128, 22, 256], BF16, "ab", st) for _ in range(2)]
            for j in range(11):
                load_w_bf16(wd[:, 2 * j:2 * j + 2, :], wdv[:, 2 * j:2 * j + 2, :], (128, 2, D), wdn_d, wd, wds, j)
            cnt = [0]

            def rhs_fn(t0, t1):
                return ab[cnt[0] % 2]

            lo = 0 if do_ctx else LC
            with ES() as st2:
                xb = [K.sb([128, KC, 256], F32, "uxb", st2) for _ in range(2)]
                for bi, t0 in enumerate(range(lo, T, 256)):
                    t1 = t0 + 256
                    row = 4 if t0 < LC else b
                    a_ = ab[bi % 2]
                    x_ = xb[bi % 2]
                    K.dma("sp", a_[:, :, :], aTv[:, :, t0:t1], reads=[aT_d], writes=[a_])
                    K.dma("sp", x_[:, :, :], xview[:, :, t0:t1], reads=xdeps(t0, t1), writes=[x_])
                    for oc in range(KC):
                        pb = PS[oc % 4]
                        for kc in range(22):
                            K.mm(pb[:, 0:256], wd[:, kc, oc * 128:(oc + 1) * 128], a_[:, kc, :], kc == 0, kc == 21, [wd, a_], [pb])
                        K.stt("dve", x_[:, oc, :], pb[:, 0:256], modsT[:, oc, 5, row:row + 1], x_[:, oc, :],
                              ALU.mult, ALU.add, [pb, modsT, x_], [x_])
                    K.dma("pool", xview[:, :, t0:t1], x_[:, :, :], reads=[x_], writes=xdeps(t0, t1))
        K.barrier()

    def stage_out(b):
        with ES() as st:
            xb = [K.sb([128, KC, 512], F32, "oxb", st) for _ in range(2)]
            sq = [K.sb([128, 512], BF16, "sq", st) for _ in range(3)]
            rstd = [K.sb([128, 512], F32, "rstd", st) for _ in range(2)]
            yT = [K.sb([128, KC, 512], F32, "yT", st) for _ in range(2)]
            ob = [K.sb([128, D], F32, "ob", st) for _ in range(3)]
            for bi, (t0, t1) in enumerate(BLOCKS[1:]):
                n = t1 - t0
                x_ = xb[bi % 2]
                K.dma("sp", x_[:, :, 0:n], xview[:, :, t0:t1], reads=xdeps(t0, t1), writes=[x_])
                pb = PS[bi % 2]
                for cc in range(KC):
                    s = sq[cc % 3]
                    K.act(s[:, 0:n], x_[:, cc, 0:n], AF.Square, [x_], [s])
                    K.mm(pb[:, 0:n], ones_bf[:, :], s[:, 0:n], cc == 0, cc == KC - 1, [ones_bf, s], [pb])
                r = rstd[bi % 2]
                K.rsqrt(r[:, 0:n], pb[:, 0:n], 1.0 / D, EPS, [pb], [r])
                y = yT[bi % 2]
                for cc in range(KC):
                    K.stt("dve", y[:, cc, 0:n], x_[:, cc, 0:n], gcols[:, 4, cc:cc + 1], r[:, 0:n],
                          ALU.mult, ALU.mult, [x_, gcols, r], [y])
                for ti in range(n // 128):
                    o = ob[ti % 3]
                    for half in range(2):
                        pb2 = PS[2 + (2 * ti + half) % 4]
                        for q in range(4):
                            cc = half * 4 + q
                            K.transpose(pb2[:, q * 128:(q + 1) * 128], y[:, cc, ti * 128:(ti + 1) * 128], ident[:, :], [y, ident], [pb2])
                        K.copy("act", o[:, half * 512:(half + 1) * 512], pb2[:, :], [pb2], [o])
                    tok = t0 - LC + ti * 128
                    K.dma("pool", out_d[b, tok:tok + 128, :], o[:, :], reads=[o], writes=[out_d])
        K.barrier()

    def stage_swa(b, hT):
        win = cdin_d[0, :, :].rearrange("(kc p) n -> p kc n", p=128)
        with ES() as st:
            attT = K.sb([64, 8, LL], BF16, "attT", st)
            wo = K.sb([64, 8, D], BF16, "wo_att", st)
            with ES() as st1:
                wstg = [K.sb([128, 4096], F32, "wstg", st1)]
                wq = K.sb([128, KC, 512], BF16, "wq", st1)
                wqs = K.sb([128, KC, 512], BF16, "wqs", st1)
                wk = K.sb([128, KC, 128], BF16, "wk", st1)
                wks = K.sb([128, KC, 128], BF16, "wks", st1)
                wv = K.sb([128, KC, 128], BF16, "wv", st1)
                cosT = K.sb([64, LL], F32, "cosT", st1)
                sinT = K.sb([64, LL], F32, "sinT", st1)
                qT = K.sb([64, 8, LL], BF16, "qT", st1)
                kT = K.sb([64, 2, T], BF16, "kT", st1)
                vtok = K.sb([128, 18, 128], BF16, "vtok", st1)
                esink = K.sb([64, 8], F32, "esink", st1)
                maskP = K.sb([128, 128], F32, "maskP", st1)
                maskN = K.sb([128, 128], F32, "maskN", st1)
                t1b = [K.sb([64, 512], F32, "rt1", st1) for _ in range(2)]
                t2b = [K.sb([64, 512], F32, "rt2", st1) for _ in range(2)]
                pT = [K.sb([128, 512], BF16, "pT", st1) for _ in range(3)]
                dsum = [K.sb([64, 512], F32, "dsum", st1) for _ in range(2)]
                K.dma("sp", cosT[:, :], swacos_d[:, :], reads=[swacos_d], writes=[cosT])
                K.dma("sp", sinT[:, :], swasin_d[:, :], reads=[swasin_d], writes=[sinT])
                K.dma("sp", maskP[:, :], triB_d[:, :], reads=[triB_d], writes=[maskP])
                K.dma("sp", maskN[:, :], triF_d[:, :], reads=[triF_d], writes=[maskN])
                K.dma("sp", esink[:, :], sink_d[0, :].partition_broadcast(64), reads=[sink_d], writes=[esink])
                K.act(esink[:, :], esink[:, :], AF.Exp, [esink], [esink])
                load_w_bf16(wq[:, :, :], win[:, :, CD_Q:CD_Q + 512], (128, KC, 512), cdin_d, wq, wstg, 0)
                load_w_bf16(wk[:, :, :], win[:, :, CD_K:CD_K + 128], (128, KC, 128), cdin_d, wk, wstg, 0)
                load_w_bf16(wv[:, :, :], win[:, :, CD_V:CD_V + 128], (128, KC, 128), cdin_d, wv, wstg, 0)
                for (w_, ws_, nh) in ((wq, wqs, 64), (wk, wks, 16)):
                    wv4 = w_[:, :, :].rearrange("p k (h two d) -> p (k h) two d", two=2, d=32)
                    ws4 = ws_[:, :, :].rearrange("p k (h two d) -> p (k h) two d", two=2, d=32)
                    K.ts("pool", ws4[:, :, 0, :], wv4[:, :, 1, :], -1.0, None, ALU.mult, None, [w_], [ws_])
                    K.copy("pool", ws4[:, :, 1, :], wv4[:, :, 0, :], [w_], [ws_])
                wov = cdout_d[0, 1024:1536, :].rearrange("(h d) n -> d h n", d=64)
                for j in range(2):
                    load_w_bf16(wo[:, j * 4:(j + 1) * 4, :], wov[:, j * 4:(j + 1) * 4, :], (64, 4, D), cdout_d, wo, wstg, 0)
                cnt = 0
                for h in range(8):
                    for j in range(4):
                        t0 = LC + j * 512
                        pa, pb = PS[(cnt * 2) % 4], PS[(cnt * 2 + 1) % 4]
                        for kc in range(KC):
                            K.mm(pa[0:64, :], wq[:, kc, h * 64:(h + 1) * 64], hT[:, kc, t0:t0 + 512], kc == 0, kc == KC - 1, [wq, hT], [pa])
                        for kc in range(KC):
                            K.mm(pb[0:64, :], wqs[:, kc, h * 64:(h + 1) * 64], hT[:, kc, t0:t0 + 512], kc == 0, kc == KC - 1, [wqs, hT], [pb])
                        a_, b_ = t1b[cnt % 2], t2b[cnt % 2]
                        K.tt("dve", a_[:, :], pa[0:64, :], cosT[:, j * 512:(j + 1) * 512], ALU.mult, [pa, cosT], [a_])
                        K.tt("dve", b_[:, :], pb[0:64, :], sinT[:, j * 512:(j + 1) * 512], ALU.mult, [pb, sinT], [b_])
                        K.tt("pool", qT[:, h, j * 512:(j + 1) * 512], a_[:, :], b_[:, :], ALU.add, [a_, b_], [qT])
                        cnt += 1
                for g in range(2):
                    pa = PS[cnt % 4]
                    for kc in range(KC):
                        K.mm(pa[0:64, 0:LC], wk[:, kc, g * 64:(g + 1) * 64], hT[:, kc, 0:LC], kc == 0, kc == KC - 1, [wk, hT], [pa])
                    K.copy("act", kT[:, g, 0:LC], pa[0:64, 0:LC], [pa], [kT])
                    cnt += 1
                    for j in range(4):
                        t0 = LC + j * 512
                        pa, pb = PS[(cnt * 2) % 4], PS[(cnt * 2 + 1) % 4]
                        for kc in range(KC):
                            K.mm(pa[0:64, :], wk[:, kc, g * 64:(g + 1) * 64], hT[:, kc, t0:t0 + 512], kc == 0, kc == KC - 1, [wk, hT], [pa])
                        for kc in range(KC):
                            K.mm(pb[0:64, :], wks[:, kc, g * 64:(g + 1) * 64], hT[:, kc, t0:t0 + 512], kc == 0, kc == KC - 1, [wks, hT], [pb])
                        a_, b_ = t1b[cnt % 2], t2b[cnt % 2]
                        K.tt("dve", a_[:, :], pa[0:64, :], cosT[:, j * 512:(j + 1) * 512], ALU.mult, [pa, cosT], [a_])
                        K.tt("dve", b_[:, :], pb[0:64, :], sinT[:, j * 512:(j + 1) * 512], ALU.mult, [pb, sinT], [b_])
                        K.tt("pool", kT[:, g, t0:t0 + 512], a_[:, :], b_[:, :], ALU.add, [a_, b_], [kT])
                        cnt += 1
                for ti in range(18):
                    pa = PS[ti % 4]
                    for kc in range(KC):
                        K.mm(pa[:, 0:128], hT[:, kc, ti * 128:(ti + 1) * 128], wv[:, kc, :], kc == 0, kc == KC - 1, [hT, wv], [pa])
                    K.copy("act", vtok[:, ti, :], pa[:, 0:128], [pa], [vtok])
                units = []
                u = 0
                for i in range(16):
                    for g in range(2):
                        keys = [(0, None), (1, None)]
                        if i > 0:
                            keys.append((2 + i - 1, maskP))
                        keys.append((2 + i, None))
                        if i < 15:
                            keys.append((2 + i + 1, maskN))
                        for ki, (kt, mask) in enumerate(keys):
                            units.append((i, g, ki, kt, mask, len(keys), u))
                        u += 1

                def front(un, idx):
                    i, g, ki, kt, mask, nk, uu = un
                    psc = PS[idx % 4]
                    K.mm(psc[:, :].rearrange("p (h q) -> p h q", h=4), kT[:, g, kt * 128:(kt + 1) * 128],
                         qT[:, g * 4:(g + 1) * 4, i * 128:(i + 1) * 128], True, True, [kT, qT], [psc])

                def back(un, idx):
                    i, g, ki, kt, mask, nk, uu = un
                    psc = PS[idx % 4]
                    pnum, pden = PS[4 + (uu % 2) * 2], PS[5 + (uu % 2) * 2]
                    p_ = pT[idx % 3]
                    K.act(p_[:, :], psc[:, :], AF.Exp, [psc], [p_], scale=0.125)
                    if mask is not None:
                        K.tt("pool", p_[:, :].rearrange("p (h q) -> p h q", h=4), p_[:, :].rearrange("p (h q) -> p h q", h=4),
                             mask[:, :].unsqueeze(1).to_broadcast([128, 4, 128]), ALU.mult, [p_, mask], [p_])
                    K.mm(pnum[0:64, :], vtok[:, kt, g * 64:(g + 1) * 64], p_[:, :], ki == 0, ki == nk - 1, [vtok, p_], [pnum])
                    K.mm(pden[0:64, :], ones_bf[:, 0:64], p_[:, :], ki == 0, ki == nk - 1, [ones_bf, p_], [pden])
                    if ki == nk - 1:
                        d_ = dsum[uu % 2]
                        K.tt("dve", d_[:, :].rearrange("p (h q) -> p h q", h=4), pden[0:64, :].rearrange("p (h q) -> p h q", h=4),
                             esink[:, g * 4:(g + 1) * 4].unsqueeze(2).to_broadcast([64, 4, 128]), ALU.add, [pden, esink], [d_])
                        K.op("dve", lambda e, d_=d_: e.reciprocal(out=d_[:, :], in_=d_[:, :]), [d_], [d_])
                        K.tt("dve", attT[:, g * 4:(g + 1) * 4, i * 128:(i + 1) * 128], pnum[0:64, :].rearrange("p (h q) -> p h q", h=4),
                             d_[:, :].rearrange("p (h q) -> p h q", h=4), ALU.mult, [pnum, d_], [attT])

                front(units[0], 0)
                for idx, un in enumerate(units):
                    if idx + 1 < len(units):
                        front(units[idx + 1], idx + 1)
                    back(un, idx)
            K.barrier()
            pieces = [((lambda oc, h=h: wo[:, h, oc * 128:(oc + 1) * 128]),
                       (lambda t0, t1, h=h: attT[:, h, t0 - LC:t1 - LC]), [wo, attT]) for h in range(8)]
            apply_out(b, pieces, 2, LC)
        K.barrier()

    def stage_ssd(b, hT_scope_fn):
        win = cdin_d[0, :, :].rearrange("(kc p) n -> p kc n", p=128)
        with ES() as st:
            uT = K.sb([128, 8, LL], BF16, "uT", st)
            with ES() as st1:
                xs_tok = K.sb([128, 18, 1024], BF16, "xs_tok", st1)
                B_tok = K.sb([128, 18, 256], BF16, "B_tok", st1)
                BCT = K.sb([128, 4, T], BF16, "BCT", st1)
                dtv = K.sb([128, 18, 32], F32, "dtv", st1)
                dtA = K.sb([128, 18, 32], F32, "dtA", st1)
                a_bc = K.sb([128, 32], F32, "a_bc", st1)
                dtb_bc = K.sb([128, 32], F32, "dtb_bc", st1)
                D_bc = K.sb([128, 16], F32, "D_bc", st1)
                sng = K.sb([128, 8], F32, "sng", st1)
                scw = K.sb([128, 5, 12], F32, "scw", st1)
                scb = K.sb([128, 12], F32, "scb", st1)
                triF = K.sb([128, 128], F32, "triF", st1)
                triB = K.sb([128, 128], F32, "triB", st1)
                strF = K.sb([128, 128], F32, "strF", st1)
                strB = K.sb([128, 128], F32, "strB", st1)
                for (dst, src) in ((triF, triF_d), (triB, triB_d), (strF, strF_d), (strB, strB_d)):
                    K.dma("sp", dst[:, :], src[:, :], reads=[src], writes=[dst])
                K.dma("sp", a_bc[:, :], salog_d[0, :, :].rearrange("a b -> (a b)").partition_broadcast(128), reads=[salog_d], writes=[a_bc])
                K.act(a_bc[:, :], a_bc[:, :], AF.Exp, [a_bc], [a_bc])
                K.ts("dve", a_bc[:, :], a_bc[:, :], -1.0, None, ALU.mult, None, [a_bc], [a_bc])
                K.dma("sp", dtb_bc[:, :], sdtb_d[0, :, :].rearrange("a b -> (a b)").partition_broadcast(128), reads=[sdtb_d], writes=[dtb_bc])
                K.dma("sp", D_bc[:, :], sd_d[0, :].partition_broadcast(128), reads=[sd_d], writes=[D_bc])
                K.dma("sp", sng[:, :], colvec(sng_d[0, :], 8), reads=[sng_d], writes=[sng], allow_slow_non_contiguous=True)
                for tap in range(5):
                    K.dma("sp", scw[:, tap, :], colvec(scw_d[0, tap, :], 12), reads=[scw_d], writes=[scw], allow_slow_non_contiguous=True)
                K.dma("sp", scb[:, :], colvec(scb_d[0, :], 12), reads=[scb_d], writes=[scb], allow_slow_non_contiguous=True)
                with ES() as st2:
                    hT = K.sb([128, KC, T], BF16, "hT", st2)
                    load_hT(hT)
                    wstg = [K.sb([128, 4096], F32, "wstg", st2)]
                    st3 = ES()
                    wch = [K.sb([128, KC, 128], BF16, "wch", st3) for _ in range(2)]
                    PW = T + 8
                    upad = [K.sb([128, PW], F32, "upad", st3) for _ in range(2)]
                    cvb = [K.sb([128, T], F32, "cvb", st3) for _ in range(1)]
                    xsT = [K.sb([128, T], BF16, "xsT", st3) for _ in range(2)]
                    for u_ in upad:
                        K.memset("pool", u_[:, :], 0.0, [u_])

                    def poff(t):
                        return t + 2 if t < LC else t + 6

                    for fc in range(12):
                        w_ = wch[fc % 2]
                        load_w_bf16(w_[:, :, :], win[:, :, CD_XBC + fc * 128:CD_XBC + (fc + 1) * 128], (128, KC, 128), cdin_d, w_, wstg, fc)
                        up = upad[fc % 2]
                        for bi, (t0, t1) in enumerate(BLOCKS):
                            n = t1 - t0
                            pb = PS[bi % 4]
                            for kc in range(KC):
                                K.mm(pb[:, 0:n], w_[:, kc, :], hT[:, kc, t0:t1], kc == 0, kc == KC - 1, [w_, hT], [pb])
                            K.copy("act", up[:, poff(t0):poff(t0) + n], pb[:, 0:n], [pb], [up])
                        cv_ = cvb[0]
                        for (s0, ln) in ((0, LC), (LC, LL)):
                            p0 = poff(s0)
                            K.act(cv_[:, s0:s0 + ln], up[:, p0:p0 + ln], AF.Identity, [up, scw, scb], [cv_],
                                  bias=scb[:, fc:fc + 1], scale=scw[:, 2, fc:fc + 1])
                            for tap in (0, 1, 3, 4):
                                K.stt("dve", cv_[:, s0:s0 + ln], up[:, p0 + tap - 2:p0 + tap - 2 + ln], scw[:, tap, fc:fc + 1], cv_[:, s0:s0 + ln],
                                      ALU.mult, ALU.add, [up, scw, cv_], [cv_])
                        if fc < 8:
                            dstT, dst_ap = xsT[fc % 2], xsT[fc % 2][:, :]
                        else:
                            dstT, dst_ap = BCT, BCT[:, fc - 8, :]
                        K.act(dst_ap, cv_[:, :], AF.Silu, [cv_], [dstT])
                        if fc < 10:
                            for grp in range(3):
                                tis = list(range(grp * 8, min(18, grp * 8 + 8)))
                                pb = PS[4 + grp % 2]
                                pbv = pb[:, :].bitcast(BF16)
                                for qi, ti in enumerate(tis):
                                    K.transpose(pbv[:, qi * 128:(qi + 1) * 128], dst_ap[:, ti * 128:(ti + 1) * 128], identb[:, :], [dstT, identb], [pb])
                                nt = len(tis)
                                src_v = pbv[:, 0:nt * 128].rearrange("p (a f) -> p a f", a=nt)
                                if fc < 8:
                                    K.copy("act", xs_tok[:, tis[0]:tis[0] + nt, fc * 128:(fc + 1) * 128], src_v, [pb], [xs_tok])
                                else:
                                    K.copy("act", B_tok[:, tis[0]:tis[0] + nt, (fc - 8) * 128:(fc - 7) * 128], src_v, [pb], [B_tok])
                    K.barrier()
                    st3.close()
                    wz = K.sb([128, KC, 1024], BF16, "wz", st2)
                    wdt = K.sb([128, KC, 32], BF16, "wdt", st2)
                    szb = [K.sb([128, 1024], BF16, "szb", st2) for _ in range(2)]
                    dtt = [K.sb([128, 32], F32, "dtt", st2) for _ in range(2)]
                    for j in range(2):
                        load_w_bf16(wz[:, :, j * 512:(j + 1) * 512], win[:, :, CD_Z + j * 512:CD_Z + (j + 1) * 512], (128, KC, 512), cdin_d, wz, wstg, j)
                    load_w_bf16(wdt[:, :, :], win[:, :, CD_DT:CD_DT + 32], (128, KC, 32), cdin_d, wdt, wstg, 0)
                    for ti in range(18):
                        pb = PS[ti % 4]
                        for kc in range(KC):
                            K.mm(pb[:, 0:32], hT[:, kc, ti * 128:(ti + 1) * 128], wdt[:, kc, :], kc == 0, kc == KC - 1, [hT, wdt], [pb])
                        d_ = dtt[ti % 2]
                        K.tt("dve", d_[:, :], pb[:, 0:32], dtb_bc[:, :], ALU.add, [pb, dtb_bc], [d_])
                        K.act(d_[:, :], d_[:, :], AF.Exp, [d_], [d_])
                        K.act(dtv[:, ti, :], d_[:, :], AF.Ln, [d_], [dtv], bias=K.eps_ap(1.0))
                    K.tt("dve", dtA[:, :, :], dtv[:, :, :], a_bc[:, :].unsqueeze(1).to_broadcast([128, 18, 32]), ALU.mult, [dtv, a_bc], [dtA])
                    for li in range(16):
                        ti = li + 2
                        s_ = szb[li % 2]
                        for j in range(2):
                            pb = PS[(li * 2 + j) % 4]
                            for kc in range(KC):
                                K.mm(pb[:, :], hT[:, kc, ti * 128:(ti + 1) * 128], wz[:, kc, j * 512:(j + 1) * 512], kc == 0, kc == KC - 1, [hT, wz], [pb])
                            K.act(s_[:, j * 512:(j + 1) * 512], pb[:, :], AF.Silu, [pb], [s_])
                        K.dma("pool", sz_d[li, :, :], s_[:, :], reads=[s_], writes=[sz_d])
                K.barrier()
                with ES() as st2:
                    Hf = K.sb([128, 2, 512], F32, "Hf", st2)
                    Hb = K.sb([128, 2, 512], F32, "Hb", st2)
                    hbf = [K.sb([128, 1024], BF16, "hbf", st2) for _ in range(2)]
                    hinf = [K.sb([128, 1024], BF16, "hinf", st2) for _ in range(2)]
                    szt = [K.sb([128, 1024], BF16, "szt", st2) for _ in range(2)]
                    prep = {}
                    for nm in ("acs0", "eac0", "dend0", "cdec0", "wgt0", "acs1", "eac1", "dend1", "cdec1", "wgt1"):
                        prep[nm] = K.sb([128, 16], F32, nm, st2)
                    xte = K.sb([128, 1024], BF16, "xte", st2)
                    rseg = K.sb([128, 16, 128], F32, "rseg", st2)
                    cbm = [K.sb([128, 2, 128], F32, "cbm", st2) for _ in range(2)]
                    eseg = [K.sb([128, 512], F32, "eseg", st2) for _ in range(2)]
                    Lt = [K.sb([128, 16, 128], BF16, "Lt", st2) for _ in range(2)]
                    xdt = [K.sb([128, 1024], BF16, "xdt", st2) for _ in range(2)]
                    yacc = K.sb([128, 1024], F32, "yacc", st2)
                    ytmp = K.sb([128, 512], F32, "ytmp", st2)
                    ub = K.sb([128, 1024], F32, "ub", st2)
                    ubf = K.sb([128, 1024], BF16, "ubf", st2)
                    ssq = K.sb([128, 2], F32, "ssq", st2)
                    junk = K.sb([128, 512], BF16, "junk", st2)
                    K.memset("dve", Hf[:, :, :], 0.0, [Hf])
                    K.memset("dve", Hb[:, :, :], 0.0, [Hb])
                    tri = (triF, triB)
                    strm = (strF, strB)

                    def do_prep(c, d):
                        pp = PS[0]
                        K.mm(pp[:, 0:16], tri[d][:, :], dtA[:, c, d * 16:(d + 1) * 16], True, True, [tri[d], dtA], [pp])
                        K.mm(pp[:, 16:32], ones_f[:, :], dtA[:, c, d * 16:(d + 1) * 16], True, True, [ones_f, dtA], [pp])
                        acs, eac, dend, cdec, wgt = (prep[n_ + str(d)] for n_ in ("acs", "eac", "dend", "cdec", "wgt"))
                        K.copy("act", acs[:, :], pp[:, 0:16], [pp], [acs])
                        K.act(eac[:, :], pp[:, 0:16], AF.Exp, [pp], [eac])
                        K.act(cdec[:, :], pp[:, 16:32], AF.Exp, [pp], [cdec])
                        K.tt("dve", dend[:, :], pp[:, 16:32], acs[:, :], ALU.subtract, [pp, acs], [dend])
                        K.act(dend[:, :], dend[:, :], AF.Exp, [dend], [dend])
                        K.tt("dve", wgt[:, :], dend[:, :], dtv[:, c, d * 16:(d + 1) * 16], ALU.mult, [dend, dtv], [wgt])

                    def state_update(c, d, H):
                        wgt, cdec = prep["wgt" + str(d)], prep["cdec" + str(d)]
                        K.tt("pool", xte[:, :].rearrange("p (h e) -> p h e", h=16), xs_tok[:, c, :].rearrange("p (h e) -> p h e", h=16),
                             wgt[:, :].unsqueeze(2).to_broadcast([128, 16, 64]), ALU.mult, [xs_tok, wgt], [xte])
                        for g in range(2):
                            pb = PS[6 + g]
                            K.mm(pb[:, :], B_tok[:, c, g * 128:(g + 1) * 128], xte[:, g * 512:(g + 1) * 512], True, True, [B_tok, xte], [pb])
                            hv = H[:, g, :].rearrange("p (h e) -> p h e", h=8)
                            K.tt("pool", hv, hv, cdec[:, g * 8:(g + 1) * 8].unsqueeze(2).to_broadcast([128, 8, 64]), ALU.mult, [H, cdec], [H])
                            K.tt("dve", H[:, g, :], H[:, g, :], pb[:, :], ALU.add, [H, pb], [H])

                    for c in range(18):
                        if c >= 2:
                            hb_ = hbf[c % 2]
                            K.copy("act", hb_[:, :], Hf[:, :, :].rearrange("p g e -> p (g e)"), [Hf], [hb_])
                            K.dma("pool", hin_d[c - 2, :, :], hb_[:, :], reads=[hb_], writes=[hin_d])
                        if c == 17:
                            break
                        do_prep(c, 0)
                        state_update(c, 0, Hf)
                    K.barrier()
                    for c in [1, 0] + list(range(17, 1, -1)):
                        do_prep(c, 1)
                        if c >= 2:
                            li = c - 2
                            do_prep(c, 0)
                            hi_ = hinf[li % 2]
                            sz_ = szt[li % 2]
                            K.dma("sp", hi_[:, :], hin_d[li, :, :], reads=[hin_d], writes=[hi_])
                            K.dma("sp", sz_[:, :], sz_d[li, :, :], reads=[sz_d], writes=[sz_])
                            hb_ = hbf[li % 2]
                            K.copy("act", hb_[:, :], Hb[:, :, :].rearrange("p g e -> p (g e)"), [Hb], [hb_])
                            tsl = slice(c * 128, (c + 1) * 128)
                            pcb = PS[1]
                            for g in range(2):
                                K.mm(pcb[:, g * 128:(g + 1) * 128], BCT[:, g, tsl], BCT[:, 2 + g, tsl], True, True, [BCT], [pcb])
                            for d in range(2):
                                K.tt("dve", cbm[d][:, :, :], pcb[:, 0:256].rearrange("p (g q) -> p g q", g=2),
                                     tri[d][:, :].unsqueeze(1).to_broadcast([128, 2, 128]), ALU.mult, [pcb, tri[d]], [cbm[d]])
                            for d in range(2):
                                K.tt("dve", rseg[:, :, :], tri[d][:, :].unsqueeze(1).to_broadcast([128, 16, 128]),
                                     dtA[:, c, d * 16:(d + 1) * 16].unsqueeze(2).to_broadcast([128, 16, 128]), ALU.mult, [tri[d], dtA], [rseg])
                                for hb4 in range(4):
                                    pseg = PS[2 + hb4 % 2]
                                    K.mm(pseg[:, :], strm[d][:, :], rseg[:, hb4 * 4:(hb4 + 1) * 4, :], True, True, [strm[d], rseg], [pseg])
                                    es_ = eseg[hb4 % 2]
                                    K.act(es_[:, :], pseg[:, :], AF.Exp, [pseg], [es_])
                                    g = hb4 // 2
                                    K.tt("dve" if hb4 % 2 == 0 else "pool", Lt[d][:, hb4 * 4:(hb4 + 1) * 4, :], es_[:, :].rearrange("p (h q) -> p h q", h=4),
                                         cbm[d][:, g, :].unsqueeze(1).to_broadcast([128, 4, 128]), ALU.mult, [es_, cbm[d]], [Lt[d]])
                                K.tt("dve", xdt[d][:, :].rearrange("p (h e) -> p h e", h=16), xs_tok[:, c, :].rearrange("p (h e) -> p h e", h=16),
                                     dtv[:, c, d * 16:(d + 1) * 16].unsqueeze(2).to_broadcast([128, 16, 64]), ALU.mult, [xs_tok, dtv], [xdt[d]])
                            for h in range(16):
                                py = PS[4 + h // 8]
                                col = (h % 8) * 64
                                for d in range(2):
                                    K.mm(py[:, col:col + 64], Lt[d][:, h, :], xdt[d][:, h * 64:(h + 1) * 64], d == 0, d == 1, [Lt[d], xdt[d]], [py])
                            K.tt("pool", yacc[:, :].rearrange("p (h e) -> p h e", h=16), xs_tok[:, c, :].rearrange("p (h e) -> p h e", h=16),
                                 D_bc[:, :].unsqueeze(2).to_broadcast([128, 16, 64]), ALU.mult, [xs_tok, D_bc], [yacc])
                            for g in range(2):
                                K.tt("dve", yacc[:, g * 512:(g + 1) * 512], yacc[:, g * 512:(g + 1) * 512], PS[4 + g][:, :], ALU.add, [yacc, PS[4 + g]], [yacc])
                            for d in range(2):
                                hsrc = hi_ if d == 0 else hb_
                                eac = prep["eac" + str(d)]
                                for g in range(2):
                                    po = PS[6 + g]
                                    K.mm(po[:, :], BCT[:, 2 + g, tsl], hsrc[:, g * 512:(g + 1) * 512], True, True, [BCT, hsrc], [po])
                                    K.tt("dve", ytmp[:, :].rearrange("p (h e) -> p h e", h=8), po[:, :].rearrange("p (h e) -> p h e", h=8),
                                         eac[:, g * 8:(g + 1) * 8].unsqueeze(2).to_broadcast([128, 8, 64]), ALU.mult, [po, eac], [ytmp])
                                    K.tt("pool", yacc[:, g * 512:(g + 1) * 512], yacc[:, g * 512:(g + 1) * 512], ytmp[:, :], ALU.add, [yacc, ytmp], [yacc])
                            K.tt("dve", ub[:, :], yacc[:, :], sz_[:, :], ALU.mult, [yacc, sz_], [ub])
                            K.memset("pool", ssq[:, :], 0.0, [ssq])
                            for g in range(2):
                                K.op("act", lambda e, g=g: e.activation(out=junk[:, :], in_=ub[:, g * 512:(g + 1) * 512], func=AF.Square,
                                                                         accum_out=ssq[:, g:g + 1]), [ub], [junk, ssq])
                            K.rsqrt(ssq[:, :], ssq[:, :], 1.0 / 512, EPS, [ssq], [ssq])
                            for g in range(2):
                                K.ts("dve", ubf[:, g * 512:(g + 1) * 512], ub[:, g * 512:(g + 1) * 512], ssq[:, g:g + 1], None, ALU.mult, None, [ub, ssq], [ubf])
                            pt = PS[1]
                            ptv = pt[:, :].bitcast(BF16)
                            for cc in range(8):
                                K.transpose(ptv[:, cc * 128:(cc + 1) * 128], ubf[:, cc * 128:(cc + 1) * 128], identb[:, :], [ubf, identb], [pt])
                            for cc in range(8):
                                K.act(uT[:, cc, li * 128:(li + 1) * 128], ptv[:, cc * 128:(cc + 1) * 128], AF.Identity, [pt, sng], [uT], scale=sng[:, cc:cc + 1])
                        if c != 2:
                            state_update(c, 1, Hb)
            K.barrier()
            wo = K.sb([128, 8, D], BF16, "wo_ssd", st)
            wostg = [K.sb([128, 4096], F32, "wostg", st)]
            for j in range(2):
                load_w_bf16(wo[:, j * 4:(j + 1) * 4, :], cdout_d[0, 0:1024, :].rearrange("(kc p) n -> p kc n", p=128)[:, j * 4:(j + 1) * 4, :],
                            (128, 4, D), cdout_d, wo, wostg, 0)
            pieces = [((lambda oc, cc=cc: wo[:, cc, oc * 128:(oc + 1) * 128]),
                       (lambda t0, t1, cc=cc: uT[:, cc, t0 - LC:t1 - LC]), [wo, uT]) for cc in range(8)]
            apply_out(b, pieces, 2, LC)
        K.barrier()


    AB_CQ, AB_CKV, AB_KR, AB_RW = 0, 384, 640, 672

    def stage_mla(b, hT):
        win = abin_d[0, :, :].rearrange("(kc p) n -> p kc n", p=128)
        sc = 96.0 ** -0.5
        with ES() as st:
            attT = K.sb([64, 8, T], BF16, "mattT", st)
            wo = K.sb([64, 8, D], BF16, "wo_mla", st)
            with ES() as st1:
                wcq = K.sb([128, KC, 384], BF16, "wcq", st1)
                wckv = K.sb([128, KC, 256], BF16, "wckv", st1)
                wkr = K.sb([128, KC, 32], BF16, "wkr", st1)
                wkrs = K.sb([128, KC, 32], BF16, "wkrs", st1)
                wqu = K.sb([128, 3, 768], BF16, "wqu", st1)
                wqus = K.sb([128, 24, 32], BF16, "wqus", st1)
                wkvu = K.sb([128, 2, 1024], BF16, "wkvu", st1)
                cqn = K.sb([128, 3, T], BF16, "cqn", st1)
                ckvn = K.sb([128, 2, T], BF16, "ckvn", st1)
                krT = K.sb([32, T], BF16, "krT", st1)
                vtok = K.sb([128, 18, 64], BF16, "mvtok", st1)
                cosT = K.sb([32, LL], F32, "mcos", st1)
                sinT = K.sb([32, LL], F32, "msin", st1)
                gq = K.sb([128, 3], F32, "gq", st1)
                gkv = K.sb([128, 2], F32, "gkv", st1)
                qn = K.sb([64, T], BF16, "qn", st1)
                qr = K.sb([32, T], BF16, "qr", st1)
                kn = K.sb([64, T], BF16, "kn", st1)
                sq = [K.sb([128, 512], BF16, "msq", st1) for _ in range(2)]
                rstd = [K.sb([128, 512], F32, "mrstd", st1) for _ in range(2)]
                t1b = [K.sb([32, 512], F32, "mt1", st1) for _ in range(1)]
                t2b = [K.sb([32, 512], F32, "mt2", st1) for _ in range(1)]
                pT = [K.sb([128, 512], BF16, "mpT", st1) for _ in range(3)]
                rec = [K.sb([64, 512], F32, "mrec", st1) for _ in range(2)]
                stw = ES()
                wstg = [K.sb([128, 4096], F32, "wstg", stw)]
                K.dma("sp", cosT[:, :], mlacos_d[:, :], reads=[mlacos_d], writes=[cosT])
                K.dma("sp", sinT[:, :], mlasin_d[:, :], reads=[mlasin_d], writes=[sinT])
                K.dma("sp", gq[:, :], colvec(mqg_d[0, :], 3), reads=[mqg_d], writes=[gq], allow_slow_non_contiguous=True)
                K.dma("sp", gkv[:, :], colvec(mkg_d[0, :], 2), reads=[mkg_d], writes=[gkv], allow_slow_non_contiguous=True)
                load_w_bf16(wcq[:, :, :], win[:, :, AB_CQ:AB_CQ + 384], (128, KC, 384), abin_d, wcq, wstg, 0)
                load_w_bf16(wckv[:, :, :], win[:, :, AB_CKV:AB_CKV + 256], (128, KC, 256), abin_d, wckv, wstg, 0)
                load_w_bf16(wkr[:, :, :], win[:, :, AB_KR:AB_KR + 32], (128, KC, 32), abin_d, wkr, wstg, 0)
                load_w_bf16(wqu[:, :, :], mqu_d[0, :, :].rearrange("(c p) n -> p c n", p=128), (128, 3, 768), mqu_d, wqu, wstg, 0)
                load_w_bf16(wkvu[:, :, :], mkvu_d[0, :, :].rearrange("(c p) n -> p c n", p=128), (128, 2, 1024), mkvu_d, wkvu, wstg, 0)
                wov = about_d[0, 0:512, :].rearrange("(h d) n -> d h n", d=64)
                for j in range(2):
                    load_w_bf16(wo[:, j * 4:(j + 1) * 4, :], wov[:, j * 4:(j + 1) * 4, :], (64, 4, D), about_d, wo, wstg, 0)
                K.ts("pool", wkrs[:, :, 0:16], wkr[:, :, 16:32], -1.0, None, ALU.mult, None, [wkr], [wkrs])
                K.copy("pool", wkrs[:, :, 16:32], wkr[:, :, 0:16], [wkr], [wkrs])
                wq24 = wqu[:, :, :].rearrange("p c (h e) -> p (c h) e", e=96)
                K.ts("pool", wqus[:, :, 0:16], wq24[:, :, 80:96], -1.0, None, ALU.mult, None, [wqu], [wqus])
                K.copy("pool", wqus[:, :, 16:32], wq24[:, :, 64:80], [wqu], [wqus])
                K.barrier()
                stw.close()
                for bi, (t0, t1) in enumerate(BLOCKS):
                    n = t1 - t0
                    for (w_, nch, g_, dst, pbase) in ((wcq, 3, gq, cqn, 0), (wckv, 2, gkv, ckvn, 4)):
                        for c3 in range(nch):
                            pb = PS[pbase + c3]
                            for kc in range(KC):
                                K.mm(pb[:, 0:n], w_[:, kc, c3 * 128:(c3 + 1) * 128], hT[:, kc, t0:t1], kc == 0, kc == KC - 1, [w_, hT], [pb])
                        pst = PS[pbase + 3] if pbase == 0 else PS[pbase + 2]
                        for c3 in range(nch):
                            s_ = sq[c3 % 2]
                            K.act(s_[:, 0:n], PS[pbase + c3][:, 0:n], AF.Square, [PS[pbase + c3]], [s_])
                            K.mm(pst[:, 0:n], ones_bf[:, :], s_[:, 0:n], c3 == 0, c3 == nch - 1, [ones_bf, s_], [pst])
                        r_ = rstd[0 if pbase == 0 else 1]
                        K.rsqrt(r_[:, 0:n], pst[:, 0:n], 1.0 / (nch * 128), EPS, [pst], [r_])
                        for c3 in range(nch):
                            K.stt("dve", dst[:, c3, t0:t1], PS[pbase + c3][:, 0:n], g_[:, c3:c3 + 1], r_[:, 0:n], ALU.mult, ALU.mult,
                                  [PS[pbase + c3], g_, r_], [dst])
                for bi, (t0, t1) in enumerate(BLOCKS):
                    n = t1 - t0
                    pa, pb = PS[0 + 2 * (bi % 2)], PS[1 + 2 * (bi % 2)]
                    for kc in range(KC):
                        K.mm(pa[0:32, 0:n], wkr[:, kc, :], hT[:, kc, t0:t1], kc == 0, kc == KC - 1, [wkr, hT], [pa])
                    if bi == 0:
                        K.copy("dve", krT[:, t0:t1], pa[0:32, 0:n], [pa], [krT])
                    else:
                        for kc in range(KC):
                            K.mm(pb[0:32, 0:n], wkrs[:, kc, :], hT[:, kc, t0:t1], kc == 0, kc == KC - 1, [wkrs, hT], [pb])
                        a_, b_ = t1b[0], t2b[0]
                        K.tt("dve", a_[:, 0:n], pa[0:32, 0:n], cosT[:, t0 - LC:t1 - LC], ALU.mult, [pa, cosT], [a_])
                        K.tt("dve", b_[:, 0:n], pb[0:32, 0:n], sinT[:, t0 - LC:t1 - LC], ALU.mult, [pb, sinT], [b_])
                        K.tt("pool", krT[:, t0:t1], a_[:, 0:n], b_[:, 0:n], ALU.add, [a_, b_], [krT])
                u = 0
                for h in range(8):
                    for ti in range(18):
                        pa = PS[4 + ti % 2]
                        for c3 in range(2):
                            K.mm(pa[:, 0:64], ckvn[:, c3, ti * 128:(ti + 1) * 128], wkvu[:, c3, h * 128 + 64:h * 128 + 128],
                                 c3 == 0, c3 == 1, [ckvn, wkvu], [pa])
                        K.copy("dve", vtok[:, ti, :], pa[:, 0:64], [pa], [vtok])
                    for bi, (t0, t1) in enumerate(BLOCKS):
                        n = t1 - t0
                        pa, pb, pc, pd = PS[0], PS[1], PS[2], PS[3]
                        for c3 in range(3):
                            K.mm(pa[0:64, 0:n], wqu[:, c3, h * 96:h * 96 + 64], cqn[:, c3, t0:t1], c3 == 0, c3 == 2, [wqu, cqn], [pa])
                        K.copy("dve", qn[:, t0:t1], pa[0:64, 0:n], [pa], [qn])
                        for c3 in range(3):
                            K.mm(pb[0:32, 0:n], wqu[:, c3, h * 96 + 64:h * 96 + 96], cqn[:, c3, t0:t1], c3 == 0, c3 == 2, [wqu, cqn], [pb])
                        if bi == 0:
                            K.copy("dve", qr[:, t0:t1], pb[0:32, 0:n], [pb], [qr])
                        else:
                            for c3 in range(3):
                                K.mm(pc[0:32, 0:n], wqus[:, c3 * 8 + h, :], cqn[:, c3, t0:t1], c3 == 0, c3 == 2, [wqus, cqn], [pc])
                            a_, b_ = t1b[0], t2b[0]
                            K.tt("dve", a_[:, 0:n], pb[0:32, 0:n], cosT[:, t0 - LC:t1 - LC], ALU.mult, [pb, cosT], [a_])
                            K.tt("dve", b_[:, 0:n], pc[0:32, 0:n], sinT[:, t0 - LC:t1 - LC], ALU.mult, [pc, sinT], [b_])
                            K.tt("pool", qr[:, t0:t1], a_[:, 0:n], b_[:, 0:n], ALU.add, [a_, b_], [qr])
                        for c3 in range(2):
                            K.mm(pd[0:64, 0:n], wkvu[:, c3, h * 128:h * 128 + 64], ckvn[:, c3, t0:t1], c3 == 0, c3 == 1, [wkvu, ckvn], [pd])
                        K.copy("dve", kn[:, t0:t1], pd[0:64, 0:n], [pd], [kn])
                    units = []
                    for bi, (t0, t1) in enumerate(BLOCKS):
                        keys = [0, 1] if bi == 0 else list(range(18))
                        for ki, kt in enumerate(keys):
                            units.append((bi, t0, t1, ki, kt, len(keys), u))
                        u += 1

                    def front(un, idx):
                        bi, t0, t1, ki, kt, nk, uu = un
                        n = t1 - t0
                        psc = PS[idx % 4]
                        ks = slice(kt * 128, (kt + 1) * 128)
                        K.mm(psc[:, 0:n], kn[:, ks], qn[:, t0:t1], True, False, [kn, qn], [psc])
                        K.mm(psc[:, 0:n], krT[:, ks], qr[:, t0:t1], False, True, [krT, qr], [psc])

                    def back(un, idx, h=h):
                        bi, t0, t1, ki, kt, nk, uu = un
                        n = t1 - t0
                        psc = PS[idx % 4]
                        pnum, pden = PS[4 + (uu % 2) * 2], PS[5 + (uu % 2) * 2]
                        p_ = pT[idx % 3]
                        K.act(p_[:, 0:n], psc[:, 0:n], AF.Exp, [psc], [p_], scale=sc)
                        K.mm(pnum[0:64, 0:n], vtok[:, kt, :], p_[:, 0:n], ki == 0, ki == nk - 1, [vtok, p_], [pnum])
                        K.mm(pden[0:64, 0:n], ones_bf[:, 0:64], p_[:, 0:n], ki == 0, ki == nk - 1, [ones_bf, p_], [pden])
                        if ki == nk - 1:
                            r_ = rec[uu % 2]
                            K.op("dve", lambda e, r_=r_, pden=pden, n=n: e.reciprocal(out=r_[:, 0:n], in_=pden[0:64, 0:n]), [pden], [r_])
                            K.tt("dve", attT[:, h, t0:t1], pnum[0:64, 0:n], r_[:, 0:n], ALU.mult, [pnum, r_], [attT])

                    front(units[0], 0)
                    for idx, un in enumerate(units):
                        if idx + 1 < len(units):
                            front(units[idx + 1], idx + 1)
                        back(un, idx)
            K.barrier()
            pieces = [((lambda oc, h=h: wo[:, h, oc * 128:(oc + 1) * 128]),
                       (lambda t0, t1, h=h: attT[:, h, t0:t1]), [wo, attT]) for h in range(8)]
            apply_out(b, pieces, 2, 0)
        K.barrier()

    CW = -math.exp(-0.5)

    def stage_rwkv(b):
        win = abin_d[0, :, :].rearrange("(kc p) n -> p kc n", p=128)
        with ES() as st:
            cols = K.sb([128, 10, 4], F32, "rcols", st)
            for i_, src in enumerate((rkk_d[0, :], rka_d[0, :], None, rrk_d[0, :, :].rearrange("a b -> (a b)"), rlg_d[0, :], rlb_d[0, :],
                                      None, ra0_d[0, 0, :], ra0_d[0, 1, :])):
                if src is not None:
                    K.dma("sp", cols[:, i_, :], colvec(src, 4), reads=[rkk_d], writes=[cols], allow_slow_non_contiguous=True)
            K.ts("dve", cols[:, 2, :], cols[:, 1, :], -1.0, 1.0, ALU.mult, ALU.add, [cols], [cols])
            with ES() as st1:
                rT = K.sb([128, 4, T], BF16, "rT", st1)
                kT = K.sb([128, 4, T], BF16, "kT", st1)
                kknT = K.sb([128, 4, T], BF16, "kknT", st1)
                vtok = K.sb([128, 18, 512], BF16, "rvtok", st1)
                xwaT = K.sb([128, T], BF16, "xwaT", st1)
                mu = K.sb([128, 3, 14], F32, "mu", st1)
                blk64 = K.sb([128, 128], BF16, "blk64", st1)
                blk64f = K.sb([128, 128], F32, "blk64f", st1)
                K.dma("sp", blk64f[:, :], blk64_d[:, :], reads=[blk64_d], writes=[blk64f])
                K.copy("dve", blk64[:, :], blk64f[:, :], [blk64f], [blk64])
                K.dma("sp", mu[:, 0, :], colvec(rmp_d[0, :], 14), reads=[rmp_d], writes=[mu], allow_slow_non_contiguous=True)
                K.dma("sp", mu[:, 1, :], colvec(rmn_d[0, :], 14), reads=[rmn_d], writes=[mu], allow_slow_non_contiguous=True)
                K.tt("dve", mu[:, 2, :], mu[:, 0, :], mu[:, 1, :], ALU.add, [mu], [mu])
                K.ts("dve", mu[:, 2, :], mu[:, 2, :], -1.0, 1.0, ALU.mult, ALU.add, [mu], [mu])
                with ES() as st2:
                    hT = K.sb([128, KC, T], BF16, "hT", st2)
                    load_hT(hT)
                    wstg = [K.sb([128, 4096], F32, "wstg", st2)]
                    wch = [K.sb([128, KC, 128], BF16, "rwch", st2) for _ in range(2)]
                    g2b = K.sb([128, 512], BF16, "g2b", st2)
                    upad = [K.sb([128, T + 4], F32, "rupad", st2) for _ in range(2)]
                    xs = [K.sb([128, T], F32, "rxs", st2) for _ in range(1)]
                    xsb = [K.sb([128, T], BF16, "rxsb", st2) for _ in range(1)]
                    t32 = [K.sb([128, 512], F32, "rt32", st2) for _ in range(2)]
                    t16 = [K.sb([128, 512], BF16, "rt16", st2) for _ in range(2)]
                    gbo = [K.sb([128, T], BF16, "gbo", st2) for _ in range(1)]
                    rk32 = [K.sb([128, 512], F32, "rk32", st2) for _ in range(2)]
                    for u_ in upad:
                        K.memset("pool", u_[:, :], 0.0, [u_])
                    load_w_bf16(g2b[:, :], rg2_d[0, :, :], (128, 512), rg2_d, g2b, wstg, 0)

                    def poff(t):
                        return t + 1 if t < LC else t + 3

                    order = [4, 5, 6, 7, 0, 1, 2, 3, 8, 9, 10, 11, 12, 13]
                    for oi, fc in enumerate(order):
                        w_ = wch[oi % 2]
                        load_w_bf16(w_[:, :, :], win[:, :, AB_RW + fc * 128:AB_RW + (fc + 1) * 128], (128, KC, 128), abin_d, w_, wstg, 0)
                        up = upad[oi % 2]
                        for bi, (t0, t1) in enumerate(BLOCKS):
                            n = t1 - t0
                            pb = PS[bi % 4]
                            for kc in range(KC):
                                K.mm(pb[:, 0:n], w_[:, kc, :], hT[:, kc, t0:t1], kc == 0, kc == KC - 1, [w_, hT], [pb])
                            K.copy("act", up[:, poff(t0):poff(t0) + n], pb[:, 0:n], [pb], [up])
                        x_ = xs[0]
                        for (s0, ln) in ((0, LC), (LC, LL)):
                            p0 = poff(s0)
                            K.act(x_[:, s0:s0 + ln], up[:, p0:p0 + ln], AF.Identity, [up, mu], [x_], scale=mu[:, 2, fc:fc + 1])
                            K.stt("dve", x_[:, s0:s0 + ln], up[:, p0 - 1:p0 - 1 + ln], mu[:, 0, fc:fc + 1], x_[:, s0:s0 + ln], ALU.mult, ALU.add, [up, mu, x_], [x_])
                            K.stt("dve", x_[:, s0:s0 + ln], up[:, p0 + 1:p0 + 1 + ln], mu[:, 1, fc:fc + 1], x_[:, s0:s0 + ln], ALU.mult, ALU.add, [up, mu, x_], [x_])
                        if fc < 4:
                            c4 = fc
                            K.copy("act", rT[:, c4, :], x_[:, :], [x_], [rT])
                            for bi, (t0, t1) in enumerate(BLOCKS):
                                n = t1 - t0
                                a_, b_ = t32[bi % 2], t16[bi % 2]
                                K.stt("dve", b_[:, 0:n], x_[:, t0:t1], cols[:, 3, c4:c4 + 1], kT[:, c4, t0:t1], ALU.mult, ALU.mult, [x_, cols, kT], [b_])
                                pb = PS[4 + bi % 2]
                                K.mm(pb[:, 0:n], blk64[:, :], b_[:, 0:n], True, True, [blk64, b_], [pb])
                                K.copy("act", gbo[0][:, t0:t1], pb[:, 0:n], [pb], [gbo[0]])
                            K.dma("pool", gb_d[1, c4, :, :], gbo[0][:, :], reads=[gbo[0]], writes=[gb_d])
                        elif fc < 8:
                            c4 = fc - 4
                            K.copy("act", kT[:, c4, :], x_[:, :], [x_], [kT])
                            for bi, (t0, t1) in enumerate(BLOCKS):
                                n = t1 - t0
                                a_, b_ = t32[bi % 2], t16[bi % 2]
                                K.ts("dve", a_[:, 0:n], x_[:, t0:t1], cols[:, 0, c4:c4 + 1], None, ALU.mult, None, [x_, cols], [a_])
                                K.act(b_[:, 0:n], a_[:, 0:n], AF.Square, [a_], [b_])
                                pb = PS[4 + bi % 2]
                                K.mm(pb[:, 0:n], blk64[:, :], b_[:, 0:n], True, True, [blk64, b_], [pb])
                                r_ = rk32[bi % 2]
                                K.rsqrt(r_[:, 0:n], pb[:, 0:n], 1.0, 1e-12, [pb], [r_])
                                K.tt("pool", kknT[:, c4, t0:t1], a_[:, 0:n], r_[:, 0:n], ALU.mult, [a_, r_], [kknT])
                        elif fc < 12:
                            c4 = fc - 8
                            xb_ = xsb[0]
                            K.copy("act", xb_[:, :], x_[:, :], [x_], [xb_])
                            for grp in range(3):
                                tis = list(range(grp * 8, min(18, grp * 8 + 8)))
                                pb = PS[4 + grp % 2]
                                pbv = pb[:, :].bitcast(BF16)
                                for qi, ti in enumerate(tis):
                                    K.transpose(pbv[:, qi * 128:(qi + 1) * 128], xb_[:, ti * 128:(ti + 1) * 128], identb[:, :], [xb_, identb], [pb])
                                nt = len(tis)
                                K.copy("dve", vtok[:, tis[0]:tis[0] + nt, c4 * 128:(c4 + 1) * 128],
                                       pbv[:, 0:nt * 128].rearrange("p (a f) -> p a f", a=nt), [pb], [vtok])
                            K.dma("pool", gb_d[0, c4, :, :], xb_[:, :], reads=[xb_], writes=[gb_d])
                        elif fc == 12:
                            K.act(xwaT[0:64, :], x_[0:64, :], AF.Tanh, [x_], [xwaT])
                            K.copy("act", xwaT[64:128, :], x_[64:128, :], [x_], [xwaT])
                        else:
                            xb_ = xsb[0]
                            K.act(xb_[:, :], x_[:, :], AF.Sigmoid, [x_], [xb_])
                            for c4 in range(4):
                                for bi, (t0, t1) in enumerate(BLOCKS):
                                    n = t1 - t0
                                    pb = PS[bi % 4]
                                    K.mm(pb[:, 0:n], g2b[:, c4 * 128:(c4 + 1) * 128], xb_[:, t0:t1], True, True, [g2b, xb_], [pb])
                                    K.copy("act", gbo[0][:, t0:t1], pb[:, 0:n], [pb], [gbo[0]])
                                K.dma("pool", gb_d[2, c4, :, :], gbo[0][:, :], reads=[gbo[0]], writes=[gb_d])
                K.barrier()
                with ES() as st2:
                    w2b = K.sb([64, 2, 512], BF16, "w2b", st2)
                    a2b = K.sb([128, 2, 512], BF16, "a2b", st2)
                    w0bc = K.sb([128, 2, 512], F32, "w0bc", st2)
                    lcm = K.sb([128, 2, 128], F32, "lcm", st2)
                    lexcm = K.sb([128, 2, 128], F32, "lexcm", st2)
                    mcol = K.sb([128, 2, 2], F32, "mcol", st2)
                    m1 = K.sb([128, 2, 128], F32, "m1", st2)
                    m3 = K.sb([128, 2, 384], F32, "m3", st2)
                    m1t = K.sb([128, 2, 128], F32, "m1t", st2)
                    for dst, src in ((lcm, rwlc_d), (lexcm, rwlexc_d), (mcol, rwmcol_d), (m1, rwm1_d), (m3, rwm3_d), (m1t, rwm1t_d)):
                        for d in range(2):
                            K.dma("sp", dst[:, d, :], src[d, :, :], reads=[src], writes=[dst])
                    with ES() as stw:
                        wstg = [K.sb([128, 1024], F32, "wstg", stw)]
                        for d in range(2):
                            load_w_bf16(w2b[:, d, :], rw2_d[0, d, :, :], (64, 512), rw2_d, w2b, wstg, 0)
                            s_ = wstg[0]
                            K.dma("sp", s_[64:128, 0:512], ra2_d[0, d, :, :], reads=[ra2_d], writes=[s_])
                            K.copy("pool", a2b[64:128, d, :], s_[64:128, 0:512], [s_], [a2b])
                            K.dma("sp", w0bc[:, d, :], rw0_d[0, d, :].partition_broadcast(128), reads=[rw0_d], writes=[w0bc])
                        K.barrier()

                    def dir_stream(d):
                        B = PS[4 * d:4 * d + 4]
                        sg = K.sb([128, 512], F32, "sg", st2)
                        aT = K.sb([128, 128], F32, "aT", st2)
                        tmpa = K.sb([128, 128], F32, "tmpa", st2)
                        tmpb = K.sb([128, 128], F32, "tmpb", st2)
                        eL = K.sb([128, 128], F32, "eL", st2)
                        enL = K.sb([128, 128], F32, "enL", st2)
                        eLex = K.sb([128, 128], F32, "eLex", st2)
                        pm_sb = K.sb([128, 4, 2], F32, "pm_sb", st2)
                        gm = K.sb([128, 4, 2], F32, "gm", st2)
                        AR = K.sb([128, 4, 256], BF16, "AR", st2)
                        BH = K.sb([128, 4, 128], BF16, "BH", st2)
                        KH = K.sb([128, 4, 128], BF16, "KH", st2)
                        BKtok = K.sb([128, 2, 512], BF16, "BKtok", st2)
                        Q = [K.sb([128, 8, 128], F32, "Qa", st2), K.sb([128, 8, 128], F32, "Qb", st2)]
                        QT = [K.sb([128, 8, 128], F32, "QTa", st2), K.sb([128, 8, 128], F32, "QTb", st2)]
                        Nm = K.sb([128, 8, 128], F32, "Nm", st2)
                        S3 = K.sb([128, 8, 384], BF16, "S3", st2)
                        H = K.sb([128, 4, 64], F32, "H", st2)
                        H0 = K.sb([128, 4, 64], F32, "H0", st2)
                        H0b = K.sb([128, 4, 64], BF16, "H0b", st2)
                        W_sb = K.sb([128, 512], F32, "W_sb", st2)
                        U_sb = K.sb([128, 512], BF16, "U_sb", st2)
                        ybuf = K.sb([128, 512], F32, "ybuf", st2)
                        yold = K.sb([128, 512], F32, "yold", st2)
                        yield
                        K.memset("dve", H[:, :, :], 0.0, [H])
                        tiles = list(range(18)) if d == 0 else [1, 0] + list(range(17, 1, -1))
                        for ci, c in enumerate(tiles):
                            tsl = slice(c * 128, (c + 1) * 128)
                            pz = B[0]
                            K.mm(pz[:, :], xwaT[0:64, tsl], w2b[:, d, :], True, True, [xwaT, w2b], [pz])
                            K.tt("dve", sg[:, :], pz[:, :], w0bc[:, d, :], ALU.add, [pz, w0bc], [sg])
                            K.act(sg[:, :], sg[:, :], AF.Sigmoid, [sg], [sg])
                            for f4 in range(4):
                                fs = slice(f4 * 128, (f4 + 1) * 128)
                                pa = B[1]
                                K.mm(pa[:, 0:128], a2b[64:128, d, fs], xwaT[64:128, tsl], True, True, [a2b, xwaT], [pa])
                                K.act(aT[:, :], pa[:, 0:128], AF.Sigmoid, [pa, cols], [aT], bias=cols[:, 7 + d, f4:f4 + 1])
                                pl = B[2 + f4 % 2]
                                K.mm(pl[:, 0:128], sg[:, fs], lcm[:, d, :], True, True, [sg, lcm], [pl])
                                K.mm(pl[:, 128:256], sg[:, fs], lexcm[:, d, :], True, True, [sg, lexcm], [pl])
                                K.mm(pl[:, 256:258], sg[:, fs], mcol[:, d, :], True, True, [sg, mcol], [pl])
                                K.act(eL[:, :], pl[:, 0:128], AF.Exp, [pl], [eL], scale=CW)
                                K.act(enL[:, :], pl[:, 0:128], AF.Exp, [pl], [enL], scale=-CW)
                                K.act(eLex[:, :], pl[:, 128:256], AF.Exp, [pl], [eLex], scale=CW)
                                K.copy("act", pm_sb[:, f4, :], pl[:, 256:258], [pl], [pm_sb])
                                K.tt("dve", AR[:, f4, 128:256], rT[:, f4, tsl], eL[:, :], ALU.mult, [rT, eL], [AR])
                                K.stt("dve", AR[:, f4, 0:128], kknT[:, f4, tsl], -1.0, eLex[:, :], ALU.mult, ALU.mult, [kknT, eLex], [AR])
                                K.tt("pool", tmpa[:, :], kknT[:, f4, tsl], aT[:, :], ALU.mult, [kknT, aT], [tmpa])
                                K.tt("dve", BH[:, f4, :], tmpa[:, :], enL[:, :], ALU.mult, [tmpa, enL], [BH])
                                K.ts("dve", tmpb[:, :], aT[:, :], cols[:, 1, f4:f4 + 1], cols[:, 2, f4:f4 + 1], ALU.mult, ALU.add, [aT, cols], [tmpb])
                                K.tt("pool", tmpb[:, :], tmpb[:, :], kT[:, f4, tsl], ALU.mult, [tmpb, kT], [tmpb])
                                K.tt("dve", KH[:, f4, :], tmpb[:, :], enL[:, :], ALU.mult, [tmpb, enL], [KH])
                                yield
                            K.tt("dve", pm_sb[:, :, 1], pm_sb[:, :, 1], pm_sb[:, :, 0], ALU.subtract, [pm_sb], [pm_sb])
                            K.act(gm[:, :, :], pm_sb[:, :, :], AF.Exp, [pm_sb], [gm], scale=CW)
                            pt = B[1]
                            ptv = pt[:, :].bitcast(BF16)
                            for f4 in range(4):
                                K.transpose(ptv[:, f4 * 128:(f4 + 1) * 128], BH[:, f4, :], identb[:, :], [BH, identb], [pt])
                                K.transpose(ptv[:, 512 + f4 * 128:512 + (f4 + 1) * 128], KH[:, f4, :], identb[:, :], [KH, identb], [pt])
                            K.copy("act", BKtok[:, :, :], ptv[:, :].rearrange("p (a f) -> p a f", a=2), [pt], [BKtok])
                            yield
                            for h in range(8):
                                f4, hr = h // 2, slice((h % 2) * 64, (h % 2) * 64 + 64)
                                ps_ = B[2 * (h % 2)]
                                K.mm(ps_[:, 0:256], BH[hr, f4, :], AR[hr, f4, :], True, True, [BH, AR], [ps_])
                                K.mm(ps_[:, 256:512], KH[hr, f4, :], AR[hr, f4, :], True, True, [KH, AR], [ps_])
                                K.tt("dve", Q[0][:, h, :], ps_[:, 0:128], m1[:, d, :], ALU.mult, [ps_, m1], [Q[0]])
                                K.tt("dve", S3[:, h, :], ps_[:, 128:512], m3[:, d, :], ALU.mult, [ps_, m3], [S3])
                                pq = B[2 * (h % 2) + 1]
                                K.mm(pq[:, 0:128], AR[hr, f4, 0:128], BH[hr, f4, :], True, True, [AR, BH], [pq])
                                K.tt("dve", QT[0][:, h, :], pq[:, 0:128], m1t[:, d, :], ALU.mult, [pq, m1t], [QT[0]])
                                if h % 2 == 1:
                                    yield
                            K.tt("pool", Nm[:, :, :], Q[0][:, :, :], ident[:, :].unsqueeze(1).to_broadcast([128, 8, 128]), ALU.add, [Q[0], ident], [Nm])
                            cur = 0
                            for lev in range(1, 7):
                                nxt = 1 - cur
                                for half in range(2):
                                    pqa, pqb, pn = B[(3 * half) % 4], B[(3 * half + 1) % 4], B[(3 * half + 2) % 4]
                                    hs = slice(half * 4, half * 4 + 4)
                                    for hh in range(4):
                                        h = half * 4 + hh
                                        cs = slice(hh * 128, (hh + 1) * 128)
                                        K.mm(pqb[:, cs], Q[cur][:, h, :], QT[cur][:, h, :], True, True, [Q[cur], QT[cur]], [pqb])
                                        if lev < 6:
                                            K.mm(pqa[:, cs], QT[cur][:, h, :], Q[cur][:, h, :], True, True, [Q[cur], QT[cur]], [pqa])
                                    K.copy("act", QT[nxt][:, hs, :], pqb[:, :].rearrange("p (a f) -> p a f", a=4), [pqb], [QT[nxt]])
                                    if lev < 6:
                                        K.copy("dve", Q[nxt][:, hs, :], pqa[:, :].rearrange("p (a f) -> p a f", a=4), [pqa], [Q[nxt]])
                                    yield
                                    for hh in range(4):
                                        h = half * 4 + hh
                                        K.mm(pn[:, hh * 128:(hh + 1) * 128], QT[nxt][:, h, :], Nm[:, h, :], True, True, [QT[nxt], Nm], [pn])
                                    K.tt("dve", Nm[:, hs, :], Nm[:, hs, :], pn[:, :].rearrange("p (a f) -> p a f", a=4), ALU.add, [Nm, pn], [Nm])
                                    yield
                                cur = nxt
                            K.tt("dve", H0[:, :, :], H[:, :, :], gm[:, :, 0:1].to_broadcast([128, 4, 64]), ALU.mult, [H, gm], [H0])
                            K.copy("act", H0b[:, :, :], H0[:, :, :], [H0], [H0b])
                            pw = B[0]
                            for h in range(8):
                                f4, hr = h // 2, slice((h % 2) * 64, (h % 2) * 64 + 64)
                                cs = slice(h * 64, (h + 1) * 64)
                                K.mm(pw[:, cs], AR[hr, f4, 0:128], H0b[hr, f4, :], True, False, [AR, H0b], [pw])
                                K.mm(pw[:, cs], S3[:, h, 128:256], vtok[:, c, cs], False, True, [S3, vtok], [pw])
                            K.copy("act", W_sb[:, :], pw[:, :], [pw], [W_sb])
                            yield
                            pu = B[1]
                            for h in range(8):
                                cs = slice(h * 64, (h + 1) * 64)
                                K.mm(pu[:, cs], Nm[:, h, :], W_sb[:, cs], True, True, [Nm, W_sb], [pu])
                            K.copy("act", U_sb[:, :], pu[:, :], [pu], [U_sb])
                            yield
                            py = B[2]
                            for h in range(8):
                                f4, hr = h // 2, slice((h % 2) * 64, (h % 2) * 64 + 64)
                                cs = slice(h * 64, (h + 1) * 64)
                                K.mm(py[:, cs], AR[hr, f4, 128:256], H0b[hr, f4, :], True, False, [AR, H0b], [py])
                                K.mm(py[:, cs], S3[:, h, 0:128], U_sb[:, cs], False, False, [S3, U_sb], [py])
                                K.mm(py[:, cs], S3[:, h, 256:384], vtok[:, c, cs], False, True, [S3, vtok], [py])
                            if d == 0:
                                K.copy("dve", ybuf[:, :], py[:, :], [py], [ybuf])
                                K.dma("pool", yf_d[c, :, :], ybuf[:, :], reads=[ybuf], writes=[yf_t[c]])
                            else:
                                K.copy("dve", ybuf[:, :], py[:, :], [py], [ybuf])
                                K.dma("pool", y_d[c, :, :], ybuf[:, :], reads=[ybuf], writes=[yb_t[c]])
                            yield
                            if ci < len(tiles) - 1:
                                ph = B[3]
                                for f4 in range(4):
                                    fs = slice(f4 * 128, (f4 + 1) * 128)
                                    K.mm(ph[:, fs], BKtok[:, 0, fs], U_sb[:, fs], True, False, [BKtok, U_sb], [ph])
                                    K.mm(ph[:, fs], BKtok[:, 1, fs], vtok[:, c, fs], False, True, [BKtok, vtok], [ph])
                                phv = ph[:, :].rearrange("p (f x) -> p f x", f=4)
                                for e2_ in range(2):
                                    rs = slice(e2_ * 64, e2_ * 64 + 64)
                                    K.tt("dve", H[rs, :, :], H0[rs, :, :], phv[rs, :, e2_ * 64:(e2_ + 1) * 64], ALU.add, [H0, ph], [H])
                                    K.tt("dve", H[rs, :, :], H[rs, :, :], gm[rs, :, 1:2].to_broadcast([64, 4, 64]), ALU.mult, [H, gm], [H])
                                yield

                    yf_t = [Buf(yf_d.t, "yf%d" % i_) for i_ in range(18)]
                    yb_t = [Buf(y_d.t, "yb%d" % i_) for i_ in range(18)]
                    streams = [dir_stream(0), dir_stream(1)]
                    for s_ in streams:
                        next(s_)
                    for _ in range(cfg.get("rw_offset", 19)):
                        next(streams[1])
                    alive = list(streams)
                    while alive:
                        for s_ in list(alive):
                            try:
                                next(s_)
                            except StopIteration:
                                alive.remove(s_)
            K.barrier()
            rwo = K.sb([128, 4, T], BF16, "rwo", st)
            with ES() as st2:
                yt = [K.sb([128, 512], F32, "yt", st2) for _ in range(2)]
                ysq = K.sb([128, 512], F32, "ysq", st2)
                s1 = K.sb([128, 8], F32, "s1", st2)
                s2 = K.sb([128, 8], F32, "s2", st2)
                ynb = K.sb([128, 512], BF16, "ynb", st2)
                vT_ = [K.sb([128, 4, 128], BF16, "vT_", st2) for _ in range(2)]
                sc_ = [K.sb([128, 4, 128], BF16, "sc_", st2) for _ in range(2)]
                gg_ = [K.sb([128, 4, 128], BF16, "gg_", st2) for _ in range(2)]
                yn32 = K.sb([128, 4, 128], F32, "yn32", st2)
                bon = K.sb([128, 4, 128], F32, "bon", st2)
                gbv = gb_d[:, :, :, :].rearrange("a c p t -> a p c t")
                for c in range(18):
                    tsl = slice(c * 128, (c + 1) * 128)
                    y_ = yt[c % 2]
                    K.dma("sp", y_[:, :], y_d[c, :, :], reads=[y_d], writes=[y_])
                    K.dma("sp", ysq[:, :], yf_d[c, :, :], reads=[yf_d], writes=[ysq])
                    K.tt("dve", y_[:, :], y_[:, :], ysq[:, :], ALU.add, [y_, ysq], [y_])
                    K.dma("sp", vT_[c % 2][:, :, :], gbv[0, :, :, tsl], reads=[gb_d], writes=[vT_[c % 2]])
                    K.dma("sp", sc_[c % 2][:, :, :], gbv[1, :, :, tsl], reads=[gb_d], writes=[sc_[c % 2]])
                    K.dma("sp", gg_[c % 2][:, :, :], gbv[2, :, :, tsl], reads=[gb_d], writes=[gg_[c % 2]])
                    yv = y_[:, :].rearrange("p (h e) -> p h e", h=8)
                    K.op("dve", lambda e, yv=yv: e.tensor_reduce(out=s1[:, :], in_=yv, axis=AX.X, op=ALU.add), [y_], [s1])
                    K.act(ysq[:, :], y_[:, :], AF.Square, [y_], [ysq])
                    K.op("dve", lambda e: e.tensor_reduce(out=s2[:, :], in_=ysq[:, :].rearrange("p (h e) -> p h e", h=8), axis=AX.X, op=ALU.add), [ysq], [s2])
                    K.ts("dve", s1[:, :], s1[:, :], 1.0 / 64, None, ALU.mult, None, [s1], [s1])
                    K.tt("dve", ysq[:, 0:8], s1[:, :], s1[:, :], ALU.mult, [s1], [ysq])
                    K.stt("dve", s2[:, :], s2[:, :], 1.0 / 64, ysq[:, 0:8], ALU.mult, ALU.subtract, [s2, ysq], [s2])
                    K.rsqrt(s2[:, :], s2[:, :], 1.0, 64e-5, [s2], [s2])
                    K.tt("dve", yv, yv, s1[:, :].unsqueeze(2).to_broadcast([128, 8, 64]), ALU.subtract, [y_, s1], [y_])
                    K.tt("dve", ynb[:, :].rearrange("p (h e) -> p h e", h=8), yv, s2[:, :].unsqueeze(2).to_broadcast([128, 8, 64]), ALU.mult, [y_, s2], [ynb])
                    pt = PS[c % 2]
                    ptv = pt[:, :].bitcast(BF16)
                    for c4 in range(4):
                        K.transpose(ptv[:, c4 * 128:(c4 + 1) * 128], ynb[:, c4 * 128:(c4 + 1) * 128], identb[:, :], [ynb, identb], [pt])
                    for c4 in range(4):
                        K.act(yn32[:, c4, :], ptv[:, c4 * 128:(c4 + 1) * 128], AF.Identity, [pt, cols], [yn32],
                              bias=cols[:, 5, c4:c4 + 1], scale=cols[:, 4, c4:c4 + 1])
                    K.tt("pool", bon[:, :, :], vT_[c % 2][:, :, :], sc_[c % 2][:, :, :], ALU.mult, [vT_[c % 2], sc_[c % 2]], [bon])
                    K.tt("pool", yn32[:, :, :], yn32[:, :, :], bon[:, :, :], ALU.add, [yn32, bon], [yn32])
                    K.tt("dve", rwo[:, :, tsl], yn32[:, :, :], gg_[c % 2][:, :, :], ALU.mult, [yn32, gg_[c % 2]], [rwo])
            K.barrier()
            wo = K.sb([128, 4, D], BF16, "wo_rw", st)
            wostg = [K.sb([128, 4096], F32, "wostg", st)]
            load_w_bf16(wo[:, :, :], about_d[0, 512:1024, :].rearrange("(kc p) n -> p kc n", p=128), (128, 4, D), about_d, wo, wostg, 0)
            pieces = [((lambda oc, cc=cc: wo[:, cc, oc * 128:(oc + 1) * 128]),
                       (lambda t0, t1, cc=cc: rwo[:, cc, t0:t1]), [wo, rwo]) for cc in range(4)]
            apply_out(b, pieces, 2, 0)
        K.barrier()

    K.barrier()
    for l in layers:
        stage_mods(l)
    for b in range(nb):
        stage_load(b)
        for li, l in enumerate(layers):
            last = (li == len(layers) - 1) and not cfg.get("force_ctx", False)
            modsT.l = l
            Amod.l = l
            if mixers:
                if l == 1:
                    with ES() as sth:
                        hT = K.sb([128, KC, T], BF16, "hT", sth)
                        stage_norm(b, 0, hT, to_dram=True)
                        if cfg.get("swa", True):
                            stage_swa(b, hT)
                    if cfg.get("ssd", True):
                        stage_ssd(b, None)
                else:
                    with ES() as sth:
                        hT = K.sb([128, KC, T], BF16, "hT", sth)
                        stage_norm(b, 0, hT, to_dram=True)
                        if cfg.get("mla", True):
                            stage_mla(b, hT)
                    if cfg.get("rwkv", True):
                        stage_rwkv(b)
            with ES() as sth:
                hT = K.sb([128, KC, T], BF16, "hT", sth)
                stage_norm(b, 1, hT, lo=0 if not last else LC)
                stage_ffn(b, l, do_ctx=not last, hT=hT)
        stage_out(b)
        if cfg.get("dbg_x", False) and b == 0:
            for cc in range(KC):
                K.dma("sp", dbgx_d[cc, :, :], xs_d[cc, :, :], reads=xblk, writes=[dbgx_d])
    K.barrier()
    K.es.close()
    return nc, K


CONST_INPUTS = None


def _rope_tables(rot_dim):
    n_freq = rot_dim // 4
    rows = np.arange(LL, dtype=np.float32) // 64
    cols = np.arange(LL, dtype=np.float32) % 64
    inv = (np.float32(10000.0) ** (-np.arange(n_freq, dtype=np.float32) / np.float32(n_freq))).astype(np.float32)
    ang = np.concatenate([rows[:, None] * inv[None, :], cols[:, None] * inv[None, :]], axis=-1).astype(np.float32)
    cos, sin = np.cos(ang).astype(np.float32), np.sin(ang).astype(np.float32)
    cosT = np.concatenate([cos.T, cos.T], axis=0)
    sinT = np.concatenate([sin.T, sin.T], axis=0)
    return np.ascontiguousarray(cosT), np.ascontiguousarray(sinT)


def const_inputs():
    global CONST_INPUTS
    if CONST_INPUTS is None:
        s = np.arange(128)
        triF = (s[:, None] <= s[None, :]).astype(np.float32)
        c = {"ident": np.eye(128, dtype=np.float32), "triF": triF, "triB": np.ascontiguousarray(triF.T),
             "strF": (s[:, None] > s[None, :]).astype(np.float32), "strB": (s[:, None] < s[None, :]).astype(np.float32)}
        c["swa_cos"], c["swa_sin"] = _rope_tables(64)
        c["mla_cos"], c["mla_sin"] = _rope_tables(32)
        triB = triF.T
        incl = [triF, triB]
        strict = [c["strB"], c["strF"]]
        m = 63
        c["rw_lc"] = np.stack([incl[d] - incl[d][:, m:m + 1] for d in range(2)]).astype(np.float32)
        c["rw_lexc"] = np.stack([strict[d] - incl[d][:, m:m + 1] for d in range(2)]).astype(np.float32)
        c["rw_mcol"] = np.stack([np.stack([incl[d][:, m], np.ones(128, np.float32)], axis=1) for d in range(2)]).astype(np.float32)
        c["rw_m1"] = np.stack([strict[d] for d in range(2)]).astype(np.float32)
        c["rw_m3"] = np.stack([np.concatenate([incl[d], strict[d], incl[d]], axis=1) for d in range(2)]).astype(np.float32)
        c["rw_m1t"] = np.stack([np.ascontiguousarray(strict[d].T) for d in range(2)]).astype(np.float32)
        blk = np.zeros((128, 128), np.float32)
        blk[:64, :64] = 1.0
        blk[64:, 64:] = 1.0
        c["blk64"] = blk
        CONST_INPUTS = c
    return CONST_INPUTS


def make_in_maps(nc_names, inputs, ncores=NCORES):
    consts = const_inputs()
    in_maps = []
    for core in range(ncores):
        m = {}
        sl = slice(core * BPC, (core + 1) * BPC)
        for k in nc_names:
            if k in consts:
                m[k] = consts[k]
            else:
                v = np.asarray(inputs[k])
                m[k] = np.ascontiguousarray(v[sl] if k in ("x", "c", "ctx") else v)
        in_maps.append(m)
    return in_maps


def kernel(**inputs):
    cfg = {}
    nc, K = build_program(cfg)
    in_maps = make_in_maps(K.in_names, inputs)
    res = run_bass_kernel_spmd(nc, in_maps, core_ids=list(range(NCORES)))
    return np.concatenate([r["out"] for r in res.results], axis=0)
```

```python
import contextlib
import math
import numpy as np
import concourse.bass as bass
import concourse.mybir as mybir
from concourse.bass_utils import run_bass_kernel_spmd

F32 = mybir.dt.float32
BF16 = mybir.dt.bfloat16
AF = mybir.ActivationFunctionType
ALU = mybir.AluOpType
AX = mybir.AxisListType

NCORES = 8
BPC = 4
D = 1024
KC = 8
LC = 256
LL = 2048
T = LC + LL
DFF = 2816
EPS = 1e-6
BLOCKS = [(0, 256), (256, 768), (768, 1280), (1280, 1792), (1792, 2304)]
SEM_EPOCH = 50000


class Buf:
    def __init__(self, t, name):
        self.t = t
        self.name = name
        self.lw = []
        self.rd = []
        self.ds = None

    def __getitem__(self, idx):
        return self.t[idx]


class Eng:
    def __init__(self, name, e, is_pe=False):
        self.name = name
        self.e = e
        self.is_pe = is_pe
        self.sems = []
        self.count = 0
        self.epoch = 0
        self.seen = {}


class Kern:
    def __init__(self, nc):
        self.nc = nc
        self.es = contextlib.ExitStack()
        self.engs = {}
        for name, e, ispe in (("pe", nc.tensor, True), ("act", nc.scalar, False),
                              ("dve", nc.vector, False), ("pool", nc.gpsimd, False),
                              ("sp", nc.sync, False)):
            en = Eng(name, e, ispe)
            en.sems.append(self.es.enter_context(nc.semaphore("s_%s_0" % name)))
            self.engs[name] = en
        self.ndsem = 40
        self.dsem = [self.es.enter_context(nc.semaphore("s_d%d" % i)) for i in range(self.ndsem)]
        self.dtot = [0] * self.ndsem
        self.drr = 0
        self.nbuf = 0
        self.n_ops = 0
        self.eps_bufs = {}
        self.in_names = []

    def sb(self, shape, dtype, name=None, stack=None):
        self.nbuf += 1
        name = "%s_%d" % (name or "sb", self.nbuf)
        t = (stack or self.es).enter_context(self.nc.sbuf_tensor(name, list(shape), dtype))
        return Buf(t, name)

    def ps(self, name=None):
        self.nbuf += 1
        name = "%s_%d" % (name or "ps", self.nbuf)
        t = self.es.enter_context(self.nc.psum_tensor(name, [128, 512], F32))
        return Buf(t, name)

    def dram(self, name, shape, dtype, kind="Internal"):
        t = self.nc.dram_tensor(name, list(shape), dtype, kind=kind)
        if kind == "ExternalInput":
            self.in_names.append(name)
        return Buf(t, name)

    def _wait(self, eng, ev):
        if ev[0] == "E":
            src = self.engs[ev[1]]
            ep, n = ev[2], ev[3]
            if src is eng and eng.is_pe:
                return
            key = ("E", ev[1], ep)
            if eng.seen.get(key, 0) >= n:
                return
            eng.e.wait_ge(src.sems[ep], n)
            eng.seen[key] = n
        else:
            i = ev[1]
            tot = self.dtot[i]
            key = ("D", i)
            if eng.seen.get(key, 0) >= tot:
                return
            eng.e.wait_ge(self.dsem[i], tot)
            eng.seen[key] = tot

    def _deps(self, eng, reads, writes):
        for b in reads:
            for ev in b.lw:
                self._wait(eng, ev)
        for b in writes:
            for ev in b.lw:
                self._wait(eng, ev)
            for ev in b.rd:
                self._wait(eng, ev)

    def _record(self, ev, reads, writes):
        for b in reads:
            if ev[0] == "E":
                b.rd = [r for r in b.rd if not (r[0] == "E" and r[1] == ev[1])]
            else:
                b.rd = [r for r in b.rd if r != ev]
            b.rd.append(ev)
        for b in writes:
            b.lw = [ev]
            b.rd = []

    def op(self, engname, fn, reads=(), writes=()):
        eng = self.engs[engname]
        self._deps(eng, reads, writes)
        if eng.count >= SEM_EPOCH:
            eng.epoch += 1
            eng.count = 0
            eng.sems.append(self.es.enter_context(self.nc.semaphore("s_%s_%d" % (engname, eng.epoch))))
        ins = fn(eng.e)
        eng.count += 1
        ins.then_inc(eng.sems[eng.epoch], 1)
        ev = ("E", engname, eng.epoch, eng.count)
        self._record(ev, reads, writes)
        self.n_ops += 1

    def dma(self, engname, out, in_, reads=(), writes=(), **kw):
        eng = self.engs[engname]
        self._deps(eng, reads, writes)
        b = None
        for cand in list(writes) + list(reads):
            if cand.ds is not None:
                b = cand
                break
        if b is None:
            b = (list(writes) + list(reads))[0]
            b.ds = self.drr
            self.drr = (self.drr + 1) % self.ndsem
        i = b.ds
        ins = eng.e.dma_start(out=out, in_=in_, **kw)
        ins.then_inc(self.dsem[i], 16)
        self.dtot[i] += 16
        ev = ("D", i)
        self._record(ev, reads, writes)
        self.n_ops += 1

    def barrier(self):
        for eng in self.engs.values():
            for other in self.engs.values():
                if other is eng:
                    continue
                if other.count > 0:
                    self._wait(eng, ("E", other.name, other.epoch, other.count))
            for i in range(self.ndsem):
                if self.dtot[i] > 0:
                    self._wait(eng, ("D", i))

    def mm(self, out, lhsT, rhs, start, stop, reads, writes):
        self.op("pe", lambda e: e.matmul(out, lhsT=lhsT, rhs=rhs, start=start, stop=stop), reads, writes)

    def transpose(self, out, in_, ident, reads, writes):
        self.op("pe", lambda e: e.transpose(out, in_, ident), reads, writes)

    def act(self, out, in_, func, reads, writes, bias=None, scale=None, eng="act"):
        kw = {}
        if bias is not None:
            kw["bias"] = bias
        if scale is not None:
            kw["scale"] = scale
        self.op(eng, lambda e: e.activation(out=out, in_=in_, func=func, **kw), reads, writes)

    def ts(self, eng, out, in0, s1, s2, op0, op1, reads, writes):
        if op1 is None:
            self.op(eng, lambda e: e.tensor_scalar(out=out, in0=in0, scalar1=s1, scalar2=None, op0=op0), reads, writes)
        else:
            self.op(eng, lambda e: e.tensor_scalar(out=out, in0=in0, scalar1=s1, scalar2=s2, op0=op0, op1=op1), reads, writes)

    def tt(self, eng, out, in0, in1, op, reads, writes):
        self.op(eng, lambda e: e.tensor_tensor(out=out, in0=in0, in1=in1, op=op), reads, writes)

    def stt(self, eng, out, in0, scalar, in1, op0, op1, reads, writes):
        self.op(eng, lambda e: e.scalar_tensor_tensor(out=out, in0=in0, scalar=scalar, in1=in1, op0=op0, op1=op1), reads, writes)

    def copy(self, eng, out, in_, reads, writes):
        if eng == "act":
            self.op(eng, lambda e: e.activation(out=out, in_=in_, func=AF.Copy), reads, writes)
        else:
            self.op(eng, lambda e: e.tensor_copy(out=out, in_=in_), reads, writes)

    def rsqrt(self, out, in_, scale, eps, reads, writes):
        self.op("act", lambda e: e.activation(out=out, in_=in_, func=AF.Sqrt, bias=self.eps_ap(eps), scale=scale), reads, writes)
        self.op("dve", lambda e: e.reciprocal(out=out, in_=out), writes, writes)

    def eps_ap(self, eps):
        if eps not in self.eps_bufs:
            b = self.sb([128, 1], F32, "eps")
            self.memset("dve", b[:, :], float(eps), [b])
            self.eps_bufs[eps] = b
        return self.eps_bufs[eps][:, 0:1]

    def memset(self, eng, ap, val, writes):
        self.op(eng, lambda e: e.memset(ap, val), (), writes)


def colvec(ap1d, n):
    return ap1d.rearrange("(c p) -> p c", p=128)


def stg_view(s, shape):
    n = 1
    for d_ in shape[1:]:
        n *= d_
    v = s[0:shape[0], 0:n]
    if len(shape) == 3:
        v = v.rearrange("p (a b) -> p a b", a=shape[1])
    return v


HD = 64
CD_Z, CD_XBC, CD_DT, CD_Q, CD_K, CD_V = 0, 1024, 2560, 2592, 3104, 3232
CD_IN = 3360


def build_program(cfg):
    nc = bass.Bass("TRN2", target_bir_lowering=False)
    K = Kern(nc)
    nb = cfg.get("nb", BPC)
    layers = cfg.get("layers", [0, 1])
    mixers = cfg.get("mixers", True)
    ES = contextlib.ExitStack

    def din(name, shape):
        return K.dram(name, shape, F32, kind="ExternalInput")

    x_d = din("x", [BPC, LL, D])
    c_d = din("c", [BPC, D])
    ctx_d = din("ctx", [BPC, LC, D])
    cctx_d = din("c_ctx", [D])
    ada_w_d = din("ada_w", [2, D, 6 * D])
    ada_b_d = din("ada_b", [2, 6 * D])
    nmix_d = din("norm_mix_g", [2, D])
    nffn_d = din("norm_ffn_g", [2, D])
    wup_d = din("ffn_w_up", [2, D, 2 * DFF])
    cw_d = din("ffn_conv_w", [2, 3, 2 * DFF])
    cb_d = din("ffn_conv_b", [2, 2 * DFF])
    wdn_d = din("ffn_w_down", [2, DFF, D])
    fng_d = din("final_norm_g", [D])
    if mixers and 1 in layers:
        cdin_d = din("cd_w_in", [1, D, CD_IN])
        cdout_d = din("cd_w_out", [1, 1536, D])
        scw_d = din("ssm_conv_w", [1, 5, 1536])
        scb_d = din("ssm_conv_b", [1, 1536])
        sdtb_d = din("ssm_dt_bias", [1, 2, 16])
        salog_d = din("ssm_a_log", [1, 2, 16])
        sd_d = din("ssm_d", [1, 16])
        sng_d = din("ssm_norm_g", [1, 1024])
        sink_d = din("swa_sink", [1, 8])
        swacos_d = din("swa_cos", [64, LL])
        swasin_d = din("swa_sin", [64, LL])
        triF_d = din("triF", [128, 128])
        triB_d = din("triB", [128, 128])
        strF_d = din("strF", [128, 128])
        strB_d = din("strB", [128, 128])
    if mixers and 0 in layers:
        abin_d = din("ab_w_in", [1, D, 2464])
        about_d = din("ab_w_out", [1, D, D])
        mqg_d = din("mla_q_norm_g", [1, 384])
        mqu_d = din("mla_w_q_up", [1, 384, 768])
        mkg_d = din("mla_kv_norm_g", [1, 256])
        mkvu_d = din("mla_w_kv_up", [1, 256, 1024])
        mlacos_d = din("mla_cos", [32, LL])
        mlasin_d = din("mla_sin", [32, LL])
        rmp_d = din("rwkv_mu_prev", [1, 1792])
        rmn_d = din("rwkv_mu_next", [1, 1792])
        rw0_d = din("rwkv_w0", [1, 2, 512])
        rw2_d = din("rwkv_w2", [1, 2, 64, 512])
        ra0_d = din("rwkv_a0", [1, 2, 512])
        ra2_d = din("rwkv_a2", [1, 2, 64, 512])
        rg2_d = din("rwkv_g2", [1, 128, 512])
        rkk_d = din("rwkv_k_k", [1, 512])
        rka_d = din("rwkv_k_a", [1, 512])
        rrk_d = din("rwkv_r_k", [1, 8, 64])
        rlg_d = din("rwkv_ln_g", [1, 512])
        rlb_d = din("rwkv_ln_b", [1, 512])
        rwlc_d = din("rw_lc", [2, 128, 128])
        rwlexc_d = din("rw_lexc", [2, 128, 128])
        rwmcol_d = din("rw_mcol", [2, 128, 2])
        rwm1_d = din("rw_m1", [2, 128, 128])
        rwm3_d = din("rw_m3", [2, 128, 384])
        rwm1t_d = din("rw_m1t", [2, 128, 128])
        blk64_d = din("blk64", [128, 128])
        gb_d = K.dram("gb_scr", [3, 4, 128, T], BF16)
        y_d = K.dram("y_scr", [18, 128, 512], F32)
        yf_d = K.dram("yf_scr", [18, 128, 512], F32)
    ident_d = din("ident", [128, 128])
    out_d = K.dram("out", [BPC, LL, D], F32, kind="ExternalOutput")
    if cfg.get("dbg_x", False):
        dbgx_d = K.dram("dbgx", [KC, 128, T], F32, kind="ExternalOutput")
    aT_d = K.dram("aT_scr", [DFF, T], BF16)
    xs_d = K.dram("x_scr", [KC, 128, T], F32)
    hT_d = K.dram("hT_scr", [KC, 128, T], BF16)
    sz_d = K.dram("sz_scr", [16, 128, 1024], BF16)
    hin_d = K.dram("hin_scr", [16, 128, 1024], BF16)
    xview = xs_d[:, :, :].rearrange("c p t -> p c t")
    hview = hT_d[:, :, :].rearrange("c p t -> p c t")
    xblk = [Buf(xs_d.t, "xblk%d" % j) for j in range(T // 256)]

    def xdeps(t0, t1):
        return xblk[t0 // 256:(t1 + 255) // 256]

    ident = K.sb([128, 128], F32, "ident")
    identb = K.sb([128, 128], BF16, "identb")
    ones_bf = K.sb([128, 128], BF16, "ones")
    ones_f = K.sb([128, 128], F32, "onesf")
    class LayerBuf(Buf):
        def __init__(self, b_):
            Buf.__init__(self, b_.t, b_.name)
            self.l = 0

        def __getitem__(self, idx):
            return self.t[(idx[0], self.l) + tuple(idx[1:])]

    modsT = LayerBuf(K.sb([128, 2, KC, 6, 5], F32, "modsT"))
    Amod = LayerBuf(K.sb([128, 2, KC, 2, 5], F32, "Amod"))
    gcols = K.sb([128, 5, KC], F32, "gcols")
    cwT = K.sb([128, 2, 3, 44], F32, "cwT")
    cbT = K.sb([128, 2, 44], F32, "cbT")
    PS = [K.ps("ps%d" % i) for i in range(8)]
    for e_ in (EPS, 64e-5, 1e-12, 1.0, 0.0):
        K.eps_ap(e_)

    K.dma("sp", ident[:, :], ident_d[:, :], reads=[ident_d], writes=[ident])
    K.copy("dve", identb[:, :], ident[:, :], [ident], [identb])
    K.memset("dve", ones_bf[:, :], 1.0, [ones_bf])
    K.memset("dve", ones_f[:, :], 1.0, [ones_f])
    for l in range(2):
        K.dma("sp", gcols[:, l, :], colvec(nmix_d[l, :], KC), reads=[nmix_d], writes=[gcols], allow_slow_non_contiguous=True)
        K.dma("sp", gcols[:, 2 + l, :], colvec(nffn_d[l, :], KC), reads=[nffn_d], writes=[gcols], allow_slow_non_contiguous=True)
        for tap in range(3):
            K.dma("sp", cwT[:, l, tap, :], colvec(cw_d[l, tap, :], 44), reads=[cw_d], writes=[cwT], allow_slow_non_contiguous=True)
        K.dma("sp", cbT[:, l, :], colvec(cb_d[l, :], 44), reads=[cb_d], writes=[cbT], allow_slow_non_contiguous=True)
    K.dma("sp", gcols[:, 4, :], colvec(fng_d[:], KC), reads=[fng_d], writes=[gcols], allow_slow_non_contiguous=True)

    def stage_mods(l):
        modsT.l = l
        Amod.l = l
        with ES() as st:
            condT = K.sb([128, KC, 5], F32, "condT", st)
            scond = K.sb([128, KC, 5], F32, "scond", st)
            abT = K.sb([128, 48], F32, "abT", st)
            wb = [K.sb([128, KC, 128], F32, "adaw", st) for _ in range(3)]
            for r in range(4):
                K.dma("sp", condT[:, :, r], colvec(c_d[r, :], KC), reads=[c_d], writes=[condT], allow_slow_non_contiguous=True)
            K.dma("sp", condT[:, :, 4], colvec(cctx_d[:], KC), reads=[cctx_d], writes=[condT], allow_slow_non_contiguous=True)
            K.dma("sp", abT[:, :], colvec(ada_b_d[l, :], 48), reads=[ada_b_d], writes=[abT], allow_slow_non_contiguous=True)
            K.act(scond[:, :, :], condT[:, :, :], AF.Silu, [condT], [scond])
            wview = ada_w_d[l, :, :].rearrange("(kc p) n -> p kc n", p=128)
            for j in range(48):
                w = wb[j % 3]
                K.dma("sp", w[:, :, :], wview[:, :, j * 128:(j + 1) * 128], reads=[ada_w_d], writes=[w])
                pb = PS[j % 2]
                for kc in range(KC):
                    K.mm(pb[:, 0:5], w[:, kc, :], scond[:, kc, :], kc == 0, kc == KC - 1, [w, scond], [pb])
                kind, cc = j // 8, j % 8
                K.ts("dve", modsT[:, cc, kind, :], pb[:, 0:5], abT[:, j:j + 1], None, ALU.add, None, [pb, abT], [modsT])
            for which, kind, gi in ((0, 1, l), (1, 4, 2 + l)):
                for cc in range(KC):
                    K.ts("dve", Amod[:, cc, which, :], modsT[:, cc, kind, :], 1.0, gcols[:, gi, cc:cc + 1],
                         ALU.add, ALU.mult, [modsT, gcols], [Amod])
        K.barrier()

    def stage_load(b):
        with ES() as st:
            xin = [K.sb([128, D], F32, "xin", st) for _ in range(3)]
            xo = [K.sb([128, KC, 128], F32, "xo", st) for _ in range(3)]
            for ti in range(T // 128):
                xb = xin[ti % 3]
                if ti < 2:
                    src, sb_ = ctx_d[b, ti * 128:(ti + 1) * 128, :], ctx_d
                else:
                    src, sb_ = x_d[b, (ti - 2) * 128:(ti - 1) * 128, :], x_d
                K.dma("sp", xb[:, :], src, reads=[sb_], writes=[xb])
                o = xo[ti % 3]
                for half in range(2):
                    pb = PS[(2 * ti + half) % 4]
                    for q in range(4):
                        cc = half * 4 + q
                        K.transpose(pb[:, q * 128:(q + 1) * 128], xb[:, cc * 128:(cc + 1) * 128], ident[:, :], [xb, ident], [pb])
                    K.copy("act" if half == 0 else "dve", o[:, half * 4:half * 4 + 4, :],
                           pb[:, :].rearrange("p (q t) -> p q t", q=4), [pb], [o])
                K.dma("pool", xview[:, :, ti * 128:(ti + 1) * 128], o[:, :, :], reads=[o], writes=xdeps(ti * 128, ti * 128 + 128))
        K.barrier()

    def stage_norm(b, which, hT, to_dram=False, lo=0):
        shift_kind = 0 if which == 0 else 3
        with ES() as st:
            xb = [K.sb([128, KC, 512], F32, "nxb", st) for _ in range(2)]
            sq = [K.sb([128, 512], BF16, "sq", st) for _ in range(3)]
            rstd = [K.sb([128, 512], F32, "rstd", st) for _ in range(2)]
            tmp = [K.sb([128, 512], F32, "ntmp", st) for _ in range(3)]
            for bi, (t0, t1) in enumerate(BLOCKS):
                if t1 <= lo:
                    continue
                n = t1 - t0
                row = 4 if bi == 0 else b
                x_ = xb[bi % 2]
                K.dma("sp", x_[:, :, 0:n], xview[:, :, t0:t1], reads=xdeps(t0, t1), writes=[x_])
                pb = PS[bi % 2]
                for cc in range(KC):
                    s = sq[cc % 3]
                    K.act(s[:, 0:n], x_[:, cc, 0:n], AF.Square, [x_], [s])
                    K.mm(pb[:, 0:n], ones_bf[:, :], s[:, 0:n], cc == 0, cc == KC - 1, [ones_bf, s], [pb])
                r = rstd[bi % 2]
                K.rsqrt(r[:, 0:n], pb[:, 0:n], 1.0 / D, EPS, [pb], [r])
                for cc in range(KC):
                    tm = tmp[cc % 3]
                    K.tt("dve" if cc % 2 == 0 else "pool", tm[:, 0:n], x_[:, cc, 0:n], r[:, 0:n], ALU.mult, [x_, r], [tm])
                    K.act(hT[:, cc, t0:t1], tm[:, 0:n], AF.Identity, [tm, Amod, modsT], [hT],
                          bias=modsT[:, cc, shift_kind, row:row + 1], scale=Amod[:, cc, which, row:row + 1])
            if to_dram:
                for cc in range(KC):
                    K.dma("pool", hT_d[cc, :, :], hT[:, cc, :], reads=[hT], writes=[hT_d])
        K.barrier()

    def load_hT(hT):
        for cc in range(KC):
            K.dma("sp", hT[:, cc, :], hT_d[cc, :, :], reads=[hT_d], writes=[hT])

    def apply_out(b, pieces, gate_kind, tok_lo):
        with ES() as st:
            xb = [K.sb([128, KC, 256], F32, "uxb", st) for _ in range(2)]
            for bi, t0 in enumerate(range(tok_lo, T, 256)):
                t1 = t0 + 256
                row = 4 if t0 < LC else b
                x_ = xb[bi % 2]
                K.dma("sp", x_[:, :, :], xview[:, :, t0:t1], reads=xdeps(t0, t1), writes=[x_])
                for oc in range(KC):
                    pb = PS[oc % 4]
                    for pi, (lf, rf, rd) in enumerate(pieces):
                        K.mm(pb[:, 0:256], lf(oc), rf(t0, t1), pi == 0, pi == len(pieces) - 1, rd, [pb])
                    K.stt("dve", x_[:, oc, :], pb[:, 0:256], modsT[:, oc, gate_kind, row:row + 1], x_[:, oc, :],
                          ALU.mult, ALU.add, [pb, modsT, x_], [x_])
                K.dma("pool", xview[:, :, t0:t1], x_[:, :, :], reads=[x_], writes=xdeps(t0, t1))

    cast_rr = [0]

    def load_w_bf16(dst_ap, src_ap, shape, src_buf, dst_buf, st_bufs, idx, eng=None):
        s = st_bufs[idx % len(st_bufs)]
        v = stg_view(s, shape)
        K.dma("sp", v, src_ap, reads=[src_buf], writes=[s])
        if eng is None:
            cast_rr[0] += 1
            eng = "dve" if cast_rr[0] % 2 == 0 else "act"
        K.copy(eng, dst_ap, v, [s], [dst_buf])

    def stage_ffn(b, l, do_ctx, hT):
        wupv = wup_d[l, :, :].rearrange("(kc p) n -> p kc n", p=128)
        segs = [(LC, LL)] + ([(0, LC)] if do_ctx else [])
        blocks = [bl for bl in BLOCKS if (do_ctx or bl[0] >= LC)]
        PADW = T + 4
        with ES() as st:
            wst = [K.sb([128, KC, 256], F32, "wst", st) for _ in range(2)]
            wbf = [K.sb([128, KC, 256], BF16, "wbf", st) for _ in range(2)]
            ug = [K.sb([128, PADW], F32, "ug", st) for _ in range(2)]
            uv = [K.sb([128, PADW], F32, "uv", st) for _ in range(2)]
            cg = K.sb([128, T], F32, "cg", st)
            cv = K.sb([128, T], F32, "cv", st)
            ao = [K.sb([128, T], BF16, "ao", st) for _ in range(2)]
            for u in ug + uv:
                K.memset("pool", u[:, :], 0.0, [u])

            def pad_off(t):
                return t + 1 if t < LC else t + 3

            for fc in range(22):
                ws, wb_ = wst[fc % 2], wbf[fc % 2]
                K.dma("sp", ws[:, :, 0:128], wupv[:, :, fc * 128:(fc + 1) * 128], reads=[wup_d], writes=[ws])
                K.dma("sp", ws[:, :, 128:256], wupv[:, :, DFF + fc * 128:DFF + (fc + 1) * 128], reads=[wup_d], writes=[ws])
                K.copy("dve", wb_[:, :, 0:128], ws[:, :, 0:128], [ws], [wb_])
                K.copy("act", wb_[:, :, 128:256], ws[:, :, 128:256], [ws], [wb_])
                g_, v_ = ug[fc % 2], uv[fc % 2]
                for bi, (t0, t1) in enumerate(blocks):
                    n = t1 - t0
                    for half, dst in ((0, g_), (1, v_)):
                        pb = PS[(bi * 2 + half) % 4]
                        for kc in range(KC):
                            K.mm(pb[:, 0:n], wb_[:, kc, half * 128:(half + 1) * 128], hT[:, kc, t0:t1],
                                 kc == 0, kc == KC - 1, [wb_, hT], [pb])
                        K.copy("act", dst[:, pad_off(t0):pad_off(t0) + n], pb[:, 0:n], [pb], [dst])
                ao_ = ao[fc % 2]
                for half, src, dst, ch in ((0, g_, cg, fc), (1, v_, cv, 22 + fc)):
                    for (s0, ln) in segs:
                        p0 = pad_off(s0)
                        K.act(dst[:, s0:s0 + ln], src[:, p0:p0 + ln], AF.Identity, [src, cwT, cbT], [dst],
                              bias=cbT[:, l, ch:ch + 1], scale=cwT[:, l, 1, ch:ch + 1])
                        K.stt("dve", dst[:, s0:s0 + ln], src[:, p0 - 1:p0 - 1 + ln], cwT[:, l, 0, ch:ch + 1], dst[:, s0:s0 + ln],
                              ALU.mult, ALU.add, [src, cwT, dst], [dst])
                        K.stt("dve", dst[:, s0:s0 + ln], src[:, p0 + 1:p0 + 1 + ln], cwT[:, l, 2, ch:ch + 1], dst[:, s0:s0 + ln],
                              ALU.mult, ALU.add, [src, cwT, dst], [dst])
                for (s0, ln) in segs:
                    K.act(cg[:, s0:s0 + ln], cg[:, s0:s0 + ln], AF.Silu, [cg], [cg])
                    K.tt("dve" if ln > 1024 else "pool", ao_[:, s0:s0 + ln], cg[:, s0:s0 + ln], cv[:, s0:s0 + ln], ALU.mult, [cg, cv], [ao_])
                lo = 0 if do_ctx else LC
                K.dma("pool", aT_d[fc * 128:(fc + 1) * 128, lo:T], ao_[:, lo:T], reads=[ao_], writes=[aT_d])
        K.barrier()
        wdv = wdn_d[l, :, :].rearrange("(kc p) n -> p kc n", p=128)
        aTv = aT_d[:, :].rearrange("(kc p) t -> p kc t", p=128)
        with ES() as st:
            wd = K.sb([128, 22, D], BF16, "wd", st)
            wds = [K.sb([128, 2 * D], F32, "wds", st) for _ in range(2)]
            ab = [K.sb([128, 22, 256], BF16, "ab", st) for _ in range(2)]
            for j in range(11):
                load_w_bf16(wd[:, 2 * j:2 * j + 2, :], wdv[:, 2 * j:2 * j + 2, :], (128, 2, D), wdn_d, wd, wds, j)
            cnt = [0]

            def rhs_fn(t0, t1):
                return ab[cnt[0] % 2]

            lo = 0 if do_ctx else LC
            with ES() as st2:
                xb = [K.sb([128, KC, 256], F32, "uxb", st2) for _ in range(2)]
                for bi, t0 in enumerate(range(lo, T, 256)):
                    t1 = t0 + 256
                    row = 4 if t0 < LC else b
                    a_ = ab[bi % 2]
                    x_ = xb[bi % 2]
                    K.dma("sp", a_[:, :, :], aTv[:, :, t0:t1], reads=[aT_d], writes=[a_])
                    K.dma("sp", x_[:, :, :], xview[:, :, t0:t1], reads=xdeps(t0, t1), writes=[x_])
                    for oc in range(KC):
                        pb = PS[oc % 4]
                        for kc in range(22):
                            K.mm(pb[:, 0:256], wd[:, kc, oc * 128:(oc + 1) * 128], a_[:, kc, :], kc == 0, kc == 21, [wd, a_], [pb])
                        K.stt("dve", x_[:, oc, :], pb[:, 0:256], modsT[:, oc, 5, row:row + 1], x_[:, oc, :],
                              ALU.mult, ALU.add, [pb, modsT, x_], [x_])
                    K.dma("pool", xview[:, :, t0:t1], x_[:, :, :], reads=[x_], writes=xdeps(t0, t1))
        K.barrier()

    def stage_out(b):
        with ES() as st:
            xb = [K.sb([128, KC, 512], F32, "oxb", st) for _ in range(2)]
            sq = [K.sb([128, 512], BF16, "sq", st) for _ in range(3)]
            rstd = [K.sb([128, 512], F32, "rstd", st) for _ in range(2)]
            yT = [K.sb([128, KC, 512], F32, "yT", st) for _ in range(2)]
            ob = [K.sb([128, D], F32, "ob", st) for _ in range(3)]
            for bi, (t0, t1) in enumerate(BLOCKS[1:]):
                n = t1 - t0
                x_ = xb[bi % 2]
                K.dma("sp", x_[:, :, 0:n], xview[:, :, t0:t1], reads=xdeps(t0, t1), writes=[x_])
                pb = PS[bi % 2]
                for cc in range(KC):
                    s = sq[cc % 3]
                    K.act(s[:, 0:n], x_[:, cc, 0:n], AF.Square, [x_], [s])
                    K.mm(pb[:, 0:n], ones_bf[:, :], s[:, 0:n], cc == 0, cc == KC - 1, [ones_bf, s], [pb])
                r = rstd[bi % 2]
                K.rsqrt(r[:, 0:n], pb[:, 0:n], 1.0 / D, EPS, [pb], [r])
                y = yT[bi % 2]
                for cc in range(KC):
                    K.stt("dve", y[:, cc, 0:n], x_[:, cc, 0:n], gcols[:, 4, cc:cc + 1], r[:, 0:n],
                          ALU.mult, ALU.mult, [x_, gcols, r], [y])
                for ti in range(n // 128):
                    o = ob[ti % 3]
                    for half in range(2):
                        pb2 = PS[2 + (2 * ti + half) % 4]
                        for q in range(4):
                            cc = half * 4 + q
                            K.transpose(pb2[:, q * 128:(q + 1) * 128], y[:, cc, ti * 128:(ti + 1) * 128], ident[:, :], [y, ident], [pb2])
                        K.copy("act", o[:, half * 512:(half + 1) * 512], pb2[:, :], [pb2], [o])
                    tok = t0 - LC + ti * 128
                    K.dma("pool", out_d[b, tok:tok + 128, :], o[:, :], reads=[o], writes=[out_d])
        K.barrier()

    def stage_swa(b, hT):
        win = cdin_d[0, :, :].rearrange("(kc p) n -> p kc n", p=128)
        with ES() as st:
            attT = K.sb([64, 8, LL], BF16, "attT", st)
            wo = K.sb([64, 8, D], BF16, "wo_att", st)
            with ES() as st1:
                wq = K.sb([128, KC, 512], BF16, "wq", st1)
                wqs = K.sb([128, KC, 512], BF16, "wqs", st1)
                wk = K.sb([128, KC, 128], BF16, "wk", st1)
                wks = K.sb([128, KC, 128], BF16, "wks", st1)
                wv = K.sb([128, KC, 128], BF16, "wv", st1)
                cosT = K.sb([64, LL], F32, "cosT", st1)
                sinT = K.sb([64, LL], F32, "sinT", st1)
                qT = K.sb([64, 8, LL], BF16, "qT", st1)
                kT = K.sb([64, 2, T], BF16, "kT", st1)
                vtok = K.sb([128, 18, 2, 128], BF16, "vtok", st1)
                esink = K.sb([128, 8], F32, "esink", st1)
                maskP = K.sb([128, 128], F32, "maskP", st1)
                maskN = K.sb([128, 128], F32, "maskN", st1)
                t1b = [K.sb([64, 512], F32, "rt1", st1) for _ in range(2)]
                t2b = [K.sb([64, 512], F32, "rt2", st1) for _ in range(2)]
                pT = [K.sb([128, 512], BF16, "pT", st1) for _ in range(4)]
                dsum = [K.sb([128, 512], F32, "dsum", st1) for _ in range(2)]
                drec = [K.sb([64, 512], F32, "drec", st1) for _ in range(2)]
                K.memset("pool", vtok[:, :, :, :], 1.0, [vtok])
                stw = ES()
                wstg = [K.sb([128, 2048], F32, "wstg", stw)]
                K.dma("sp", cosT[:, :], swacos_d[:, :], reads=[swacos_d], writes=[cosT])
                K.dma("sp", sinT[:, :], swasin_d[:, :], reads=[swasin_d], writes=[sinT])
                K.dma("sp", maskP[:, :], triB_d[:, :], reads=[triB_d], writes=[maskP])
                K.dma("sp", maskN[:, :], triF_d[:, :], reads=[triF_d], writes=[maskN])
                K.dma("sp", esink[:, :], sink_d[0, :].partition_broadcast(128), reads=[sink_d], writes=[esink])
                K.act(esink[:, :], esink[:, :], AF.Exp, [esink], [esink])
                for j in range(2):
                    load_w_bf16(wq[:, :, j * 256:(j + 1) * 256], win[:, :, CD_Q + j * 256:CD_Q + (j + 1) * 256], (128, KC, 256), cdin_d, wq, wstg, 0)
                load_w_bf16(wk[:, :, :], win[:, :, CD_K:CD_K + 128], (128, KC, 128), cdin_d, wk, wstg, 0)
                load_w_bf16(wv[:, :, :], win[:, :, CD_V:CD_V + 128], (128, KC, 128), cdin_d, wv, wstg, 0)
                for (w_, ws_, nh) in ((wq, wqs, 64), (wk, wks, 16)):
                    wv4 = w_[:, :, :].rearrange("p k (h two d) -> p (k h) two d", two=2, d=32)
                    ws4 = ws_[:, :, :].rearrange("p k (h two d) -> p (k h) two d", two=2, d=32)
                    K.ts("pool", ws4[:, :, 0, :], wv4[:, :, 1, :], -1.0, None, ALU.mult, None, [w_], [ws_])
                    K.copy("pool", ws4[:, :, 1, :], wv4[:, :, 0, :], [w_], [ws_])
                wov = cdout_d[0, 1024:1536, :].rearrange("(h d) n -> d h n", d=64)
                for j in range(4):
                    load_w_bf16(wo[:, j * 2:(j + 1) * 2, :], wov[:, j * 2:(j + 1) * 2, :], (64, 2, D), cdout_d, wo, wstg, 0)
                K.barrier()
                stw.close()
                cnt = 0
                for h in range(8):
                    for j in range(4):
                        t0 = LC + j * 512
                        pa, pb = PS[(cnt * 2) % 4], PS[(cnt * 2 + 1) % 4]
                        for kc in range(KC):
                            K.mm(pa[0:64, :], wq[:, kc, h * 64:(h + 1) * 64], hT[:, kc, t0:t0 + 512], kc == 0, kc == KC - 1, [wq, hT], [pa])
                        for kc in range(KC):
                            K.mm(pb[0:64, :], wqs[:, kc, h * 64:(h + 1) * 64], hT[:, kc, t0:t0 + 512], kc == 0, kc == KC - 1, [wqs, hT], [pb])
                        a_, b_ = t1b[cnt % 2], t2b[cnt % 2]
                        K.tt("dve", a_[:, :], pa[0:64, :], cosT[:, j * 512:(j + 1) * 512], ALU.mult, [pa, cosT], [a_])
                        K.tt("dve", b_[:, :], pb[0:64, :], sinT[:, j * 512:(j + 1) * 512], ALU.mult, [pb, sinT], [b_])
                        K.tt("pool", qT[:, h, j * 512:(j + 1) * 512], a_[:, :], b_[:, :], ALU.add, [a_, b_], [qT])
                        cnt += 1
                for g in range(2):
                    pa = PS[cnt % 4]
                    for kc in range(KC):
                        K.mm(pa[0:64, 0:LC], wk[:, kc, g * 64:(g + 1) * 64], hT[:, kc, 0:LC], kc == 0, kc == KC - 1, [wk, hT], [pa])
                    K.copy("act", kT[:, g, 0:LC], pa[0:64, 0:LC], [pa], [kT])
                    cnt += 1
                    for j in range(4):
                        t0 = LC + j * 512
                        pa, pb = PS[(cnt * 2) % 4], PS[(cnt * 2 + 1) % 4]
                        for kc in range(KC):
                            K.mm(pa[0:64, :], wk[:, kc, g * 64:(g + 1) * 64], hT[:, kc, t0:t0 + 512], kc == 0, kc == KC - 1, [wk, hT], [pa])
                        for kc in range(KC):
                            K.mm(pb[0:64, :], wks[:, kc, g * 64:(g + 1) * 64], hT[:, kc, t0:t0 + 512], kc == 0, kc == KC - 1, [wks, hT], [pb])
                        a_, b_ = t1b[cnt % 2], t2b[cnt % 2]
                        K.tt("dve", a_[:, :], pa[0:64, :], cosT[:, j * 512:(j + 1) * 512], ALU.mult, [pa, cosT], [a_])
                        K.tt("dve", b_[:, :], pb[0:64, :], sinT[:, j * 512:(j + 1) * 512], ALU.mult, [pb, sinT], [b_])
                        K.tt("pool", kT[:, g, t0:t0 + 512], a_[:, :], b_[:, :], ALU.add, [a_, b_], [kT])
                        cnt += 1
                for ti in range(18):
                    pa = PS[ti % 4]
                    for kc in range(KC):
                        K.mm(pa[:, 0:128], hT[:, kc, ti * 128:(ti + 1) * 128], wv[:, kc, :], kc == 0, kc == KC - 1, [hT, wv], [pa])
                    K.copy("act", vtok[:, ti, :, 0:64], pa[:, 0:128].rearrange("p (g e) -> p g e", g=2), [pa], [vtok])
                units = []
                u = 0
                for i in range(16):
                    for g in range(2):
                        keys = [(0, None), (1, None)]
                        if i > 0:
                            keys.append((2 + i - 1, maskP))
                        keys.append((2 + i, None))
                        if i < 15:
                            keys.append((2 + i + 1, maskN))
                        for ki, (kt, mask) in enumerate(keys):
                            units.append((i, g, ki, kt, mask, len(keys), u))
                        u += 1

                def front(un, idx):
                    i, g, ki, kt, mask, nk, uu = un
                    psc = PS[idx % 4]
                    K.mm(psc[:, :].rearrange("p (h q) -> p h q", h=4), kT[:, g, kt * 128:(kt + 1) * 128],
                         qT[:, g * 4:(g + 1) * 4, i * 128:(i + 1) * 128], True, True, [kT, qT], [psc])

                def back(un, idx):
                    i, g, ki, kt, mask, nk, uu = un
                    psc = PS[idx % 4]
                    pacc = PS[4 + (uu % 2) * 2]
                    p_ = pT[idx % 4]
                    K.act(p_[:, :], psc[:, :], AF.Exp, [psc], [p_], scale=0.125)
                    if mask is not None:
                        K.tt("pool", p_[:, :].rearrange("p (h q) -> p h q", h=4), p_[:, :].rearrange("p (h q) -> p h q", h=4),
                             mask[:, :].unsqueeze(1).to_broadcast([128, 4, 128]), ALU.mult, [p_, mask], [p_])
                    K.mm(pacc[:, :], vtok[:, kt, g, :], p_[:, :], ki == 0, ki == nk - 1, [vtok, p_], [pacc])
                    if ki == nk - 1:
                        d_ = dsum[uu % 2]
                        r_ = drec[uu % 2]
                        K.tt("dve", d_[64:128, :].rearrange("p (h q) -> p h q", h=4), pacc[64:128, :].rearrange("p (h q) -> p h q", h=4),
                             esink[64:128, g * 4:(g + 1) * 4].unsqueeze(2).to_broadcast([64, 4, 128]), ALU.add, [pacc, esink], [d_])
                        K.op("dve", lambda e, d_=d_, r_=r_: e.reciprocal(out=r_[0:64, :], in_=d_[64:128, :]), [d_], [r_])
                        K.tt("dve", attT[:, g * 4:(g + 1) * 4, i * 128:(i + 1) * 128], pacc[0:64, :].rearrange("p (h q) -> p h q", h=4),
                             r_[:, :].rearrange("p (h q) -> p h q", h=4), ALU.mult, [pacc, r_], [attT])

                LA = 2
                for idx in range(min(LA, len(units))):
                    front(units[idx], idx)
                for idx, un in enumerate(units):
                    if idx + LA < len(units):
                        front(units[idx + LA], idx + LA)
                    back(un, idx)
            K.barrier()
            pieces = [((lambda oc, h=h: wo[:, h, oc * 128:(oc + 1) * 128]),
                       (lambda t0, t1, h=h: attT[:, h, t0 - LC:t1 - LC]), [wo, attT]) for h in range(8)]
            apply_out(b, pieces, 2, LC)
        K.barrier()

    def stage_ssd(b, hT_scope_fn):
        win = cdin_d[0, :, :].rearrange("(kc p) n -> p kc n", p=128)
        with ES() as st:
            uT = K.sb([128, 8, LL], BF16, "uT", st)
            with ES() as st1:
                xs_tok = K.sb([128, 18, 1024], BF16, "xs_tok", st1)
                B_tok = K.sb([128, 18, 256], BF16, "B_tok", st1)
                BCT = K.sb([128, 4, T], BF16, "BCT", st1)
                dtv = K.sb([128, 18, 32], F32, "dtv", st1)
                dtA = K.sb([128, 18, 32], F32, "dtA", st1)
                a_bc = K.sb([128, 32], F32, "a_bc", st1)
                dtb_bc = K.sb([128, 32], F32, "dtb_bc", st1)
                D_bc = K.sb([128, 16], F32, "D_bc", st1)
                sng = K.sb([128, 8], F32, "sng", st1)
                scw = K.sb([128, 5, 12], F32, "scw", st1)
                scb = K.sb([128, 12], F32, "scb", st1)
                triF = K.sb([128, 128], F32, "triF", st1)
                triB = K.sb([128, 128], F32, "triB", st1)
                strF = K.sb([128, 128], F32, "strF", st1)
                strB = K.sb([128, 128], F32, "strB", st1)
                for (dst, src) in ((triF, triF_d), (triB, triB_d), (strF, strF_d), (strB, strB_d)):
                    K.dma("sp", dst[:, :], src[:, :], reads=[src], writes=[dst])
                K.dma("sp", a_bc[:, :], salog_d[0, :, :].rearrange("a b -> (a b)").partition_broadcast(128), reads=[salog_d], writes=[a_bc])
                K.act(a_bc[:, :], a_bc[:, :], AF.Exp, [a_bc], [a_bc])
                K.ts("dve", a_bc[:, :], a_bc[:, :], -1.0, None, ALU.mult, None, [a_bc], [a_bc])
                K.dma("sp", dtb_bc[:, :], sdtb_d[0, :, :].rearrange("a b -> (a b)").partition_broadcast(128), reads=[sdtb_d], writes=[dtb_bc])
                K.dma("sp", D_bc[:, :], sd_d[0, :].partition_broadcast(128), reads=[sd_d], writes=[D_bc])
                K.dma("sp", sng[:, :], colvec(sng_d[0, :], 8), reads=[sng_d], writes=[sng], allow_slow_non_contiguous=True)
                for tap in range(5):
                    K.dma("sp", scw[:, tap, :], colvec(scw_d[0, tap, :], 12), reads=[scw_d], writes=[scw], allow_slow_non_contiguous=True)
                K.dma("sp", scb[:, :], colvec(scb_d[0, :], 12), reads=[scb_d], writes=[scb], allow_slow_non_contiguous=True)
                with ES() as st2:
                    hT = K.sb([128, KC, T], BF16, "hT", st2)
                    load_hT(hT)
                    wstg = [K.sb([128, 4096], F32, "wstg", st2)]
                    st3 = ES()
                    wch = [K.sb([128, KC, 128], BF16, "wch", st3) for _ in range(2)]
                    PW = T + 8
                    upad = [K.sb([128, PW], F32, "upad", st3) for _ in range(2)]
                    cvb = [K.sb([128, T], F32, "cvb", st3) for _ in range(1)]
                    xsT = [K.sb([128, T], BF16, "xsT", st3) for _ in range(2)]
                    for u_ in upad:
                        K.memset("pool", u_[:, :], 0.0, [u_])

                    def poff(t):
                        return t + 2 if t < LC else t + 6

                    for fc in range(12):
                        w_ = wch[fc % 2]
                        load_w_bf16(w_[:, :, :], win[:, :, CD_XBC + fc * 128:CD_XBC + (fc + 1) * 128], (128, KC, 128), cdin_d, w_, wstg, fc)
                        up = upad[fc % 2]
                        for bi, (t0, t1) in enumerate(BLOCKS):
                            n = t1 - t0
                            pb = PS[bi % 4]
                            for kc in range(KC):
                                K.mm(pb[:, 0:n], w_[:, kc, :], hT[:, kc, t0:t1], kc == 0, kc == KC - 1, [w_, hT], [pb])
                            K.copy("act", up[:, poff(t0):poff(t0) + n], pb[:, 0:n], [pb], [up])
                        cv_ = cvb[0]
                        for (s0, ln) in ((0, LC), (LC, LL)):
                            p0 = poff(s0)
                            K.act(cv_[:, s0:s0 + ln], up[:, p0:p0 + ln], AF.Identity, [up, scw, scb], [cv_],
                                  bias=scb[:, fc:fc + 1], scale=scw[:, 2, fc:fc + 1])
                            for tap in (0, 1, 3, 4):
                                K.stt("dve", cv_[:, s0:s0 + ln], up[:, p0 + tap - 2:p0 + tap - 2 + ln], scw[:, tap, fc:fc + 1], cv_[:, s0:s0 + ln],
                                      ALU.mult, ALU.add, [up, scw, cv_], [cv_])
                        if fc < 8:
                            dstT, dst_ap = xsT[fc % 2], xsT[fc % 2][:, :]
                        else:
                            dstT, dst_ap = BCT, BCT[:, fc - 8, :]
                        K.act(dst_ap, cv_[:, :], AF.Silu, [cv_], [dstT])
                        if fc < 10:
                            for grp in range(3):
                                tis = list(range(grp * 8, min(18, grp * 8 + 8)))
                                pb = PS[4 + grp % 2]
                                pbv = pb[:, :].bitcast(BF16)
                                for qi, ti in enumerate(tis):
                                    K.transpose(pbv[:, qi * 128:(qi + 1) * 128], dst_ap[:, ti * 128:(ti + 1) * 128], identb[:, :], [dstT, identb], [pb])
                                nt = len(tis)
                                src_v = pbv[:, 0:nt * 128].rearrange("p (a f) -> p a f", a=nt)
                                if fc < 8:
                                    K.copy("act", xs_tok[:, tis[0]:tis[0] + nt, fc * 128:(fc + 1) * 128], src_v, [pb], [xs_tok])
                                else:
                                    K.copy("act", B_tok[:, tis[0]:tis[0] + nt, (fc - 8) * 128:(fc - 7) * 128], src_v, [pb], [B_tok])
                    K.barrier()
                    st3.close()
                    wz = K.sb([128, KC, 1024], BF16, "wz", st2)
                    wdt = K.sb([128, KC, 32], BF16, "wdt", st2)
                    szb = [K.sb([128, 1024], BF16, "szb", st2) for _ in range(2)]
                    dtt = [K.sb([128, 32], F32, "dtt", st2) for _ in range(2)]
                    for j in range(2):
                        load_w_bf16(wz[:, :, j * 512:(j + 1) * 512], win[:, :, CD_Z + j * 512:CD_Z + (j + 1) * 512], (128, KC, 512), cdin_d, wz, wstg, j)
                    load_w_bf16(wdt[:, :, :], win[:, :, CD_DT:CD_DT + 32], (128, KC, 32), cdin_d, wdt, wstg, 0)
                    for ti in range(18):
                        pb = PS[ti % 4]
                        for kc in range(KC):
                            K.mm(pb[:, 0:32], hT[:, kc, ti * 128:(ti + 1) * 128], wdt[:, kc, :], kc == 0, kc == KC - 1, [hT, wdt], [pb])
                        d_ = dtt[ti % 2]
                        K.tt("dve", d_[:, :], pb[:, 0:32], dtb_bc[:, :], ALU.add, [pb, dtb_bc], [d_])
                        K.act(d_[:, :], d_[:, :], AF.Exp, [d_], [d_])
                        K.act(dtv[:, ti, :], d_[:, :], AF.Ln, [d_], [dtv], bias=K.eps_ap(1.0))
                    K.tt("dve", dtA[:, :, :], dtv[:, :, :], a_bc[:, :].unsqueeze(1).to_broadcast([128, 18, 32]), ALU.mult, [dtv, a_bc], [dtA])
                    for li in range(16):
                        ti = li + 2
                        s_ = szb[li % 2]
                        for j in range(2):
                            pb = PS[(li * 2 + j) % 4]
                            for kc in range(KC):
                                K.mm(pb[:, :], hT[:, kc, ti * 128:(ti + 1) * 128], wz[:, kc, j * 512:(j + 1) * 512], kc == 0, kc == KC - 1, [hT, wz], [pb])
                            K.act(s_[:, j * 512:(j + 1) * 512], pb[:, :], AF.Silu, [pb], [s_])
                        K.dma("pool", sz_d[li, :, :], s_[:, :], reads=[s_], writes=[sz_d])
                K.barrier()
                with ES() as st2:
                    Hf = K.sb([128, 2, 512], F32, "Hf", st2)
                    Hb = K.sb([128, 2, 512], F32, "Hb", st2)
                    hbf = [K.sb([128, 1024], BF16, "hbf", st2) for _ in range(2)]
                    hinf = [K.sb([128, 1024], BF16, "hinf", st2) for _ in range(2)]
                    szt = [K.sb([128, 1024], BF16, "szt", st2) for _ in range(2)]
                    prep = {}
                    for nm in ("acs0", "eac0", "dend0", "cdec0", "wgt0", "acs1", "eac1", "dend1", "cdec1", "wgt1"):
                        prep[nm] = K.sb([128, 16], F32, nm, st2)
                    xte = K.sb([128, 1024], BF16, "xte", st2)
                    rseg = K.sb([128, 16, 128], F32, "rseg", st2)
                    cbm = [K.sb([128, 2, 128], F32, "cbm", st2) for _ in range(2)]
                    eseg = [K.sb([128, 512], F32, "eseg", st2) for _ in range(2)]
                    Lt = [K.sb([128, 16, 128], BF16, "Lt", st2) for _ in range(2)]
                    xdt = [K.sb([128, 1024], BF16, "xdt", st2) for _ in range(2)]
                    yacc = K.sb([128, 1024], F32, "yacc", st2)
                    ytmp = K.sb([128, 512], F32, "ytmp", st2)
                    ub = K.sb([128, 1024], F32, "ub", st2)
                    ubf = K.sb([128, 1024], BF16, "ubf", st2)
                    ssq = K.sb([128, 2], F32, "ssq", st2)
                    junk = K.sb([128, 512], BF16, "junk", st2)
                    K.memset("dve", Hf[:, :, :], 0.0, [Hf])
                    K.memset("dve", Hb[:, :, :], 0.0, [Hb])
                    tri = (triF, triB)
                    strm = (strF, strB)

                    def do_prep(c, d):
                        pp = PS[0]
                        K.mm(pp[:, 0:16], tri[d][:, :], dtA[:, c, d * 16:(d + 1) * 16], True, True, [tri[d], dtA], [pp])
                        K.mm(pp[:, 16:32], ones_f[:, :], dtA[:, c, d * 16:(d + 1) * 16], True, True, [ones_f, dtA], [pp])
                        acs, eac, dend, cdec, wgt = (prep[n_ + str(d)] for n_ in ("acs", "eac", "dend", "cdec", "wgt"))
                        K.copy("act", acs[:, :], pp[:, 0:16], [pp], [acs])
                        K.act(eac[:, :], pp[:, 0:16], AF.Exp, [pp], [eac])
                        K.act(cdec[:, :], pp[:, 16:32], AF.Exp, [pp], [cdec])
                        K.tt("dve", dend[:, :], pp[:, 16:32], acs[:, :], ALU.subtract, [pp, acs], [dend])
                        K.act(dend[:, :], dend[:, :], AF.Exp, [dend], [dend])
                        K.tt("dve", wgt[:, :], dend[:, :], dtv[:, c, d * 16:(d + 1) * 16], ALU.mult, [dend, dtv], [wgt])

                    def state_update(c, d, H):
                        wgt, cdec = prep["wgt" + str(d)], prep["cdec" + str(d)]
                        K.tt("pool", xte[:, :].rearrange("p (h e) -> p h e", h=16), xs_tok[:, c, :].rearrange("p (h e) -> p h e", h=16),
                             wgt[:, :].unsqueeze(2).to_broadcast([128, 16, 64]), ALU.mult, [xs_tok, wgt], [xte])
                        for g in range(2):
                            pb = PS[6 + g]
                            K.mm(pb[:, :], B_tok[:, c, g * 128:(g + 1) * 128], xte[:, g * 512:(g + 1) * 512], True, True, [B_tok, xte], [pb])
                            hv = H[:, g, :].rearrange("p (h e) -> p h e", h=8)
                            K.tt("pool", hv, hv, cdec[:, g * 8:(g + 1) * 8].unsqueeze(2).to_broadcast([128, 8, 64]), ALU.mult, [H, cdec], [H])
                            K.tt("dve", H[:, g, :], H[:, g, :], pb[:, :], ALU.add, [H, pb], [H])

                    for c in range(18):
                        if c >= 2:
                            hb_ = hbf[c % 2]
                            K.copy("act", hb_[:, :], Hf[:, :, :].rearrange("p g e -> p (g e)"), [Hf], [hb_])
                            K.dma("pool", hin_d[c - 2, :, :], hb_[:, :], reads=[hb_], writes=[hin_d])
                        if c == 17:
                            break
                        do_prep(c, 0)
                        state_update(c, 0, Hf)
                    K.barrier()
                    for c in [1, 0] + list(range(17, 1, -1)):
                        do_prep(c, 1)
                        if c >= 2:
                            li = c - 2
                            do_prep(c, 0)
                            hi_ = hinf[li % 2]
                            sz_ = szt[li % 2]
                            K.dma("sp", hi_[:, :], hin_d[li, :, :], reads=[hin_d], writes=[hi_])
                            K.dma("sp", sz_[:, :], sz_d[li, :, :], reads=[sz_d], writes=[sz_])
                            hb_ = hbf[li % 2]
                            K.copy("act", hb_[:, :], Hb[:, :, :].rearrange("p g e -> p (g e)"), [Hb], [hb_])
                            tsl = slice(c * 128, (c + 1) * 128)
                            pcb = PS[1]
                            for g in range(2):
                                K.mm(pcb[:, g * 128:(g + 1) * 128], BCT[:, g, tsl], BCT[:, 2 + g, tsl], True, True, [BCT], [pcb])
                            for d in range(2):
                                K.tt("dve", cbm[d][:, :, :], pcb[:, 0:256].rearrange("p (g q) -> p g q", g=2),
                                     tri[d][:, :].unsqueeze(1).to_broadcast([128, 2, 128]), ALU.mult, [pcb, tri[d]], [cbm[d]])
                            for d in range(2):
                                K.tt("dve", rseg[:, :, :], tri[d][:, :].unsqueeze(1).to_broadcast([128, 16, 128]),
                                     dtA[:, c, d * 16:(d + 1) * 16].unsqueeze(2).to_broadcast([128, 16, 128]), ALU.mult, [tri[d], dtA], [rseg])
                                for hb4 in range(4):
                                    pseg = PS[2 + hb4 % 2]
                                    K.mm(pseg[:, :], strm[d][:, :], rseg[:, hb4 * 4:(hb4 + 1) * 4, :], True, True, [strm[d], rseg], [pseg])
                                    es_ = eseg[hb4 % 2]
                                    K.act(es_[:, :], pseg[:, :], AF.Exp, [pseg], [es_])
                                    g = hb4 // 2
                                    K.tt("dve" if hb4 % 2 == 0 else "pool", Lt[d][:, hb4 * 4:(hb4 + 1) * 4, :], es_[:, :].rearrange("p (h q) -> p h q", h=4),
                                         cbm[d][:, g, :].unsqueeze(1).to_broadcast([128, 4, 128]), ALU.mult, [es_, cbm[d]], [Lt[d]])
                                K.tt("dve", xdt[d][:, :].rearrange("p (h e) -> p h e", h=16), xs_tok[:, c, :].rearrange("p (h e) -> p h e", h=16),
                                     dtv[:, c, d * 16:(d + 1) * 16].unsqueeze(2).to_broadcast([128, 16, 64]), ALU.mult, [xs_tok, dtv], [xdt[d]])
                            for h in range(16):
                                py = PS[4 + h // 8]
                                col = (h % 8) * 64
                                for d in range(2):
                                    K.mm(py[:, col:col + 64], Lt[d][:, h, :], xdt[d][:, h * 64:(h + 1) * 64], d == 0, d == 1, [Lt[d], xdt[d]], [py])
                            K.tt("pool", yacc[:, :].rearrange("p (h e) -> p h e", h=16), xs_tok[:, c, :].rearrange("p (h e) -> p h e", h=16),
                                 D_bc[:, :].unsqueeze(2).to_broadcast([128, 16, 64]), ALU.mult, [xs_tok, D_bc], [yacc])
                            for g in range(2):
                                K.tt("dve", yacc[:, g * 512:(g + 1) * 512], yacc[:, g * 512:(g + 1) * 512], PS[4 + g][:, :], ALU.add, [yacc, PS[4 + g]], [yacc])
                            for d in range(2):
                                hsrc = hi_ if d == 0 else hb_
                                eac = prep["eac" + str(d)]
                                for g in range(2):
                                    po = PS[6 + g]
                                    K.mm(po[:, :], BCT[:, 2 + g, tsl], hsrc[:, g * 512:(g + 1) * 512], True, True, [BCT, hsrc], [po])
                                    K.tt("dve", ytmp[:, :].rearrange("p (h e) -> p h e", h=8), po[:, :].rearrange("p (h e) -> p h e", h=8),
                                         eac[:, g * 8:(g + 1) * 8].unsqueeze(2).to_broadcast([128, 8, 64]), ALU.mult, [po, eac], [ytmp])
                                    K.tt("pool", yacc[:, g * 512:(g + 1) * 512], yacc[:, g * 512:(g + 1) * 512], ytmp[:, :], ALU.add, [yacc, ytmp], [yacc])
                            K.tt("dve", ub[:, :], yacc[:, :], sz_[:, :], ALU.mult, [yacc, sz_], [ub])
                            K.memset("pool", ssq[:, :], 0.0, [ssq])
                            for g in range(2):
                                K.op("act", lambda e, g=g: e.activation(out=junk[:, :], in_=ub[:, g * 512:(g + 1) * 512], func=AF.Square,
                                                                         accum_out=ssq[:, g:g + 1]), [ub], [junk, ssq])
                            K.rsqrt(ssq[:, :], ssq[:, :], 1.0 / 512, EPS, [ssq], [ssq])
                            for g in range(2):
                                K.ts("dve", ubf[:, g * 512:(g + 1) * 512], ub[:, g * 512:(g + 1) * 512], ssq[:, g:g + 1], None, ALU.mult, None, [ub, ssq], [ubf])
                            pt = PS[1]
                            ptv = pt[:, :].bitcast(BF16)
                            for cc in range(8):
                                K.transpose(ptv[:, cc * 128:(cc + 1) * 128], ubf[:, cc * 128:(cc + 1) * 128], identb[:, :], [ubf, identb], [pt])
                            for cc in range(8):
                                K.act(uT[:, cc, li * 128:(li + 1) * 128], ptv[:, cc * 128:(cc + 1) * 128], AF.Identity, [pt, sng], [uT], scale=sng[:, cc:cc + 1])
                        if c != 2:
                            state_update(c, 1, Hb)
            K.barrier()
            wo = K.sb([128, 8, D], BF16, "wo_ssd", st)
            wostg = [K.sb([128, 4096], F32, "wostg", st)]
            for j in range(2):
                load_w_bf16(wo[:, j * 4:(j + 1) * 4, :], cdout_d[0, 0:1024, :].rearrange("(kc p) n -> p kc n", p=128)[:, j * 4:(j + 1) * 4, :],
                            (128, 4, D), cdout_d, wo, wostg, 0)
            pieces = [((lambda oc, cc=cc: wo[:, cc, oc * 128:(oc + 1) * 128]),
                       (lambda t0, t1, cc=cc: uT[:, cc, t0 - LC:t1 - LC]), [wo, uT]) for cc in range(8)]
            apply_out(b, pieces, 2, LC)
        K.barrier()


    AB_CQ, AB_CKV, AB_KR, AB_RW = 0, 384, 640, 672

    def stage_mla(b, hT):
        win = abin_d[0, :, :].rearrange("(kc p) n -> p kc n", p=128)
        sc = 96.0 ** -0.5
        with ES() as st:
            attT = K.sb([64, 8, T], BF16, "mattT", st)
            wo = K.sb([64, 8, D], BF16, "wo_mla", st)
            with ES() as st1:
                wcq = K.sb([128, KC, 384], BF16, "wcq", st1)
                wckv = K.sb([128, KC, 256], BF16, "wckv", st1)
                wkr = K.sb([128, KC, 96], BF16, "wkr", st1)
                wkrs = K.sb([128, KC, 96], BF16, "wkrs", st1)
                wqu = K.sb([128, 3, 768], BF16, "wqu", st1)
                wqus = K.sb([128, 24, 96], BF16, "wqus", st1)
                wkvu = K.sb([128, 2, 1024], BF16, "wkvu", st1)
                cqn = K.sb([128, 3, T], BF16, "cqn", st1)
                ckvn = K.sb([128, 2, T], BF16, "ckvn", st1)
                vo = K.sb([128, 18, 128], BF16, "mvo", st1)
                cosT = K.sb([96, LL], F32, "mcos", st1)
                sinT = K.sb([96, LL], F32, "msin", st1)
                gq = K.sb([128, 3], F32, "gq", st1)
                gkv = K.sb([128, 2], F32, "gkv", st1)
                qf = K.sb([96, T], BF16, "qf", st1)
                kf = K.sb([96, T], BF16, "kf", st1)
                sq = [K.sb([128, 512], BF16, "msq", st1) for _ in range(2)]
                rstd = [K.sb([128, 512], F32, "mrstd", st1) for _ in range(2)]
                t1b = [K.sb([96, 512], F32, "mt1", st1) for _ in range(1)]
                t2b = [K.sb([96, 512], F32, "mt2", st1) for _ in range(1)]
                pT = [K.sb([128, 512], BF16, "mpT", st1) for _ in range(4)]
                rec = [K.sb([64, 512], F32, "mrec", st1) for _ in range(2)]
                stw = ES()
                wstg = [K.sb([128, 4096], F32, "wstg", stw)]
                K.dma("sp", cosT[64:96, :], mlacos_d[:, :], reads=[mlacos_d], writes=[cosT])
                K.dma("sp", sinT[64:96, :], mlasin_d[:, :], reads=[mlasin_d], writes=[sinT])
                K.dma("sp", gq[:, :], colvec(mqg_d[0, :], 3), reads=[mqg_d], writes=[gq], allow_slow_non_contiguous=True)
                K.dma("sp", gkv[:, :], colvec(mkg_d[0, :], 2), reads=[mkg_d], writes=[gkv], allow_slow_non_contiguous=True)
                K.memset("pool", wkr[:, :, :], 0.0, [wkr])
                K.memset("pool", wkrs[:, :, :], 0.0, [wkrs])
                K.memset("pool", wqus[:, :, :], 0.0, [wqus])
                K.memset("pool", vo[:, :, :], 1.0, [vo])
                load_w_bf16(wcq[:, :, :], win[:, :, AB_CQ:AB_CQ + 384], (128, KC, 384), abin_d, wcq, wstg, 0)
                load_w_bf16(wckv[:, :, :], win[:, :, AB_CKV:AB_CKV + 256], (128, KC, 256), abin_d, wckv, wstg, 0)
                load_w_bf16(wkr[:, :, 64:96], win[:, :, AB_KR:AB_KR + 32], (128, KC, 32), abin_d, wkr, wstg, 0)
                load_w_bf16(wqu[:, :, :], mqu_d[0, :, :].rearrange("(c p) n -> p c n", p=128), (128, 3, 768), mqu_d, wqu, wstg, 0)
                load_w_bf16(wkvu[:, :, :], mkvu_d[0, :, :].rearrange("(c p) n -> p c n", p=128), (128, 2, 1024), mkvu_d, wkvu, wstg, 0)
                wov = about_d[0, 0:512, :].rearrange("(h d) n -> d h n", d=64)
                for j in range(2):
                    load_w_bf16(wo[:, j * 4:(j + 1) * 4, :], wov[:, j * 4:(j + 1) * 4, :], (64, 4, D), about_d, wo, wstg, 0)
                K.ts("pool", wkrs[:, :, 64:80], wkr[:, :, 80:96], -1.0, None, ALU.mult, None, [wkr], [wkrs])
                K.copy("pool", wkrs[:, :, 80:96], wkr[:, :, 64:80], [wkr], [wkrs])
                wq24 = wqu[:, :, :].rearrange("p c (h e) -> p (c h) e", e=96)
                K.ts("pool", wqus[:, :, 64:80], wq24[:, :, 80:96], -1.0, None, ALU.mult, None, [wqu], [wqus])
                K.copy("pool", wqus[:, :, 80:96], wq24[:, :, 64:80], [wqu], [wqus])
                K.barrier()
                stw.close()
                for bi, (t0, t1) in enumerate(BLOCKS):
                    n = t1 - t0
                    for (w_, nch, g_, dst, pbase) in ((wcq, 3, gq, cqn, 0), (wckv, 2, gkv, ckvn, 4)):
                        for c3 in range(nch):
                            pb = PS[pbase + c3]
                            for kc in range(KC):
                                K.mm(pb[:, 0:n], w_[:, kc, c3 * 128:(c3 + 1) * 128], hT[:, kc, t0:t1], kc == 0, kc == KC - 1, [w_, hT], [pb])
                        pst = PS[pbase + 3] if pbase == 0 else PS[pbase + 2]
                        for c3 in range(nch):
                            s_ = sq[c3 % 2]
                            K.act(s_[:, 0:n], PS[pbase + c3][:, 0:n], AF.Square, [PS[pbase + c3]], [s_])
                            K.mm(pst[:, 0:n], ones_bf[:, :], s_[:, 0:n], c3 == 0, c3 == nch - 1, [ones_bf, s_], [pst])
                        r_ = rstd[0 if pbase == 0 else 1]
                        K.rsqrt(r_[:, 0:n], pst[:, 0:n], 1.0 / (nch * 128), EPS, [pst], [r_])
                        for c3 in range(nch):
                            K.stt("dve", dst[:, c3, t0:t1], PS[pbase + c3][:, 0:n], g_[:, c3:c3 + 1], r_[:, 0:n], ALU.mult, ALU.mult,
                                  [PS[pbase + c3], g_, r_], [dst])
                R_ = slice(64, 96)
                for bi, (t0, t1) in enumerate(BLOCKS):
                    n = t1 - t0
                    pa, pb = PS[0 + 2 * (bi % 2)], PS[1 + 2 * (bi % 2)]
                    for kc in range(KC):
                        K.mm(pa[0:96, 0:n], wkr[:, kc, :], hT[:, kc, t0:t1], kc == 0, kc == KC - 1, [wkr, hT], [pa])
                    if bi == 0:
                        K.copy("dve", kf[R_, t0:t1], pa[R_, 0:n], [pa], [kf])
                    else:
                        for kc in range(KC):
                            K.mm(pb[0:96, 0:n], wkrs[:, kc, :], hT[:, kc, t0:t1], kc == 0, kc == KC - 1, [wkrs, hT], [pb])
                        a_, b_ = t1b[0], t2b[0]
                        K.tt("dve", a_[R_, 0:n], pa[R_, 0:n], cosT[R_, t0 - LC:t1 - LC], ALU.mult, [pa, cosT], [a_])
                        K.tt("dve", b_[R_, 0:n], pb[R_, 0:n], sinT[R_, t0 - LC:t1 - LC], ALU.mult, [pb, sinT], [b_])
                        K.tt("pool", kf[R_, t0:t1], a_[R_, 0:n], b_[R_, 0:n], ALU.add, [a_, b_], [kf])
                u = 0
                for h in range(8):
                    for ti in range(18):
                        pa = PS[4 + ti % 2]
                        for c3 in range(2):
                            K.mm(pa[:, 0:64], ckvn[:, c3, ti * 128:(ti + 1) * 128], wkvu[:, c3, h * 128 + 64:h * 128 + 128],
                                 c3 == 0, c3 == 1, [ckvn, wkvu], [pa])
                        K.copy("dve", vo[:, ti, 0:64], pa[:, 0:64], [pa], [vo])
                    for bi, (t0, t1) in enumerate(BLOCKS):
                        n = t1 - t0
                        pa, pc, pd = PS[0], PS[2], PS[3]
                        for c3 in range(3):
                            K.mm(pa[0:96, 0:n], wqu[:, c3, h * 96:h * 96 + 96], cqn[:, c3, t0:t1], c3 == 0, c3 == 2, [wqu, cqn], [pa])
                        K.copy("act", qf[0:64, t0:t1], pa[0:64, 0:n], [pa], [qf])
                        if bi == 0:
                            K.copy("dve", qf[R_, t0:t1], pa[R_, 0:n], [pa], [qf])
                        else:
                            for c3 in range(3):
                                K.mm(pc[0:96, 0:n], wqus[:, c3 * 8 + h, :], cqn[:, c3, t0:t1], c3 == 0, c3 == 2, [wqus, cqn], [pc])
                            a_, b_ = t1b[0], t2b[0]
                            K.tt("dve", a_[R_, 0:n], pa[R_, 0:n], cosT[R_, t0 - LC:t1 - LC], ALU.mult, [pa, cosT], [a_])
                            K.tt("dve", b_[R_, 0:n], pc[R_, 0:n], sinT[R_, t0 - LC:t1 - LC], ALU.mult, [pc, sinT], [b_])
                            K.tt("pool", qf[R_, t0:t1], a_[R_, 0:n], b_[R_, 0:n], ALU.add, [a_, b_], [qf])
                        for c3 in range(2):
                            K.mm(pd[0:64, 0:n], wkvu[:, c3, h * 128:h * 128 + 64], ckvn[:, c3, t0:t1], c3 == 0, c3 == 1, [wkvu, ckvn], [pd])
                        K.copy("act", kf[0:64, t0:t1], pd[0:64, 0:n], [pd], [kf])
                    units = []
                    for bi, (t0, t1) in enumerate(BLOCKS):
                        keys = [0, 1] if bi == 0 else list(range(18))
                        for ki, kt in enumerate(keys):
                            units.append((bi, t0, t1, ki, kt, len(keys), u))
                        u += 1

                    def front(un, idx):
                        bi, t0, t1, ki, kt, nk, uu = un
                        n = t1 - t0
                        psc = PS[idx % 4]
                        K.mm(psc[:, 0:n], kf[:, kt * 128:(kt + 1) * 128], qf[:, t0:t1], True, True, [kf, qf], [psc])

                    def back(un, idx, h=h):
                        bi, t0, t1, ki, kt, nk, uu = un
                        n = t1 - t0
                        psc = PS[idx % 4]
                        pacc = PS[4 + (uu % 2) * 2]
                        p_ = pT[idx % 4]
                        K.act(p_[:, 0:n], psc[:, 0:n], AF.Exp, [psc], [p_], scale=sc)
                        K.mm(pacc[:, 0:n], vo[:, kt, :], p_[:, 0:n], ki == 0, ki == nk - 1, [vo, p_], [pacc])
                        if ki == nk - 1:
                            r_ = rec[uu % 2]
                            K.op("dve", lambda e, r_=r_, pacc=pacc, n=n: e.reciprocal(out=r_[0:64, 0:n], in_=pacc[64:128, 0:n]), [pacc], [r_])
                            K.tt("dve", attT[:, h, t0:t1], pacc[0:64, 0:n], r_[:, 0:n], ALU.mult, [pacc, r_], [attT])

                    LA = 2
                    for idx in range(min(LA, len(units))):
                        front(units[idx], idx)
                    for idx, un in enumerate(units):
                        if idx + LA < len(units):
                            front(units[idx + LA], idx + LA)
                        back(un, idx)
            K.barrier()
            pieces = [((lambda oc, h=h: wo[:, h, oc * 128:(oc + 1) * 128]),
                       (lambda t0, t1, h=h: attT[:, h, t0:t1]), [wo, attT]) for h in range(8)]
            apply_out(b, pieces, 2, 0)
        K.barrier()

    CW = -math.exp(-0.5)

    def stage_rwkv(b):
        win = abin_d[0, :, :].rearrange("(kc p) n -> p kc n", p=128)
        with ES() as st:
            cols = K.sb([128, 10, 4], F32, "rcols", st)
            for i_, src in enumerate((rkk_d[0, :], rka_d[0, :], None, rrk_d[0, :, :].rearrange("a b -> (a b)"), rlg_d[0, :], rlb_d[0, :],
                                      None, ra0_d[0, 0, :], ra0_d[0, 1, :])):
                if src is not None:
                    K.dma("sp", cols[:, i_, :], colvec(src, 4), reads=[rkk_d], writes=[cols], allow_slow_non_contiguous=True)
            K.ts("dve", cols[:, 2, :], cols[:, 1, :], -1.0, 1.0, ALU.mult, ALU.add, [cols], [cols])
            with ES() as st1:
                rT = K.sb([128, 4, T], BF16, "rT", st1)
                kT = K.sb([128, 4, T], BF16, "kT", st1)
                kknT = K.sb([128, 4, T], BF16, "kknT", st1)
                vtok = K.sb([128, 18, 512], BF16, "rvtok", st1)
                xwaT = K.sb([128, T], BF16, "xwaT", st1)
                mu = K.sb([128, 3, 14], F32, "mu", st1)
                blk64 = K.sb([128, 128], BF16, "blk64", st1)
                blk64f = K.sb([128, 128], F32, "blk64f", st1)
                K.dma("sp", blk64f[:, :], blk64_d[:, :], reads=[blk64_d], writes=[blk64f])
                K.copy("dve", blk64[:, :], blk64f[:, :], [blk64f], [blk64])
                K.dma("sp", mu[:, 0, :], colvec(rmp_d[0, :], 14), reads=[rmp_d], writes=[mu], allow_slow_non_contiguous=True)
                K.dma("sp", mu[:, 1, :], colvec(rmn_d[0, :], 14), reads=[rmn_d], writes=[mu], allow_slow_non_contiguous=True)
                K.tt("dve", mu[:, 2, :], mu[:, 0, :], mu[:, 1, :], ALU.add, [mu], [mu])
                K.ts("dve", mu[:, 2, :], mu[:, 2, :], -1.0, 1.0, ALU.mult, ALU.add, [mu], [mu])
                with ES() as st2:
                    hT = K.sb([128, KC, T], BF16, "hT", st2)
                    load_hT(hT)
                    wstg = [K.sb([128, 4096], F32, "wstg", st2)]
                    wch = [K.sb([128, KC, 128], BF16, "rwch", st2) for _ in range(2)]
                    g2b = K.sb([128, 512], BF16, "g2b", st2)
                    upad = [K.sb([128, T + 4], F32, "rupad", st2) for _ in range(2)]
                    xs = [K.sb([128, T], F32, "rxs", st2) for _ in range(1)]
                    xsb = [K.sb([128, T], BF16, "rxsb", st2) for _ in range(1)]
                    t32 = [K.sb([128, 512], F32, "rt32", st2) for _ in range(2)]
                    t16 = [K.sb([128, 512], BF16, "rt16", st2) for _ in range(2)]
                    gbo = [K.sb([128, T], BF16, "gbo", st2) for _ in range(1)]
                    rk32 = [K.sb([128, 512], F32, "rk32", st2) for _ in range(2)]
                    for u_ in upad:
                        K.memset("pool", u_[:, :], 0.0, [u_])
                    load_w_bf16(g2b[:, :], rg2_d[0, :, :], (128, 512), rg2_d, g2b, wstg, 0)

                    def poff(t):
                        return t + 1 if t < LC else t + 3

                    order = [4, 5, 6, 7, 0, 1, 2, 3, 8, 9, 10, 11, 12, 13]
                    for oi, fc in enumerate(order):
                        w_ = wch[oi % 2]
                        load_w_bf16(w_[:, :, :], win[:, :, AB_RW + fc * 128:AB_RW + (fc + 1) * 128], (128, KC, 128), abin_d, w_, wstg, 0)
                        up = upad[oi % 2]
                        for bi, (t0, t1) in enumerate(BLOCKS):
                            n = t1 - t0
                            pb = PS[bi % 4]
                            for kc in range(KC):
                                K.mm(pb[:, 0:n], w_[:, kc, :], hT[:, kc, t0:t1], kc == 0, kc == KC - 1, [w_, hT], [pb])
                            K.copy("act", up[:, poff(t0):poff(t0) + n], pb[:, 0:n], [pb], [up])
                        x_ = xs[0]
                        for (s0, ln) in ((0, LC), (LC, LL)):
                            p0 = poff(s0)
                            K.act(x_[:, s0:s0 + ln], up[:, p0:p0 + ln], AF.Identity, [up, mu], [x_], scale=mu[:, 2, fc:fc + 1])
                            K.stt("dve", x_[:, s0:s0 + ln], up[:, p0 - 1:p0 - 1 + ln], mu[:, 0, fc:fc + 1], x_[:, s0:s0 + ln], ALU.mult, ALU.add, [up, mu, x_], [x_])
                            K.stt("dve", x_[:, s0:s0 + ln], up[:, p0 + 1:p0 + 1 + ln], mu[:, 1, fc:fc + 1], x_[:, s0:s0 + ln], ALU.mult, ALU.add, [up, mu, x_], [x_])
                        if fc < 4:
                            c4 = fc
                            K.copy("act", rT[:, c4, :], x_[:, :], [x_], [rT])
                            for bi, (t0, t1) in enumerate(BLOCKS):
                                n = t1 - t0
                                a_, b_ = t32[bi % 2], t16[bi % 2]
                                K.stt("dve", b_[:, 0:n], x_[:, t0:t1], cols[:, 3, c4:c4 + 1], kT[:, c4, t0:t1], ALU.mult, ALU.mult, [x_, cols, kT], [b_])
                                pb = PS[4 + bi % 2]
                                K.mm(pb[:, 0:n], blk64[:, :], b_[:, 0:n], True, True, [blk64, b_], [pb])
                                K.copy("act", gbo[0][:, t0:t1], pb[:, 0:n], [pb], [gbo[0]])
                            K.dma("pool", gb_d[1, c4, :, :], gbo[0][:, :], reads=[gbo[0]], writes=[gb_d])
                        elif fc < 8:
                            c4 = fc - 4
                            K.copy("act", kT[:, c4, :], x_[:, :], [x_], [kT])
                            for bi, (t0, t1) in enumerate(BLOCKS):
                                n = t1 - t0
                                a_, b_ = t32[bi % 2], t16[bi % 2]
                                K.ts("dve", a_[:, 0:n], x_[:, t0:t1], cols[:, 0, c4:c4 + 1], None, ALU.mult, None, [x_, cols], [a_])
                                K.act(b_[:, 0:n], a_[:, 0:n], AF.Square, [a_], [b_])
                                pb = PS[4 + bi % 2]
                                K.mm(pb[:, 0:n], blk64[:, :], b_[:, 0:n], True, True, [blk64, b_], [pb])
                                r_ = rk32[bi % 2]
                                K.rsqrt(r_[:, 0:n], pb[:, 0:n], 1.0, 1e-12, [pb], [r_])
                                K.tt("pool", kknT[:, c4, t0:t1], a_[:, 0:n], r_[:, 0:n], ALU.mult, [a_, r_], [kknT])
                        elif fc < 12:
                            c4 = fc - 8
                            xb_ = xsb[0]
                            K.copy("act", xb_[:, :], x_[:, :], [x_], [xb_])
                            for grp in range(3):
                                tis = list(range(grp * 8, min(18, grp * 8 + 8)))
                                pb = PS[4 + grp % 2]
                                pbv = pb[:, :].bitcast(BF16)
                                for qi, ti in enumerate(tis):
                                    K.transpose(pbv[:, qi * 128:(qi + 1) * 128], xb_[:, ti * 128:(ti + 1) * 128], identb[:, :], [xb_, identb], [pb])
                                nt = len(tis)
                                K.copy("dve", vtok[:, tis[0]:tis[0] + nt, c4 * 128:(c4 + 1) * 128],
                                       pbv[:, 0:nt * 128].rearrange("p (a f) -> p a f", a=nt), [pb], [vtok])
                            K.dma("pool", gb_d[0, c4, :, :], xb_[:, :], reads=[xb_], writes=[gb_d])
                        elif fc == 12:
                            K.act(xwaT[0:64, :], x_[0:64, :], AF.Tanh, [x_], [xwaT])
                            K.copy("act", xwaT[64:128, :], x_[64:128, :], [x_], [xwaT])
                        else:
                            xb_ = xsb[0]
                            K.act(xb_[:, :], x_[:, :], AF.Sigmoid, [x_], [xb_])
                            for c4 in range(4):
                                for bi, (t0, t1) in enumerate(BLOCKS):
                                    n = t1 - t0
                                    pb = PS[bi % 4]
                                    K.mm(pb[:, 0:n], g2b[:, c4 * 128:(c4 + 1) * 128], xb_[:, t0:t1], True, True, [g2b, xb_], [pb])
                                    K.copy("act", gbo[0][:, t0:t1], pb[:, 0:n], [pb], [gbo[0]])
                                K.dma("pool", gb_d[2, c4, :, :], gbo[0][:, :], reads=[gbo[0]], writes=[gb_d])
                K.barrier()
                with ES() as st2:
                    w2b = K.sb([64, 2, 512], BF16, "w2b", st2)
                    a2b = K.sb([128, 2, 512], BF16, "a2b", st2)
                    w0bc = K.sb([128, 2, 512], F32, "w0bc", st2)
                    lcm = K.sb([128, 2, 128], F32, "lcm", st2)
                    lexcm = K.sb([128, 2, 128], F32, "lexcm", st2)
                    mcol = K.sb([128, 2, 2], F32, "mcol", st2)
                    m1 = K.sb([128, 2, 128], F32, "m1", st2)
                    m3 = K.sb([128, 2, 384], F32, "m3", st2)
                    m1t = K.sb([128, 2, 128], F32, "m1t", st2)
                    for dst, src in ((lcm, rwlc_d), (lexcm, rwlexc_d), (mcol, rwmcol_d), (m1, rwm1_d), (m3, rwm3_d), (m1t, rwm1t_d)):
                        for d in range(2):
                            K.dma("sp", dst[:, d, :], src[d, :, :], reads=[src], writes=[dst])
                    with ES() as stw:
                        wstg = [K.sb([128, 1024], F32, "wstg", stw)]
                        for d in range(2):
                            load_w_bf16(w2b[:, d, :], rw2_d[0, d, :, :], (64, 512), rw2_d, w2b, wstg, 0)
                            s_ = wstg[0]
                            K.dma("sp", s_[64:128, 0:512], ra2_d[0, d, :, :], reads=[ra2_d], writes=[s_])
                            K.copy("pool", a2b[64:128, d, :], s_[64:128, 0:512], [s_], [a2b])
                            K.dma("sp", w0bc[:, d, :], rw0_d[0, d, :].partition_broadcast(128), reads=[rw0_d], writes=[w0bc])
                        K.barrier()

                    def dir_stream(d):
                        B = PS[4 * d:4 * d + 4]
                        sg = K.sb([128, 512], F32, "sg", st2)
                        aT = K.sb([128, 128], F32, "aT", st2)
                        tmpa = K.sb([128, 128], F32, "tmpa", st2)
                        tmpb = K.sb([128, 128], F32, "tmpb", st2)
                        eL = K.sb([128, 128], F32, "eL", st2)
                        enL = K.sb([128, 128], F32, "enL", st2)
                        eLex = K.sb([128, 128], F32, "eLex", st2)
                        pm_sb = K.sb([128, 4, 2], F32, "pm_sb", st2)
                        gm = K.sb([128, 4, 2], F32, "gm", st2)
                        AR = K.sb([128, 4, 256], BF16, "AR", st2)
                        BH = K.sb([128, 4, 128], BF16, "BH", st2)
                        KH = K.sb([128, 4, 128], BF16, "KH", st2)
                        BKtok = K.sb([128, 2, 512], BF16, "BKtok", st2)
                        Q = [K.sb([128, 8, 128], F32, "Qa", st2), K.sb([128, 8, 128], F32, "Qb", st2)]
                        QT = [K.sb([128, 8, 128], F32, "QTa", st2), K.sb([128, 8, 128], F32, "QTb", st2)]
                        Nm = K.sb([128, 8, 128], F32, "Nm", st2)
                        S3 = K.sb([128, 8, 384], BF16, "S3", st2)
                        H = K.sb([128, 4, 64], F32, "H", st2)
                        H0 = K.sb([128, 4, 64], F32, "H0", st2)
                        H0b = K.sb([128, 4, 64], BF16, "H0b", st2)
                        W_sb = K.sb([128, 512], F32, "W_sb", st2)
                        U_sb = K.sb([128, 512], BF16, "U_sb", st2)
                        ybuf = K.sb([128, 512], F32, "ybuf", st2)
                        yold = K.sb([128, 512], F32, "yold", st2)
                        yield
                        K.memset("dve", H[:, :, :], 0.0, [H])
                        tiles = list(range(18)) if d == 0 else [1, 0] + list(range(17, 1, -1))
                        for ci, c in enumerate(tiles):
                            tsl = slice(c * 128, (c + 1) * 128)
                            pz = B[0]
                            K.mm(pz[:, :], xwaT[0:64, tsl], w2b[:, d, :], True, True, [xwaT, w2b], [pz])
                            K.tt("dve", sg[:, :], pz[:, :], w0bc[:, d, :], ALU.add, [pz, w0bc], [sg])
                            K.act(sg[:, :], sg[:, :], AF.Sigmoid, [sg], [sg])
                            for f4 in range(4):
                                fs = slice(f4 * 128, (f4 + 1) * 128)
                                pa = B[1]
                                K.mm(pa[:, 0:128], a2b[64:128, d, fs], xwaT[64:128, tsl], True, True, [a2b, xwaT], [pa])
                                K.act(aT[:, :], pa[:, 0:128], AF.Sigmoid, [pa, cols], [aT], bias=cols[:, 7 + d, f4:f4 + 1])
                                pl = B[2 + f4 % 2]
                                K.mm(pl[:, 0:128], sg[:, fs], lcm[:, d, :], True, True, [sg, lcm], [pl])
                                K.mm(pl[:, 128:256], sg[:, fs], lexcm[:, d, :], True, True, [sg, lexcm], [pl])
                                K.mm(pl[:, 256:258], sg[:, fs], mcol[:, d, :], True, True, [sg, mcol], [pl])
                                K.act(eL[:, :], pl[:, 0:128], AF.Exp, [pl], [eL], scale=CW)
                                K.act(enL[:, :], pl[:, 0:128], AF.Exp, [pl], [enL], scale=-CW)
                                K.act(eLex[:, :], pl[:, 128:256], AF.Exp, [pl], [eLex], scale=CW)
                                K.copy("act", pm_sb[:, f4, :], pl[:, 256:258], [pl], [pm_sb])
                                K.tt("dve", AR[:, f4, 128:256], rT[:, f4, tsl], eL[:, :], ALU.mult, [rT, eL], [AR])
                                K.stt("dve", AR[:, f4, 0:128], kknT[:, f4, tsl], -1.0, eLex[:, :], ALU.mult, ALU.mult, [kknT, eLex], [AR])
                                K.tt("pool", tmpa[:, :], kknT[:, f4, tsl], aT[:, :], ALU.mult, [kknT, aT], [tmpa])
                                K.tt("dve", BH[:, f4, :], tmpa[:, :], enL[:, :], ALU.mult, [tmpa, enL], [BH])
                                K.ts("dve", tmpb[:, :], aT[:, :], cols[:, 1, f4:f4 + 1], cols[:, 2, f4:f4 + 1], ALU.mult, ALU.add, [aT, cols], [tmpb])
                                K.tt("pool", tmpb[:, :], tmpb[:, :], kT[:, f4, tsl], ALU.mult, [tmpb, kT], [tmpb])
                                K.tt("dve", KH[:, f4, :], tmpb[:, :], enL[:, :], ALU.mult, [tmpb, enL], [KH])
                                yield
                            K.tt("dve", pm_sb[:, :, 1], pm_sb[:, :, 1], pm_sb[:, :, 0], ALU.subtract, [pm_sb], [pm_sb])
                            K.act(gm[:, :, :], pm_sb[:, :, :], AF.Exp, [pm_sb], [gm], scale=CW)
                            pt = B[1]
                            ptv = pt[:, :].bitcast(BF16)
                            for f4 in range(4):
                                K.transpose(ptv[:, f4 * 128:(f4 + 1) * 128], BH[:, f4, :], identb[:, :], [BH, identb], [pt])
                                K.transpose(ptv[:, 512 + f4 * 128:512 + (f4 + 1) * 128], KH[:, f4, :], identb[:, :], [KH, identb], [pt])
                            K.copy("act", BKtok[:, :, :], ptv[:, :].rearrange("p (a f) -> p a f", a=2), [pt], [BKtok])
                            yield
                            for h in range(8):
                                f4, hr = h // 2, slice((h % 2) * 64, (h % 2) * 64 + 64)
                                ps_ = B[2 * (h % 2)]
                                K.mm(ps_[:, 0:256], BH[hr, f4, :], AR[hr, f4, :], True, True, [BH, AR], [ps_])
                                K.mm(ps_[:, 256:512], KH[hr, f4, :], AR[hr, f4, :], True, True, [KH, AR], [ps_])
                                K.tt("dve", Q[0][:, h, :], ps_[:, 0:128], m1[:, d, :], ALU.mult, [ps_, m1], [Q[0]])
                                K.tt("dve", S3[:, h, :], ps_[:, 128:512], m3[:, d, :], ALU.mult, [ps_, m3], [S3])
                                pq = B[2 * (h % 2) + 1]
                                K.mm(pq[:, 0:128], AR[hr, f4, 0:128], BH[hr, f4, :], True, True, [AR, BH], [pq])
                                K.tt("dve", QT[0][:, h, :], pq[:, 0:128], m1t[:, d, :], ALU.mult, [pq, m1t], [QT[0]])
                                if h % 2 == 1:
                                    yield
                            K.tt("pool", Nm[:, :, :], Q[0][:, :, :], ident[:, :].unsqueeze(1).to_broadcast([128, 8, 128]), ALU.add, [Q[0], ident], [Nm])
                            cur = 0
                            for lev in range(1, 7):
                                nxt = 1 - cur
                                for half in range(2):
                                    pqa, pqb, pn = B[(3 * half) % 4], B[(3 * half + 1) % 4], B[(3 * half + 2) % 4]
                                    hs = slice(half * 4, half * 4 + 4)
                                    for hh in range(4):
                                        h = half * 4 + hh
                                        cs = slice(hh * 128, (hh + 1) * 128)
                                        K.mm(pqb[:, cs], Q[cur][:, h, :], QT[cur][:, h, :], True, True, [Q[cur], QT[cur]], [pqb])
                                        if lev < 6:
                                            K.mm(pqa[:, cs], QT[cur][:, h, :], Q[cur][:, h, :], True, True, [Q[cur], QT[cur]], [pqa])
                                    K.copy("act", QT[nxt][:, hs, :], pqb[:, :].rearrange("p (a f) -> p a f", a=4), [pqb], [QT[nxt]])
                                    if lev < 6:
                                        K.copy("dve", Q[nxt][:, hs, :], pqa[:, :].rearrange("p (a f) -> p a f", a=4), [pqa], [Q[nxt]])
                                    yield
                                    for hh in range(4):
                                        h = half * 4 + hh
                                        K.mm(pn[:, hh * 128:(hh + 1) * 128], QT[nxt][:, h, :], Nm[:, h, :], True, True, [QT[nxt], Nm], [pn])
                                    K.tt("dve", Nm[:, hs, :], Nm[:, hs, :], pn[:, :].rearrange("p (a f) -> p a f", a=4), ALU.add, [Nm, pn], [Nm])
                                    yield
                                cur = nxt
                            K.tt("dve", H0[:, :, :], H[:, :, :], gm[:, :, 0:1].to_broadcast([128, 4, 64]), ALU.mult, [H, gm], [H0])
                            K.copy("act", H0b[:, :, :], H0[:, :, :], [H0], [H0b])
                            pw = B[0]
                            for h in range(8):
                                f4, hr = h // 2, slice((h % 2) * 64, (h % 2) * 64 + 64)
                                cs = slice(h * 64, (h + 1) * 64)
                                K.mm(pw[:, cs], AR[hr, f4, 0:128], H0b[hr, f4, :], True, False, [AR, H0b], [pw])
                                K.mm(pw[:, cs], S3[:, h, 128:256], vtok[:, c, cs], False, True, [S3, vtok], [pw])
                            K.copy("act", W_sb[:, :], pw[:, :], [pw], [W_sb])
                            yield
                            pu = B[1]
                            for h in range(8):
                                cs = slice(h * 64, (h + 1) * 64)
                                K.mm(pu[:, cs], Nm[:, h, :], W_sb[:, cs], True, True, [Nm, W_sb], [pu])
                            K.copy("act", U_sb[:, :], pu[:, :], [pu], [U_sb])
                            yield
                            py = B[2]
                            for h in range(8):
                                f4, hr = h // 2, slice((h % 2) * 64, (h % 2) * 64 + 64)
                                cs = slice(h * 64, (h + 1) * 64)
                                K.mm(py[:, cs], AR[hr, f4, 128:256], H0b[hr, f4, :], True, False, [AR, H0b], [py])
                                K.mm(py[:, cs], S3[:, h, 0:128], U_sb[:, cs], False, False, [S3, U_sb], [py])
                                K.mm(py[:, cs], S3[:, h, 256:384], vtok[:, c, cs], False, True, [S3, vtok], [py])
                            if d == 0:
                                K.copy("dve", ybuf[:, :], py[:, :], [py], [ybuf])
                                K.dma("pool", yf_d[c, :, :], ybuf[:, :], reads=[ybuf], writes=[yf_t[c]])
                            else:
                                K.copy("dve", ybuf[:, :], py[:, :], [py], [ybuf])
                                K.dma("pool", y_d[c, :, :], ybuf[:, :], reads=[ybuf], writes=[yb_t[c]])
                            yield
                            if ci < len(tiles) - 1:
                                ph = B[3]
                                for f4 in range(4):
                                    fs = slice(f4 * 128, (f4 + 1) * 128)
                                    K.mm(ph[:, fs], BKtok[:, 0, fs], U_sb[:, fs], True, False, [BKtok, U_sb], [ph])
                                    K.mm(ph[:, fs], BKtok[:, 1, fs], vtok[:, c, fs], False, True, [BKtok, vtok], [ph])
                                phv = ph[:, :].rearrange("p (f x) -> p f x", f=4)
                                for e2_ in range(2):
                                    rs = slice(e2_ * 64, e2_ * 64 + 64)
                                    K.tt("dve", H[rs, :, :], H0[rs, :, :], phv[rs, :, e2_ * 64:(e2_ + 1) * 64], ALU.add, [H0, ph], [H])
                                    K.tt("dve", H[rs, :, :], H[rs, :, :], gm[rs, :, 1:2].to_broadcast([64, 4, 64]), ALU.mult, [H, gm], [H])
                                yield

                    yf_t = [Buf(yf_d.t, "yf%d" % i_) for i_ in range(18)]
                    yb_t = [Buf(y_d.t, "yb%d" % i_) for i_ in range(18)]
                    streams = [dir_stream(0), dir_stream(1)]
                    for s_ in streams:
                        next(s_)
                    for _ in range(cfg.get("rw_offset", 19)):
                        next(streams[1])
                    alive = list(streams)
                    while alive:
                        for s_ in list(alive):
                            try:
                                next(s_)
                            except StopIteration:
                                alive.remove(s_)
            K.barrier()
            rwo = K.sb([128, 4, T], BF16, "rwo", st)
            with ES() as st2:
                yt = [K.sb([128, 512], F32, "yt", st2) for _ in range(2)]
                ysq = K.sb([128, 512], F32, "ysq", st2)
                s1 = K.sb([128, 8], F32, "s1", st2)
                s2 = K.sb([128, 8], F32, "s2", st2)
                ynb = K.sb([128, 512], BF16, "ynb", st2)
                vT_ = [K.sb([128, 4, 128], BF16, "vT_", st2) for _ in range(2)]
                sc_ = [K.sb([128, 4, 128], BF16, "sc_", st2) for _ in range(2)]
                gg_ = [K.sb([128, 4, 128], BF16, "gg_", st2) for _ in range(2)]
                yn32 = K.sb([128, 4, 128], F32, "yn32", st2)
                bon = K.sb([128, 4, 128], F32, "bon", st2)
                gbv = gb_d[:, :, :, :].rearrange("a c p t -> a p c t")
                for c in range(18):
                    tsl = slice(c * 128, (c + 1) * 128)
                    y_ = yt[c % 2]
                    K.dma("sp", y_[:, :], y_d[c, :, :], reads=[y_d], writes=[y_])
                    K.dma("sp", ysq[:, :], yf_d[c, :, :], reads=[yf_d], writes=[ysq])
                    K.tt("dve", y_[:, :], y_[:, :], ysq[:, :], ALU.add, [y_, ysq], [y_])
                    K.dma("sp", vT_[c % 2][:, :, :], gbv[0, :, :, tsl], reads=[gb_d], writes=[vT_[c % 2]])
                    K.dma("sp", sc_[c % 2][:, :, :], gbv[1, :, :, tsl], reads=[gb_d], writes=[sc_[c % 2]])
                    K.dma("sp", gg_[c % 2][:, :, :], gbv[2, :, :, tsl], reads=[gb_d], writes=[gg_[c % 2]])
                    yv = y_[:, :].rearrange("p (h e) -> p h e", h=8)
                    K.op("dve", lambda e, yv=yv: e.tensor_reduce(out=s1[:, :], in_=yv, axis=AX.X, op=ALU.add), [y_], [s1])
                    K.act(ysq[:, :], y_[:, :], AF.Square, [y_], [ysq])
                    K.op("dve", lambda e: e.tensor_reduce(out=s2[:, :], in_=ysq[:, :].rearrange("p (h e) -> p h e", h=8), axis=AX.X, op=ALU.add), [ysq], [s2])
                    K.ts("dve", s1[:, :], s1[:, :], 1.0 / 64, None, ALU.mult, None, [s1], [s1])
                    K.tt("dve", ysq[:, 0:8], s1[:, :], s1[:, :], ALU.mult, [s1], [ysq])
                    K.stt("dve", s2[:, :], s2[:, :], 1.0 / 64, ysq[:, 0:8], ALU.mult, ALU.subtract, [s2, ysq], [s2])
                    K.rsqrt(s2[:, :], s2[:, :], 1.0, 64e-5, [s2], [s2])
                    K.tt("dve", yv, yv, s1[:, :].unsqueeze(2).to_broadcast([128, 8, 64]), ALU.subtract, [y_, s1], [y_])
                    K.tt("dve", ynb[:, :].rearrange("p (h e) -> p h e", h=8), yv, s2[:, :].unsqueeze(2).to_broadcast([128, 8, 64]), ALU.mult, [y_, s2], [ynb])
                    pt = PS[c % 2]
                    ptv = pt[:, :].bitcast(BF16)
                    for c4 in range(4):
                        K.transpose(ptv[:, c4 * 128:(c4 + 1) * 128], ynb[:, c4 * 128:(c4 + 1) * 128], identb[:, :], [ynb, identb], [pt])
                    for c4 in range(4):
                        K.act(yn32[:, c4, :], ptv[:, c4 * 128:(c4 + 1) * 128], AF.Identity, [pt, cols], [yn32],
                              bias=cols[:, 5, c4:c4 + 1], scale=cols[:, 4, c4:c4 + 1])
                    K.tt("pool", bon[:, :, :], vT_[c % 2][:, :, :], sc_[c % 2][:, :, :], ALU.mult, [vT_[c % 2], sc_[c % 2]], [bon])
                    K.tt("pool", yn32[:, :, :], yn32[:, :, :], bon[:, :, :], ALU.add, [yn32, bon], [yn32])
                    K.tt("dve", rwo[:, :, tsl], yn32[:, :, :], gg_[c % 2][:, :, :], ALU.mult, [yn32, gg_[c % 2]], [rwo])
            K.barrier()
            wo = K.sb([128, 4, D], BF16, "wo_rw", st)
            wostg = [K.sb([128, 4096], F32, "wostg", st)]
            load_w_bf16(wo[:, :, :], about_d[0, 512:1024, :].rearrange("(kc p) n -> p kc n", p=128), (128, 4, D), about_d, wo, wostg, 0)
            pieces = [((lambda oc, cc=cc: wo[:, cc, oc * 128:(oc + 1) * 128]),
                       (lambda t0, t1, cc=cc: rwo[:, cc, t0:t1]), [wo, rwo]) for cc in range(4)]
            apply_out(b, pieces, 2, 0)
        K.barrier()

    K.barrier()
    for l in layers:
        stage_mods(l)
    for b in range(nb):
        stage_load(b)
        for li, l in enumerate(layers):
            last = (li == len(layers) - 1) and not cfg.get("force_ctx", False)
            modsT.l = l
            Amod.l = l
            if mixers:
                if l == 1:
                    with ES() as sth:
                        hT = K.sb([128, KC, T], BF16, "hT", sth)
                        stage_norm(b, 0, hT, to_dram=True)
                        if cfg.get("swa", True):
                            stage_swa(b, hT)
                    if cfg.get("ssd", True):
                        stage_ssd(b, None)
                else:
                    with ES() as sth:
                        hT = K.sb([128, KC, T], BF16, "hT", sth)
                        stage_norm(b, 0, hT, to_dram=True)
                        if cfg.get("mla", True):
                            stage_mla(b, hT)
                    if cfg.get("rwkv", True):
                        stage_rwkv(b)
            with ES() as sth:
                hT = K.sb([128, KC, T], BF16, "hT", sth)
                stage_norm(b, 1, hT, lo=0 if not last else LC)
                stage_ffn(b, l, do_ctx=not last, hT=hT)
        stage_out(b)
        if cfg.get("dbg_x", False) and b == 0:
            for cc in range(KC):
                K.dma("sp", dbgx_d[cc, :, :], xs_d[cc, :, :], reads=xblk, writes=[dbgx_d])
    K.barrier()
    K.es.close()
    return nc, K


CONST_INPUTS = None


def _rope_tables(rot_dim):
    n_freq = rot_dim // 4
    rows = np.arange(LL, dtype=np.float32) // 64
    cols = np.arange(LL, dtype=np.float32) % 64
    inv = (np.float32(10000.0) ** (-np.arange(n_freq, dtype=np.float32) / np.float32(n_freq))).astype(np.float32)
    ang = np.concatenate([rows[:, None] * inv[None, :], cols[:, None] * inv[None, :]], axis=-1).astype(np.float32)
    cos, sin = np.cos(ang).astype(np.float32), np.sin(ang).astype(np.float32)
    cosT = np.concatenate([cos.T, cos.T], axis=0)
    sinT = np.concatenate([sin.T, sin.T], axis=0)
    return np.ascontiguousarray(cosT), np.ascontiguousarray(sinT)


def const_inputs():
    global CONST_INPUTS
    if CONST_INPUTS is None:
        s = np.arange(128)
        triF = (s[:, None] <= s[None, :]).astype(np.float32)
        c = {"ident": np.eye(128, dtype=np.float32), "triF": triF, "triB": np.ascontiguousarray(triF.T),
             "strF": (s[:, None] > s[None, :]).astype(np.float32), "strB": (s[:, None] < s[None, :]).astype(np.float32)}
        c["swa_cos"], c["swa_sin"] = _rope_tables(64)
        c["mla_cos"], c["mla_sin"] = _rope_tables(32)
        triB = triF.T
        incl = [triF, triB]
        strict = [c["strB"], c["strF"]]
        m = 63
        c["rw_lc"] = np.stack([incl[d] - incl[d][:, m:m + 1] for d in range(2)]).astype(np.float32)
        c["rw_lexc"] = np.stack([strict[d] - incl[d][:, m:m + 1] for d in range(2)]).astype(np.float32)
        c["rw_mcol"] = np.stack([np.stack([incl[d][:, m], np.ones(128, np.float32)], axis=1) for d in range(2)]).astype(np.float32)
        c["rw_m1"] = np.stack([strict[d] for d in range(2)]).astype(np.float32)
        c["rw_m3"] = np.stack([np.concatenate([incl[d], strict[d], incl[d]], axis=1) for d in range(2)]).astype(np.float32)
        c["rw_m1t"] = np.stack([np.ascontiguousarray(strict[d].T) for d in range(2)]).astype(np.float32)
        blk = np.zeros((128, 128), np.float32)
        blk[:64, :64] = 1.0
        blk[64:, 64:] = 1.0
        c["blk64"] = blk
        CONST_INPUTS = c
    return CONST_INPUTS


def make_in_maps(nc_names, inputs, ncores=NCORES):
    consts = const_inputs()
    in_maps = []
    for core in range(ncores):
        m = {}
        sl = slice(core * BPC, (core + 1) * BPC)
        for k in nc_names:
            if k in consts:
                m[k] = consts[k]
            else:
                v = np.asarray(inputs[k])
                m[k] = np.ascontiguousarray(v[sl] if k in ("x", "c", "ctx") else v)
        in_maps.append(m)
    return in_maps


def kernel(**inputs):
    cfg = {}
    nc, K = build_program(cfg)
    in_maps = make_in_maps(K.in_names, inputs)
    res = run_bass_kernel_spmd(nc, in_maps, core_ids=list(range(NCORES)))
    return np.concatenate([r["out"] for r in res.results], axis=0)
```

```python
import contextlib
import math
import numpy as np
import concourse.bass as bass
import concourse.mybir as mybir
from concourse.bass_utils import run_bass_kernel_spmd

F32 = mybir.dt.float32
BF16 = mybir.dt.bfloat16
AF = mybir.ActivationFunctionType
ALU = mybir.AluOpType
AX = mybir.AxisListType

NCORES = 8
BPC = 4
D = 1024
KC = 8
LC = 256
LL = 2048
T = LC + LL
DFF = 2816
EPS = 1e-6
BLOCKS = [(0, 256), (256, 768), (768, 1280), (1280, 1792), (1792, 2304)]
SEM_EPOCH = 50000


class Buf:
    def __init__(self, t, name):
        self.t = t
        self.name = name
        self.lw = []
        self.rd = []
        self.ds = None

    def __getitem__(self, idx):
        return self.t[idx]


class Eng:
    def __init__(self, name, e, is_pe=False):
        self.name = name
        self.e = e
        self.is_pe = is_pe
        self.sems = []
        self.count = 0
        self.epoch = 0
        self.seen = {}


class Kern:
    def __init__(self, nc):
        self.nc = nc
        self.es = contextlib.ExitStack()
        self.engs = {}
        for name, e, ispe in (("pe", nc.tensor, True), ("act", nc.scalar, False),
                              ("dve", nc.vector, False), ("pool", nc.gpsimd, False),
                              ("sp", nc.sync, False)):
            en = Eng(name, e, ispe)
            en.sems.append(self.es.enter_context(nc.semaphore("s_%s_0" % name)))
            self.engs[name] = en
        self.ndsem = 40
        self.dsem = [self.es.enter_context(nc.semaphore("s_d%d" % i)) for i in range(self.ndsem)]
        self.dtot = [0] * self.ndsem
        self.drr = 0
        self.nbuf = 0
        self.n_ops = 0
        self.eps_bufs = {}
        self.in_names = []

    def sb(self, shape, dtype, name=None, stack=None):
        self.nbuf += 1
        name = "%s_%d" % (name or "sb", self.nbuf)
        t = (stack or self.es).enter_context(self.nc.sbuf_tensor(name, list(shape), dtype))
        return Buf(t, name)

    def ps(self, name=None):
        self.nbuf += 1
        name = "%s_%d" % (name or "ps", self.nbuf)
        t = self.es.enter_context(self.nc.psum_tensor(name, [128, 512], F32))
        return Buf(t, name)

    def dram(self, name, shape, dtype, kind="Internal"):
        t = self.nc.dram_tensor(name, list(shape), dtype, kind=kind)
        if kind == "ExternalInput":
            self.in_names.append(name)
        return Buf(t, name)

    def _wait(self, eng, ev):
        if ev[0] == "E":
            src = self.engs[ev[1]]
            ep, n = ev[2], ev[3]
            if src is eng and eng.is_pe:
                return
            key = ("E", ev[1], ep)
            if eng.seen.get(key, 0) >= n:
                return
            eng.e.wait_ge(src.sems[ep], n)
            eng.seen[key] = n
        else:
            i = ev[1]
            tot = self.dtot[i]
            key = ("D", i)
            if eng.seen.get(key, 0) >= tot:
                return
            eng.e.wait_ge(self.dsem[i], tot)
            eng.seen[key] = tot

    def _deps(self, eng, reads, writes):
        for b in reads:
            for ev in b.lw:
                self._wait(eng, ev)
        for b in writes:
            for ev in b.lw:
                self._wait(eng, ev)
            for ev in b.rd:
                self._wait(eng, ev)

    def _record(self, ev, reads, writes):
        for b in reads:
            if ev[0] == "E":
                b.rd = [r for r in b.rd if not (r[0] == "E" and r[1] == ev[1])]
            else:
                b.rd = [r for r in b.rd if r != ev]
            b.rd.append(ev)
        for b in writes:
            b.lw = [ev]
            b.rd = []

    def op(self, engname, fn, reads=(), writes=()):
        eng = self.engs[engname]
        self._deps(eng, reads, writes)
        if eng.count >= SEM_EPOCH:
            eng.epoch += 1
            eng.count = 0
            eng.sems.append(self.es.enter_context(self.nc.semaphore("s_%s_%d" % (engname, eng.epoch))))
        ins = fn(eng.e)
        eng.count += 1
        ins.then_inc(eng.sems[eng.epoch], 1)
        ev = ("E", engname, eng.epoch, eng.count)
        self._record(ev, reads, writes)
        self.n_ops += 1

    def dma(self, engname, out, in_, reads=(), writes=(), **kw):
        eng = self.engs[engname]
        self._deps(eng, reads, writes)
        b = None
        for cand in list(writes) + list(reads):
            if cand.ds is not None:
                b = cand
                break
        if b is None:
            b = (list(writes) + list(reads))[0]
            b.ds = self.drr
            self.drr = (self.drr + 1) % self.ndsem
        i = b.ds
        ins = eng.e.dma_start(out=out, in_=in_, **kw)
        ins.then_inc(self.dsem[i], 16)
        self.dtot[i] += 16
        ev = ("D", i)
        self._record(ev, reads, writes)
        self.n_ops += 1

    def barrier(self):
        for eng in self.engs.values():
            for other in self.engs.values():
                if other is eng:
                    continue
                if other.count > 0:
                    self._wait(eng, ("E", other.name, other.epoch, other.count))
            for i in range(self.ndsem):
                if self.dtot[i] > 0:
                    self._wait(eng, ("D", i))

    def mm(self, out, lhsT, rhs, start, stop, reads, writes):
        self.op("pe", lambda e: e.matmul(out, lhsT=lhsT, rhs=rhs, start=start, stop=stop), reads, writes)

    def transpose(self, out, in_, ident, reads, writes):
        self.op("pe", lambda e: e.transpose(out, in_, ident), reads, writes)

    def act(self, out, in_, func, reads, writes, bias=None, scale=None, eng="act"):
        kw = {}
        if bias is not None:
            kw["bias"] = bias
        if scale is not None:
            kw["scale"] = scale
        self.op(eng, lambda e: e.activation(out=out, in_=in_, func=func, **kw), reads, writes)

    def ts(self, eng, out, in0, s1, s2, op0, op1, reads, writes):
        if op1 is None:
            self.op(eng, lambda e: e.tensor_scalar(out=out, in0=in0, scalar1=s1, scalar2=None, op0=op0), reads, writes)
        else:
            self.op(eng, lambda e: e.tensor_scalar(out=out, in0=in0, scalar1=s1, scalar2=s2, op0=op0, op1=op1), reads, writes)

    def tt(self, eng, out, in0, in1, op, reads, writes):
        self.op(eng, lambda e: e.tensor_tensor(out=out, in0=in0, in1=in1, op=op), reads, writes)

    def stt(self, eng, out, in0, scalar, in1, op0, op1, reads, writes):
        self.op(eng, lambda e: e.scalar_tensor_tensor(out=out, in0=in0, scalar=scalar, in1=in1, op0=op0, op1=op1), reads, writes)

    def copy(self, eng, out, in_, reads, writes):
        if eng == "act":
            self.op(eng, lambda e: e.activation(out=out, in_=in_, func=AF.Copy), reads, writes)
        else:
            self.op(eng, lambda e: e.tensor_copy(out=out, in_=in_), reads, writes)

    def rsqrt(self, out, in_, scale, eps, reads, writes):
        self.op("act", lambda e: e.activation(out=out, in_=in_, func=AF.Sqrt, bias=self.eps_ap(eps), scale=scale), reads, writes)
        self.op("dve", lambda e: e.reciprocal(out=out, in_=out), writes, writes)

    def eps_ap(self, eps):
        if eps not in self.eps_bufs:
            b = self.sb([128, 1], F32, "eps")
            self.memset("dve", b[:, :], float(eps), [b])
            self.eps_bufs[eps] = b
        return self.eps_bufs[eps][:, 0:1]

    def memset(self, eng, ap, val, writes):
        self.op(eng, lambda e: e.memset(ap, val), (), writes)


def colvec(ap1d, n):
    return ap1d.rearrange("(c p) -> p c", p=128)


def stg_view(s, shape):
    n = 1
    for d_ in shape[1:]:
        n *= d_
    v = s[0:shape[0], 0:n]
    if len(shape) == 3:
        v = v.rearrange("p (a b) -> p a b", a=shape[1])
    return v


HD = 64
CD_Z, CD_XBC, CD_DT, CD_Q, CD_K, CD_V = 0, 1024, 2560, 2592, 3104, 3232
CD_IN = 3360


def build_program(cfg):
    nc = bass.Bass("TRN2", target_bir_lowering=False)
    K = Kern(nc)
    nb = cfg.get("nb", BPC)
    layers = cfg.get("layers", [0, 1])
    mixers = cfg.get("mixers", True)
    ES = contextlib.ExitStack

    def din(name, shape):
        return K.dram(name, shape, F32, kind="ExternalInput")

    x_d = din("x", [BPC, LL, D])
    c_d = din("c", [BPC, D])
    ctx_d = din("ctx", [BPC, LC, D])
    cctx_d = din("c_ctx", [D])
    ada_w_d = din("ada_w", [2, D, 6 * D])
    ada_b_d = din("ada_b", [2, 6 * D])
    nmix_d = din("norm_mix_g", [2, D])
    nffn_d = din("norm_ffn_g", [2, D])
    wup_d = din("ffn_w_up", [2, D, 2 * DFF])
    cw_d = din("ffn_conv_w", [2, 3, 2 * DFF])
    cb_d = din("ffn_conv_b", [2, 2 * DFF])
    wdn_d = din("ffn_w_down", [2, DFF, D])
    fng_d = din("final_norm_g", [D])
    if mixers and 1 in layers:
        cdin_d = din("cd_w_in", [1, D, CD_IN])
        cdout_d = din("cd_w_out", [1, 1536, D])
        scw_d = din("ssm_conv_w", [1, 5, 1536])
        scb_d = din("ssm_conv_b", [1, 1536])
        sdtb_d = din("ssm_dt_bias", [1, 2, 16])
        salog_d = din("ssm_a_log", [1, 2, 16])
        sd_d = din("ssm_d", [1, 16])
        sng_d = din("ssm_norm_g", [1, 1024])
        sink_d = din("swa_sink", [1, 8])
        swacos_d = din("swa_cos", [64, LL])
        swasin_d = din("swa_sin", [64, LL])
        triF_d = din("triF", [128, 128])
        triB_d = din("triB", [128, 128])
        strF_d = din("strF", [128, 128])
        strB_d = din("strB", [128, 128])
    if mixers and 0 in layers:
        abin_d = din("ab_w_in", [1, D, 2464])
        about_d = din("ab_w_out", [1, D, D])
        mqg_d = din("mla_q_norm_g", [1, 384])
        mqu_d = din("mla_w_q_up", [1, 384, 768])
        mkg_d = din("mla_kv_norm_g", [1, 256])
        mkvu_d = din("mla_w_kv_up", [1, 256, 1024])
        mlacos_d = din("mla_cos", [32, LL])
        mlasin_d = din("mla_sin", [32, LL])
        rmp_d = din("rwkv_mu_prev", [1, 1792])
        rmn_d = din("rwkv_mu_next", [1, 1792])
        rw0_d = din("rwkv_w0", [1, 2, 512])
        rw2_d = din("rwkv_w2", [1, 2, 64, 512])
        ra0_d = din("rwkv_a0", [1, 2, 512])
        ra2_d = din("rwkv_a2", [1, 2, 64, 512])
        rg2_d = din("rwkv_g2", [1, 128, 512])
        rkk_d = din("rwkv_k_k", [1, 512])
        rka_d = din("rwkv_k_a", [1, 512])
        rrk_d = din("rwkv_r_k", [1, 8, 64])
        rlg_d = din("rwkv_ln_g", [1, 512])
        rlb_d = din("rwkv_ln_b", [1, 512])
        rwlc_d = din("rw_lc", [2, 128, 128])
        rwlexc_d = din("rw_lexc", [2, 128, 128])
        rwmcol_d = din("rw_mcol", [2, 128, 2])
        rwm1_d = din("rw_m1", [2, 128, 128])
        rwm3_d = din("rw_m3", [2, 128, 384])
        rwm1t_d = din("rw_m1t", [2, 128, 128])
        blk64_d = din("blk64", [128, 128])
        gb_d = K.dram("gb_scr", [3, 4, 128, T], BF16)
        y_d = K.dram("y_scr", [18, 128, 512], F32)
        yf_d = K.dram("yf_scr", [18, 128, 512], F32)
    ident_d = din("ident", [128, 128])
    out_d = K.dram("out", [BPC, LL, D], F32, kind="ExternalOutput")
    if cfg.get("dbg_x", False):
        dbgx_d = K.dram("dbgx", [KC, 128, T], F32, kind="ExternalOutput")
    aT_d = K.dram("aT_scr", [DFF, T], BF16)
    xs_d = K.dram("x_scr", [KC, 128, T], F32)
    hT_d = K.dram("hT_scr", [KC, 128, T], BF16)
    sz_d = K.dram("sz_scr", [16, 128, 1024], BF16)
    hin_d = K.dram("hin_scr", [16, 128, 1024], BF16)
    xview = xs_d[:, :, :].rearrange("c p t -> p c t")
    hview = hT_d[:, :, :].rearrange("c p t -> p c t")
    xblk = [Buf(xs_d.t, "xblk%d" % j) for j in range(T // 256)]

    def xdeps(t0, t1):
        return xblk[t0 // 256:(t1 + 255) // 256]

    ident = K.sb([128, 128], F32, "ident")
    identb = K.sb([128, 128], BF16, "identb")
    ones_bf = K.sb([128, 128], BF16, "ones")
    ones_f = K.sb([128, 128], F32, "onesf")
    class LayerBuf(Buf):
        def __init__(self, b_):
            Buf.__init__(self, b_.t, b_.name)
            self.l = 0

        def __getitem__(self, idx):
            return self.t[(idx[0], self.l) + tuple(idx[1:])]

    modsT = LayerBuf(K.sb([128, 2, KC, 6, 5], F32, "modsT"))
    Amod = LayerBuf(K.sb([128, 2, KC, 2, 5], F32, "Amod"))
    gcols = K.sb([128, 5, KC], F32, "gcols")
    cwT = K.sb([128, 2, 3, 44], F32, "cwT")
    cbT = K.sb([128, 2, 44], F32, "cbT")
    PS = [K.ps("ps%d" % i) for i in range(8)]
    for e_ in (EPS, 64e-5, 1e-12, 1.0, 0.0):
        K.eps_ap(e_)

    K.dma("sp", ident[:, :], ident_d[:, :], reads=[ident_d], writes=[ident])
    K.copy("dve", identb[:, :], ident[:, :], [ident], [identb])
    K.memset("dve", ones_bf[:, :], 1.0, [ones_bf])
    K.memset("dve", ones_f[:, :], 1.0, [ones_f])
    for l in range(2):
        K.dma("sp", gcols[:, l, :], colvec(nmix_d[l, :], KC), reads=[nmix_d], writes=[gcols], allow_slow_non_contiguous=True)
        K.dma("sp", gcols[:, 2 + l, :], colvec(nffn_d[l, :], KC), reads=[nffn_d], writes=[gcols], allow_slow_non_contiguous=True)
        for tap in range(3):
            K.dma("sp", cwT[:, l, tap, :], colvec(cw_d[l, tap, :], 44), reads=[cw_d], writes=[cwT], allow_slow_non_contiguous=True)
        K.dma("sp", cbT[:, l, :], colvec(cb_d[l, :], 44), reads=[cb_d], writes=[cbT], allow_slow_non_contiguous=True)
    K.dma("sp", gcols[:, 4, :], colvec(fng_d[:], KC), reads=[fng_d], writes=[gcols], allow_slow_non_contiguous=True)

    def stage_mods(l):
        modsT.l = l
        Amod.l = l
        with ES() as st:
            condT = K.sb([128, KC, 5], F32, "condT", st)
            scond = K.sb([128, KC, 5], F32, "scond", st)
            abT = K.sb([128, 48], F32, "abT", st)
            wb = [K.sb([128, KC, 128], F32, "adaw", st) for _ in range(3)]
            for r in range(4):
                K.dma("sp", condT[:, :, r], colvec(c_d[r, :], KC), reads=[c_d], writes=[condT], allow_slow_non_contiguous=True)
            K.dma("sp", condT[:, :, 4], colvec(cctx_d[:], KC), reads=[cctx_d], writes=[condT], allow_slow_non_contiguous=True)
            K.dma("sp", abT[:, :], colvec(ada_b_d[l, :], 48), reads=[ada_b_d], writes=[abT], allow_slow_non_contiguous=True)
            K.act(scond[:, :, :], condT[:, :, :], AF.Silu, [condT], [scond])
            wview = ada_w_d[l, :, :].rearrange("(kc p) n -> p kc n", p=128)
            for j in range(48):
                w = wb[j % 3]
                K.dma("sp", w[:, :, :], wview[:, :, j * 128:(j + 1) * 128], reads=[ada_w_d], writes=[w])
                pb = PS[j % 2]
                for kc in range(KC):
                    K.mm(pb[:, 0:5], w[:, kc, :], scond[:, kc, :], kc == 0, kc == KC - 1, [w, scond], [pb])
                kind, cc = j // 8, j % 8
                K.ts("dve", modsT[:, cc, kind, :], pb[:, 0:5], abT[:, j:j + 1], None, ALU.add, None, [pb, abT], [modsT])
            for which, kind, gi in ((0, 1, l), (1, 4, 2 + l)):
                for cc in range(KC):
                    K.ts("dve", Amod[:, cc, which, :], modsT[:, cc, kind, :], 1.0, gcols[:, gi, cc:cc + 1],
                         ALU.add, ALU.mult, [modsT, gcols], [Amod])
        K.barrier()

    def stage_load(b):
        with ES() as st:
            xin = [K.sb([128, D], F32, "xin", st) for _ in range(3)]
            xo = [K.sb([128, KC, 128], F32, "xo", st) for _ in range(3)]
            for ti in range(T // 128):
                xb = xin[ti % 3]
                if ti < 2:
                    src, sb_ = ctx_d[b, ti * 128:(ti + 1) * 128, :], ctx_d
                else:
                    src, sb_ = x_d[b, (ti - 2) * 128:(ti - 1) * 128, :], x_d
                K.dma("sp", xb[:, :], src, reads=[sb_], writes=[xb])
                o = xo[ti % 3]
                for half in range(2):
                    pb = PS[(2 * ti + half) % 4]
                    for q in range(4):
                        cc = half * 4 + q
                        K.transpose(pb[:, q * 128:(q + 1) * 128], xb[:, cc * 128:(cc + 1) * 128], ident[:, :], [xb, ident], [pb])
                    K.copy("act" if half == 0 else "dve", o[:, half * 4:half * 4 + 4, :],
                           pb[:, :].rearrange("p (q t) -> p q t", q=4), [pb], [o])
                K.dma("pool", xview[:, :, ti * 128:(ti + 1) * 128], o[:, :, :], reads=[o], writes=xdeps(ti * 128, ti * 128 + 128))
        K.barrier()

    def stage_norm(b, which, hT, to_dram=False, lo=0):
        shift_kind = 0 if which == 0 else 3
        with ES() as st:
            xb = [K.sb([128, KC, 512], F32, "nxb", st) for _ in range(2)]
            sq = [K.sb([128, 512], BF16, "sq", st) for _ in range(3)]
            rstd = [K.sb([128, 512], F32, "rstd", st) for _ in range(2)]
            tmp = [K.sb([128, 512], F32, "ntmp", st) for _ in range(3)]
            for bi, (t0, t1) in enumerate(BLOCKS):
                if t1 <= lo:
                    continue
                n = t1 - t0
                row = 4 if bi == 0 else b
                x_ = xb[bi % 2]
                K.dma("sp", x_[:, :, 0:n], xview[:, :, t0:t1], reads=xdeps(t0, t1), writes=[x_])
                pb = PS[bi % 2]
                for cc in range(KC):
                    s = sq[cc % 3]
                    K.act(s[:, 0:n], x_[:, cc, 0:n], AF.Square, [x_], [s])
                    K.mm(pb[:, 0:n], ones_bf[:, :], s[:, 0:n], cc == 0, cc == KC - 1, [ones_bf, s], [pb])
                r = rstd[bi % 2]
                K.rsqrt(r[:, 0:n], pb[:, 0:n], 1.0 / D, EPS, [pb], [r])
                for cc in range(KC):
                    tm = tmp[cc % 3]
                    K.tt("dve" if cc % 2 == 0 else "pool", tm[:, 0:n], x_[:, cc, 0:n], r[:, 0:n], ALU.mult, [x_, r], [tm])
                    K.act(hT[:, cc, t0:t1], tm[:, 0:n], AF.Identity, [tm, Amod, modsT], [hT],
                          bias=modsT[:, cc, shift_kind, row:row + 1], scale=Amod[:, cc, which, row:row + 1])
            if to_dram:
                for cc in range(KC):
                    K.dma("pool", hT_d[cc, :, :], hT[:, cc, :], reads=[hT], writes=[hT_d])
        K.barrier()

    def load_hT(hT):
        for cc in range(KC):
            K.dma("sp", hT[:, cc, :], hT_d[cc, :, :], reads=[hT_d], writes=[hT])

    def apply_out(b, pieces, gate_kind, tok_lo):
        with ES() as st:
            xb = [K.sb([128, KC, 256], F32, "uxb", st) for _ in range(2)]
            for bi, t0 in enumerate(range(tok_lo, T, 256)):
                t1 = t0 + 256
                row = 4 if t0 < LC else b
                x_ = xb[bi % 2]
                K.dma("sp", x_[:, :, :], xview[:, :, t0:t1], reads=xdeps(t0, t1), writes=[x_])
                for oc in range(KC):
                    pb = PS[oc % 4]
                    for pi, (lf, rf, rd) in enumerate(pieces):
                        K.mm(pb[:, 0:256], lf(oc), rf(t0, t1), pi == 0, pi == len(pieces) - 1, rd, [pb])
                    K.stt("dve", x_[:, oc, :], pb[:, 0:256], modsT[:, oc, gate_kind, row:row + 1], x_[:, oc, :],
                          ALU.mult, ALU.add, [pb, modsT, x_], [x_])
                K.dma("pool", xview[:, :, t0:t1], x_[:, :, :], reads=[x_], writes=xdeps(t0, t1))

    cast_rr = [0]

    def load_w_bf16(dst_ap, src_ap, shape, src_buf, dst_buf, st_bufs, idx, eng=None):
        s = st_bufs[idx % len(st_bufs)]
        v = stg_view(s, shape)
        K.dma("sp", v, src_ap, reads=[src_buf], writes=[s])
        if eng is None:
            cast_rr[0] += 1
            eng = "dve" if cast_rr[0] % 2 == 0 else "act"
        K.copy(eng, dst_ap, v, [s], [dst_buf])

    def stage_ffn(b, l, do_ctx, hT):
        wupv = wup_d[l, :, :].rearrange("(kc p) n -> p kc n", p=128)
        segs = [(LC, LL)] + ([(0, LC)] if do_ctx else [])
        blocks = [bl for bl in BLOCKS if (do_ctx or bl[0] >= LC)]
        PADW = T + 4
        with ES() as st:
            wst = [K.sb([128, KC, 128], F32, "wst", st) for _ in range(4)]
            wbf = [K.sb([128, KC, 128], BF16, "wbf", st) for _ in range(4)]
            ug = [K.sb([128, PADW], F32, "ug", st) for _ in range(2)]
            uv = [K.sb([128, PADW], F32, "uv", st) for _ in range(2)]
            cg = [K.sb([128, T], F32, "cg", st) for _ in range(2)]
            cv = [K.sb([128, T], F32, "cv", st) for _ in range(2)]
            ao = [K.sb([128, T], BF16, "ao", st) for _ in range(2)]
            for u in ug + uv:
                K.memset("pool", u[:, :], 0.0, [u])

            def pad_off(t):
                return t + 1 if t < LC else t + 3

            def ffn_wload(fc):
                for half in range(2):
                    ws, wb_ = wst[2 * (fc % 2) + half], wbf[2 * (fc % 2) + half]
                    K.dma("sp", ws[:, :, :], wupv[:, :, half * DFF + fc * 128:half * DFF + (fc + 1) * 128], reads=[wup_d], writes=[ws])
                    K.copy("dve" if half == 0 else "act", wb_[:, :, :], ws[:, :, :], [ws], [wb_])

            def ffn_tail(fc):
                cg_, cv_, ao_ = cg[fc % 2], cv[fc % 2], ao[fc % 2]
                for (s0, ln) in segs:
                    K.act(cg_[:, s0:s0 + ln], cg_[:, s0:s0 + ln], AF.Silu, [cg_], [cg_])
                    K.tt("dve" if ln > 1024 else "pool", ao_[:, s0:s0 + ln], cg_[:, s0:s0 + ln], cv_[:, s0:s0 + ln], ALU.mult, [cg_, cv_], [ao_])
                lo = 0 if do_ctx else LC
                K.dma("pool", aT_d[fc * 128:(fc + 1) * 128, lo:T], ao_[:, lo:T], reads=[ao_], writes=[aT_d])

            ffn_wload(0)
            for fc in range(22):
                if fc + 1 < 22:
                    ffn_wload(fc + 1)
                g_, v_ = ug[fc % 2], uv[fc % 2]
                for bi, (t0, t1) in enumerate(blocks):
                    n = t1 - t0
                    for half, dst in ((0, g_), (1, v_)):
                        pb = PS[(bi * 2 + half) % 8]
                        wb_ = wbf[2 * (fc % 2) + half]
                        for kc in range(KC):
                            K.mm(pb[:, 0:n], wb_[:, kc, :], hT[:, kc, t0:t1],
                                 kc == 0, kc == KC - 1, [wb_, hT], [pb])
                        K.copy("act", dst[:, pad_off(t0):pad_off(t0) + n], pb[:, 0:n], [pb], [dst])
                cg_, cv_ = cg[fc % 2], cv[fc % 2]
                for half, src, dst, ch in ((0, g_, cg_, fc), (1, v_, cv_, 22 + fc)):
                    for (s0, ln) in segs:
                        p0 = pad_off(s0)
                        K.act(dst[:, s0:s0 + ln], src[:, p0:p0 + ln], AF.Identity, [src, cwT, cbT], [dst],
                              bias=cbT[:, l, ch:ch + 1], scale=cwT[:, l, 1, ch:ch + 1])
                        K.stt("dve", dst[:, s0:s0 + ln], src[:, p0 - 1:p0 - 1 + ln], cwT[:, l, 0, ch:ch + 1], dst[:, s0:s0 + ln],
                              ALU.mult, ALU.add, [src, cwT, dst], [dst])
                        K.stt("dve", dst[:, s0:s0 + ln], src[:, p0 + 1:p0 + 1 + ln], cwT[:, l, 2, ch:ch + 1], dst[:, s0:s0 + ln],
                              ALU.mult, ALU.add, [src, cwT, dst], [dst])
                if fc >= 1:
                    ffn_tail(fc - 1)
            ffn_tail(21)
        K.barrier()
        wdv = wdn_d[l, :, :].rearrange("(kc p) n -> p kc n", p=128)
        aTv = aT_d[:, :].rearrange("(kc p) t -> p kc t", p=128)
        with ES() as st:
            wd = K.sb([128, 22, D], BF16, "wd", st)
            wds = [K.sb([128, 2 * D], F32, "wds", st) for _ in range(2)]
            ab = [K.sb([128, 22, 256], BF16, "ab", st) for _ in range(2)]
            for j in range(11):
                load_w_bf16(wd[:, 2 * j:2 * j + 2, :], wdv[:, 2 * j:2 * j + 2, :], (128, 2, D), wdn_d, wd, wds, j)
            cnt = [0]

            def rhs_fn(t0, t1):
                return ab[cnt[0] % 2]

            lo = 0 if do_ctx else LC
            with ES() as st2:
                xb = [K.sb([128, KC, 256], F32, "uxb", st2) for _ in range(2)]
                for bi, t0 in enumerate(range(lo, T, 256)):
                    t1 = t0 + 256
                    row = 4 if t0 < LC else b
                    a_ = ab[bi % 2]
                    x_ = xb[bi % 2]
                    K.dma("sp", a_[:, :, :], aTv[:, :, t0:t1], reads=[aT_d], writes=[a_])
                    K.dma("sp", x_[:, :, :], xview[:, :, t0:t1], reads=xdeps(t0, t1), writes=[x_])
                    for oc in range(KC):
                        pb = PS[oc % 4]
                        for kc in range(22):
                            K.mm(pb[:, 0:256], wd[:, kc, oc * 128:(oc + 1) * 128], a_[:, kc, :], kc == 0, kc == 21, [wd, a_], [pb])
                        K.stt("dve", x_[:, oc, :], pb[:, 0:256], modsT[:, oc, 5, row:row + 1], x_[:, oc, :],
                              ALU.mult, ALU.add, [pb, modsT, x_], [x_])
                    K.dma("pool", xview[:, :, t0:t1], x_[:, :, :], reads=[x_], writes=xdeps(t0, t1))
        K.barrier()

    def stage_out(b):
        with ES() as st:
            xb = [K.sb([128, KC, 512], F32, "oxb", st) for _ in range(2)]
            sq = [K.sb([128, 512], BF16, "sq", st) for _ in range(3)]
            rstd = [K.sb([128, 512], F32, "rstd", st) for _ in range(2)]
            yT = [K.sb([128, KC, 512], F32, "yT", st) for _ in range(2)]
            ob = [K.sb([128, D], F32, "ob", st) for _ in range(3)]
            for bi, (t0, t1) in enumerate(BLOCKS[1:]):
                n = t1 - t0
                x_ = xb[bi % 2]
                K.dma("sp", x_[:, :, 0:n], xview[:, :, t0:t1], reads=xdeps(t0, t1), writes=[x_])
                pb = PS[bi % 2]
                for cc in range(KC):
                    s = sq[cc % 3]
                    K.act(s[:, 0:n], x_[:, cc, 0:n], AF.Square, [x_], [s])
                    K.mm(pb[:, 0:n], ones_bf[:, :], s[:, 0:n], cc == 0, cc == KC - 1, [ones_bf, s], [pb])
                r = rstd[bi % 2]
                K.rsqrt(r[:, 0:n], pb[:, 0:n], 1.0 / D, EPS, [pb], [r])
                y = yT[bi % 2]
                for cc in range(KC):
                    K.stt("dve", y[:, cc, 0:n], x_[:, cc, 0:n], gcols[:, 4, cc:cc + 1], r[:, 0:n],
                          ALU.mult, ALU.mult, [x_, gcols, r], [y])
                for ti in range(n // 128):
                    o = ob[ti % 3]
                    for half in range(2):
                        pb2 = PS[2 + (2 * ti + half) % 4]
                        for q in range(4):
                            cc = half * 4 + q
                            K.transpose(pb2[:, q * 128:(q + 1) * 128], y[:, cc, ti * 128:(ti + 1) * 128], ident[:, :], [y, ident], [pb2])
                        K.copy("act", o[:, half * 512:(half + 1) * 512], pb2[:, :], [pb2], [o])
                    tok = t0 - LC + ti * 128
                    K.dma("pool", out_d[b, tok:tok + 128, :], o[:, :], reads=[o], writes=[out_d])
        K.barrier()

    def stage_swa(b, hT):
        win = cdin_d[0, :, :].rearrange("(kc p) n -> p kc n", p=128)
        with ES() as st:
            attT = K.sb([64, 8, LL], BF16, "attT", st)
            wo = K.sb([64, 8, D], BF16, "wo_att", st)
            with ES() as st1:
                wq = K.sb([128, KC, 512], BF16, "wq", st1)
                wqs = K.sb([128, KC, 512], BF16, "wqs", st1)
                wk = K.sb([128, KC, 128], BF16, "wk", st1)
                wks = K.sb([128, KC, 128], BF16, "wks", st1)
                wv = K.sb([128, KC, 128], BF16, "wv", st1)
                cosT = K.sb([64, LL], F32, "cosT", st1)
                sinT = K.sb([64, LL], F32, "sinT", st1)
                qT = K.sb([64, 8, LL], BF16, "qT", st1)
                kT = K.sb([64, 2, T], BF16, "kT", st1)
                vtok = K.sb([128, 18, 2, 128], BF16, "vtok", st1)
                esink = K.sb([128, 8], F32, "esink", st1)
                maskP = K.sb([128, 128], F32, "maskP", st1)
                maskN = K.sb([128, 128], F32, "maskN", st1)
                t1b = [K.sb([64, 512], F32, "rt1", st1) for _ in range(2)]
                t2b = [K.sb([64, 512], F32, "rt2", st1) for _ in range(2)]
                pT = [K.sb([128, 512], BF16, "pT", st1) for _ in range(4)]
                dsum = [K.sb([128, 512], F32, "dsum", st1) for _ in range(2)]
                drec = [K.sb([64, 512], F32, "drec", st1) for _ in range(2)]
                K.memset("pool", vtok[:, :, :, :], 1.0, [vtok])
                stw = ES()
                wstg = [K.sb([128, 2048], F32, "wstg", stw)]
                K.dma("sp", cosT[:, :], swacos_d[:, :], reads=[swacos_d], writes=[cosT])
                K.dma("sp", sinT[:, :], swasin_d[:, :], reads=[swasin_d], writes=[sinT])
                K.dma("sp", maskP[:, :], triB_d[:, :], reads=[triB_d], writes=[maskP])
                K.dma("sp", maskN[:, :], triF_d[:, :], reads=[triF_d], writes=[maskN])
                K.dma("sp", esink[:, :], sink_d[0, :].partition_broadcast(128), reads=[sink_d], writes=[esink])
                K.act(esink[:, :], esink[:, :], AF.Exp, [esink], [esink])
                for j in range(2):
                    load_w_bf16(wq[:, :, j * 256:(j + 1) * 256], win[:, :, CD_Q + j * 256:CD_Q + (j + 1) * 256], (128, KC, 256), cdin_d, wq, wstg, 0)
                load_w_bf16(wk[:, :, :], win[:, :, CD_K:CD_K + 128], (128, KC, 128), cdin_d, wk, wstg, 0)
                load_w_bf16(wv[:, :, :], win[:, :, CD_V:CD_V + 128], (128, KC, 128), cdin_d, wv, wstg, 0)
                for (w_, ws_, nh) in ((wq, wqs, 64), (wk, wks, 16)):
                    wv4 = w_[:, :, :].rearrange("p k (h two d) -> p (k h) two d", two=2, d=32)
                    ws4 = ws_[:, :, :].rearrange("p k (h two d) -> p (k h) two d", two=2, d=32)
                    K.ts("pool", ws4[:, :, 0, :], wv4[:, :, 1, :], -1.0, None, ALU.mult, None, [w_], [ws_])
                    K.copy("pool", ws4[:, :, 1, :], wv4[:, :, 0, :], [w_], [ws_])
                wov = cdout_d[0, 1024:1536, :].rearrange("(h d) n -> d h n", d=64)
                for j in range(4):
                    load_w_bf16(wo[:, j * 2:(j + 1) * 2, :], wov[:, j * 2:(j + 1) * 2, :], (64, 2, D), cdout_d, wo, wstg, 0)
                K.barrier()
                stw.close()
                cnt = 0
                for h in range(8):
                    for j in range(4):
                        t0 = LC + j * 512
                        pa, pb = PS[(cnt * 2) % 4], PS[(cnt * 2 + 1) % 4]
                        for kc in range(KC):
                            K.mm(pa[0:64, :], wq[:, kc, h * 64:(h + 1) * 64], hT[:, kc, t0:t0 + 512], kc == 0, kc == KC - 1, [wq, hT], [pa])
                        for kc in range(KC):
                            K.mm(pb[0:64, :], wqs[:, kc, h * 64:(h + 1) * 64], hT[:, kc, t0:t0 + 512], kc == 0, kc == KC - 1, [wqs, hT], [pb])
                        a_, b_ = t1b[cnt % 2], t2b[cnt % 2]
                        K.tt("dve", a_[:, :], pa[0:64, :], cosT[:, j * 512:(j + 1) * 512], ALU.mult, [pa, cosT], [a_])
                        K.tt("dve", b_[:, :], pb[0:64, :], sinT[:, j * 512:(j + 1) * 512], ALU.mult, [pb, sinT], [b_])
                        K.tt("pool", qT[:, h, j * 512:(j + 1) * 512], a_[:, :], b_[:, :], ALU.add, [a_, b_], [qT])
                        cnt += 1
                for g in range(2):
                    pa = PS[cnt % 4]
                    for kc in range(KC):
                        K.mm(pa[0:64, 0:LC], wk[:, kc, g * 64:(g + 1) * 64], hT[:, kc, 0:LC], kc == 0, kc == KC - 1, [wk, hT], [pa])
                    K.copy("act", kT[:, g, 0:LC], pa[0:64, 0:LC], [pa], [kT])
                    cnt += 1
                    for j in range(4):
                        t0 = LC + j * 512
                        pa, pb = PS[(cnt * 2) % 4], PS[(cnt * 2 + 1) % 4]
                        for kc in range(KC):
                            K.mm(pa[0:64, :], wk[:, kc, g * 64:(g + 1) * 64], hT[:, kc, t0:t0 + 512], kc == 0, kc == KC - 1, [wk, hT], [pa])
                        for kc in range(KC):
                            K.mm(pb[0:64, :], wks[:, kc, g * 64:(g + 1) * 64], hT[:, kc, t0:t0 + 512], kc == 0, kc == KC - 1, [wks, hT], [pb])
                        a_, b_ = t1b[cnt % 2], t2b[cnt % 2]
                        K.tt("dve", a_[:, :], pa[0:64, :], cosT[:, j * 512:(j + 1) * 512], ALU.mult, [pa, cosT], [a_])
                        K.tt("dve", b_[:, :], pb[0:64, :], sinT[:, j * 512:(j + 1) * 512], ALU.mult, [pb, sinT], [b_])
                        K.tt("pool", kT[:, g, t0:t0 + 512], a_[:, :], b_[:, :], ALU.add, [a_, b_], [kT])
                        cnt += 1
                for ti in range(18):
                    pa = PS[ti % 4]
                    for kc in range(KC):
                        K.mm(pa[:, 0:128], hT[:, kc, ti * 128:(ti + 1) * 128], wv[:, kc, :], kc == 0, kc == KC - 1, [hT, wv], [pa])
                    K.copy("act", vtok[:, ti, :, 0:64], pa[:, 0:128].rearrange("p (g e) -> p g e", g=2), [pa], [vtok])
                units = []
                u = 0
                for i in range(16):
                    for g in range(2):
                        keys = [(0, None), (1, None)]
                        if i > 0:
                            keys.append((2 + i - 1, maskP))
                        keys.append((2 + i, None))
                        if i < 15:
                            keys.append((2 + i + 1, maskN))
                        for ki, (kt, mask) in enumerate(keys):
                            units.append((i, g, ki, kt, mask, len(keys), u))
                        u += 1

                def front(un, idx):
                    i, g, ki, kt, mask, nk, uu = un
                    psc = PS[idx % 4]
                    K.mm(psc[:, :].rearrange("p (h q) -> p h q", h=4), kT[:, g, kt * 128:(kt + 1) * 128],
                         qT[:, g * 4:(g + 1) * 4, i * 128:(i + 1) * 128], True, True, [kT, qT], [psc])

                def back(un, idx):
                    i, g, ki, kt, mask, nk, uu = un
                    psc = PS[idx % 4]
                    pacc = PS[4 + (uu % 2) * 2]
                    p_ = pT[idx % 4]
                    K.act(p_[:, :], psc[:, :], AF.Exp, [psc], [p_], scale=0.125)
                    if mask is not None:
                        K.tt("pool", p_[:, :].rearrange("p (h q) -> p h q", h=4), p_[:, :].rearrange("p (h q) -> p h q", h=4),
                             mask[:, :].unsqueeze(1).to_broadcast([128, 4, 128]), ALU.mult, [p_, mask], [p_])
                    K.mm(pacc[:, :], vtok[:, kt, g, :], p_[:, :], ki == 0, ki == nk - 1, [vtok, p_], [pacc])
                    if ki == nk - 1:
                        d_ = dsum[uu % 2]
                        r_ = drec[uu % 2]
                        K.tt("dve", d_[64:128, :].rearrange("p (h q) -> p h q", h=4), pacc[64:128, :].rearrange("p (h q) -> p h q", h=4),
                             esink[64:128, g * 4:(g + 1) * 4].unsqueeze(2).to_broadcast([64, 4, 128]), ALU.add, [pacc, esink], [d_])
                        K.op("dve", lambda e, d_=d_, r_=r_: e.reciprocal(out=r_[0:64, :], in_=d_[64:128, :]), [d_], [r_])
                        K.tt("dve", attT[:, g * 4:(g + 1) * 4, i * 128:(i + 1) * 128], pacc[0:64, :].rearrange("p (h q) -> p h q", h=4),
                             r_[:, :].rearrange("p (h q) -> p h q", h=4), ALU.mult, [pacc, r_], [attT])

                LA = 2
                for idx in range(min(LA, len(units))):
                    front(units[idx], idx)
                for idx, un in enumerate(units):
                    if idx + LA < len(units):
                        front(units[idx + LA], idx + LA)
                    back(un, idx)
            K.barrier()
            pieces = [((lambda oc, h=h: wo[:, h, oc * 128:(oc + 1) * 128]),
                       (lambda t0, t1, h=h: attT[:, h, t0 - LC:t1 - LC]), [wo, attT]) for h in range(8)]
            apply_out(b, pieces, 2, LC)
        K.barrier()

    def stage_ssd(b, hT_scope_fn):
        win = cdin_d[0, :, :].rearrange("(kc p) n -> p kc n", p=128)
        with ES() as st:
            uT = K.sb([128, 8, LL], BF16, "uT", st)
            with ES() as st1:
                xs_tok = K.sb([128, 18, 1024], BF16, "xs_tok", st1)
                B_tok = K.sb([128, 18, 256], BF16, "B_tok", st1)
                BCT = K.sb([128, 4, T], BF16, "BCT", st1)
                dtv = K.sb([128, 18, 32], F32, "dtv", st1)
                dtA = K.sb([128, 18, 32], F32, "dtA", st1)
                a_bc = K.sb([128, 32], F32, "a_bc", st1)
                dtb_bc = K.sb([128, 32], F32, "dtb_bc", st1)
                D_bc = K.sb([128, 16], F32, "D_bc", st1)
                sng = K.sb([128, 8], F32, "sng", st1)
                scw = K.sb([128, 5, 12], F32, "scw", st1)
                scb = K.sb([128, 12], F32, "scb", st1)
                triF = K.sb([128, 128], F32, "triF", st1)
                triB = K.sb([128, 128], F32, "triB", st1)
                strF = K.sb([128, 128], F32, "strF", st1)
                strB = K.sb([128, 128], F32, "strB", st1)
                for (dst, src) in ((triF, triF_d), (triB, triB_d), (strF, strF_d), (strB, strB_d)):
                    K.dma("sp", dst[:, :], src[:, :], reads=[src], writes=[dst])
                K.dma("sp", a_bc[:, :], salog_d[0, :, :].rearrange("a b -> (a b)").partition_broadcast(128), reads=[salog_d], writes=[a_bc])
                K.act(a_bc[:, :], a_bc[:, :], AF.Exp, [a_bc], [a_bc])
                K.ts("dve", a_bc[:, :], a_bc[:, :], -1.0, None, ALU.mult, None, [a_bc], [a_bc])
                K.dma("sp", dtb_bc[:, :], sdtb_d[0, :, :].rearrange("a b -> (a b)").partition_broadcast(128), reads=[sdtb_d], writes=[dtb_bc])
                K.dma("sp", D_bc[:, :], sd_d[0, :].partition_broadcast(128), reads=[sd_d], writes=[D_bc])
                K.dma("sp", sng[:, :], colvec(sng_d[0, :], 8), reads=[sng_d], writes=[sng], allow_slow_non_contiguous=True)
                for tap in range(5):
                    K.dma("sp", scw[:, tap, :], colvec(scw_d[0, tap, :], 12), reads=[scw_d], writes=[scw], allow_slow_non_contiguous=True)
                K.dma("sp", scb[:, :], colvec(scb_d[0, :], 12), reads=[scb_d], writes=[scb], allow_slow_non_contiguous=True)
                with ES() as st2:
                    hT = K.sb([128, KC, T], BF16, "hT", st2)
                    load_hT(hT)
                    wstg = [K.sb([128, 4096], F32, "wstg", st2)]
                    st3 = ES()
                    wch = [K.sb([128, KC, 128], BF16, "wch", st3) for _ in range(2)]
                    PW = T + 8
                    upad = [K.sb([128, PW], F32, "upad", st3) for _ in range(2)]
                    cvb = [K.sb([128, T], F32, "cvb", st3) for _ in range(1)]
                    xsT = [K.sb([128, T], BF16, "xsT", st3) for _ in range(2)]
                    for u_ in upad:
                        K.memset("pool", u_[:, :], 0.0, [u_])

                    def poff(t):
                        return t + 2 if t < LC else t + 6

                    def ssd_wload(fc):
                        w_ = wch[fc % 2]
                        load_w_bf16(w_[:, :, :], win[:, :, CD_XBC + fc * 128:CD_XBC + (fc + 1) * 128], (128, KC, 128), cdin_d, w_, wstg, fc)

                    ssd_wload(0)
                    for fc in range(12):
                        w_ = wch[fc % 2]
                        if fc + 1 < 12:
                            ssd_wload(fc + 1)
                        up = upad[fc % 2]
                        for bi, (t0, t1) in enumerate(BLOCKS):
                            n = t1 - t0
                            pb = PS[bi % 4]
                            for kc in range(KC):
                                K.mm(pb[:, 0:n], w_[:, kc, :], hT[:, kc, t0:t1], kc == 0, kc == KC - 1, [w_, hT], [pb])
                            K.copy("act", up[:, poff(t0):poff(t0) + n], pb[:, 0:n], [pb], [up])
                        cv_ = cvb[0]
                        for (s0, ln) in ((0, LC), (LC, LL)):
                            p0 = poff(s0)
                            K.act(cv_[:, s0:s0 + ln], up[:, p0:p0 + ln], AF.Identity, [up, scw, scb], [cv_],
                                  bias=scb[:, fc:fc + 1], scale=scw[:, 2, fc:fc + 1])
                            for tap in (0, 1, 3, 4):
                                K.stt("dve", cv_[:, s0:s0 + ln], up[:, p0 + tap - 2:p0 + tap - 2 + ln], scw[:, tap, fc:fc + 1], cv_[:, s0:s0 + ln],
                                      ALU.mult, ALU.add, [up, scw, cv_], [cv_])
                        if fc < 8:
                            dstT, dst_ap = xsT[fc % 2], xsT[fc % 2][:, :]
                        else:
                            dstT, dst_ap = BCT, BCT[:, fc - 8, :]
                        K.act(dst_ap, cv_[:, :], AF.Silu, [cv_], [dstT])
                        if fc < 10:
                            for grp in range(3):
                                tis = list(range(grp * 8, min(18, grp * 8 + 8)))
                                pb = PS[4 + grp % 2]
                                pbv = pb[:, :].bitcast(BF16)
                                for qi, ti in enumerate(tis):
                                    K.transpose(pbv[:, qi * 128:(qi + 1) * 128], dst_ap[:, ti * 128:(ti + 1) * 128], identb[:, :], [dstT, identb], [pb])
                                nt = len(tis)
                                src_v = pbv[:, 0:nt * 128].rearrange("p (a f) -> p a f", a=nt)
                                if fc < 8:
                                    K.copy("act", xs_tok[:, tis[0]:tis[0] + nt, fc * 128:(fc + 1) * 128], src_v, [pb], [xs_tok])
                                else:
                                    K.copy("act", B_tok[:, tis[0]:tis[0] + nt, (fc - 8) * 128:(fc - 7) * 128], src_v, [pb], [B_tok])
                    K.barrier()
                    st3.close()
                    wz = K.sb([128, KC, 1024], BF16, "wz", st2)
                    wdt = K.sb([128, KC, 32], BF16, "wdt", st2)
                    szb = [K.sb([128, 1024], BF16, "szb", st2) for _ in range(2)]
                    dtt = [K.sb([128, 32], F32, "dtt", st2) for _ in range(2)]
                    for j in range(2):
                        load_w_bf16(wz[:, :, j * 512:(j + 1) * 512], win[:, :, CD_Z + j * 512:CD_Z + (j + 1) * 512], (128, KC, 512), cdin_d, wz, wstg, j)
                    load_w_bf16(wdt[:, :, :], win[:, :, CD_DT:CD_DT + 32], (128, KC, 32), cdin_d, wdt, wstg, 0)
                    for ti in range(18):
                        pb = PS[ti % 4]
                        for kc in range(KC):
                            K.mm(pb[:, 0:32], hT[:, kc, ti * 128:(ti + 1) * 128], wdt[:, kc, :], kc == 0, kc == KC - 1, [hT, wdt], [pb])
                        d_ = dtt[ti % 2]
                        K.tt("dve", d_[:, :], pb[:, 0:32], dtb_bc[:, :], ALU.add, [pb, dtb_bc], [d_])
                        K.act(d_[:, :], d_[:, :], AF.Exp, [d_], [d_])
                        K.act(dtv[:, ti, :], d_[:, :], AF.Ln, [d_], [dtv], bias=K.eps_ap(1.0))
                    K.tt("dve", dtA[:, :, :], dtv[:, :, :], a_bc[:, :].unsqueeze(1).to_broadcast([128, 18, 32]), ALU.mult, [dtv, a_bc], [dtA])
                    for li in range(16):
                        ti = li + 2
                        s_ = szb[li % 2]
                        for j in range(2):
                            pb = PS[(li * 2 + j) % 4]
                            for kc in range(KC):
                                K.mm(pb[:, :], hT[:, kc, ti * 128:(ti + 1) * 128], wz[:, kc, j * 512:(j + 1) * 512], kc == 0, kc == KC - 1, [hT, wz], [pb])
                            K.act(s_[:, j * 512:(j + 1) * 512], pb[:, :], AF.Silu, [pb], [s_])
                        K.dma("pool", sz_d[li, :, :], s_[:, :], reads=[s_], writes=[sz_d])
                K.barrier()
                with ES() as st2:
                    Hf = K.sb([128, 2, 512], F32, "Hf", st2)
                    Hb = K.sb([128, 2, 512], F32, "Hb", st2)
                    hbf = [K.sb([128, 1024], BF16, "hbf", st2) for _ in range(2)]
                    hinf = [K.sb([128, 1024], BF16, "hinf", st2) for _ in range(2)]
                    szt = [K.sb([128, 1024], BF16, "szt", st2) for _ in range(2)]
                    prep = {}
                    for nm in ("acs0", "eac0", "dend0", "cdec0", "wgt0", "acs1", "eac1", "dend1", "cdec1", "wgt1"):
                        prep[nm] = K.sb([128, 16], F32, nm, st2)
                    xte = K.sb([128, 1024], BF16, "xte", st2)
                    rseg = K.sb([128, 16, 128], F32, "rseg", st2)
                    cbm = [K.sb([128, 2, 128], F32, "cbm", st2) for _ in range(2)]
                    eseg = [K.sb([128, 512], F32, "eseg", st2) for _ in range(2)]
                    Lt = [K.sb([128, 16, 128], BF16, "Lt", st2) for _ in range(2)]
                    xdt = [K.sb([128, 1024], BF16, "xdt", st2) for _ in range(2)]
                    yacc = K.sb([128, 1024], F32, "yacc", st2)
                    ytmp = K.sb([128, 512], F32, "ytmp", st2)
                    ub = K.sb([128, 1024], F32, "ub", st2)
                    ubf = K.sb([128, 1024], BF16, "ubf", st2)
                    ssq = K.sb([128, 2], F32, "ssq", st2)
                    junk = K.sb([128, 512], BF16, "junk", st2)
                    K.memset("dve", Hf[:, :, :], 0.0, [Hf])
                    K.memset("dve", Hb[:, :, :], 0.0, [Hb])
                    tri = (triF, triB)
                    strm = (strF, strB)

                    def do_prep(c, d):
                        pp = PS[0]
                        K.mm(pp[:, 0:16], tri[d][:, :], dtA[:, c, d * 16:(d + 1) * 16], True, True, [tri[d], dtA], [pp])
                        K.mm(pp[:, 16:32], ones_f[:, :], dtA[:, c, d * 16:(d + 1) * 16], True, True, [ones_f, dtA], [pp])
                        acs, eac, dend, cdec, wgt = (prep[n_ + str(d)] for n_ in ("acs", "eac", "dend", "cdec", "wgt"))
                        K.copy("act", acs[:, :], pp[:, 0:16], [pp], [acs])
                        K.act(eac[:, :], pp[:, 0:16], AF.Exp, [pp], [eac])
                        K.act(cdec[:, :], pp[:, 16:32], AF.Exp, [pp], [cdec])
                        K.tt("dve", dend[:, :], pp[:, 16:32], acs[:, :], ALU.subtract, [pp, acs], [dend])
                        K.act(dend[:, :], dend[:, :], AF.Exp, [dend], [dend])
                        K.tt("dve", wgt[:, :], dend[:, :], dtv[:, c, d * 16:(d + 1) * 16], ALU.mult, [dend, dtv], [wgt])

                    def state_update(c, d, H):
                        wgt, cdec = prep["wgt" + str(d)], prep["cdec" + str(d)]
                        K.tt("pool", xte[:, :].rearrange("p (h e) -> p h e", h=16), xs_tok[:, c, :].rearrange("p (h e) -> p h e", h=16),
                             wgt[:, :].unsqueeze(2).to_broadcast([128, 16, 64]), ALU.mult, [xs_tok, wgt], [xte])
                        for g in range(2):
                            pb = PS[6 + g]
                            K.mm(pb[:, :], B_tok[:, c, g * 128:(g + 1) * 128], xte[:, g * 512:(g + 1) * 512], True, True, [B_tok, xte], [pb])
                            hv = H[:, g, :].rearrange("p (h e) -> p h e", h=8)
                            K.tt("pool", hv, hv, cdec[:, g * 8:(g + 1) * 8].unsqueeze(2).to_broadcast([128, 8, 64]), ALU.mult, [H, cdec], [H])
                            K.tt("dve", H[:, g, :], H[:, g, :], pb[:, :], ALU.add, [H, pb], [H])

                    for c in range(18):
                        if c >= 2:
                            hb_ = hbf[c % 2]
                            K.copy("act", hb_[:, :], Hf[:, :, :].rearrange("p g e -> p (g e)"), [Hf], [hb_])
                            K.dma("pool", hin_d[c - 2, :, :], hb_[:, :], reads=[hb_], writes=[hin_d])
                        if c == 17:
                            break
                        do_prep(c, 0)
                        state_update(c, 0, Hf)
                    K.barrier()
                    for c in [1, 0] + list(range(17, 1, -1)):
                        do_prep(c, 1)
                        if c >= 2:
                            li = c - 2
                            do_prep(c, 0)
                            hi_ = hinf[li % 2]
                            sz_ = szt[li % 2]
                            K.dma("sp", hi_[:, :], hin_d[li, :, :], reads=[hin_d], writes=[hi_])
                            K.dma("sp", sz_[:, :], sz_d[li, :, :], reads=[sz_d], writes=[sz_])
                            hb_ = hbf[li % 2]
                            K.copy("act", hb_[:, :], Hb[:, :, :].rearrange("p g e -> p (g e)"), [Hb], [hb_])
                            tsl = slice(c * 128, (c + 1) * 128)
                            pcb = PS[1]
                            for g in range(2):
                                K.mm(pcb[:, g * 128:(g + 1) * 128], BCT[:, g, tsl], BCT[:, 2 + g, tsl], True, True, [BCT], [pcb])
                            for d in range(2):
                                K.tt("dve", cbm[d][:, :, :], pcb[:, 0:256].rearrange("p (g q) -> p g q", g=2),
                                     tri[d][:, :].unsqueeze(1).to_broadcast([128, 2, 128]), ALU.mult, [pcb, tri[d]], [cbm[d]])
                            for d in range(2):
                                K.tt("dve", rseg[:, :, :], tri[d][:, :].unsqueeze(1).to_broadcast([128, 16, 128]),
                                     dtA[:, c, d * 16:(d + 1) * 16].unsqueeze(2).to_broadcast([128, 16, 128]), ALU.mult, [tri[d], dtA], [rseg])
                                for hb4 in range(4):
                                    pseg = PS[2 + hb4 % 2]
                                    K.mm(pseg[:, :], strm[d][:, :], rseg[:, hb4 * 4:(hb4 + 1) * 4, :], True, True, [strm[d], rseg], [pseg])
                                    es_ = eseg[hb4 % 2]
                                    K.act(es_[:, :], pseg[:, :], AF.Exp, [pseg], [es_])
                                    g = hb4 // 2
                                    K.tt("dve" if hb4 % 2 == 0 else "pool", Lt[d][:, hb4 * 4:(hb4 + 1) * 4, :], es_[:, :].rearrange("p (h q) -> p h q", h=4),
                                         cbm[d][:, g, :].unsqueeze(1).to_broadcast([128, 4, 128]), ALU.mult, [es_, cbm[d]], [Lt[d]])
                                K.tt("dve", xdt[d][:, :].rearrange("p (h e) -> p h e", h=16), xs_tok[:, c, :].rearrange("p (h e) -> p h e", h=16),
                                     dtv[:, c, d * 16:(d + 1) * 16].unsqueeze(2).to_broadcast([128, 16, 64]), ALU.mult, [xs_tok, dtv], [xdt[d]])
                            for h in range(16):
                                py = PS[4 + h // 8]
                                col = (h % 8) * 64
                                for d in range(2):
                                    K.mm(py[:, col:col + 64], Lt[d][:, h, :], xdt[d][:, h * 64:(h + 1) * 64], d == 0, d == 1, [Lt[d], xdt[d]], [py])
                            K.tt("pool", yacc[:, :].rearrange("p (h e) -> p h e", h=16), xs_tok[:, c, :].rearrange("p (h e) -> p h e", h=16),
                                 D_bc[:, :].unsqueeze(2).to_broadcast([128, 16, 64]), ALU.mult, [xs_tok, D_bc], [yacc])
                            for g in range(2):
                                K.tt("dve", yacc[:, g * 512:(g + 1) * 512], yacc[:, g * 512:(g + 1) * 512], PS[4 + g][:, :], ALU.add, [yacc, PS[4 + g]], [yacc])
                            for d in range(2):
                                hsrc = hi_ if d == 0 else hb_
                                eac = prep["eac" + str(d)]
                                for g in range(2):
                                    po = PS[6 + g]
                                    K.mm(po[:, :], BCT[:, 2 + g, tsl], hsrc[:, g * 512:(g + 1) * 512], True, True, [BCT, hsrc], [po])
                                    K.tt("dve", ytmp[:, :].rearrange("p (h e) -> p h e", h=8), po[:, :].rearrange("p (h e) -> p h e", h=8),
                                         eac[:, g * 8:(g + 1) * 8].unsqueeze(2).to_broadcast([128, 8, 64]), ALU.mult, [po, eac], [ytmp])
                                    K.tt("pool", yacc[:, g * 512:(g + 1) * 512], yacc[:, g * 512:(g + 1) * 512], ytmp[:, :], ALU.add, [yacc, ytmp], [yacc])
                            K.tt("dve", ub[:, :], yacc[:, :], sz_[:, :], ALU.mult, [yacc, sz_], [ub])
                            K.memset("pool", ssq[:, :], 0.0, [ssq])
                            for g in range(2):
                                K.op("act", lambda e, g=g: e.activation(out=junk[:, :], in_=ub[:, g * 512:(g + 1) * 512], func=AF.Square,
                                                                         accum_out=ssq[:, g:g + 1]), [ub], [junk, ssq])
                            K.rsqrt(ssq[:, :], ssq[:, :], 1.0 / 512, EPS, [ssq], [ssq])
                            for g in range(2):
                                K.ts("dve", ubf[:, g * 512:(g + 1) * 512], ub[:, g * 512:(g + 1) * 512], ssq[:, g:g + 1], None, ALU.mult, None, [ub, ssq], [ubf])
                            pt = PS[1]
                            ptv = pt[:, :].bitcast(BF16)
                            for cc in range(8):
                                K.transpose(ptv[:, cc * 128:(cc + 1) * 128], ubf[:, cc * 128:(cc + 1) * 128], identb[:, :], [ubf, identb], [pt])
                            for cc in range(8):
                                K.act(uT[:, cc, li * 128:(li + 1) * 128], ptv[:, cc * 128:(cc + 1) * 128], AF.Identity, [pt, sng], [uT], scale=sng[:, cc:cc + 1])
                        if c != 2:
                            state_update(c, 1, Hb)
            K.barrier()
            wo = K.sb([128, 8, D], BF16, "wo_ssd", st)
            wostg = [K.sb([128, 4096], F32, "wostg", st)]
            for j in range(2):
                load_w_bf16(wo[:, j * 4:(j + 1) * 4, :], cdout_d[0, 0:1024, :].rearrange("(kc p) n -> p kc n", p=128)[:, j * 4:(j + 1) * 4, :],
                            (128, 4, D), cdout_d, wo, wostg, 0)
            pieces = [((lambda oc, cc=cc: wo[:, cc, oc * 128:(oc + 1) * 128]),
                       (lambda t0, t1, cc=cc: uT[:, cc, t0 - LC:t1 - LC]), [wo, uT]) for cc in range(8)]
            apply_out(b, pieces, 2, LC)
        K.barrier()


    AB_CQ, AB_CKV, AB_KR, AB_RW = 0, 384, 640, 672

    def stage_mla(b, hT):
        win = abin_d[0, :, :].rearrange("(kc p) n -> p kc n", p=128)
        sc = 96.0 ** -0.5
        with ES() as st:
            attT = K.sb([64, 8, T], BF16, "mattT", st)
            wo = K.sb([64, 8, D], BF16, "wo_mla", st)
            with ES() as st1:
                wcq = K.sb([128, KC, 384], BF16, "wcq", st1)
                wckv = K.sb([128, KC, 256], BF16, "wckv", st1)
                wkr = K.sb([128, KC, 96], BF16, "wkr", st1)
                wkrs = K.sb([128, KC, 96], BF16, "wkrs", st1)
                wqu = K.sb([128, 3, 768], BF16, "wqu", st1)
                wqus = K.sb([128, 24, 96], BF16, "wqus", st1)
                wkvu = K.sb([128, 2, 1024], BF16, "wkvu", st1)
                cqn = K.sb([128, 3, T], BF16, "cqn", st1)
                ckvn = K.sb([128, 2, T], BF16, "ckvn", st1)
                vo = K.sb([128, 18, 128], BF16, "mvo", st1)
                cosT = K.sb([96, LL], F32, "mcos", st1)
                sinT = K.sb([96, LL], F32, "msin", st1)
                gq = K.sb([128, 3], F32, "gq", st1)
                gkv = K.sb([128, 2], F32, "gkv", st1)
                qf = K.sb([96, T], BF16, "qf", st1)
                kf = K.sb([96, T], BF16, "kf", st1)
                sq = [K.sb([128, 512], BF16, "msq", st1) for _ in range(2)]
                rstd = [K.sb([128, 512], F32, "mrstd", st1) for _ in range(2)]
                t1b = [K.sb([96, 512], F32, "mt1", st1) for _ in range(1)]
                t2b = [K.sb([96, 512], F32, "mt2", st1) for _ in range(1)]
                pT = [K.sb([128, 512], BF16, "mpT", st1) for _ in range(4)]
                rec = [K.sb([64, 512], F32, "mrec", st1) for _ in range(2)]
                stw = ES()
                wstg = [K.sb([128, 4096], F32, "wstg", stw)]
                K.dma("sp", cosT[64:96, :], mlacos_d[:, :], reads=[mlacos_d], writes=[cosT])
                K.dma("sp", sinT[64:96, :], mlasin_d[:, :], reads=[mlasin_d], writes=[sinT])
                K.dma("sp", gq[:, :], colvec(mqg_d[0, :], 3), reads=[mqg_d], writes=[gq], allow_slow_non_contiguous=True)
                K.dma("sp", gkv[:, :], colvec(mkg_d[0, :], 2), reads=[mkg_d], writes=[gkv], allow_slow_non_contiguous=True)
                K.memset("pool", wkr[:, :, :], 0.0, [wkr])
                K.memset("pool", wkrs[:, :, :], 0.0, [wkrs])
                K.memset("pool", wqus[:, :, :], 0.0, [wqus])
                K.memset("pool", vo[:, :, :], 1.0, [vo])
                load_w_bf16(wcq[:, :, :], win[:, :, AB_CQ:AB_CQ + 384], (128, KC, 384), abin_d, wcq, wstg, 0)
                load_w_bf16(wckv[:, :, :], win[:, :, AB_CKV:AB_CKV + 256], (128, KC, 256), abin_d, wckv, wstg, 0)
                load_w_bf16(wkr[:, :, 64:96], win[:, :, AB_KR:AB_KR + 32], (128, KC, 32), abin_d, wkr, wstg, 0)
                load_w_bf16(wqu[:, :, :], mqu_d[0, :, :].rearrange("(c p) n -> p c n", p=128), (128, 3, 768), mqu_d, wqu, wstg, 0)
                load_w_bf16(wkvu[:, :, :], mkvu_d[0, :, :].rearrange("(c p) n -> p c n", p=128), (128, 2, 1024), mkvu_d, wkvu, wstg, 0)
                wov = about_d[0, 0:512, :].rearrange("(h d) n -> d h n", d=64)
                for j in range(2):
                    load_w_bf16(wo[:, j * 4:(j + 1) * 4, :], wov[:, j * 4:(j + 1) * 4, :], (64, 4, D), about_d, wo, wstg, 0)
                K.ts("pool", wkrs[:, :, 64:80], wkr[:, :, 80:96], -1.0, None, ALU.mult, None, [wkr], [wkrs])
                K.copy("pool", wkrs[:, :, 80:96], wkr[:, :, 64:80], [wkr], [wkrs])
                wq24 = wqu[:, :, :].rearrange("p c (h e) -> p (c h) e", e=96)
                K.ts("pool", wqus[:, :, 64:80], wq24[:, :, 80:96], -1.0, None, ALU.mult, None, [wqu], [wqus])
                K.copy("pool", wqus[:, :, 80:96], wq24[:, :, 64:80], [wqu], [wqus])
                K.barrier()
                stw.close()
                for bi, (t0, t1) in enumerate(BLOCKS):
                    n = t1 - t0
                    for (w_, nch, g_, dst, pbase) in ((wcq, 3, gq, cqn, 0), (wckv, 2, gkv, ckvn, 4)):
                        for c3 in range(nch):
                            pb = PS[pbase + c3]
                            for kc in range(KC):
                                K.mm(pb[:, 0:n], w_[:, kc, c3 * 128:(c3 + 1) * 128], hT[:, kc, t0:t1], kc == 0, kc == KC - 1, [w_, hT], [pb])
                        pst = PS[pbase + 3] if pbase == 0 else PS[pbase + 2]
                        for c3 in range(nch):
                            s_ = sq[c3 % 2]
                            K.act(s_[:, 0:n], PS[pbase + c3][:, 0:n], AF.Square, [PS[pbase + c3]], [s_])
                            K.mm(pst[:, 0:n], ones_bf[:, :], s_[:, 0:n], c3 == 0, c3 == nch - 1, [ones_bf, s_], [pst])
                        r_ = rstd[0 if pbase == 0 else 1]
                        K.rsqrt(r_[:, 0:n], pst[:, 0:n], 1.0 / (nch * 128), EPS, [pst], [r_])
                        for c3 in range(nch):
                            K.stt("dve", dst[:, c3, t0:t1], PS[pbase + c3][:, 0:n], g_[:, c3:c3 + 1], r_[:, 0:n], ALU.mult, ALU.mult,
                                  [PS[pbase + c3], g_, r_], [dst])
                R_ = slice(64, 96)
                for bi, (t0, t1) in enumerate(BLOCKS):
                    n = t1 - t0
                    pa, pb = PS[0 + 2 * (bi % 2)], PS[1 + 2 * (bi % 2)]
                    for kc in range(KC):
                        K.mm(pa[0:96, 0:n], wkr[:, kc, :], hT[:, kc, t0:t1], kc == 0, kc == KC - 1, [wkr, hT], [pa])
                    if bi == 0:
                        K.copy("dve", kf[R_, t0:t1], pa[R_, 0:n], [pa], [kf])
                    else:
                        for kc in range(KC):
                            K.mm(pb[0:96, 0:n], wkrs[:, kc, :], hT[:, kc, t0:t1], kc == 0, kc == KC - 1, [wkrs, hT], [pb])
                        a_, b_ = t1b[0], t2b[0]
                        K.tt("dve", a_[R_, 0:n], pa[R_, 0:n], cosT[R_, t0 - LC:t1 - LC], ALU.mult, [pa, cosT], [a_])
                        K.tt("dve", b_[R_, 0:n], pb[R_, 0:n], sinT[R_, t0 - LC:t1 - LC], ALU.mult, [pb, sinT], [b_])
                        K.tt("pool", kf[R_, t0:t1], a_[R_, 0:n], b_[R_, 0:n], ALU.add, [a_, b_], [kf])
                u = 0
                for h in range(8):
                    for ti in range(18):
                        pa = PS[4 + ti % 2]
                        for c3 in range(2):
                            K.mm(pa[:, 0:64], ckvn[:, c3, ti * 128:(ti + 1) * 128], wkvu[:, c3, h * 128 + 64:h * 128 + 128],
                                 c3 == 0, c3 == 1, [ckvn, wkvu], [pa])
                        K.copy("dve", vo[:, ti, 0:64], pa[:, 0:64], [pa], [vo])
                    for bi, (t0, t1) in enumerate(BLOCKS):
                        n = t1 - t0
                        pa, pc, pd = PS[0], PS[2], PS[3]
                        for c3 in range(3):
                            K.mm(pa[0:96, 0:n], wqu[:, c3, h * 96:h * 96 + 96], cqn[:, c3, t0:t1], c3 == 0, c3 == 2, [wqu, cqn], [pa])
                        K.copy("act", qf[0:64, t0:t1], pa[0:64, 0:n], [pa], [qf])
                        if bi == 0:
                            K.copy("dve", qf[R_, t0:t1], pa[R_, 0:n], [pa], [qf])
                        else:
                            for c3 in range(3):
                                K.mm(pc[0:96, 0:n], wqus[:, c3 * 8 + h, :], cqn[:, c3, t0:t1], c3 == 0, c3 == 2, [wqus, cqn], [pc])
                            a_, b_ = t1b[0], t2b[0]
                            K.tt("dve", a_[R_, 0:n], pa[R_, 0:n], cosT[R_, t0 - LC:t1 - LC], ALU.mult, [pa, cosT], [a_])
                            K.tt("dve", b_[R_, 0:n], pc[R_, 0:n], sinT[R_, t0 - LC:t1 - LC], ALU.mult, [pc, sinT], [b_])
                            K.tt("pool", qf[R_, t0:t1], a_[R_, 0:n], b_[R_, 0:n], ALU.add, [a_, b_], [qf])
                        for c3 in range(2):
                            K.mm(pd[0:64, 0:n], wkvu[:, c3, h * 128:h * 128 + 64], ckvn[:, c3, t0:t1], c3 == 0, c3 == 1, [wkvu, ckvn], [pd])
                        K.copy("act", kf[0:64, t0:t1], pd[0:64, 0:n], [pd], [kf])
                    units = []
                    for bi, (t0, t1) in enumerate(BLOCKS):
                        keys = [0, 1] if bi == 0 else list(range(18))
                        for ki, kt in enumerate(keys):
                            units.append((bi, t0, t1, ki, kt, len(keys), u))
                        u += 1

                    def front(un, idx):
                        bi, t0, t1, ki, kt, nk, uu = un
                        n = t1 - t0
                        psc = PS[idx % 4]
                        K.mm(psc[:, 0:n], kf[:, kt * 128:(kt + 1) * 128], qf[:, t0:t1], True, True, [kf, qf], [psc])

                    def back(un, idx, h=h):
                        bi, t0, t1, ki, kt, nk, uu = un
                        n = t1 - t0
                        psc = PS[idx % 4]
                        pacc = PS[4 + (uu % 2) * 2]
                        p_ = pT[idx % 4]
                        K.act(p_[:, 0:n], psc[:, 0:n], AF.Exp, [psc], [p_], scale=sc)
                        K.mm(pacc[:, 0:n], vo[:, kt, :], p_[:, 0:n], ki == 0, ki == nk - 1, [vo, p_], [pacc])
                        if ki == nk - 1:
                            r_ = rec[uu % 2]
                            K.op("dve", lambda e, r_=r_, pacc=pacc, n=n: e.reciprocal(out=r_[0:64, 0:n], in_=pacc[64:128, 0:n]), [pacc], [r_])
                            K.tt("dve", attT[:, h, t0:t1], pacc[0:64, 0:n], r_[:, 0:n], ALU.mult, [pacc, r_], [attT])

                    LA = 2
                    for idx in range(min(LA, len(units))):
                        front(units[idx], idx)
                    for idx, un in enumerate(units):
                        if idx + LA < len(units):
                            front(units[idx + LA], idx + LA)
                        back(un, idx)
            K.barrier()
            pieces = [((lambda oc, h=h: wo[:, h, oc * 128:(oc + 1) * 128]),
                       (lambda t0, t1, h=h: attT[:, h, t0:t1]), [wo, attT]) for h in range(8)]
            apply_out(b, pieces, 2, 0)
        K.barrier()

    CW = -math.exp(-0.5)

    def stage_rwkv(b):
        win = abin_d[0, :, :].rearrange("(kc p) n -> p kc n", p=128)
        with ES() as st:
            cols = K.sb([128, 10, 4], F32, "rcols", st)
            for i_, src in enumerate((rkk_d[0, :], rka_d[0, :], None, rrk_d[0, :, :].rearrange("a b -> (a b)"), rlg_d[0, :], rlb_d[0, :],
                                      None, ra0_d[0, 0, :], ra0_d[0, 1, :])):
                if src is not None:
                    K.dma("sp", cols[:, i_, :], colvec(src, 4), reads=[rkk_d], writes=[cols], allow_slow_non_contiguous=True)
            K.ts("dve", cols[:, 2, :], cols[:, 1, :], -1.0, 1.0, ALU.mult, ALU.add, [cols], [cols])
            with ES() as st1:
                rT = K.sb([128, 4, T], BF16, "rT", st1)
                kT = K.sb([128, 4, T], BF16, "kT", st1)
                kknT = K.sb([128, 4, T], BF16, "kknT", st1)
                vtok = K.sb([128, 18, 512], BF16, "rvtok", st1)
                xwaT = K.sb([128, T], BF16, "xwaT", st1)
                mu = K.sb([128, 3, 14], F32, "mu", st1)
                blk64 = K.sb([128, 128], BF16, "blk64", st1)
                blk64f = K.sb([128, 128], F32, "blk64f", st1)
                K.dma("sp", blk64f[:, :], blk64_d[:, :], reads=[blk64_d], writes=[blk64f])
                K.copy("dve", blk64[:, :], blk64f[:, :], [blk64f], [blk64])
                K.dma("sp", mu[:, 0, :], colvec(rmp_d[0, :], 14), reads=[rmp_d], writes=[mu], allow_slow_non_contiguous=True)
                K.dma("sp", mu[:, 1, :], colvec(rmn_d[0, :], 14), reads=[rmn_d], writes=[mu], allow_slow_non_contiguous=True)
                K.tt("dve", mu[:, 2, :], mu[:, 0, :], mu[:, 1, :], ALU.add, [mu], [mu])
                K.ts("dve", mu[:, 2, :], mu[:, 2, :], -1.0, 1.0, ALU.mult, ALU.add, [mu], [mu])
                with ES() as st2:
                    hT = K.sb([128, KC, T], BF16, "hT", st2)
                    load_hT(hT)
                    wstg = [K.sb([128, 4096], F32, "wstg", st2)]
                    wch = [K.sb([128, KC, 128], BF16, "rwch", st2) for _ in range(2)]
                    g2b = K.sb([128, 512], BF16, "g2b", st2)
                    upad = [K.sb([128, T + 4], F32, "rupad", st2) for _ in range(2)]
                    xs = [K.sb([128, T], F32, "rxs", st2) for _ in range(1)]
                    xsb = [K.sb([128, T], BF16, "rxsb", st2) for _ in range(1)]
                    t32 = [K.sb([128, 512], F32, "rt32", st2) for _ in range(2)]
                    t16 = [K.sb([128, 512], BF16, "rt16", st2) for _ in range(2)]
                    gbo = [K.sb([128, T], BF16, "gbo", st2) for _ in range(1)]
                    rk32 = [K.sb([128, 512], F32, "rk32", st2) for _ in range(2)]
                    for u_ in upad:
                        K.memset("pool", u_[:, :], 0.0, [u_])
                    load_w_bf16(g2b[:, :], rg2_d[0, :, :], (128, 512), rg2_d, g2b, wstg, 0)

                    def poff(t):
                        return t + 1 if t < LC else t + 3

                    order = [4, 5, 6, 7, 0, 1, 2, 3, 8, 9, 10, 11, 12, 13]
                    def rw_wload(oi):
                        fc_ = order[oi]
                        w_ = wch[oi % 2]
                        load_w_bf16(w_[:, :, :], win[:, :, AB_RW + fc_ * 128:AB_RW + (fc_ + 1) * 128], (128, KC, 128), abin_d, w_, wstg, 0)

                    rw_wload(0)
                    for oi, fc in enumerate(order):
                        w_ = wch[oi % 2]
                        if oi + 1 < len(order):
                            rw_wload(oi + 1)
                        up = upad[oi % 2]
                        for bi, (t0, t1) in enumerate(BLOCKS):
                            n = t1 - t0
                            pb = PS[bi % 4]
                            for kc in range(KC):
                                K.mm(pb[:, 0:n], w_[:, kc, :], hT[:, kc, t0:t1], kc == 0, kc == KC - 1, [w_, hT], [pb])
                            K.copy("act", up[:, poff(t0):poff(t0) + n], pb[:, 0:n], [pb], [up])
                        x_ = xs[0]
                        for (s0, ln) in ((0, LC), (LC, LL)):
                            p0 = poff(s0)
                            K.act(x_[:, s0:s0 + ln], up[:, p0:p0 + ln], AF.Identity, [up, mu], [x_], scale=mu[:, 2, fc:fc + 1])
                            K.stt("dve", x_[:, s0:s0 + ln], up[:, p0 - 1:p0 - 1 + ln], mu[:, 0, fc:fc + 1], x_[:, s0:s0 + ln], ALU.mult, ALU.add, [up, mu, x_], [x_])
                            K.stt("dve", x_[:, s0:s0 + ln], up[:, p0 + 1:p0 + 1 + ln], mu[:, 1, fc:fc + 1], x_[:, s0:s0 + ln], ALU.mult, ALU.add, [up, mu, x_], [x_])
                        if fc < 4:
                            c4 = fc
                            K.copy("act", rT[:, c4, :], x_[:, :], [x_], [rT])
                            for bi, (t0, t1) in enumerate(BLOCKS):
                                n = t1 - t0
                                a_, b_ = t32[bi % 2], t16[bi % 2]
                                K.stt("dve", b_[:, 0:n], x_[:, t0:t1], cols[:, 3, c4:c4 + 1], kT[:, c4, t0:t1], ALU.mult, ALU.mult, [x_, cols, kT], [b_])
                                pb = PS[4 + bi % 2]
                                K.mm(pb[:, 0:n], blk64[:, :], b_[:, 0:n], True, True, [blk64, b_], [pb])
                                K.copy("act", gbo[0][:, t0:t1], pb[:, 0:n], [pb], [gbo[0]])
                            K.dma("pool", gb_d[1, c4, :, :], gbo[0][:, :], reads=[gbo[0]], writes=[gb_d])
                        elif fc < 8:
                            c4 = fc - 4
                            K.copy("act", kT[:, c4, :], x_[:, :], [x_], [kT])
                            for bi, (t0, t1) in enumerate(BLOCKS):
                                n = t1 - t0
                                a_, b_ = t32[bi % 2], t16[bi % 2]
                                K.ts("dve", a_[:, 0:n], x_[:, t0:t1], cols[:, 0, c4:c4 + 1], None, ALU.mult, None, [x_, cols], [a_])
                                K.act(b_[:, 0:n], a_[:, 0:n], AF.Square, [a_], [b_])
                                pb = PS[4 + bi % 2]
                                K.mm(pb[:, 0:n], blk64[:, :], b_[:, 0:n], True, True, [blk64, b_], [pb])
                                r_ = rk32[bi % 2]
                                K.rsqrt(r_[:, 0:n], pb[:, 0:n], 1.0, 1e-12, [pb], [r_])
                                K.tt("pool", kknT[:, c4, t0:t1], a_[:, 0:n], r_[:, 0:n], ALU.mult, [a_, r_], [kknT])
                        elif fc < 12:
                            c4 = fc - 8
                            xb_ = xsb[0]
                            K.copy("act", xb_[:, :], x_[:, :], [x_], [xb_])
                            for grp in range(3):
                                tis = list(range(grp * 8, min(18, grp * 8 + 8)))
                                pb = PS[4 + grp % 2]
                                pbv = pb[:, :].bitcast(BF16)
                                for qi, ti in enumerate(tis):
                                    K.transpose(pbv[:, qi * 128:(qi + 1) * 128], xb_[:, ti * 128:(ti + 1) * 128], identb[:, :], [xb_, identb], [pb])
                                nt = len(tis)
                                K.copy("dve", vtok[:, tis[0]:tis[0] + nt, c4 * 128:(c4 + 1) * 128],
                                       pbv[:, 0:nt * 128].rearrange("p (a f) -> p a f", a=nt), [pb], [vtok])
                            K.dma("pool", gb_d[0, c4, :, :], xb_[:, :], reads=[xb_], writes=[gb_d])
                        elif fc == 12:
                            K.act(xwaT[0:64, :], x_[0:64, :], AF.Tanh, [x_], [xwaT])
                            K.copy("act", xwaT[64:128, :], x_[64:128, :], [x_], [xwaT])
                        else:
                            xb_ = xsb[0]
                            K.act(xb_[:, :], x_[:, :], AF.Sigmoid, [x_], [xb_])
                            for c4 in range(4):
                                for bi, (t0, t1) in enumerate(BLOCKS):
                                    n = t1 - t0
                                    pb = PS[bi % 4]
                                    K.mm(pb[:, 0:n], g2b[:, c4 * 128:(c4 + 1) * 128], xb_[:, t0:t1], True, True, [g2b, xb_], [pb])
                                    K.copy("act", gbo[0][:, t0:t1], pb[:, 0:n], [pb], [gbo[0]])
                                K.dma("pool", gb_d[2, c4, :, :], gbo[0][:, :], reads=[gbo[0]], writes=[gb_d])
                K.barrier()
                with ES() as st2:
                    w2b = K.sb([64, 2, 512], BF16, "w2b", st2)
                    a2b = K.sb([128, 2, 512], BF16, "a2b", st2)
                    w0bc = K.sb([128, 2, 512], F32, "w0bc", st2)
                    lcm = K.sb([128, 2, 128], F32, "lcm", st2)
                    lexcm = K.sb([128, 2, 128], F32, "lexcm", st2)
                    mcol = K.sb([128, 2, 2], F32, "mcol", st2)
                    m1 = K.sb([128, 2, 128], F32, "m1", st2)
                    m3 = K.sb([128, 2, 384], F32, "m3", st2)
                    m1t = K.sb([128, 2, 128], F32, "m1t", st2)
                    for dst, src in ((lcm, rwlc_d), (lexcm, rwlexc_d), (mcol, rwmcol_d), (m1, rwm1_d), (m3, rwm3_d), (m1t, rwm1t_d)):
                        for d in range(2):
                            K.dma("sp", dst[:, d, :], src[d, :, :], reads=[src], writes=[dst])
                    with ES() as stw:
                        wstg = [K.sb([128, 1024], F32, "wstg", stw)]
                        for d in range(2):
                            load_w_bf16(w2b[:, d, :], rw2_d[0, d, :, :], (64, 512), rw2_d, w2b, wstg, 0)
                            s_ = wstg[0]
                            K.dma("sp", s_[64:128, 0:512], ra2_d[0, d, :, :], reads=[ra2_d], writes=[s_])
                            K.copy("pool", a2b[64:128, d, :], s_[64:128, 0:512], [s_], [a2b])
                            K.dma("sp", w0bc[:, d, :], rw0_d[0, d, :].partition_broadcast(128), reads=[rw0_d], writes=[w0bc])
                        K.barrier()

                    def dir_stream(d):
                        B = PS[4 * d:4 * d + 4]
                        sg = K.sb([128, 512], F32, "sg", st2)
                        aT = K.sb([128, 128], F32, "aT", st2)
                        tmpa = K.sb([128, 128], F32, "tmpa", st2)
                        tmpb = K.sb([128, 128], F32, "tmpb", st2)
                        eL = K.sb([128, 128], F32, "eL", st2)
                        enL = K.sb([128, 128], F32, "enL", st2)
                        eLex = K.sb([128, 128], F32, "eLex", st2)
                        pm_sb = K.sb([128, 4, 2], F32, "pm_sb", st2)
                        gm = K.sb([128, 4, 2], F32, "gm", st2)
                        AR = K.sb([128, 4, 256], BF16, "AR", st2)
                        BH = K.sb([128, 4, 128], BF16, "BH", st2)
                        KH = K.sb([128, 4, 128], BF16, "KH", st2)
                        BKtok = K.sb([128, 2, 512], BF16, "BKtok", st2)
                        Q = [K.sb([128, 8, 128], F32, "Qa", st2), K.sb([128, 8, 128], F32, "Qb", st2)]
                        QT = [K.sb([128, 8, 128], F32, "QTa", st2), K.sb([128, 8, 128], F32, "QTb", st2)]
                        Nm = K.sb([128, 8, 128], F32, "Nm", st2)
                        S3 = K.sb([128, 8, 384], BF16, "S3", st2)
                        H = K.sb([128, 4, 64], F32, "H", st2)
                        H0 = K.sb([128, 4, 64], F32, "H0", st2)
                        H0b = K.sb([128, 4, 64], BF16, "H0b", st2)
                        W_sb = K.sb([128, 512], F32, "W_sb", st2)
                        U_sb = K.sb([128, 512], BF16, "U_sb", st2)
                        ybuf = K.sb([128, 512], F32, "ybuf", st2)
                        yold = K.sb([128, 512], F32, "yold", st2)
                        yield
                        K.memset("dve", H[:, :, :], 0.0, [H])
                        tiles = list(range(18)) if d == 0 else [1, 0] + list(range(17, 1, -1))
                        for ci, c in enumerate(tiles):
                            tsl = slice(c * 128, (c + 1) * 128)
                            pz = B[0]
                            K.mm(pz[:, :], xwaT[0:64, tsl], w2b[:, d, :], True, True, [xwaT, w2b], [pz])
                            K.tt("dve", sg[:, :], pz[:, :], w0bc[:, d, :], ALU.add, [pz, w0bc], [sg])
                            K.act(sg[:, :], sg[:, :], AF.Sigmoid, [sg], [sg])
                            for f4 in range(4):
                                fs = slice(f4 * 128, (f4 + 1) * 128)
                                pa = B[1]
                                K.mm(pa[:, 0:128], a2b[64:128, d, fs], xwaT[64:128, tsl], True, True, [a2b, xwaT], [pa])
                                K.act(aT[:, :], pa[:, 0:128], AF.Sigmoid, [pa, cols], [aT], bias=cols[:, 7 + d, f4:f4 + 1])
                                pl = B[2 + f4 % 2]
                                K.mm(pl[:, 0:128], sg[:, fs], lcm[:, d, :], True, True, [sg, lcm], [pl])
                                K.mm(pl[:, 128:256], sg[:, fs], lexcm[:, d, :], True, True, [sg, lexcm], [pl])
                                K.mm(pl[:, 256:258], sg[:, fs], mcol[:, d, :], True, True, [sg, mcol], [pl])
                                K.act(eL[:, :], pl[:, 0:128], AF.Exp, [pl], [eL], scale=CW)
                                K.act(enL[:, :], pl[:, 0:128], AF.Exp, [pl], [enL], scale=-CW)
                                K.act(eLex[:, :], pl[:, 128:256], AF.Exp, [pl], [eLex], scale=CW)
                                K.copy("act", pm_sb[:, f4, :], pl[:, 256:258], [pl], [pm_sb])
                                K.tt("dve", AR[:, f4, 128:256], rT[:, f4, tsl], eL[:, :], ALU.mult, [rT, eL], [AR])
                                K.stt("dve", AR[:, f4, 0:128], kknT[:, f4, tsl], -1.0, eLex[:, :], ALU.mult, ALU.mult, [kknT, eLex], [AR])
                                K.tt("pool", tmpa[:, :], kknT[:, f4, tsl], aT[:, :], ALU.mult, [kknT, aT], [tmpa])
                                K.tt("dve", BH[:, f4, :], tmpa[:, :], enL[:, :], ALU.mult, [tmpa, enL], [BH])
                                K.ts("dve", tmpb[:, :], aT[:, :], cols[:, 1, f4:f4 + 1], cols[:, 2, f4:f4 + 1], ALU.mult, ALU.add, [aT, cols], [tmpb])
                                K.tt("pool", tmpb[:, :], tmpb[:, :], kT[:, f4, tsl], ALU.mult, [tmpb, kT], [tmpb])
                                K.tt("dve", KH[:, f4, :], tmpb[:, :], enL[:, :], ALU.mult, [tmpb, enL], [KH])
                                yield
                            K.tt("dve", pm_sb[:, :, 1], pm_sb[:, :, 1], pm_sb[:, :, 0], ALU.subtract, [pm_sb], [pm_sb])
                            K.act(gm[:, :, :], pm_sb[:, :, :], AF.Exp, [pm_sb], [gm], scale=CW)
                            pt = B[1]
                            ptv = pt[:, :].bitcast(BF16)
                            for f4 in range(4):
                                K.transpose(ptv[:, f4 * 128:(f4 + 1) * 128], BH[:, f4, :], identb[:, :], [BH, identb], [pt])
                                K.transpose(ptv[:, 512 + f4 * 128:512 + (f4 + 1) * 128], KH[:, f4, :], identb[:, :], [KH, identb], [pt])
                            K.copy("act", BKtok[:, :, :], ptv[:, :].rearrange("p (a f) -> p a f", a=2), [pt], [BKtok])
                            yield
                            for h in range(8):
                                f4, hr = h // 2, slice((h % 2) * 64, (h % 2) * 64 + 64)
                                ps_ = B[2 * (h % 2)]
                                K.mm(ps_[:, 0:256], BH[hr, f4, :], AR[hr, f4, :], True, True, [BH, AR], [ps_])
                                K.mm(ps_[:, 256:512], KH[hr, f4, :], AR[hr, f4, :], True, True, [KH, AR], [ps_])
                                K.tt("dve", Q[0][:, h, :], ps_[:, 0:128], m1[:, d, :], ALU.mult, [ps_, m1], [Q[0]])
                                K.tt("dve", S3[:, h, :], ps_[:, 128:512], m3[:, d, :], ALU.mult, [ps_, m3], [S3])
                                pq = B[2 * (h % 2) + 1]
                                K.mm(pq[:, 0:128], AR[hr, f4, 0:128], BH[hr, f4, :], True, True, [AR, BH], [pq])
                                K.tt("dve", QT[0][:, h, :], pq[:, 0:128], m1t[:, d, :], ALU.mult, [pq, m1t], [QT[0]])
                                if h % 2 == 1:
                                    yield
                            K.tt("pool", Nm[:, :, :], Q[0][:, :, :], ident[:, :].unsqueeze(1).to_broadcast([128, 8, 128]), ALU.add, [Q[0], ident], [Nm])
                            cur = 0
                            for lev in range(1, 7):
                                nxt = 1 - cur
                                for half in range(2):
                                    pqa, pqb, pn = B[(3 * half) % 4], B[(3 * half + 1) % 4], B[(3 * half + 2) % 4]
                                    hs = slice(half * 4, half * 4 + 4)
                                    for hh in range(4):
                                        h = half * 4 + hh
                                        cs = slice(hh * 128, (hh + 1) * 128)
                                        K.mm(pqb[:, cs], Q[cur][:, h, :], QT[cur][:, h, :], True, True, [Q[cur], QT[cur]], [pqb])
                                        if lev < 6:
                                            K.mm(pqa[:, cs], QT[cur][:, h, :], Q[cur][:, h, :], True, True, [Q[cur], QT[cur]], [pqa])
                                    K.copy("act", QT[nxt][:, hs, :], pqb[:, :].rearrange("p (a f) -> p a f", a=4), [pqb], [QT[nxt]])
                                    if lev < 6:
                                        K.copy("dve", Q[nxt][:, hs, :], pqa[:, :].rearrange("p (a f) -> p a f", a=4), [pqa], [Q[nxt]])
                                    yield
                                    for hh in range(4):
                                        h = half * 4 + hh
                                        K.mm(pn[:, hh * 128:(hh + 1) * 128], QT[nxt][:, h, :], Nm[:, h, :], True, True, [QT[nxt], Nm], [pn])
                                    K.tt("dve", Nm[:, hs, :], Nm[:, hs, :], pn[:, :].rearrange("p (a f) -> p a f", a=4), ALU.add, [Nm, pn], [Nm])
                                    yield
                                cur = nxt
                            K.tt("dve", H0[:, :, :], H[:, :, :], gm[:, :, 0:1].to_broadcast([128, 4, 64]), ALU.mult, [H, gm], [H0])
                            K.copy("act", H0b[:, :, :], H0[:, :, :], [H0], [H0b])
                            pw = B[0]
                            for h in range(8):
                                f4, hr = h // 2, slice((h % 2) * 64, (h % 2) * 64 + 64)
                                cs = slice(h * 64, (h + 1) * 64)
                                K.mm(pw[:, cs], AR[hr, f4, 0:128], H0b[hr, f4, :], True, False, [AR, H0b], [pw])
                                K.mm(pw[:, cs], S3[:, h, 128:256], vtok[:, c, cs], False, True, [S3, vtok], [pw])
                            K.copy("act", W_sb[:, :], pw[:, :], [pw], [W_sb])
                            yield
                            pu = B[1]
                            for h in range(8):
                                cs = slice(h * 64, (h + 1) * 64)
                                K.mm(pu[:, cs], Nm[:, h, :], W_sb[:, cs], True, True, [Nm, W_sb], [pu])
                            K.copy("act", U_sb[:, :], pu[:, :], [pu], [U_sb])
                            yield
                            py = B[2]
                            for h in range(8):
                                f4, hr = h // 2, slice((h % 2) * 64, (h % 2) * 64 + 64)
                                cs = slice(h * 64, (h + 1) * 64)
                                K.mm(py[:, cs], AR[hr, f4, 128:256], H0b[hr, f4, :], True, False, [AR, H0b], [py])
                                K.mm(py[:, cs], S3[:, h, 0:128], U_sb[:, cs], False, False, [S3, U_sb], [py])
                                K.mm(py[:, cs], S3[:, h, 256:384], vtok[:, c, cs], False, True, [S3, vtok], [py])
                            if d == 0:
                                K.copy("dve", ybuf[:, :], py[:, :], [py], [ybuf])
                                K.dma("pool", yf_d[c, :, :], ybuf[:, :], reads=[ybuf], writes=[yf_t[c]])
                            else:
                                K.copy("dve", ybuf[:, :], py[:, :], [py], [ybuf])
                                K.dma("pool", y_d[c, :, :], ybuf[:, :], reads=[ybuf], writes=[yb_t[c]])
                            yield
                            if ci < len(tiles) - 1:
                                ph = B[3]
                                for f4 in range(4):
                                    fs = slice(f4 * 128, (f4 + 1) * 128)
                                    K.mm(ph[:, fs], BKtok[:, 0, fs], U_sb[:, fs], True, False, [BKtok, U_sb], [ph])
                                    K.mm(ph[:, fs], BKtok[:, 1, fs], vtok[:, c, fs], False, True, [BKtok, vtok], [ph])
                                phv = ph[:, :].rearrange("p (f x) -> p f x", f=4)
                                for e2_ in range(2):
                                    rs = slice(e2_ * 64, e2_ * 64 + 64)
                                    K.tt("dve", H[rs, :, :], H0[rs, :, :], phv[rs, :, e2_ * 64:(e2_ + 1) * 64], ALU.add, [H0, ph], [H])
                                    K.tt("dve", H[rs, :, :], H[rs, :, :], gm[rs, :, 1:2].to_broadcast([64, 4, 64]), ALU.mult, [H, gm], [H])
                                yield

                    yf_t = [Buf(yf_d.t, "yf%d" % i_) for i_ in range(18)]
                    yb_t = [Buf(y_d.t, "yb%d" % i_) for i_ in range(18)]
                    streams = [dir_stream(0), dir_stream(1)]
                    for s_ in streams:
                        next(s_)
                    for _ in range(cfg.get("rw_offset", 19)):
                        next(streams[1])
                    alive = list(streams)
                    while alive:
                        for s_ in list(alive):
                            try:
                                next(s_)
                            except StopIteration:
                                alive.remove(s_)
            K.barrier()
            rwo = K.sb([128, 4, T], BF16, "rwo", st)
            with ES() as st2:
                yt = [K.sb([128, 512], F32, "yt", st2) for _ in range(2)]
                ysq = K.sb([128, 512], F32, "ysq", st2)
                s1 = K.sb([128, 8], F32, "s1", st2)
                s2 = K.sb([128, 8], F32, "s2", st2)
                ynb = K.sb([128, 512], BF16, "ynb", st2)
                vT_ = [K.sb([128, 4, 128], BF16, "vT_", st2) for _ in range(2)]
                sc_ = [K.sb([128, 4, 128], BF16, "sc_", st2) for _ in range(2)]
                gg_ = [K.sb([128, 4, 128], BF16, "gg_", st2) for _ in range(2)]
                yn32 = K.sb([128, 4, 128], F32, "yn32", st2)
                bon = K.sb([128, 4, 128], F32, "bon", st2)
                gbv = gb_d[:, :, :, :].rearrange("a c p t -> a p c t")
                for c in range(18):
                    tsl = slice(c * 128, (c + 1) * 128)
                    y_ = yt[c % 2]
                    K.dma("sp", y_[:, :], y_d[c, :, :], reads=[y_d], writes=[y_])
                    K.dma("sp", ysq[:, :], yf_d[c, :, :], reads=[yf_d], writes=[ysq])
                    K.tt("dve", y_[:, :], y_[:, :], ysq[:, :], ALU.add, [y_, ysq], [y_])
                    K.dma("sp", vT_[c % 2][:, :, :], gbv[0, :, :, tsl], reads=[gb_d], writes=[vT_[c % 2]])
                    K.dma("sp", sc_[c % 2][:, :, :], gbv[1, :, :, tsl], reads=[gb_d], writes=[sc_[c % 2]])
                    K.dma("sp", gg_[c % 2][:, :, :], gbv[2, :, :, tsl], reads=[gb_d], writes=[gg_[c % 2]])
                    yv = y_[:, :].rearrange("p (h e) -> p h e", h=8)
                    K.op("dve", lambda e, yv=yv: e.tensor_reduce(out=s1[:, :], in_=yv, axis=AX.X, op=ALU.add), [y_], [s1])
                    K.act(ysq[:, :], y_[:, :], AF.Square, [y_], [ysq])
                    K.op("dve", lambda e: e.tensor_reduce(out=s2[:, :], in_=ysq[:, :].rearrange("p (h e) -> p h e", h=8), axis=AX.X, op=ALU.add), [ysq], [s2])
                    K.ts("dve", s1[:, :], s1[:, :], 1.0 / 64, None, ALU.mult, None, [s1], [s1])
                    K.tt("dve", ysq[:, 0:8], s1[:, :], s1[:, :], ALU.mult, [s1], [ysq])
                    K.stt("dve", s2[:, :], s2[:, :], 1.0 / 64, ysq[:, 0:8], ALU.mult, ALU.subtract, [s2, ysq], [s2])
                    K.rsqrt(s2[:, :], s2[:, :], 1.0, 64e-5, [s2], [s2])
                    K.tt("dve", yv, yv, s1[:, :].unsqueeze(2).to_broadcast([128, 8, 64]), ALU.subtract, [y_, s1], [y_])
                    K.tt("dve", ynb[:, :].rearrange("p (h e) -> p h e", h=8), yv, s2[:, :].unsqueeze(2).to_broadcast([128, 8, 64]), ALU.mult, [y_, s2], [ynb])
                    pt = PS[c % 2]
                    ptv = pt[:, :].bitcast(BF16)
                    for c4 in range(4):
                        K.transpose(ptv[:, c4 * 128:(c4 + 1) * 128], ynb[:, c4 * 128:(c4 + 1) * 128], identb[:, :], [ynb, identb], [pt])
                    for c4 in range(4):
                        K.act(yn32[:, c4, :], ptv[:, c4 * 128:(c4 + 1) * 128], AF.Identity, [pt, cols], [yn32],
                              bias=cols[:, 5, c4:c4 + 1], scale=cols[:, 4, c4:c4 + 1])
                    K.tt("pool", bon[:, :, :], vT_[c % 2][:, :, :], sc_[c % 2][:, :, :], ALU.mult, [vT_[c % 2], sc_[c % 2]], [bon])
                    K.tt("pool", yn32[:, :, :], yn32[:, :, :], bon[:, :, :], ALU.add, [yn32, bon], [yn32])
                    K.tt("dve", rwo[:, :, tsl], yn32[:, :, :], gg_[c % 2][:, :, :], ALU.mult, [yn32, gg_[c % 2]], [rwo])
            K.barrier()
            wo = K.sb([128, 4, D], BF16, "wo_rw", st)
            wostg = [K.sb([128, 4096], F32, "wostg", st)]
            load_w_bf16(wo[:, :, :], about_d[0, 512:1024, :].rearrange("(kc p) n -> p kc n", p=128), (128, 4, D), about_d, wo, wostg, 0)
            pieces = [((lambda oc, cc=cc: wo[:, cc, oc * 128:(oc + 1) * 128]),
                       (lambda t0, t1, cc=cc: rwo[:, cc, t0:t1]), [wo, rwo]) for cc in range(4)]
            apply_out(b, pieces, 2, 0)
        K.barrier()

    K.barrier()
    for l in layers:
        stage_mods(l)
    for b in range(nb):
        stage_load(b)
        for li, l in enumerate(layers):
            last = (li == len(layers) - 1) and not cfg.get("force_ctx", False)
            modsT.l = l
            Amod.l = l
            if mixers:
                if l == 1:
                    with ES() as sth:
                        hT = K.sb([128, KC, T], BF16, "hT", sth)
                        stage_norm(b, 0, hT, to_dram=True)
                        if cfg.get("swa", True):
                            stage_swa(b, hT)
                    if cfg.get("ssd", True):
                        stage_ssd(b, None)
                else:
                    with ES() as sth:
                        hT = K.sb([128, KC, T], BF16, "hT", sth)
                        stage_norm(b, 0, hT, to_dram=True)
                        if cfg.get("mla", True):
                            stage_mla(b, hT)
                    if cfg.get("rwkv", True):
                        stage_rwkv(b)
            with ES() as sth:
                hT = K.sb([128, KC, T], BF16, "hT", sth)
                stage_norm(b, 1, hT, lo=0 if not last else LC)
                stage_ffn(b, l, do_ctx=not last, hT=hT)
        stage_out(b)
        if cfg.get("dbg_x", False) and b == 0:
            for cc in range(KC):
                K.dma("sp", dbgx_d[cc, :, :], xs_d[cc, :, :], reads=xblk, writes=[dbgx_d])
    K.barrier()
    K.es.close()
    return nc, K


CONST_INPUTS = None


def _rope_tables(rot_dim):
    n_freq = rot_dim // 4
    rows = np.arange(LL, dtype=np.float32) // 64
    cols = np.arange(LL, dtype=np.float32) % 64
    inv = (np.float32(10000.0) ** (-np.arange(n_freq, dtype=np.float32) / np.float32(n_freq))).astype(np.float32)
    ang = np.concatenate([rows[:, None] * inv[None, :], cols[:, None] * inv[None, :]], axis=-1).astype(np.float32)
    cos, sin = np.cos(ang).astype(np.float32), np.sin(ang).astype(np.float32)
    cosT = np.concatenate([cos.T, cos.T], axis=0)
    sinT = np.concatenate([sin.T, sin.T], axis=0)
    return np.ascontiguousarray(cosT), np.ascontiguousarray(sinT)


def const_inputs():
    global CONST_INPUTS
    if CONST_INPUTS is None:
        s = np.arange(128)
        triF = (s[:, None] <= s[None, :]).astype(np.float32)
        c = {"ident": np.eye(128, dtype=np.float32), "triF": triF, "triB": np.ascontiguousarray(triF.T),
             "strF": (s[:, None] > s[None, :]).astype(np.float32), "strB": (s[:, None] < s[None, :]).astype(np.float32)}
        c["swa_cos"], c["swa_sin"] = _rope_tables(64)
        c["mla_cos"], c["mla_sin"] = _rope_tables(32)
        triB = triF.T
        incl = [triF, triB]
        strict = [c["strB"], c["strF"]]
        m = 63
        c["rw_lc"] = np.stack([incl[d] - incl[d][:, m:m + 1] for d in range(2)]).astype(np.float32)
        c["rw_lexc"] = np.stack([strict[d] - incl[d][:, m:m + 1] for d in range(2)]).astype(np.float32)
        c["rw_mcol"] = np.stack([np.stack([incl[d][:, m], np.ones(128, np.float32)], axis=1) for d in range(2)]).astype(np.float32)
        c["rw_m1"] = np.stack([strict[d] for d in range(2)]).astype(np.float32)
        c["rw_m3"] = np.stack([np.concatenate([incl[d], strict[d], incl[d]], axis=1) for d in range(2)]).astype(np.float32)
        c["rw_m1t"] = np.stack([np.ascontiguousarray(strict[d].T) for d in range(2)]).astype(np.float32)
        blk = np.zeros((128, 128), np.float32)
        blk[:64, :64] = 1.0
        blk[64:, 64:] = 1.0
        c["blk64"] = blk
        CONST_INPUTS = c
    return CONST_INPUTS


def make_in_maps(nc_names, inputs, ncores=NCORES):
    consts = const_inputs()
    in_maps = []
    for core in range(ncores):
        m = {}
        sl = slice(core * BPC, (core + 1) * BPC)
        for k in nc_names:
            if k in consts:
                m[k] = consts[k]
            else:
                v = np.asarray(inputs[k])
                m[k] = np.ascontiguousarray(v[sl] if k in ("x", "c", "ctx") else v)
        in_maps.append(m)
    return in_maps


def kernel(**inputs):
    cfg = {}
    nc, K = build_program(cfg)
    in_maps = make_in_maps(K.in_names, inputs)
    res = run_bass_kernel_spmd(nc, in_maps, core_ids=list(range(NCORES)))
    return np.concatenate([r["out"] for r in res.results], axis=0)
```

```python
import contextlib
import math
import numpy as np
import concourse.bass as bass
import concourse.mybir as mybir
from concourse.bass_utils import run_bass_kernel_spmd

F32 = mybir.dt.float32
BF16 = mybir.dt.bfloat16
AF = mybir.ActivationFunctionType
ALU = mybir.AluOpType
AX = mybir.AxisListType

NCORES = 8
BPC = 4
D = 1024
KC = 8
LC = 256
LL = 2048
T = LC + LL
DFF = 2816
EPS = 1e-6
BLOCKS = [(0, 256), (256, 768), (768, 1280), (1280, 1792), (1792, 2304)]
SEM_EPOCH = 50000


class Buf:
    def __init__(self, t, name):
        self.t = t
        self.name = name
        self.lw = []
        self.rd = []
        self.ds = None

    def __getitem__(self, idx):
        return self.t[idx]


class Eng:
    def __init__(self, name, e, is_pe=False):
        self.name = name
        self.e = e
        self.is_pe = is_pe
        self.sems = []
        self.count = 0
        self.epoch = 0
        self.seen = {}


class Kern:
    def __init__(self, nc):
        self.nc = nc
        self.es = contextlib.ExitStack()
        self.engs = {}
        for name, e, ispe in (("pe", nc.tensor, True), ("act", nc.scalar, False),
                              ("dve", nc.vector, False), ("pool", nc.gpsimd, False),
                              ("sp", nc.sync, False)):
            en = Eng(name, e, ispe)
            en.sems.append(self.es.enter_context(nc.semaphore("s_%s_0" % name)))
            self.engs[name] = en
        self.ndsem = 44
        self.dsem = [self.es.enter_context(nc.semaphore("s_d%d" % i)) for i in range(self.ndsem)]
        self.dtot = [0] * self.ndsem
        self.dranges = {"sp": (0, 26), "pool": (26, 44), "act": (0, 26), "dve": (0, 26), "pe": (0, 26)}
        self.drr = {k: 0 for k in self.dranges}
        self.nbuf = 0
        self.n_ops = 0
        self.eps_bufs = {}
        self.in_names = []

    def sb(self, shape, dtype, name=None, stack=None):
        self.nbuf += 1
        name = "%s_%d" % (name or "sb", self.nbuf)
        t = (stack or self.es).enter_context(self.nc.sbuf_tensor(name, list(shape), dtype))
        return Buf(t, name)

    def ps(self, name=None):
        self.nbuf += 1
        name = "%s_%d" % (name or "ps", self.nbuf)
        t = self.es.enter_context(self.nc.psum_tensor(name, [128, 512], F32))
        return Buf(t, name)

    def dram(self, name, shape, dtype, kind="Internal"):
        t = self.nc.dram_tensor(name, list(shape), dtype, kind=kind)
        if kind == "ExternalInput":
            self.in_names.append(name)
        return Buf(t, name)

    def _wait(self, eng, ev):
        if ev[0] == "E":
            src = self.engs[ev[1]]
            ep, n = ev[2], ev[3]
            if src is eng and eng.is_pe:
                return
            key = ("E", ev[1], ep)
            if eng.seen.get(key, 0) >= n:
                return
            eng.e.wait_ge(src.sems[ep], n)
            eng.seen[key] = n
        else:
            i = ev[1]
            tot = self.dtot[i]
            key = ("D", i)
            if eng.seen.get(key, 0) >= tot:
                return
            eng.e.wait_ge(self.dsem[i], tot)
            eng.seen[key] = tot

    def _deps(self, eng, reads, writes):
        for b in reads:
            for ev in b.lw:
                self._wait(eng, ev)
        for b in writes:
            for ev in b.lw:
                self._wait(eng, ev)
            for ev in b.rd:
                self._wait(eng, ev)

    def _record(self, ev, reads, writes):
        for b in reads:
            if ev[0] == "E":
                b.rd = [r for r in b.rd if not (r[0] == "E" and r[1] == ev[1])]
            else:
                b.rd = [r for r in b.rd if r != ev]
            b.rd.append(ev)
        for b in writes:
            b.lw = [ev]
            b.rd = []

    def op(self, engname, fn, reads=(), writes=()):
        eng = self.engs[engname]
        self._deps(eng, reads, writes)
        if eng.count >= SEM_EPOCH:
            eng.epoch += 1
            eng.count = 0
            eng.sems.append(self.es.enter_context(self.nc.semaphore("s_%s_%d" % (engname, eng.epoch))))
        ins = fn(eng.e)
        eng.count += 1
        ins.then_inc(eng.sems[eng.epoch], 1)
        ev = ("E", engname, eng.epoch, eng.count)
        self._record(ev, reads, writes)
        self.n_ops += 1

    def dma(self, engname, out, in_, reads=(), writes=(), **kw):
        eng = self.engs[engname]
        self._deps(eng, reads, writes)
        b = None
        for cand in list(writes) + list(reads):
            if cand.ds is not None and engname in cand.ds:
                b = cand
                break
        if b is None:
            b = (list(writes) + list(reads))[0]
            if b.ds is None:
                b.ds = {}
            lo, hi = self.dranges[engname]
            b.ds[engname] = lo + self.drr[engname]
            self.drr[engname] = (self.drr[engname] + 1) % (hi - lo)
        i = b.ds[engname]
        ins = eng.e.dma_start(out=out, in_=in_, **kw)
        ins.then_inc(self.dsem[i], 16)
        self.dtot[i] += 16
        ev = ("D", i)
        self._record(ev, reads, writes)
        self.n_ops += 1

    def barrier(self):
        for eng in self.engs.values():
            for other in self.engs.values():
                if other is eng:
                    continue
                if other.count > 0:
                    self._wait(eng, ("E", other.name, other.epoch, other.count))
            for i in range(self.ndsem):
                if self.dtot[i] > 0:
                    self._wait(eng, ("D", i))

    def mm(self, out, lhsT, rhs, start, stop, reads, writes):
        self.op("pe", lambda e: e.matmul(out, lhsT=lhsT, rhs=rhs, start=start, stop=stop), reads, writes)

    def transpose(self, out, in_, ident, reads, writes):
        self.op("pe", lambda e: e.transpose(out, in_, ident), reads, writes)

    def act(self, out, in_, func, reads, writes, bias=None, scale=None, eng="act"):
        kw = {}
        if bias is not None:
            kw["bias"] = bias
        if scale is not None:
            kw["scale"] = scale
        self.op(eng, lambda e: e.activation(out=out, in_=in_, func=func, **kw), reads, writes)

    def ts(self, eng, out, in0, s1, s2, op0, op1, reads, writes):
        if op1 is None:
            self.op(eng, lambda e: e.tensor_scalar(out=out, in0=in0, scalar1=s1, scalar2=None, op0=op0), reads, writes)
        else:
            self.op(eng, lambda e: e.tensor_scalar(out=out, in0=in0, scalar1=s1, scalar2=s2, op0=op0, op1=op1), reads, writes)

    def tt(self, eng, out, in0, in1, op, reads, writes):
        self.op(eng, lambda e: e.tensor_tensor(out=out, in0=in0, in1=in1, op=op), reads, writes)

    def stt(self, eng, out, in0, scalar, in1, op0, op1, reads, writes):
        self.op(eng, lambda e: e.scalar_tensor_tensor(out=out, in0=in0, scalar=scalar, in1=in1, op0=op0, op1=op1), reads, writes)

    def copy(self, eng, out, in_, reads, writes):
        if eng == "act":
            self.op(eng, lambda e: e.activation(out=out, in_=in_, func=AF.Copy), reads, writes)
        else:
            self.op(eng, lambda e: e.tensor_copy(out=out, in_=in_), reads, writes)

    def rsqrt(self, out, in_, scale, eps, reads, writes):
        self.op("act", lambda e: e.activation(out=out, in_=in_, func=AF.Sqrt, bias=self.eps_ap(eps), scale=scale), reads, writes)
        self.op("dve", lambda e: e.reciprocal(out=out, in_=out), writes, writes)

    def eps_ap(self, eps):
        if eps not in self.eps_bufs:
            b = self.sb([128, 1], F32, "eps")
            self.memset("dve", b[:, :], float(eps), [b])
            self.eps_bufs[eps] = b
        return self.eps_bufs[eps][:, 0:1]

    def memset(self, eng, ap, val, writes):
        self.op(eng, lambda e: e.memset(ap, val), (), writes)


def colvec(ap1d, n):
    return ap1d.rearrange("(c p) -> p c", p=128)


def stg_view(s, shape):
    n = 1
    for d_ in shape[1:]:
        n *= d_
    v = s[0:shape[0], 0:n]
    if len(shape) == 3:
        v = v.rearrange("p (a b) -> p a b", a=shape[1])
    return v


HD = 64
CD_Z, CD_XBC, CD_DT, CD_Q, CD_K, CD_V = 0, 1024, 2560, 2592, 3104, 3232
CD_IN = 3360


def build_program(cfg):
    nc = bass.Bass("TRN2", target_bir_lowering=False)
    K = Kern(nc)
    nb = cfg.get("nb", BPC)
    layers = cfg.get("layers", [0, 1])
    mixers = cfg.get("mixers", True)
    ES = contextlib.ExitStack

    def din(name, shape):
        return K.dram(name, shape, F32, kind="ExternalInput")

    x_d = din("x", [BPC, LL, D])
    c_d = din("c", [BPC, D])
    ctx_d = din("ctx", [BPC, LC, D])
    cctx_d = din("c_ctx", [D])
    ada_w_d = din("ada_w", [2, D, 6 * D])
    ada_b_d = din("ada_b", [2, 6 * D])
    nmix_d = din("norm_mix_g", [2, D])
    nffn_d = din("norm_ffn_g", [2, D])
    wup_d = din("ffn_w_up", [2, D, 2 * DFF])
    cw_d = din("ffn_conv_w", [2, 3, 2 * DFF])
    cb_d = din("ffn_conv_b", [2, 2 * DFF])
    wdn_d = din("ffn_w_down", [2, DFF, D])
    fng_d = din("final_norm_g", [D])
    if mixers and 1 in layers:
        cdin_d = din("cd_w_in", [1, D, CD_IN])
        cdout_d = din("cd_w_out", [1, 1536, D])
        scw_d = din("ssm_conv_w", [1, 5, 1536])
        scb_d = din("ssm_conv_b", [1, 1536])
        sdtb_d = din("ssm_dt_bias", [1, 2, 16])
        salog_d = din("ssm_a_log", [1, 2, 16])
        sd_d = din("ssm_d", [1, 16])
        sng_d = din("ssm_norm_g", [1, 1024])
        sink_d = din("swa_sink", [1, 8])
        swacos_d = din("swa_cos", [64, LL])
        swasin_d = din("swa_sin", [64, LL])
        triF_d = din("triF", [128, 128])
        triB_d = din("triB", [128, 128])
        strF_d = din("strF", [128, 128])
        strB_d = din("strB", [128, 128])
    if mixers and 0 in layers:
        abin_d = din("ab_w_in", [1, D, 2464])
        about_d = din("ab_w_out", [1, D, D])
        mqg_d = din("mla_q_norm_g", [1, 384])
        mqu_d = din("mla_w_q_up", [1, 384, 768])
        mkg_d = din("mla_kv_norm_g", [1, 256])
        mkvu_d = din("mla_w_kv_up", [1, 256, 1024])
        mlacos_d = din("mla_cos", [32, LL])
        mlasin_d = din("mla_sin", [32, LL])
        rmp_d = din("rwkv_mu_prev", [1, 1792])
        rmn_d = din("rwkv_mu_next", [1, 1792])
        rw0_d = din("rwkv_w0", [1, 2, 512])
        rw2_d = din("rwkv_w2", [1, 2, 64, 512])
        ra0_d = din("rwkv_a0", [1, 2, 512])
        ra2_d = din("rwkv_a2", [1, 2, 64, 512])
        rg2_d = din("rwkv_g2", [1, 128, 512])
        rkk_d = din("rwkv_k_k", [1, 512])
        rka_d = din("rwkv_k_a", [1, 512])
        rrk_d = din("rwkv_r_k", [1, 8, 64])
        rlg_d = din("rwkv_ln_g", [1, 512])
        rlb_d = din("rwkv_ln_b", [1, 512])
        rwlc_d = din("rw_lc", [2, 128, 128])
        rwlexc_d = din("rw_lexc", [2, 128, 128])
        rwmcol_d = din("rw_mcol", [2, 128, 2])
        rwm1_d = din("rw_m1", [2, 128, 128])
        rwm3_d = din("rw_m3", [2, 128, 384])
        rwm1t_d = din("rw_m1t", [2, 128, 128])
        blk64_d = din("blk64", [128, 128])
        gb_d = K.dram("gb_scr", [3, 4, 128, T], BF16)
        y_d = K.dram("y_scr", [18, 128, 512], F32)
        yf_d = K.dram("yf_scr", [18, 128, 512], F32)
    ident_d = din("ident", [128, 128])
    out_d = K.dram("out", [BPC, LL, D], F32, kind="ExternalOutput")
    if cfg.get("dbg_x", False):
        dbgx_d = K.dram("dbgx", [KC, 128, T], F32, kind="ExternalOutput")
    aT_d = K.dram("aT_scr", [DFF, T], BF16)
    xs_d = K.dram("x_scr", [KC, 128, T], F32)
    hT_d = K.dram("hT_scr", [KC, 128, T], BF16)
    sz_d = K.dram("sz_scr", [16, 128, 1024], BF16)
    hin_d = K.dram("hin_scr", [16, 128, 1024], BF16)
    xview = xs_d[:, :, :].rearrange("c p t -> p c t")
    hview = hT_d[:, :, :].rearrange("c p t -> p c t")
    xblk = [Buf(xs_d.t, "xblk%d" % j) for j in range(T // 256)]

    def xdeps(t0, t1):
        return xblk[t0 // 256:(t1 + 255) // 256]

    ident = K.sb([128, 128], F32, "ident")
    identb = K.sb([128, 128], BF16, "identb")
    ones_bf = K.sb([128, 128], BF16, "ones")
    ones_f = K.sb([128, 128], F32, "onesf")
    class LayerBuf(Buf):
        def __init__(self, b_):
            Buf.__init__(self, b_.t, b_.name)
            self.l = 0

        def __getitem__(self, idx):
            return self.t[(idx[0], self.l) + tuple(idx[1:])]

    modsT = LayerBuf(K.sb([128, 2, KC, 6, 5], F32, "modsT"))
    Amod = LayerBuf(K.sb([128, 2, KC, 2, 5], F32, "Amod"))
    gcols = K.sb([128, 5, KC], F32, "gcols")
    cwT = K.sb([128, 2, 3, 44], F32, "cwT")
    cbT = K.sb([128, 2, 44], F32, "cbT")
    PS = [K.ps("ps%d" % i) for i in range(8)]
    for e_ in (EPS, 64e-5, 1e-12, 1.0, 0.0):
        K.eps_ap(e_)

    K.dma("sp", ident[:, :], ident_d[:, :], reads=[ident_d], writes=[ident])
    K.copy("dve", identb[:, :], ident[:, :], [ident], [identb])
    K.memset("dve", ones_bf[:, :], 1.0, [ones_bf])
    K.memset("dve", ones_f[:, :], 1.0, [ones_f])
    for l in range(2):
        K.dma("sp", gcols[:, l, :], colvec(nmix_d[l, :], KC), reads=[nmix_d], writes=[gcols], allow_slow_non_contiguous=True)
        K.dma("sp", gcols[:, 2 + l, :], colvec(nffn_d[l, :], KC), reads=[nffn_d], writes=[gcols], allow_slow_non_contiguous=True)
        for tap in range(3):
            K.dma("sp", cwT[:, l, tap, :], colvec(cw_d[l, tap, :], 44), reads=[cw_d], writes=[cwT], allow_slow_non_contiguous=True)
        K.dma("sp", cbT[:, l, :], colvec(cb_d[l, :], 44), reads=[cb_d], writes=[cbT], allow_slow_non_contiguous=True)
    K.dma("sp", gcols[:, 4, :], colvec(fng_d[:], KC), reads=[fng_d], writes=[gcols], allow_slow_non_contiguous=True)

    def stage_mods(l):
        modsT.l = l
        Amod.l = l
        with ES() as st:
            condT = K.sb([128, KC, 5], F32, "condT", st)
            scond = K.sb([128, KC, 5], F32, "scond", st)
            abT = K.sb([128, 48], F32, "abT", st)
            wb = [K.sb([128, KC, 128], F32, "adaw", st) for _ in range(3)]
            for r in range(4):
                K.dma("sp", condT[:, :, r], colvec(c_d[r, :], KC), reads=[c_d], writes=[condT], allow_slow_non_contiguous=True)
            K.dma("sp", condT[:, :, 4], colvec(cctx_d[:], KC), reads=[cctx_d], writes=[condT], allow_slow_non_contiguous=True)
            K.dma("sp", abT[:, :], colvec(ada_b_d[l, :], 48), reads=[ada_b_d], writes=[abT], allow_slow_non_contiguous=True)
            K.act(scond[:, :, :], condT[:, :, :], AF.Silu, [condT], [scond])
            wview = ada_w_d[l, :, :].rearrange("(kc p) n -> p kc n", p=128)
            for j in range(48):
                w = wb[j % 3]
                K.dma("sp", w[:, :, :], wview[:, :, j * 128:(j + 1) * 128], reads=[ada_w_d], writes=[w])
                pb = PS[j % 2]
                for kc in range(KC):
                    K.mm(pb[:, 0:5], w[:, kc, :], scond[:, kc, :], kc == 0, kc == KC - 1, [w, scond], [pb])
                kind, cc = j // 8, j % 8
                K.ts("dve", modsT[:, cc, kind, :], pb[:, 0:5], abT[:, j:j + 1], None, ALU.add, None, [pb, abT], [modsT])
            for which, kind, gi in ((0, 1, l), (1, 4, 2 + l)):
                for cc in range(KC):
                    K.ts("dve", Amod[:, cc, which, :], modsT[:, cc, kind, :], 1.0, gcols[:, gi, cc:cc + 1],
                         ALU.add, ALU.mult, [modsT, gcols], [Amod])
        K.barrier()

    def stage_load(b):
        with ES() as st:
            xin = [K.sb([128, D], F32, "xin", st) for _ in range(3)]
            xo = [K.sb([128, KC, 128], F32, "xo", st) for _ in range(3)]
            for ti in range(T // 128):
                xb = xin[ti % 3]
                if ti < 2:
                    src, sb_ = ctx_d[b, ti * 128:(ti + 1) * 128, :], ctx_d
                else:
                    src, sb_ = x_d[b, (ti - 2) * 128:(ti - 1) * 128, :], x_d
                K.dma("sp", xb[:, :], src, reads=[sb_], writes=[xb])
                o = xo[ti % 3]
                for half in range(2):
                    pb = PS[(2 * ti + half) % 4]
                    for q in range(4):
                        cc = half * 4 + q
                        K.transpose(pb[:, q * 128:(q + 1) * 128], xb[:, cc * 128:(cc + 1) * 128], ident[:, :], [xb, ident], [pb])
                    K.copy("act" if half == 0 else "dve", o[:, half * 4:half * 4 + 4, :],
                           pb[:, :].rearrange("p (q t) -> p q t", q=4), [pb], [o])
                K.dma("pool", xview[:, :, ti * 128:(ti + 1) * 128], o[:, :, :], reads=[o], writes=xdeps(ti * 128, ti * 128 + 128))
        K.barrier()

    def stage_norm(b, which, hT, to_dram=False, lo=0):
        shift_kind = 0 if which == 0 else 3
        with ES() as st:
            xb = [K.sb([128, KC, 512], F32, "nxb", st) for _ in range(2)]
            sq = [K.sb([128, 512], BF16, "sq", st) for _ in range(3)]
            rstd = [K.sb([128, 512], F32, "rstd", st) for _ in range(2)]
            tmp = [K.sb([128, 512], F32, "ntmp", st) for _ in range(3)]
            for bi, (t0, t1) in enumerate(BLOCKS):
                if t1 <= lo:
                    continue
                n = t1 - t0
                row = 4 if bi == 0 else b
                x_ = xb[bi % 2]
                K.dma("sp", x_[:, :, 0:n], xview[:, :, t0:t1], reads=xdeps(t0, t1), writes=[x_])
                pb = PS[bi % 2]
                for cc in range(KC):
                    s = sq[cc % 3]
                    K.act(s[:, 0:n], x_[:, cc, 0:n], AF.Square, [x_], [s])
                    K.mm(pb[:, 0:n], ones_bf[:, :], s[:, 0:n], cc == 0, cc == KC - 1, [ones_bf, s], [pb])
                r = rstd[bi % 2]
                K.rsqrt(r[:, 0:n], pb[:, 0:n], 1.0 / D, EPS, [pb], [r])
                for cc in range(KC):
                    tm = tmp[cc % 3]
                    K.tt("dve" if cc % 2 == 0 else "pool", tm[:, 0:n], x_[:, cc, 0:n], r[:, 0:n], ALU.mult, [x_, r], [tm])
                    K.act(hT[:, cc, t0:t1], tm[:, 0:n], AF.Identity, [tm, Amod, modsT], [hT],
                          bias=modsT[:, cc, shift_kind, row:row + 1], scale=Amod[:, cc, which, row:row + 1])
            if to_dram:
                for cc in range(KC):
                    K.dma("pool", hT_d[cc, :, :], hT[:, cc, :], reads=[hT], writes=[hT_d])
        K.barrier()

    def load_hT(hT):
        for cc in range(KC):
            K.dma("sp", hT[:, cc, :], hT_d[cc, :, :], reads=[hT_d], writes=[hT])

    def apply_out(b, pieces, gate_kind, tok_lo):
        with ES() as st:
            xb = [K.sb([128, KC, 256], F32, "uxb", st) for _ in range(2)]
            for bi, t0 in enumerate(range(tok_lo, T, 256)):
                t1 = t0 + 256
                row = 4 if t0 < LC else b
                x_ = xb[bi % 2]
                K.dma("sp", x_[:, :, :], xview[:, :, t0:t1], reads=xdeps(t0, t1), writes=[x_])
                for oc in range(KC):
                    pb = PS[oc % 4]
                    for pi, (lf, rf, rd) in enumerate(pieces):
                        K.mm(pb[:, 0:256], lf(oc), rf(t0, t1), pi == 0, pi == len(pieces) - 1, rd, [pb])
                    K.stt("dve", x_[:, oc, :], pb[:, 0:256], modsT[:, oc, gate_kind, row:row + 1], x_[:, oc, :],
                          ALU.mult, ALU.add, [pb, modsT, x_], [x_])
                K.dma("pool", xview[:, :, t0:t1], x_[:, :, :], reads=[x_], writes=xdeps(t0, t1))

    cast_rr = [0]

    def load_w_bf16(dst_ap, src_ap, shape, src_buf, dst_buf, st_bufs, idx, eng=None):
        s = st_bufs[idx % len(st_bufs)]
        v = stg_view(s, shape)
        K.dma("sp", v, src_ap, reads=[src_buf], writes=[s])
        if eng is None:
            cast_rr[0] += 1
            eng = "dve" if cast_rr[0] % 2 == 0 else "act"
        K.copy(eng, dst_ap, v, [s], [dst_buf])

    def stage_ffn(b, l, do_ctx, hT):
        wupv = wup_d[l, :, :].rearrange("(kc p) n -> p kc n", p=128)
        segs = [(LC, LL)] + ([(0, LC)] if do_ctx else [])
        blocks = [bl for bl in BLOCKS if (do_ctx or bl[0] >= LC)]
        PADW = T + 4
        with ES() as st:
            wst = [K.sb([128, KC, 128], F32, "wst", st) for _ in range(4)]
            wbf = [K.sb([128, KC, 128], BF16, "wbf", st) for _ in range(4)]
            ug = [K.sb([128, PADW], F32, "ug", st) for _ in range(2)]
            uv = [K.sb([128, PADW], F32, "uv", st) for _ in range(2)]
            cg = [K.sb([128, T], F32, "cg", st) for _ in range(2)]
            cv = [K.sb([128, T], F32, "cv", st) for _ in range(2)]
            ao = [K.sb([128, T], BF16, "ao", st) for _ in range(2)]
            for u in ug + uv:
                K.memset("pool", u[:, :], 0.0, [u])

            def pad_off(t):
                return t + 1 if t < LC else t + 3

            def ffn_wload(fc):
                for half in range(2):
                    ws, wb_ = wst[2 * (fc % 2) + half], wbf[2 * (fc % 2) + half]
                    K.dma("sp", ws[:, :, :], wupv[:, :, half * DFF + fc * 128:half * DFF + (fc + 1) * 128], reads=[wup_d], writes=[ws])
                    K.copy("dve" if half == 0 else "act", wb_[:, :, :], ws[:, :, :], [ws], [wb_])

            def ffn_tail(fc):
                cg_, cv_, ao_ = cg[fc % 2], cv[fc % 2], ao[fc % 2]
                for (s0, ln) in segs:
                    K.act(cg_[:, s0:s0 + ln], cg_[:, s0:s0 + ln], AF.Silu, [cg_], [cg_])
                    K.tt("dve" if ln > 1024 else "pool", ao_[:, s0:s0 + ln], cg_[:, s0:s0 + ln], cv_[:, s0:s0 + ln], ALU.mult, [cg_, cv_], [ao_])
                lo = 0 if do_ctx else LC
                K.dma("pool", aT_d[fc * 128:(fc + 1) * 128, lo:T], ao_[:, lo:T], reads=[ao_], writes=[aT_d])

            ffn_wload(0)
            for fc in range(22):
                if fc + 1 < 22:
                    ffn_wload(fc + 1)
                g_, v_ = ug[fc % 2], uv[fc % 2]
                for bi, (t0, t1) in enumerate(blocks):
                    n = t1 - t0
                    for half, dst in ((0, g_), (1, v_)):
                        pb = PS[(bi * 2 + half) % 8]
                        wb_ = wbf[2 * (fc % 2) + half]
                        for kc in range(KC):
                            K.mm(pb[:, 0:n], wb_[:, kc, :], hT[:, kc, t0:t1],
                                 kc == 0, kc == KC - 1, [wb_, hT], [pb])
                        K.copy("act", dst[:, pad_off(t0):pad_off(t0) + n], pb[:, 0:n], [pb], [dst])
                cg_, cv_ = cg[fc % 2], cv[fc % 2]
                for half, src, dst, ch in ((0, g_, cg_, fc), (1, v_, cv_, 22 + fc)):
                    for (s0, ln) in segs:
                        p0 = pad_off(s0)
                        K.act(dst[:, s0:s0 + ln], src[:, p0:p0 + ln], AF.Identity, [src, cwT, cbT], [dst],
                              bias=cbT[:, l, ch:ch + 1], scale=cwT[:, l, 1, ch:ch + 1])
                        K.stt("dve", dst[:, s0:s0 + ln], src[:, p0 - 1:p0 - 1 + ln], cwT[:, l, 0, ch:ch + 1], dst[:, s0:s0 + ln],
                              ALU.mult, ALU.add, [src, cwT, dst], [dst])
                        K.stt("dve", dst[:, s0:s0 + ln], src[:, p0 + 1:p0 + 1 + ln], cwT[:, l, 2, ch:ch + 1], dst[:, s0:s0 + ln],
                              ALU.mult, ALU.add, [src, cwT, dst], [dst])
                if fc >= 1:
                    ffn_tail(fc - 1)
            ffn_tail(21)
        K.barrier()
        wdv = wdn_d[l, :, :].rearrange("(kc p) n -> p kc n", p=128)
        aTv = aT_d[:, :].rearrange("(kc p) t -> p kc t", p=128)
        with ES() as st:
            wd = K.sb([128, 22, D], BF16, "wd", st)
            wds = [K.sb([128, 2 * D], F32, "wds", st) for _ in range(2)]
            ab = [K.sb([128, 22, 256], BF16, "ab", st) for _ in range(2)]
            for j in range(11):
                load_w_bf16(wd[:, 2 * j:2 * j + 2, :], wdv[:, 2 * j:2 * j + 2, :], (128, 2, D), wdn_d, wd, wds, j)
            cnt = [0]

            def rhs_fn(t0, t1):
                return ab[cnt[0] % 2]

            lo = 0 if do_ctx else LC
            with ES() as st2:
                xb = [K.sb([128, KC, 256], F32, "uxb", st2) for _ in range(2)]
                for bi, t0 in enumerate(range(lo, T, 256)):
                    t1 = t0 + 256
                    row = 4 if t0 < LC else b
                    a_ = ab[bi % 2]
                    x_ = xb[bi % 2]
                    K.dma("sp", a_[:, :, :], aTv[:, :, t0:t1], reads=[aT_d], writes=[a_])
                    K.dma("sp", x_[:, :, :], xview[:, :, t0:t1], reads=xdeps(t0, t1), writes=[x_])
                    for oc in range(KC):
                        pb = PS[oc % 4]
                        for kc in range(22):
                            K.mm(pb[:, 0:256], wd[:, kc, oc * 128:(oc + 1) * 128], a_[:, kc, :], kc == 0, kc == 21, [wd, a_], [pb])
                        K.stt("dve", x_[:, oc, :], pb[:, 0:256], modsT[:, oc, 5, row:row + 1], x_[:, oc, :],
                              ALU.mult, ALU.add, [pb, modsT, x_], [x_])
                    K.dma("pool", xview[:, :, t0:t1], x_[:, :, :], reads=[x_], writes=xdeps(t0, t1))
        K.barrier()

    def stage_out(b):
        with ES() as st:
            xb = [K.sb([128, KC, 512], F32, "oxb", st) for _ in range(2)]
            sq = [K.sb([128, 512], BF16, "sq", st) for _ in range(3)]
            rstd = [K.sb([128, 512], F32, "rstd", st) for _ in range(2)]
            yT = [K.sb([128, KC, 512], F32, "yT", st) for _ in range(2)]
            ob = [K.sb([128, D], F32, "ob", st) for _ in range(3)]
            for bi, (t0, t1) in enumerate(BLOCKS[1:]):
                n = t1 - t0
                x_ = xb[bi % 2]
                K.dma("sp", x_[:, :, 0:n], xview[:, :, t0:t1], reads=xdeps(t0, t1), writes=[x_])
                pb = PS[bi % 2]
                for cc in range(KC):
                    s = sq[cc % 3]
                    K.act(s[:, 0:n], x_[:, cc, 0:n], AF.Square, [x_], [s])
                    K.mm(pb[:, 0:n], ones_bf[:, :], s[:, 0:n], cc == 0, cc == KC - 1, [ones_bf, s], [pb])
                r = rstd[bi % 2]
                K.rsqrt(r[:, 0:n], pb[:, 0:n], 1.0 / D, EPS, [pb], [r])
                y = yT[bi % 2]
                for cc in range(KC):
                    K.stt("dve", y[:, cc, 0:n], x_[:, cc, 0:n], gcols[:, 4, cc:cc + 1], r[:, 0:n],
                          ALU.mult, ALU.mult, [x_, gcols, r], [y])
                for ti in range(n // 128):
                    o = ob[ti % 3]
                    for half in range(2):
                        pb2 = PS[2 + (2 * ti + half) % 4]
                        for q in range(4):
                            cc = half * 4 + q
                            K.transpose(pb2[:, q * 128:(q + 1) * 128], y[:, cc, ti * 128:(ti + 1) * 128], ident[:, :], [y, ident], [pb2])
                        K.copy("act", o[:, half * 512:(half + 1) * 512], pb2[:, :], [pb2], [o])
                    tok = t0 - LC + ti * 128
                    K.dma("pool", out_d[b, tok:tok + 128, :], o[:, :], reads=[o], writes=[out_d])
        K.barrier()

    def stage_swa(b, hT):
        win = cdin_d[0, :, :].rearrange("(kc p) n -> p kc n", p=128)
        with ES() as st:
            attT = K.sb([64, 8, LL], BF16, "attT", st)
            wo = K.sb([64, 8, D], BF16, "wo_att", st)
            with ES() as st1:
                wq = K.sb([128, KC, 512], BF16, "wq", st1)
                wqs = K.sb([128, KC, 512], BF16, "wqs", st1)
                wk = K.sb([128, KC, 128], BF16, "wk", st1)
                wks = K.sb([128, KC, 128], BF16, "wks", st1)
                wv = K.sb([128, KC, 128], BF16, "wv", st1)
                cosT = K.sb([64, LL], F32, "cosT", st1)
                sinT = K.sb([64, LL], F32, "sinT", st1)
                qT = K.sb([64, 8, LL], BF16, "qT", st1)
                kT = K.sb([64, 2, T], BF16, "kT", st1)
                vtok = K.sb([128, 18, 2, 128], BF16, "vtok", st1)
                esink = K.sb([128, 8], F32, "esink", st1)
                maskP = K.sb([128, 128], F32, "maskP", st1)
                maskN = K.sb([128, 128], F32, "maskN", st1)
                t1b = [K.sb([64, 512], F32, "rt1", st1) for _ in range(2)]
                t2b = [K.sb([64, 512], F32, "rt2", st1) for _ in range(2)]
                pT = [K.sb([128, 512], BF16, "pT", st1) for _ in range(4)]
                dsum = [K.sb([128, 512], F32, "dsum", st1) for _ in range(2)]
                drec = [K.sb([64, 512], F32, "drec", st1) for _ in range(2)]
                K.memset("pool", vtok[:, :, :, :], 1.0, [vtok])
                stw = ES()
                wstg = [K.sb([128, 2048], F32, "wstg", stw)]
                K.dma("sp", cosT[:, :], swacos_d[:, :], reads=[swacos_d], writes=[cosT])
                K.dma("sp", sinT[:, :], swasin_d[:, :], reads=[swasin_d], writes=[sinT])
                K.dma("sp", maskP[:, :], triB_d[:, :], reads=[triB_d], writes=[maskP])
                K.dma("sp", maskN[:, :], triF_d[:, :], reads=[triF_d], writes=[maskN])
                K.dma("sp", esink[:, :], sink_d[0, :].partition_broadcast(128), reads=[sink_d], writes=[esink])
                K.act(esink[:, :], esink[:, :], AF.Exp, [esink], [esink])
                for j in range(2):
                    load_w_bf16(wq[:, :, j * 256:(j + 1) * 256], win[:, :, CD_Q + j * 256:CD_Q + (j + 1) * 256], (128, KC, 256), cdin_d, wq, wstg, 0)
                load_w_bf16(wk[:, :, :], win[:, :, CD_K:CD_K + 128], (128, KC, 128), cdin_d, wk, wstg, 0)
                load_w_bf16(wv[:, :, :], win[:, :, CD_V:CD_V + 128], (128, KC, 128), cdin_d, wv, wstg, 0)
                for (w_, ws_, nh) in ((wq, wqs, 64), (wk, wks, 16)):
                    wv4 = w_[:, :, :].rearrange("p k (h two d) -> p (k h) two d", two=2, d=32)
                    ws4 = ws_[:, :, :].rearrange("p k (h two d) -> p (k h) two d", two=2, d=32)
                    K.ts("pool", ws4[:, :, 0, :], wv4[:, :, 1, :], -1.0, None, ALU.mult, None, [w_], [ws_])
                    K.copy("pool", ws4[:, :, 1, :], wv4[:, :, 0, :], [w_], [ws_])
                wov = cdout_d[0, 1024:1536, :].rearrange("(h d) n -> d h n", d=64)
                for j in range(4):
                    load_w_bf16(wo[:, j * 2:(j + 1) * 2, :], wov[:, j * 2:(j + 1) * 2, :], (64, 2, D), cdout_d, wo, wstg, 0)
                K.barrier()
                stw.close()
                cnt = 0
                for h in range(8):
                    for j in range(4):
                        t0 = LC + j * 512
                        pa, pb = PS[(cnt * 2) % 4], PS[(cnt * 2 + 1) % 4]
                        for kc in range(KC):
                            K.mm(pa[0:64, :], wq[:, kc, h * 64:(h + 1) * 64], hT[:, kc, t0:t0 + 512], kc == 0, kc == KC - 1, [wq, hT], [pa])
                        for kc in range(KC):
                            K.mm(pb[0:64, :], wqs[:, kc, h * 64:(h + 1) * 64], hT[:, kc, t0:t0 + 512], kc == 0, kc == KC - 1, [wqs, hT], [pb])
                        a_, b_ = t1b[cnt % 2], t2b[cnt % 2]
                        K.tt("dve", a_[:, :], pa[0:64, :], cosT[:, j * 512:(j + 1) * 512], ALU.mult, [pa, cosT], [a_])
                        K.tt("dve", b_[:, :], pb[0:64, :], sinT[:, j * 512:(j + 1) * 512], ALU.mult, [pb, sinT], [b_])
                        K.tt("pool", qT[:, h, j * 512:(j + 1) * 512], a_[:, :], b_[:, :], ALU.add, [a_, b_], [qT])
                        cnt += 1
                for g in range(2):
                    pa = PS[cnt % 4]
                    for kc in range(KC):
                        K.mm(pa[0:64, 0:LC], wk[:, kc, g * 64:(g + 1) * 64], hT[:, kc, 0:LC], kc == 0, kc == KC - 1, [wk, hT], [pa])
                    K.copy("act", kT[:, g, 0:LC], pa[0:64, 0:LC], [pa], [kT])
                    cnt += 1
                    for j in range(4):
                        t0 = LC + j * 512
                        pa, pb = PS[(cnt * 2) % 4], PS[(cnt * 2 + 1) % 4]
                        for kc in range(KC):
                            K.mm(pa[0:64, :], wk[:, kc, g * 64:(g + 1) * 64], hT[:, kc, t0:t0 + 512], kc == 0, kc == KC - 1, [wk, hT], [pa])
                        for kc in range(KC):
                            K.mm(pb[0:64, :], wks[:, kc, g * 64:(g + 1) * 64], hT[:, kc, t0:t0 + 512], kc == 0, kc == KC - 1, [wks, hT], [pb])
                        a_, b_ = t1b[cnt % 2], t2b[cnt % 2]
                        K.tt("dve", a_[:, :], pa[0:64, :], cosT[:, j * 512:(j + 1) * 512], ALU.mult, [pa, cosT], [a_])
                        K.tt("dve", b_[:, :], pb[0:64, :], sinT[:, j * 512:(j + 1) * 512], ALU.mult, [pb, sinT], [b_])
                        K.tt("pool", kT[:, g, t0:t0 + 512], a_[:, :], b_[:, :], ALU.add, [a_, b_], [kT])
                        cnt += 1
                for ti in range(18):
                    pa = PS[ti % 4]
                    for kc in range(KC):
                        K.mm(pa[:, 0:128], hT[:, kc, ti * 128:(ti + 1) * 128], wv[:, kc, :], kc == 0, kc == KC - 1, [hT, wv], [pa])
                    K.copy("act", vtok[:, ti, :, 0:64], pa[:, 0:128].rearrange("p (g e) -> p g e", g=2), [pa], [vtok])
                units = []
                u = 0
                for i in range(16):
                    for g in range(2):
                        keys = [(0, None), (1, None)]
                        if i > 0:
                            keys.append((2 + i - 1, maskP))
                        keys.append((2 + i, None))
                        if i < 15:
                            keys.append((2 + i + 1, maskN))
                        for ki, (kt, mask) in enumerate(keys):
                            units.append((i, g, ki, kt, mask, len(keys), u))
                        u += 1

                def front(un, idx):
                    i, g, ki, kt, mask, nk, uu = un
                    psc = PS[idx % 4]
                    K.mm(psc[:, :].rearrange("p (h q) -> p h q", h=4), kT[:, g, kt * 128:(kt + 1) * 128],
                         qT[:, g * 4:(g + 1) * 4, i * 128:(i + 1) * 128], True, True, [kT, qT], [psc])

                def back(un, idx):
                    i, g, ki, kt, mask, nk, uu = un
                    psc = PS[idx % 4]
                    pacc = PS[4 + (uu % 2) * 2]
                    p_ = pT[idx % 4]
                    K.act(p_[:, :], psc[:, :], AF.Exp, [psc], [p_], scale=0.125)
                    if mask is not None:
                        K.tt("pool", p_[:, :].rearrange("p (h q) -> p h q", h=4), p_[:, :].rearrange("p (h q) -> p h q", h=4),
                             mask[:, :].unsqueeze(1).to_broadcast([128, 4, 128]), ALU.mult, [p_, mask], [p_])
                    K.mm(pacc[:, :], vtok[:, kt, g, :], p_[:, :], ki == 0, ki == nk - 1, [vtok, p_], [pacc])
                    if ki == nk - 1:
                        d_ = dsum[uu % 2]
                        r_ = drec[uu % 2]
                        K.tt("dve", d_[64:128, :].rearrange("p (h q) -> p h q", h=4), pacc[64:128, :].rearrange("p (h q) -> p h q", h=4),
                             esink[64:128, g * 4:(g + 1) * 4].unsqueeze(2).to_broadcast([64, 4, 128]), ALU.add, [pacc, esink], [d_])
                        K.op("dve", lambda e, d_=d_, r_=r_: e.reciprocal(out=r_[0:64, :], in_=d_[64:128, :]), [d_], [r_])
                        K.tt("dve", attT[:, g * 4:(g + 1) * 4, i * 128:(i + 1) * 128], pacc[0:64, :].rearrange("p (h q) -> p h q", h=4),
                             r_[:, :].rearrange("p (h q) -> p h q", h=4), ALU.mult, [pacc, r_], [attT])

                LA = 2
                for idx in range(min(LA, len(units))):
                    front(units[idx], idx)
                for idx, un in enumerate(units):
                    if idx + LA < len(units):
                        front(units[idx + LA], idx + LA)
                    back(un, idx)
            K.barrier()
            pieces = [((lambda oc, h=h: wo[:, h, oc * 128:(oc + 1) * 128]),
                       (lambda t0, t1, h=h: attT[:, h, t0 - LC:t1 - LC]), [wo, attT]) for h in range(8)]
            apply_out(b, pieces, 2, LC)
        K.barrier()

    def stage_ssd(b, hT_scope_fn):
        win = cdin_d[0, :, :].rearrange("(kc p) n -> p kc n", p=128)
        with ES() as st:
            uT = K.sb([128, 8, LL], BF16, "uT", st)
            with ES() as st1:
                xs_tok = K.sb([128, 18, 1024], BF16, "xs_tok", st1)
                B_tok = K.sb([128, 18, 256], BF16, "B_tok", st1)
                BCT = K.sb([128, 4, T], BF16, "BCT", st1)
                dtv = K.sb([128, 18, 32], F32, "dtv", st1)
                dtA = K.sb([128, 18, 32], F32, "dtA", st1)
                a_bc = K.sb([128, 32], F32, "a_bc", st1)
                dtb_bc = K.sb([128, 32], F32, "dtb_bc", st1)
                D_bc = K.sb([128, 16], F32, "D_bc", st1)
                sng = K.sb([128, 8], F32, "sng", st1)
                scw = K.sb([128, 5, 12], F32, "scw", st1)
                scb = K.sb([128, 12], F32, "scb", st1)
                triF = K.sb([128, 128], F32, "triF", st1)
                triB = K.sb([128, 128], F32, "triB", st1)
                strF = K.sb([128, 128], F32, "strF", st1)
                strB = K.sb([128, 128], F32, "strB", st1)
                for (dst, src) in ((triF, triF_d), (triB, triB_d), (strF, strF_d), (strB, strB_d)):
                    K.dma("sp", dst[:, :], src[:, :], reads=[src], writes=[dst])
                K.dma("sp", a_bc[:, :], salog_d[0, :, :].rearrange("a b -> (a b)").partition_broadcast(128), reads=[salog_d], writes=[a_bc])
                K.act(a_bc[:, :], a_bc[:, :], AF.Exp, [a_bc], [a_bc])
                K.ts("dve", a_bc[:, :], a_bc[:, :], -1.0, None, ALU.mult, None, [a_bc], [a_bc])
                K.dma("sp", dtb_bc[:, :], sdtb_d[0, :, :].rearrange("a b -> (a b)").partition_broadcast(128), reads=[sdtb_d], writes=[dtb_bc])
                K.dma("sp", D_bc[:, :], sd_d[0, :].partition_broadcast(128), reads=[sd_d], writes=[D_bc])
                K.dma("sp", sng[:, :], colvec(sng_d[0, :], 8), reads=[sng_d], writes=[sng], allow_slow_non_contiguous=True)
                for tap in range(5):
                    K.dma("sp", scw[:, tap, :], colvec(scw_d[0, tap, :], 12), reads=[scw_d], writes=[scw], allow_slow_non_contiguous=True)
                K.dma("sp", scb[:, :], colvec(scb_d[0, :], 12), reads=[scb_d], writes=[scb], allow_slow_non_contiguous=True)
                with ES() as st2:
                    hT = K.sb([128, KC, T], BF16, "hT", st2)
                    load_hT(hT)
                    wstg = [K.sb([128, 4096], F32, "wstg", st2)]
                    st3 = ES()
                    wch = [K.sb([128, KC, 128], BF16, "wch", st3) for _ in range(2)]
                    PW = T + 8
                    upad = [K.sb([128, PW], F32, "upad", st3) for _ in range(2)]
                    cvb = [K.sb([128, T], F32, "cvb", st3) for _ in range(1)]
                    xsT = [K.sb([128, T], BF16, "xsT", st3) for _ in range(2)]
                    for u_ in upad:
                        K.memset("pool", u_[:, :], 0.0, [u_])

                    def poff(t):
                        return t + 2 if t < LC else t + 6

                    def ssd_wload(fc):
                        w_ = wch[fc % 2]
                        load_w_bf16(w_[:, :, :], win[:, :, CD_XBC + fc * 128:CD_XBC + (fc + 1) * 128], (128, KC, 128), cdin_d, w_, wstg, fc)

                    ssd_wload(0)
                    for fc in range(12):
                        w_ = wch[fc % 2]
                        if fc + 1 < 12:
                            ssd_wload(fc + 1)
                        up = upad[fc % 2]
                        for bi, (t0, t1) in enumerate(BLOCKS):
                            n = t1 - t0
                            pb = PS[bi % 4]
                            for kc in range(KC):
                                K.mm(pb[:, 0:n], w_[:, kc, :], hT[:, kc, t0:t1], kc == 0, kc == KC - 1, [w_, hT], [pb])
                            K.copy("act", up[:, poff(t0):poff(t0) + n], pb[:, 0:n], [pb], [up])
                        cv_ = cvb[0]
                        for (s0, ln) in ((0, LC), (LC, LL)):
                            p0 = poff(s0)
                            K.act(cv_[:, s0:s0 + ln], up[:, p0:p0 + ln], AF.Identity, [up, scw, scb], [cv_],
                                  bias=scb[:, fc:fc + 1], scale=scw[:, 2, fc:fc + 1])
                            for tap in (0, 1, 3, 4):
                                K.stt("dve", cv_[:, s0:s0 + ln], up[:, p0 + tap - 2:p0 + tap - 2 + ln], scw[:, tap, fc:fc + 1], cv_[:, s0:s0 + ln],
                                      ALU.mult, ALU.add, [up, scw, cv_], [cv_])
                        if fc < 8:
                            dstT, dst_ap = xsT[fc % 2], xsT[fc % 2][:, :]
                        else:
                            dstT, dst_ap = BCT, BCT[:, fc - 8, :]
                        K.act(dst_ap, cv_[:, :], AF.Silu, [cv_], [dstT])
                        if fc < 10:
                            for grp in range(3):
                                tis = list(range(grp * 8, min(18, grp * 8 + 8)))
                                pb = PS[4 + grp % 2]
                                pbv = pb[:, :].bitcast(BF16)
                                for qi, ti in enumerate(tis):
                                    K.transpose(pbv[:, qi * 128:(qi + 1) * 128], dst_ap[:, ti * 128:(ti + 1) * 128], identb[:, :], [dstT, identb], [pb])
                                nt = len(tis)
                                src_v = pbv[:, 0:nt * 128].rearrange("p (a f) -> p a f", a=nt)
                                if fc < 8:
                                    K.copy("act", xs_tok[:, tis[0]:tis[0] + nt, fc * 128:(fc + 1) * 128], src_v, [pb], [xs_tok])
                                else:
                                    K.copy("act", B_tok[:, tis[0]:tis[0] + nt, (fc - 8) * 128:(fc - 7) * 128], src_v, [pb], [B_tok])
                    K.barrier()
                    st3.close()
                    wz = K.sb([128, KC, 1024], BF16, "wz", st2)
                    wdt = K.sb([128, KC, 32], BF16, "wdt", st2)
                    szb = [K.sb([128, 1024], BF16, "szb", st2) for _ in range(2)]
                    dtt = [K.sb([128, 32], F32, "dtt", st2) for _ in range(2)]
                    for j in range(2):
                        load_w_bf16(wz[:, :, j * 512:(j + 1) * 512], win[:, :, CD_Z + j * 512:CD_Z + (j + 1) * 512], (128, KC, 512), cdin_d, wz, wstg, j)
                    load_w_bf16(wdt[:, :, :], win[:, :, CD_DT:CD_DT + 32], (128, KC, 32), cdin_d, wdt, wstg, 0)
                    for ti in range(18):
                        pb = PS[ti % 4]
                        for kc in range(KC):
                            K.mm(pb[:, 0:32], hT[:, kc, ti * 128:(ti + 1) * 128], wdt[:, kc, :], kc == 0, kc == KC - 1, [hT, wdt], [pb])
                        d_ = dtt[ti % 2]
                        K.tt("dve", d_[:, :], pb[:, 0:32], dtb_bc[:, :], ALU.add, [pb, dtb_bc], [d_])
                        K.act(d_[:, :], d_[:, :], AF.Exp, [d_], [d_])
                        K.act(dtv[:, ti, :], d_[:, :], AF.Ln, [d_], [dtv], bias=K.eps_ap(1.0))
                    K.tt("dve", dtA[:, :, :], dtv[:, :, :], a_bc[:, :].unsqueeze(1).to_broadcast([128, 18, 32]), ALU.mult, [dtv, a_bc], [dtA])
                    for li in range(16):
                        ti = li + 2
                        s_ = szb[li % 2]
                        for j in range(2):
                            pb = PS[(li * 2 + j) % 4]
                            for kc in range(KC):
                                K.mm(pb[:, :], hT[:, kc, ti * 128:(ti + 1) * 128], wz[:, kc, j * 512:(j + 1) * 512], kc == 0, kc == KC - 1, [hT, wz], [pb])
                            K.act(s_[:, j * 512:(j + 1) * 512], pb[:, :], AF.Silu, [pb], [s_])
                        K.dma("pool", sz_d[li, :, :], s_[:, :], reads=[s_], writes=[sz_d])
                K.barrier()
                with ES() as st2:
                    Hf = K.sb([128, 2, 512], F32, "Hf", st2)
                    Hb = K.sb([128, 2, 512], F32, "Hb", st2)
                    hbf = [K.sb([128, 1024], BF16, "hbf", st2) for _ in range(2)]
                    hinf = [K.sb([128, 1024], BF16, "hinf", st2) for _ in range(2)]
                    szt = [K.sb([128, 1024], BF16, "szt", st2) for _ in range(2)]
                    prep = {}
                    for nm in ("acs0", "eac0", "dend0", "cdec0", "wgt0", "acs1", "eac1", "dend1", "cdec1", "wgt1"):
                        prep[nm] = K.sb([128, 16], F32, nm, st2)
                    xte = K.sb([128, 1024], BF16, "xte", st2)
                    rseg = K.sb([128, 16, 128], F32, "rseg", st2)
                    cbm = [K.sb([128, 2, 128], F32, "cbm", st2) for _ in range(2)]
                    eseg = [K.sb([128, 512], F32, "eseg", st2) for _ in range(2)]
                    Lt = [K.sb([128, 16, 128], BF16, "Lt", st2) for _ in range(2)]
                    xdt = [K.sb([128, 1024], BF16, "xdt", st2) for _ in range(2)]
                    yacc = K.sb([128, 1024], F32, "yacc", st2)
                    ytmp = K.sb([128, 512], F32, "ytmp", st2)
                    ub = K.sb([128, 1024], F32, "ub", st2)
                    ubf = K.sb([128, 1024], BF16, "ubf", st2)
                    ssq = K.sb([128, 2], F32, "ssq", st2)
                    junk = K.sb([128, 512], BF16, "junk", st2)
                    K.memset("dve", Hf[:, :, :], 0.0, [Hf])
                    K.memset("dve", Hb[:, :, :], 0.0, [Hb])
                    tri = (triF, triB)
                    strm = (strF, strB)

                    def do_prep(c, d):
                        pp = PS[0]
                        K.mm(pp[:, 0:16], tri[d][:, :], dtA[:, c, d * 16:(d + 1) * 16], True, True, [tri[d], dtA], [pp])
                        K.mm(pp[:, 16:32], ones_f[:, :], dtA[:, c, d * 16:(d + 1) * 16], True, True, [ones_f, dtA], [pp])
                        acs, eac, dend, cdec, wgt = (prep[n_ + str(d)] for n_ in ("acs", "eac", "dend", "cdec", "wgt"))
                        K.copy("act", acs[:, :], pp[:, 0:16], [pp], [acs])
                        K.act(eac[:, :], pp[:, 0:16], AF.Exp, [pp], [eac])
                        K.act(cdec[:, :], pp[:, 16:32], AF.Exp, [pp], [cdec])
                        K.tt("dve", dend[:, :], pp[:, 16:32], acs[:, :], ALU.subtract, [pp, acs], [dend])
                        K.act(dend[:, :], dend[:, :], AF.Exp, [dend], [dend])
                        K.tt("dve", wgt[:, :], dend[:, :], dtv[:, c, d * 16:(d + 1) * 16], ALU.mult, [dend, dtv], [wgt])

                    def state_update(c, d, H):
                        wgt, cdec = prep["wgt" + str(d)], prep["cdec" + str(d)]
                        K.tt("pool", xte[:, :].rearrange("p (h e) -> p h e", h=16), xs_tok[:, c, :].rearrange("p (h e) -> p h e", h=16),
                             wgt[:, :].unsqueeze(2).to_broadcast([128, 16, 64]), ALU.mult, [xs_tok, wgt], [xte])
                        for g in range(2):
                            pb = PS[6 + g]
                            K.mm(pb[:, :], B_tok[:, c, g * 128:(g + 1) * 128], xte[:, g * 512:(g + 1) * 512], True, True, [B_tok, xte], [pb])
                            hv = H[:, g, :].rearrange("p (h e) -> p h e", h=8)
                            K.tt("pool", hv, hv, cdec[:, g * 8:(g + 1) * 8].unsqueeze(2).to_broadcast([128, 8, 64]), ALU.mult, [H, cdec], [H])
                            K.tt("dve", H[:, g, :], H[:, g, :], pb[:, :], ALU.add, [H, pb], [H])

                    for c in range(18):
                        if c >= 2:
                            hb_ = hbf[c % 2]
                            K.copy("act", hb_[:, :], Hf[:, :, :].rearrange("p g e -> p (g e)"), [Hf], [hb_])
                            K.dma("pool", hin_d[c - 2, :, :], hb_[:, :], reads=[hb_], writes=[hin_d])
                        if c == 17:
                            break
                        do_prep(c, 0)
                        state_update(c, 0, Hf)
                    K.barrier()
                    for c in [1, 0] + list(range(17, 1, -1)):
                        do_prep(c, 1)
                        if c >= 2:
                            li = c - 2
                            do_prep(c, 0)
                            hi_ = hinf[li % 2]
                            sz_ = szt[li % 2]
                            K.dma("sp", hi_[:, :], hin_d[li, :, :], reads=[hin_d], writes=[hi_])
                            K.dma("sp", sz_[:, :], sz_d[li, :, :], reads=[sz_d], writes=[sz_])
                            hb_ = hbf[li % 2]
                            K.copy("act", hb_[:, :], Hb[:, :, :].rearrange("p g e -> p (g e)"), [Hb], [hb_])
                            tsl = slice(c * 128, (c + 1) * 128)
                            pcb = PS[1]
                            for g in range(2):
                                K.mm(pcb[:, g * 128:(g + 1) * 128], BCT[:, g, tsl], BCT[:, 2 + g, tsl], True, True, [BCT], [pcb])
                            for d in range(2):
                                K.tt("dve", cbm[d][:, :, :], pcb[:, 0:256].rearrange("p (g q) -> p g q", g=2),
                                     tri[d][:, :].unsqueeze(1).to_broadcast([128, 2, 128]), ALU.mult, [pcb, tri[d]], [cbm[d]])
                            for d in range(2):
                                K.tt("dve", rseg[:, :, :], tri[d][:, :].unsqueeze(1).to_broadcast([128, 16, 128]),
                                     dtA[:, c, d * 16:(d + 1) * 16].unsqueeze(2).to_broadcast([128, 16, 128]), ALU.mult, [tri[d], dtA], [rseg])
                                for hb4 in range(4):
                                    pseg = PS[2 + hb4 % 2]
                                    K.mm(pseg[:, :], strm[d][:, :], rseg[:, hb4 * 4:(hb4 + 1) * 4, :], True, True, [strm[d], rseg], [pseg])
                                    es_ = eseg[hb4 % 2]
                                    K.act(es_[:, :], pseg[:, :], AF.Exp, [pseg], [es_])
                                    g = hb4 // 2
                                    K.tt("dve" if hb4 % 2 == 0 else "pool", Lt[d][:, hb4 * 4:(hb4 + 1) * 4, :], es_[:, :].rearrange("p (h q) -> p h q", h=4),
                                         cbm[d][:, g, :].unsqueeze(1).to_broadcast([128, 4, 128]), ALU.mult, [es_, cbm[d]], [Lt[d]])
                                K.tt("dve", xdt[d][:, :].rearrange("p (h e) -> p h e", h=16), xs_tok[:, c, :].rearrange("p (h e) -> p h e", h=16),
                                     dtv[:, c, d * 16:(d + 1) * 16].unsqueeze(2).to_broadcast([128, 16, 64]), ALU.mult, [xs_tok, dtv], [xdt[d]])
                            for h in range(16):
                                py = PS[4 + h // 8]
                                col = (h % 8) * 64
                                for d in range(2):
                                    K.mm(py[:, col:col + 64], Lt[d][:, h, :], xdt[d][:, h * 64:(h + 1) * 64], d == 0, d == 1, [Lt[d], xdt[d]], [py])
                            K.tt("pool", yacc[:, :].rearrange("p (h e) -> p h e", h=16), xs_tok[:, c, :].rearrange("p (h e) -> p h e", h=16),
                                 D_bc[:, :].unsqueeze(2).to_broadcast([128, 16, 64]), ALU.mult, [xs_tok, D_bc], [yacc])
                            for g in range(2):
                                K.tt("dve", yacc[:, g * 512:(g + 1) * 512], yacc[:, g * 512:(g + 1) * 512], PS[4 + g][:, :], ALU.add, [yacc, PS[4 + g]], [yacc])
                            for d in range(2):
                                hsrc = hi_ if d == 0 else hb_
                                eac = prep["eac" + str(d)]
                                for g in range(2):
                                    po = PS[6 + g]
                                    K.mm(po[:, :], BCT[:, 2 + g, tsl], hsrc[:, g * 512:(g + 1) * 512], True, True, [BCT, hsrc], [po])
                                    K.tt("dve", ytmp[:, :].rearrange("p (h e) -> p h e", h=8), po[:, :].rearrange("p (h e) -> p h e", h=8),
                                         eac[:, g * 8:(g + 1) * 8].unsqueeze(2).to_broadcast([128, 8, 64]), ALU.mult, [po, eac], [ytmp])
                                    K.tt("pool", yacc[:, g * 512:(g + 1) * 512], yacc[:, g * 512:(g + 1) * 512], ytmp[:, :], ALU.add, [yacc, ytmp], [yacc])
                            K.tt("dve", ub[:, :], yacc[:, :], sz_[:, :], ALU.mult, [yacc, sz_], [ub])
                            K.memset("pool", ssq[:, :], 0.0, [ssq])
                            for g in range(2):
                                K.op("act", lambda e, g=g: e.activation(out=junk[:, :], in_=ub[:, g * 512:(g + 1) * 512], func=AF.Square,
                                                                         accum_out=ssq[:, g:g + 1]), [ub], [junk, ssq])
                            K.rsqrt(ssq[:, :], ssq[:, :], 1.0 / 512, EPS, [ssq], [ssq])
                            for g in range(2):
                                K.ts("dve", ubf[:, g * 512:(g + 1) * 512], ub[:, g * 512:(g + 1) * 512], ssq[:, g:g + 1], None, ALU.mult, None, [ub, ssq], [ubf])
                            pt = PS[1]
                            ptv = pt[:, :].bitcast(BF16)
                            for cc in range(8):
                                K.transpose(ptv[:, cc * 128:(cc + 1) * 128], ubf[:, cc * 128:(cc + 1) * 128], identb[:, :], [ubf, identb], [pt])
                            for cc in range(8):
                                K.act(uT[:, cc, li * 128:(li + 1) * 128], ptv[:, cc * 128:(cc + 1) * 128], AF.Identity, [pt, sng], [uT], scale=sng[:, cc:cc + 1])
                        if c != 2:
                            state_update(c, 1, Hb)
            K.barrier()
            wo = K.sb([128, 8, D], BF16, "wo_ssd", st)
            wostg = [K.sb([128, 4096], F32, "wostg", st)]
            for j in range(2):
                load_w_bf16(wo[:, j * 4:(j + 1) * 4, :], cdout_d[0, 0:1024, :].rearrange("(kc p) n -> p kc n", p=128)[:, j * 4:(j + 1) * 4, :],
                            (128, 4, D), cdout_d, wo, wostg, 0)
            pieces = [((lambda oc, cc=cc: wo[:, cc, oc * 128:(oc + 1) * 128]),
                       (lambda t0, t1, cc=cc: uT[:, cc, t0 - LC:t1 - LC]), [wo, uT]) for cc in range(8)]
            apply_out(b, pieces, 2, LC)
        K.barrier()


    AB_CQ, AB_CKV, AB_KR, AB_RW = 0, 384, 640, 672

    def stage_mla(b, hT):
        win = abin_d[0, :, :].rearrange("(kc p) n -> p kc n", p=128)
        sc = 96.0 ** -0.5
        with ES() as st:
            attT = K.sb([64, 8, T], BF16, "mattT", st)
            wo = K.sb([64, 8, D], BF16, "wo_mla", st)
            with ES() as st1:
                wcq = K.sb([128, KC, 384], BF16, "wcq", st1)
                wckv = K.sb([128, KC, 256], BF16, "wckv", st1)
                wkr = K.sb([128, KC, 96], BF16, "wkr", st1)
                wkrs = K.sb([128, KC, 96], BF16, "wkrs", st1)
                wqu = K.sb([128, 3, 768], BF16, "wqu", st1)
                wqus = K.sb([128, 24, 96], BF16, "wqus", st1)
                wkvu = K.sb([128, 2, 1024], BF16, "wkvu", st1)
                cqn = K.sb([128, 3, T], BF16, "cqn", st1)
                ckvn = K.sb([128, 2, T], BF16, "ckvn", st1)
                vo = K.sb([128, 18, 128], BF16, "mvo", st1)
                cosT = K.sb([96, LL], F32, "mcos", st1)
                sinT = K.sb([96, LL], F32, "msin", st1)
                gq = K.sb([128, 3], F32, "gq", st1)
                gkv = K.sb([128, 2], F32, "gkv", st1)
                qf = K.sb([96, T], BF16, "qf", st1)
                kf = K.sb([96, T], BF16, "kf", st1)
                sq = [K.sb([128, 512], BF16, "msq", st1) for _ in range(2)]
                rstd = [K.sb([128, 512], F32, "mrstd", st1) for _ in range(2)]
                t1b = [K.sb([96, 512], F32, "mt1", st1) for _ in range(1)]
                t2b = [K.sb([96, 512], F32, "mt2", st1) for _ in range(1)]
                pT = [K.sb([128, 512], BF16, "mpT", st1) for _ in range(4)]
                rec = [K.sb([64, 512], F32, "mrec", st1) for _ in range(2)]
                stw = ES()
                wstg = [K.sb([128, 4096], F32, "wstg", stw)]
                K.dma("sp", cosT[64:96, :], mlacos_d[:, :], reads=[mlacos_d], writes=[cosT])
                K.dma("sp", sinT[64:96, :], mlasin_d[:, :], reads=[mlasin_d], writes=[sinT])
                K.dma("sp", gq[:, :], colvec(mqg_d[0, :], 3), reads=[mqg_d], writes=[gq], allow_slow_non_contiguous=True)
                K.dma("sp", gkv[:, :], colvec(mkg_d[0, :], 2), reads=[mkg_d], writes=[gkv], allow_slow_non_contiguous=True)
                K.memset("pool", wkr[:, :, :], 0.0, [wkr])
                K.memset("pool", wkrs[:, :, :], 0.0, [wkrs])
                K.memset("pool", wqus[:, :, :], 0.0, [wqus])
                K.memset("pool", vo[:, :, :], 1.0, [vo])
                load_w_bf16(wcq[:, :, :], win[:, :, AB_CQ:AB_CQ + 384], (128, KC, 384), abin_d, wcq, wstg, 0)
                load_w_bf16(wckv[:, :, :], win[:, :, AB_CKV:AB_CKV + 256], (128, KC, 256), abin_d, wckv, wstg, 0)
                load_w_bf16(wkr[:, :, 64:96], win[:, :, AB_KR:AB_KR + 32], (128, KC, 32), abin_d, wkr, wstg, 0)
                load_w_bf16(wqu[:, :, :], mqu_d[0, :, :].rearrange("(c p) n -> p c n", p=128), (128, 3, 768), mqu_d, wqu, wstg, 0)
                load_w_bf16(wkvu[:, :, :], mkvu_d[0, :, :].rearrange("(c p) n -> p c n", p=128), (128, 2, 1024), mkvu_d, wkvu, wstg, 0)
                wov = about_d[0, 0:512, :].rearrange("(h d) n -> d h n", d=64)
                for j in range(2):
                    load_w_bf16(wo[:, j * 4:(j + 1) * 4, :], wov[:, j * 4:(j + 1) * 4, :], (64, 4, D), about_d, wo, wstg, 0)
                K.ts("pool", wkrs[:, :, 64:80], wkr[:, :, 80:96], -1.0, None, ALU.mult, None, [wkr], [wkrs])
                K.copy("pool", wkrs[:, :, 80:96], wkr[:, :, 64:80], [wkr], [wkrs])
                wq24 = wqu[:, :, :].rearrange("p c (h e) -> p (c h) e", e=96)
                K.ts("pool", wqus[:, :, 64:80], wq24[:, :, 80:96], -1.0, None, ALU.mult, None, [wqu], [wqus])
                K.copy("pool", wqus[:, :, 80:96], wq24[:, :, 64:80], [wqu], [wqus])
                K.barrier()
                stw.close()
                for bi, (t0, t1) in enumerate(BLOCKS):
                    n = t1 - t0
                    for (w_, nch, g_, dst, pbase) in ((wcq, 3, gq, cqn, 0), (wckv, 2, gkv, ckvn, 4)):
                        for c3 in range(nch):
                            pb = PS[pbase + c3]
                            for kc in range(KC):
                                K.mm(pb[:, 0:n], w_[:, kc, c3 * 128:(c3 + 1) * 128], hT[:, kc, t0:t1], kc == 0, kc == KC - 1, [w_, hT], [pb])
                        pst = PS[pbase + 3] if pbase == 0 else PS[pbase + 2]
                        for c3 in range(nch):
                            s_ = sq[c3 % 2]
                            K.act(s_[:, 0:n], PS[pbase + c3][:, 0:n], AF.Square, [PS[pbase + c3]], [s_])
                            K.mm(pst[:, 0:n], ones_bf[:, :], s_[:, 0:n], c3 == 0, c3 == nch - 1, [ones_bf, s_], [pst])
                        r_ = rstd[0 if pbase == 0 else 1]
                        K.rsqrt(r_[:, 0:n], pst[:, 0:n], 1.0 / (nch * 128), EPS, [pst], [r_])
                        for c3 in range(nch):
                            K.stt("dve", dst[:, c3, t0:t1], PS[pbase + c3][:, 0:n], g_[:, c3:c3 + 1], r_[:, 0:n], ALU.mult, ALU.mult,
                                  [PS[pbase + c3], g_, r_], [dst])
                R_ = slice(64, 96)
                for bi, (t0, t1) in enumerate(BLOCKS):
                    n = t1 - t0
                    pa, pb = PS[0 + 2 * (bi % 2)], PS[1 + 2 * (bi % 2)]
                    for kc in range(KC):
                        K.mm(pa[0:96, 0:n], wkr[:, kc, :], hT[:, kc, t0:t1], kc == 0, kc == KC - 1, [wkr, hT], [pa])
                    if bi == 0:
                        K.copy("dve", kf[R_, t0:t1], pa[R_, 0:n], [pa], [kf])
                    else:
                        for kc in range(KC):
                            K.mm(pb[0:96, 0:n], wkrs[:, kc, :], hT[:, kc, t0:t1], kc == 0, kc == KC - 1, [wkrs, hT], [pb])
                        a_, b_ = t1b[0], t2b[0]
                        K.tt("dve", a_[R_, 0:n], pa[R_, 0:n], cosT[R_, t0 - LC:t1 - LC], ALU.mult, [pa, cosT], [a_])
                        K.tt("dve", b_[R_, 0:n], pb[R_, 0:n], sinT[R_, t0 - LC:t1 - LC], ALU.mult, [pb, sinT], [b_])
                        K.tt("pool", kf[R_, t0:t1], a_[R_, 0:n], b_[R_, 0:n], ALU.add, [a_, b_], [kf])
                u = 0
                for h in range(8):
                    for ti in range(18):
                        pa = PS[4 + ti % 2]
                        for c3 in range(2):
                            K.mm(pa[:, 0:64], ckvn[:, c3, ti * 128:(ti + 1) * 128], wkvu[:, c3, h * 128 + 64:h * 128 + 128],
                                 c3 == 0, c3 == 1, [ckvn, wkvu], [pa])
                        K.copy("dve", vo[:, ti, 0:64], pa[:, 0:64], [pa], [vo])
                    for bi, (t0, t1) in enumerate(BLOCKS):
                        n = t1 - t0
                        pa, pc, pd = PS[0], PS[2], PS[3]
                        for c3 in range(3):
                            K.mm(pa[0:96, 0:n], wqu[:, c3, h * 96:h * 96 + 96], cqn[:, c3, t0:t1], c3 == 0, c3 == 2, [wqu, cqn], [pa])
                        K.copy("act", qf[0:64, t0:t1], pa[0:64, 0:n], [pa], [qf])
                        if bi == 0:
                            K.copy("dve", qf[R_, t0:t1], pa[R_, 0:n], [pa], [qf])
                        else:
                            for c3 in range(3):
                                K.mm(pc[0:96, 0:n], wqus[:, c3 * 8 + h, :], cqn[:, c3, t0:t1], c3 == 0, c3 == 2, [wqus, cqn], [pc])
                            a_, b_ = t1b[0], t2b[0]
                            K.tt("dve", a_[R_, 0:n], pa[R_, 0:n], cosT[R_, t0 - LC:t1 - LC], ALU.mult, [pa, cosT], [a_])
                            K.tt("dve", b_[R_, 0:n], pc[R_, 0:n], sinT[R_, t0 - LC:t1 - LC], ALU.mult, [pc, sinT], [b_])
                            K.tt("pool", qf[R_, t0:t1], a_[R_, 0:n], b_[R_, 0:n], ALU.add, [a_, b_], [qf])
                        for c3 in range(2):
                            K.mm(pd[0:64, 0:n], wkvu[:, c3, h * 128:h * 128 + 64], ckvn[:, c3, t0:t1], c3 == 0, c3 == 1, [wkvu, ckvn], [pd])
                        K.copy("act", kf[0:64, t0:t1], pd[0:64, 0:n], [pd], [kf])
                    units = []
                    for bi, (t0, t1) in enumerate(BLOCKS):
                        keys = [0, 1] if bi == 0 else list(range(18))
                        for ki, kt in enumerate(keys):
                            units.append((bi, t0, t1, ki, kt, len(keys), u))
                        u += 1

                    def front(un, idx):
                        bi, t0, t1, ki, kt, nk, uu = un
                        n = t1 - t0
                        psc = PS[idx % 4]
                        K.mm(psc[:, 0:n], kf[:, kt * 128:(kt + 1) * 128], qf[:, t0:t1], True, True, [kf, qf], [psc])

                    def back(un, idx, h=h):
                        bi, t0, t1, ki, kt, nk, uu = un
                        n = t1 - t0
                        psc = PS[idx % 4]
                        pacc = PS[4 + (uu % 2) * 2]
                        p_ = pT[idx % 4]
                        K.act(p_[:, 0:n], psc[:, 0:n], AF.Exp, [psc], [p_], scale=sc)
                        K.mm(pacc[:, 0:n], vo[:, kt, :], p_[:, 0:n], ki == 0, ki == nk - 1, [vo, p_], [pacc])
                        if ki == nk - 1:
                            r_ = rec[uu % 2]
                            K.op("dve", lambda e, r_=r_, pacc=pacc, n=n: e.reciprocal(out=r_[0:64, 0:n], in_=pacc[64:128, 0:n]), [pacc], [r_])
                            K.tt("dve", attT[:, h, t0:t1], pacc[0:64, 0:n], r_[:, 0:n], ALU.mult, [pacc, r_], [attT])

                    LA = 2
                    for idx in range(min(LA, len(units))):
                        front(units[idx], idx)
                    for idx, un in enumerate(units):
                        if idx + LA < len(units):
                            front(units[idx + LA], idx + LA)
                        back(un, idx)
            K.barrier()
            pieces = [((lambda oc, h=h: wo[:, h, oc * 128:(oc + 1) * 128]),
                       (lambda t0, t1, h=h: attT[:, h, t0:t1]), [wo, attT]) for h in range(8)]
            apply_out(b, pieces, 2, 0)
        K.barrier()

    CW = -math.exp(-0.5)

    def stage_rwkv(b):
        win = abin_d[0, :, :].rearrange("(kc p) n -> p kc n", p=128)
        with ES() as st:
            cols = K.sb([128, 10, 4], F32, "rcols", st)
            for i_, src in enumerate((rkk_d[0, :], rka_d[0, :], None, rrk_d[0, :, :].rearrange("a b -> (a b)"), rlg_d[0, :], rlb_d[0, :],
                                      None, ra0_d[0, 0, :], ra0_d[0, 1, :])):
                if src is not None:
                    K.dma("sp", cols[:, i_, :], colvec(src, 4), reads=[rkk_d], writes=[cols], allow_slow_non_contiguous=True)
            K.ts("dve", cols[:, 2, :], cols[:, 1, :], -1.0, 1.0, ALU.mult, ALU.add, [cols], [cols])
            with ES() as st1:
                rT = K.sb([128, 4, T], BF16, "rT", st1)
                kT = K.sb([128, 4, T], BF16, "kT", st1)
                kknT = K.sb([128, 4, T], BF16, "kknT", st1)
                vtok = K.sb([128, 18, 512], BF16, "rvtok", st1)
                xwaT = K.sb([128, T], BF16, "xwaT", st1)
                mu = K.sb([128, 3, 14], F32, "mu", st1)
                blk64 = K.sb([128, 128], BF16, "blk64", st1)
                blk64f = K.sb([128, 128], F32, "blk64f", st1)
                K.dma("sp", blk64f[:, :], blk64_d[:, :], reads=[blk64_d], writes=[blk64f])
                K.copy("dve", blk64[:, :], blk64f[:, :], [blk64f], [blk64])
                K.dma("sp", mu[:, 0, :], colvec(rmp_d[0, :], 14), reads=[rmp_d], writes=[mu], allow_slow_non_contiguous=True)
                K.dma("sp", mu[:, 1, :], colvec(rmn_d[0, :], 14), reads=[rmn_d], writes=[mu], allow_slow_non_contiguous=True)
                K.tt("dve", mu[:, 2, :], mu[:, 0, :], mu[:, 1, :], ALU.add, [mu], [mu])
                K.ts("dve", mu[:, 2, :], mu[:, 2, :], -1.0, 1.0, ALU.mult, ALU.add, [mu], [mu])
                with ES() as st2:
                    hT = K.sb([128, KC, T], BF16, "hT", st2)
                    load_hT(hT)
                    wstg = [K.sb([128, 4096], F32, "wstg", st2)]
                    wch = [K.sb([128, KC, 128], BF16, "rwch", st2) for _ in range(2)]
                    g2b = K.sb([128, 512], BF16, "g2b", st2)
                    upad = [K.sb([128, T + 4], F32, "rupad", st2) for _ in range(2)]
                    xs = [K.sb([128, T], F32, "rxs", st2) for _ in range(1)]
                    xsb = [K.sb([128, T], BF16, "rxsb", st2) for _ in range(1)]
                    t32 = [K.sb([128, 512], F32, "rt32", st2) for _ in range(2)]
                    t16 = [K.sb([128, 512], BF16, "rt16", st2) for _ in range(2)]
                    gbo = [K.sb([128, T], BF16, "gbo", st2) for _ in range(1)]
                    rk32 = [K.sb([128, 512], F32, "rk32", st2) for _ in range(2)]
                    for u_ in upad:
                        K.memset("pool", u_[:, :], 0.0, [u_])
                    load_w_bf16(g2b[:, :], rg2_d[0, :, :], (128, 512), rg2_d, g2b, wstg, 0)

                    def poff(t):
                        return t + 1 if t < LC else t + 3

                    order = [4, 5, 6, 7, 0, 1, 2, 3, 8, 9, 10, 11, 12, 13]
                    def rw_wload(oi):
                        fc_ = order[oi]
                        w_ = wch[oi % 2]
                        load_w_bf16(w_[:, :, :], win[:, :, AB_RW + fc_ * 128:AB_RW + (fc_ + 1) * 128], (128, KC, 128), abin_d, w_, wstg, 0)

                    rw_wload(0)
                    for oi, fc in enumerate(order):
                        w_ = wch[oi % 2]
                        if oi + 1 < len(order):
                            rw_wload(oi + 1)
                        up = upad[oi % 2]
                        for bi, (t0, t1) in enumerate(BLOCKS):
                            n = t1 - t0
                            pb = PS[bi % 4]
                            for kc in range(KC):
                                K.mm(pb[:, 0:n], w_[:, kc, :], hT[:, kc, t0:t1], kc == 0, kc == KC - 1, [w_, hT], [pb])
                            K.copy("act", up[:, poff(t0):poff(t0) + n], pb[:, 0:n], [pb], [up])
                        x_ = xs[0]
                        for (s0, ln) in ((0, LC), (LC, LL)):
                            p0 = poff(s0)
                            K.act(x_[:, s0:s0 + ln], up[:, p0:p0 + ln], AF.Identity, [up, mu], [x_], scale=mu[:, 2, fc:fc + 1])
                            K.stt("dve", x_[:, s0:s0 + ln], up[:, p0 - 1:p0 - 1 + ln], mu[:, 0, fc:fc + 1], x_[:, s0:s0 + ln], ALU.mult, ALU.add, [up, mu, x_], [x_])
                            K.stt("dve", x_[:, s0:s0 + ln], up[:, p0 + 1:p0 + 1 + ln], mu[:, 1, fc:fc + 1], x_[:, s0:s0 + ln], ALU.mult, ALU.add, [up, mu, x_], [x_])
                        if fc < 4:
                            c4 = fc
                            K.copy("act", rT[:, c4, :], x_[:, :], [x_], [rT])
                            for bi, (t0, t1) in enumerate(BLOCKS):
                                n = t1 - t0
                                a_, b_ = t32[bi % 2], t16[bi % 2]
                                K.stt("dve", b_[:, 0:n], x_[:, t0:t1], cols[:, 3, c4:c4 + 1], kT[:, c4, t0:t1], ALU.mult, ALU.mult, [x_, cols, kT], [b_])
                                pb = PS[4 + bi % 2]
                                K.mm(pb[:, 0:n], blk64[:, :], b_[:, 0:n], True, True, [blk64, b_], [pb])
                                K.copy("act", gbo[0][:, t0:t1], pb[:, 0:n], [pb], [gbo[0]])
                            K.dma("pool", gb_d[1, c4, :, :], gbo[0][:, :], reads=[gbo[0]], writes=[gb_d])
                        elif fc < 8:
                            c4 = fc - 4
                            K.copy("act", kT[:, c4, :], x_[:, :], [x_], [kT])
                            for bi, (t0, t1) in enumerate(BLOCKS):
                                n = t1 - t0
                                a_, b_ = t32[bi % 2], t16[bi % 2]
                                K.ts("dve", a_[:, 0:n], x_[:, t0:t1], cols[:, 0, c4:c4 + 1], None, ALU.mult, None, [x_, cols], [a_])
                                K.act(b_[:, 0:n], a_[:, 0:n], AF.Square, [a_], [b_])
                                pb = PS[4 + bi % 2]
                                K.mm(pb[:, 0:n], blk64[:, :], b_[:, 0:n], True, True, [blk64, b_], [pb])
                                r_ = rk32[bi % 2]
                                K.rsqrt(r_[:, 0:n], pb[:, 0:n], 1.0, 1e-12, [pb], [r_])
                                K.tt("pool", kknT[:, c4, t0:t1], a_[:, 0:n], r_[:, 0:n], ALU.mult, [a_, r_], [kknT])
                        elif fc < 12:
                            c4 = fc - 8
                            xb_ = xsb[0]
                            K.copy("act", xb_[:, :], x_[:, :], [x_], [xb_])
                            for grp in range(3):
                                tis = list(range(grp * 8, min(18, grp * 8 + 8)))
                                pb = PS[4 + grp % 2]
                                pbv = pb[:, :].bitcast(BF16)
                                for qi, ti in enumerate(tis):
                                    K.transpose(pbv[:, qi * 128:(qi + 1) * 128], xb_[:, ti * 128:(ti + 1) * 128], identb[:, :], [xb_, identb], [pb])
                                nt = len(tis)
                                K.copy("dve", vtok[:, tis[0]:tis[0] + nt, c4 * 128:(c4 + 1) * 128],
                                       pbv[:, 0:nt * 128].rearrange("p (a f) -> p a f", a=nt), [pb], [vtok])
                            K.dma("pool", gb_d[0, c4, :, :], xb_[:, :], reads=[xb_], writes=[gb_d])
                        elif fc == 12:
                            K.act(xwaT[0:64, :], x_[0:64, :], AF.Tanh, [x_], [xwaT])
                            K.copy("act", xwaT[64:128, :], x_[64:128, :], [x_], [xwaT])
                        else:
                            xb_ = xsb[0]
                            K.act(xb_[:, :], x_[:, :], AF.Sigmoid, [x_], [xb_])
                            for c4 in range(4):
                                for bi, (t0, t1) in enumerate(BLOCKS):
                                    n = t1 - t0
                                    pb = PS[bi % 4]
                                    K.mm(pb[:, 0:n], g2b[:, c4 * 128:(c4 + 1) * 128], xb_[:, t0:t1], True, True, [g2b, xb_], [pb])
                                    K.copy("act", gbo[0][:, t0:t1], pb[:, 0:n], [pb], [gbo[0]])
                                K.dma("pool", gb_d[2, c4, :, :], gbo[0][:, :], reads=[gbo[0]], writes=[gb_d])
                K.barrier()
                with ES() as st2:
                    w2b = K.sb([64, 2, 512], BF16, "w2b", st2)
                    a2b = K.sb([128, 2, 512], BF16, "a2b", st2)
                    w0bc = K.sb([128, 2, 512], F32, "w0bc", st2)
                    lcm = K.sb([128, 2, 128], F32, "lcm", st2)
                    lexcm = K.sb([128, 2, 128], F32, "lexcm", st2)
                    mcol = K.sb([128, 2, 2], F32, "mcol", st2)
                    m1 = K.sb([128, 2, 128], F32, "m1", st2)
                    m3 = K.sb([128, 2, 384], F32, "m3", st2)
                    m1t = K.sb([128, 2, 128], F32, "m1t", st2)
                    for dst, src in ((lcm, rwlc_d), (lexcm, rwlexc_d), (mcol, rwmcol_d), (m1, rwm1_d), (m3, rwm3_d), (m1t, rwm1t_d)):
                        for d in range(2):
                            K.dma("sp", dst[:, d, :], src[d, :, :], reads=[src], writes=[dst])
                    with ES() as stw:
                        wstg = [K.sb([128, 1024], F32, "wstg", stw)]
                        for d in range(2):
                            load_w_bf16(w2b[:, d, :], rw2_d[0, d, :, :], (64, 512), rw2_d, w2b, wstg, 0)
                            s_ = wstg[0]
                            K.dma("sp", s_[64:128, 0:512], ra2_d[0, d, :, :], reads=[ra2_d], writes=[s_])
                            K.copy("pool", a2b[64:128, d, :], s_[64:128, 0:512], [s_], [a2b])
                            K.dma("sp", w0bc[:, d, :], rw0_d[0, d, :].partition_broadcast(128), reads=[rw0_d], writes=[w0bc])
                        K.barrier()

                    def dir_stream(d):
                        B = PS[4 * d:4 * d + 4]
                        sg = K.sb([128, 512], F32, "sg", st2)
                        aT = K.sb([128, 128], F32, "aT", st2)
                        tmpa = K.sb([128, 128], F32, "tmpa", st2)
                        tmpb = K.sb([128, 128], F32, "tmpb", st2)
                        eL = K.sb([128, 128], F32, "eL", st2)
                        enL = K.sb([128, 128], F32, "enL", st2)
                        eLex = K.sb([128, 128], F32, "eLex", st2)
                        pm_sb = K.sb([128, 4, 2], F32, "pm_sb", st2)
                        gm = K.sb([128, 4, 2], F32, "gm", st2)
                        AR = K.sb([128, 4, 256], BF16, "AR", st2)
                        BH = K.sb([128, 4, 128], BF16, "BH", st2)
                        KH = K.sb([128, 4, 128], BF16, "KH", st2)
                        BKtok = K.sb([128, 2, 512], BF16, "BKtok", st2)
                        Q = [K.sb([128, 8, 128], F32, "Qa", st2), K.sb([128, 8, 128], F32, "Qb", st2)]
                        QT = [K.sb([128, 8, 128], F32, "QTa", st2), K.sb([128, 8, 128], F32, "QTb", st2)]
                        Nm = K.sb([128, 8, 128], F32, "Nm", st2)
                        S3 = K.sb([128, 8, 384], BF16, "S3", st2)
                        H = K.sb([128, 4, 64], F32, "H", st2)
                        H0 = K.sb([128, 4, 64], F32, "H0", st2)
                        H0b = K.sb([128, 4, 64], BF16, "H0b", st2)
                        W_sb = K.sb([128, 512], F32, "W_sb", st2)
                        U_sb = K.sb([128, 512], BF16, "U_sb", st2)
                        ybuf = K.sb([128, 512], F32, "ybuf", st2)
                        yold = K.sb([128, 512], F32, "yold", st2)
                        yield
                        K.memset("dve", H[:, :, :], 0.0, [H])
                        tiles = list(range(18)) if d == 0 else [1, 0] + list(range(17, 1, -1))
                        for ci, c in enumerate(tiles):
                            tsl = slice(c * 128, (c + 1) * 128)
                            pz = B[0]
                            K.mm(pz[:, :], xwaT[0:64, tsl], w2b[:, d, :], True, True, [xwaT, w2b], [pz])
                            K.tt("dve", sg[:, :], pz[:, :], w0bc[:, d, :], ALU.add, [pz, w0bc], [sg])
                            K.act(sg[:, :], sg[:, :], AF.Sigmoid, [sg], [sg])
                            for f4 in range(4):
                                fs = slice(f4 * 128, (f4 + 1) * 128)
                                pa = B[1]
                                K.mm(pa[:, 0:128], a2b[64:128, d, fs], xwaT[64:128, tsl], True, True, [a2b, xwaT], [pa])
                                K.act(aT[:, :], pa[:, 0:128], AF.Sigmoid, [pa, cols], [aT], bias=cols[:, 7 + d, f4:f4 + 1])
                                pl = B[2 + f4 % 2]
                                K.mm(pl[:, 0:128], sg[:, fs], lcm[:, d, :], True, True, [sg, lcm], [pl])
                                K.mm(pl[:, 128:256], sg[:, fs], lexcm[:, d, :], True, True, [sg, lexcm], [pl])
                                K.mm(pl[:, 256:258], sg[:, fs], mcol[:, d, :], True, True, [sg, mcol], [pl])
                                K.act(eL[:, :], pl[:, 0:128], AF.Exp, [pl], [eL], scale=CW)
                                K.act(enL[:, :], pl[:, 0:128], AF.Exp, [pl], [enL], scale=-CW)
                                K.act(eLex[:, :], pl[:, 128:256], AF.Exp, [pl], [eLex], scale=CW)
                                K.copy("act", pm_sb[:, f4, :], pl[:, 256:258], [pl], [pm_sb])
                                K.tt("dve", AR[:, f4, 128:256], rT[:, f4, tsl], eL[:, :], ALU.mult, [rT, eL], [AR])
                                K.stt("dve", AR[:, f4, 0:128], kknT[:, f4, tsl], -1.0, eLex[:, :], ALU.mult, ALU.mult, [kknT, eLex], [AR])
                                K.tt("pool", tmpa[:, :], kknT[:, f4, tsl], aT[:, :], ALU.mult, [kknT, aT], [tmpa])
                                K.tt("dve", BH[:, f4, :], tmpa[:, :], enL[:, :], ALU.mult, [tmpa, enL], [BH])
                                K.ts("dve", tmpb[:, :], aT[:, :], cols[:, 1, f4:f4 + 1], cols[:, 2, f4:f4 + 1], ALU.mult, ALU.add, [aT, cols], [tmpb])
                                K.tt("pool", tmpb[:, :], tmpb[:, :], kT[:, f4, tsl], ALU.mult, [tmpb, kT], [tmpb])
                                K.tt("dve", KH[:, f4, :], tmpb[:, :], enL[:, :], ALU.mult, [tmpb, enL], [KH])
                                yield
                            K.tt("dve", pm_sb[:, :, 1], pm_sb[:, :, 1], pm_sb[:, :, 0], ALU.subtract, [pm_sb], [pm_sb])
                            K.act(gm[:, :, :], pm_sb[:, :, :], AF.Exp, [pm_sb], [gm], scale=CW)
                            pt = B[1]
                            ptv = pt[:, :].bitcast(BF16)
                            for f4 in range(4):
                                K.transpose(ptv[:, f4 * 128:(f4 + 1) * 128], BH[:, f4, :], identb[:, :], [BH, identb], [pt])
                                K.transpose(ptv[:, 512 + f4 * 128:512 + (f4 + 1) * 128], KH[:, f4, :], identb[:, :], [KH, identb], [pt])
                            K.copy("act", BKtok[:, :, :], ptv[:, :].rearrange("p (a f) -> p a f", a=2), [pt], [BKtok])
                            yield
                            for h in range(8):
                                f4, hr = h // 2, slice((h % 2) * 64, (h % 2) * 64 + 64)
                                ps_ = B[2 * (h % 2)]
                                K.mm(ps_[:, 0:256], BH[hr, f4, :], AR[hr, f4, :], True, True, [BH, AR], [ps_])
                                K.mm(ps_[:, 256:512], KH[hr, f4, :], AR[hr, f4, :], True, True, [KH, AR], [ps_])
                                K.tt("dve", Q[0][:, h, :], ps_[:, 0:128], m1[:, d, :], ALU.mult, [ps_, m1], [Q[0]])
                                K.tt("dve", S3[:, h, :], ps_[:, 128:512], m3[:, d, :], ALU.mult, [ps_, m3], [S3])
                                pq = B[2 * (h % 2) + 1]
                                K.mm(pq[:, 0:128], AR[hr, f4, 0:128], BH[hr, f4, :], True, True, [AR, BH], [pq])
                                K.tt("dve", QT[0][:, h, :], pq[:, 0:128], m1t[:, d, :], ALU.mult, [pq, m1t], [QT[0]])
                                if h % 2 == 1:
                                    yield
                            K.tt("pool", Nm[:, :, :], Q[0][:, :, :], ident[:, :].unsqueeze(1).to_broadcast([128, 8, 128]), ALU.add, [Q[0], ident], [Nm])
                            cur = 0
                            for lev in range(1, 7):
                                nxt = 1 - cur
                                for half in range(2):
                                    pqa, pqb, pn = B[(3 * half) % 4], B[(3 * half + 1) % 4], B[(3 * half + 2) % 4]
                                    hs = slice(half * 4, half * 4 + 4)
                                    for hh in range(4):
                                        h = half * 4 + hh
                                        cs = slice(hh * 128, (hh + 1) * 128)
                                        K.mm(pqb[:, cs], Q[cur][:, h, :], QT[cur][:, h, :], True, True, [Q[cur], QT[cur]], [pqb])
                                        if lev < 6:
                                            K.mm(pqa[:, cs], QT[cur][:, h, :], Q[cur][:, h, :], True, True, [Q[cur], QT[cur]], [pqa])
                                    K.copy("act", QT[nxt][:, hs, :], pqb[:, :].rearrange("p (a f) -> p a f", a=4), [pqb], [QT[nxt]])
                                    if lev < 6:
                                        K.copy("dve", Q[nxt][:, hs, :], pqa[:, :].rearrange("p (a f) -> p a f", a=4), [pqa], [Q[nxt]])
                                    yield
                                    for hh in range(4):
                                        h = half * 4 + hh
                                        K.mm(pn[:, hh * 128:(hh + 1) * 128], QT[nxt][:, h, :], Nm[:, h, :], True, True, [QT[nxt], Nm], [pn])
                                    K.tt("dve", Nm[:, hs, :], Nm[:, hs, :], pn[:, :].rearrange("p (a f) -> p a f", a=4), ALU.add, [Nm, pn], [Nm])
                                    yield
                                cur = nxt
                            K.tt("dve", H0[:, :, :], H[:, :, :], gm[:, :, 0:1].to_broadcast([128, 4, 64]), ALU.mult, [H, gm], [H0])
                            K.copy("act", H0b[:, :, :], H0[:, :, :], [H0], [H0b])
                            pw = B[0]
                            for h in range(8):
                                f4, hr = h // 2, slice((h % 2) * 64, (h % 2) * 64 + 64)
                                cs = slice(h * 64, (h + 1) * 64)
                                K.mm(pw[:, cs], AR[hr, f4, 0:128], H0b[hr, f4, :], True, False, [AR, H0b], [pw])
                                K.mm(pw[:, cs], S3[:, h, 128:256], vtok[:, c, cs], False, True, [S3, vtok], [pw])
                            K.copy("act", W_sb[:, :], pw[:, :], [pw], [W_sb])
                            yield
                            pu = B[1]
                            for h in range(8):
                                cs = slice(h * 64, (h + 1) * 64)
                                K.mm(pu[:, cs], Nm[:, h, :], W_sb[:, cs], True, True, [Nm, W_sb], [pu])
                            K.copy("act", U_sb[:, :], pu[:, :], [pu], [U_sb])
                            yield
                            py = B[2]
                            for h in range(8):
                                f4, hr = h // 2, slice((h % 2) * 64, (h % 2) * 64 + 64)
                                cs = slice(h * 64, (h + 1) * 64)
                                K.mm(py[:, cs], AR[hr, f4, 128:256], H0b[hr, f4, :], True, False, [AR, H0b], [py])
                                K.mm(py[:, cs], S3[:, h, 0:128], U_sb[:, cs], False, False, [S3, U_sb], [py])
                                K.mm(py[:, cs], S3[:, h, 256:384], vtok[:, c, cs], False, True, [S3, vtok], [py])
                            if d == 0:
                                K.copy("dve", ybuf[:, :], py[:, :], [py], [ybuf])
                                K.dma("pool", yf_d[c, :, :], ybuf[:, :], reads=[ybuf], writes=[yf_t[c]])
                            else:
                                K.copy("dve", ybuf[:, :], py[:, :], [py], [ybuf])
                                K.dma("pool", y_d[c, :, :], ybuf[:, :], reads=[ybuf], writes=[yb_t[c]])
                            yield
                            if ci < len(tiles) - 1:
                                ph = B[3]
                                for f4 in range(4):
                                    fs = slice(f4 * 128, (f4 + 1) * 128)
                                    K.mm(ph[:, fs], BKtok[:, 0, fs], U_sb[:, fs], True, False, [BKtok, U_sb], [ph])
                                    K.mm(ph[:, fs], BKtok[:, 1, fs], vtok[:, c, fs], False, True, [BKtok, vtok], [ph])
                                phv = ph[:, :].rearrange("p (f x) -> p f x", f=4)
                                for e2_ in range(2):
                                    rs = slice(e2_ * 64, e2_ * 64 + 64)
                                    K.tt("dve", H[rs, :, :], H0[rs, :, :], phv[rs, :, e2_ * 64:(e2_ + 1) * 64], ALU.add, [H0, ph], [H])
                                    K.tt("dve", H[rs, :, :], H[rs, :, :], gm[rs, :, 1:2].to_broadcast([64, 4, 64]), ALU.mult, [H, gm], [H])
                                yield

                    yf_t = [Buf(yf_d.t, "yf%d" % i_) for i_ in range(18)]
                    yb_t = [Buf(y_d.t, "yb%d" % i_) for i_ in range(18)]
                    streams = [dir_stream(0), dir_stream(1)]
                    for s_ in streams:
                        next(s_)
                    for _ in range(cfg.get("rw_offset", 19)):
                        next(streams[1])
                    alive = list(streams)
                    while alive:
                        for s_ in list(alive):
                            try:
                                next(s_)
                            except StopIteration:
                                alive.remove(s_)
            K.barrier()
            rwo = K.sb([128, 4, T], BF16, "rwo", st)
            with ES() as st2:
                yt = [K.sb([128, 512], F32, "yt", st2) for _ in range(2)]
                ysq = K.sb([128, 512], F32, "ysq", st2)
                s1 = K.sb([128, 8], F32, "s1", st2)
                s2 = K.sb([128, 8], F32, "s2", st2)
                ynb = K.sb([128, 512], BF16, "ynb", st2)
                vT_ = [K.sb([128, 4, 128], BF16, "vT_", st2) for _ in range(2)]
                sc_ = [K.sb([128, 4, 128], BF16, "sc_", st2) for _ in range(2)]
                gg_ = [K.sb([128, 4, 128], BF16, "gg_", st2) for _ in range(2)]
                yn32 = K.sb([128, 4, 128], F32, "yn32", st2)
                bon = K.sb([128, 4, 128], F32, "bon", st2)
                gbv = gb_d[:, :, :, :].rearrange("a c p t -> a p c t")
                for c in range(18):
                    tsl = slice(c * 128, (c + 1) * 128)
                    y_ = yt[c % 2]
                    K.dma("sp", y_[:, :], y_d[c, :, :], reads=[y_d], writes=[y_])
                    K.dma("sp", ysq[:, :], yf_d[c, :, :], reads=[yf_d], writes=[ysq])
                    K.tt("dve", y_[:, :], y_[:, :], ysq[:, :], ALU.add, [y_, ysq], [y_])
                    K.dma("sp", vT_[c % 2][:, :, :], gbv[0, :, :, tsl], reads=[gb_d], writes=[vT_[c % 2]])
                    K.dma("sp", sc_[c % 2][:, :, :], gbv[1, :, :, tsl], reads=[gb_d], writes=[sc_[c % 2]])
                    K.dma("sp", gg_[c % 2][:, :, :], gbv[2, :, :, tsl], reads=[gb_d], writes=[gg_[c % 2]])
                    yv = y_[:, :].rearrange("p (h e) -> p h e", h=8)
                    K.op("dve", lambda e, yv=yv: e.tensor_reduce(out=s1[:, :], in_=yv, axis=AX.X, op=ALU.add), [y_], [s1])
                    K.act(ysq[:, :], y_[:, :], AF.Square, [y_], [ysq])
                    K.op("dve", lambda e: e.tensor_reduce(out=s2[:, :], in_=ysq[:, :].rearrange("p (h e) -> p h e", h=8), axis=AX.X, op=ALU.add), [ysq], [s2])
                    K.ts("dve", s1[:, :], s1[:, :], 1.0 / 64, None, ALU.mult, None, [s1], [s1])
                    K.tt("dve", ysq[:, 0:8], s1[:, :], s1[:, :], ALU.mult, [s1], [ysq])
                    K.stt("dve", s2[:, :], s2[:, :], 1.0 / 64, ysq[:, 0:8], ALU.mult, ALU.subtract, [s2, ysq], [s2])
                    K.rsqrt(s2[:, :], s2[:, :], 1.0, 64e-5, [s2], [s2])
                    K.tt("dve", yv, yv, s1[:, :].unsqueeze(2).to_broadcast([128, 8, 64]), ALU.subtract, [y_, s1], [y_])
                    K.tt("dve", ynb[:, :].rearrange("p (h e) -> p h e", h=8), yv, s2[:, :].unsqueeze(2).to_broadcast([128, 8, 64]), ALU.mult, [y_, s2], [ynb])
                    pt = PS[c % 2]
                    ptv = pt[:, :].bitcast(BF16)
                    for c4 in range(4):
                        K.transpose(ptv[:, c4 * 128:(c4 + 1) * 128], ynb[:, c4 * 128:(c4 + 1) * 128], identb[:, :], [ynb, identb], [pt])
                    for c4 in range(4):
                        K.act(yn32[:, c4, :], ptv[:, c4 * 128:(c4 + 1) * 128], AF.Identity, [pt, cols], [yn32],
                              bias=cols[:, 5, c4:c4 + 1], scale=cols[:, 4, c4:c4 + 1])
                    K.tt("pool", bon[:, :, :], vT_[c % 2][:, :, :], sc_[c % 2][:, :, :], ALU.mult, [vT_[c % 2], sc_[c % 2]], [bon])
                    K.tt("pool", yn32[:, :, :], yn32[:, :, :], bon[:, :, :], ALU.add, [yn32, bon], [yn32])
                    K.tt("dve", rwo[:, :, tsl], yn32[:, :, :], gg_[c % 2][:, :, :], ALU.mult, [yn32, gg_[c % 2]], [rwo])
            K.barrier()
            wo = K.sb([128, 4, D], BF16, "wo_rw", st)
            wostg = [K.sb([128, 4096], F32, "wostg", st)]
            load_w_bf16(wo[:, :, :], about_d[0, 512:1024, :].rearrange("(kc p) n -> p kc n", p=128), (128, 4, D), about_d, wo, wostg, 0)
            pieces = [((lambda oc, cc=cc: wo[:, cc, oc * 128:(oc + 1) * 128]),
                       (lambda t0, t1, cc=cc: rwo[:, cc, t0:t1]), [wo, rwo]) for cc in range(4)]
            apply_out(b, pieces, 2, 0)
        K.barrier()

    K.barrier()
    for l in layers:
        stage_mods(l)
    for b in range(nb):
        stage_load(b)
        for li, l in enumerate(layers):
            last = (li == len(layers) - 1) and not cfg.get("force_ctx", False)
            modsT.l = l
            Amod.l = l
            if mixers:
                if l == 1:
                    with ES() as sth:
                        hT = K.sb([128, KC, T], BF16, "hT", sth)
                        stage_norm(b, 0, hT, to_dram=True)
                        if cfg.get("swa", True):
                            stage_swa(b, hT)
                    if cfg.get("ssd", True):
                        stage_ssd(b, None)
                else:
                    with ES() as sth:
                        hT = K.sb([128, KC, T], BF16, "hT", sth)
                        stage_norm(b, 0, hT, to_dram=True)
                        if cfg.get("mla", True):
                            stage_mla(b, hT)
                    if cfg.get("rwkv", True):
                        stage_rwkv(b)
            with ES() as sth:
                hT = K.sb([128, KC, T], BF16, "hT", sth)
                stage_norm(b, 1, hT, lo=0 if not last else LC)
                stage_ffn(b, l, do_ctx=not last, hT=hT)
        stage_out(b)
        if cfg.get("dbg_x", False) and b == 0:
            for cc in range(KC):
                K.dma("sp", dbgx_d[cc, :, :], xs_d[cc, :, :], reads=xblk, writes=[dbgx_d])
    K.barrier()
    K.es.close()
    return nc, K


CONST_INPUTS = None


def _rope_tables(rot_dim):
    n_freq = rot_dim // 4
    rows = np.arange(LL, dtype=np.float32) // 64
    cols = np.arange(LL, dtype=np.float32) % 64
    inv = (np.float32(10000.0) ** (-np.arange(n_freq, dtype=np.float32) / np.float32(n_freq))).astype(np.float32)
    ang = np.concatenate([rows[:, None] * inv[None, :], cols[:, None] * inv[None, :]], axis=-1).astype(np.float32)
    cos, sin = np.cos(ang).astype(np.float32), np.sin(ang).astype(np.float32)
    cosT = np.concatenate([cos.T, cos.T], axis=0)
    sinT = np.concatenate([sin.T, sin.T], axis=0)
    return np.ascontiguousarray(cosT), np.ascontiguousarray(sinT)


def const_inputs():
    global CONST_INPUTS
    if CONST_INPUTS is None:
        s = np.arange(128)
        triF = (s[:, None] <= s[None, :]).astype(np.float32)
        c = {"ident": np.eye(128, dtype=np.float32), "triF": triF, "triB": np.ascontiguousarray(triF.T),
             "strF": (s[:, None] > s[None, :]).astype(np.float32), "strB": (s[:, None] < s[None, :]).astype(np.float32)}
        c["swa_cos"], c["swa_sin"] = _rope_tables(64)
        c["mla_cos"], c["mla_sin"] = _rope_tables(32)
        triB = triF.T
        incl = [triF, triB]
        strict = [c["strB"], c["strF"]]
        m = 63
        c["rw_lc"] = np.stack([incl[d] - incl[d][:, m:m + 1] for d in range(2)]).astype(np.float32)
        c["rw_lexc"] = np.stack([strict[d] - incl[d][:, m:m + 1] for d in range(2)]).astype(np.float32)
        c["rw_mcol"] = np.stack([np.stack([incl[d][:, m], np.ones(128, np.float32)], axis=1) for d in range(2)]).astype(np.float32)
        c["rw_m1"] = np.stack([strict[d] for d in range(2)]).astype(np.float32)
        c["rw_m3"] = np.stack([np.concatenate([incl[d], strict[d], incl[d]], axis=1) for d in range(2)]).astype(np.float32)
        c["rw_m1t"] = np.stack([np.ascontiguousarray(strict[d].T) for d in range(2)]).astype(np.float32)
        blk = np.zeros((128, 128), np.float32)
        blk[:64, :64] = 1.0
        blk[64:, 64:] = 1.0
        c["blk64"] = blk
        CONST_INPUTS = c
    return CONST_INPUTS


def make_in_maps(nc_names, inputs, ncores=NCORES):
    consts = const_inputs()
    in_maps = []
    for core in range(ncores):
        m = {}
        sl = slice(core * BPC, (core + 1) * BPC)
        for k in nc_names:
            if k in consts:
                m[k] = consts[k]
            else:
                v = np.asarray(inputs[k])
                m[k] = np.ascontiguousarray(v[sl] if k in ("x", "c", "ctx") else v)
        in_maps.append(m)
    return in_maps


def kernel(**inputs):
    cfg = {}
    nc, K = build_program(cfg)
    in_maps = make_in_maps(K.in_names, inputs)
    res = run_bass_kernel_spmd(nc, in_maps, core_ids=list(range(NCORES)))
    return np.concatenate([r["out"] for r in res.results], axis=0)
```

```python
import contextlib
import math
import numpy as np
import concourse.bass as bass
import concourse.mybir as mybir
from concourse.bass_utils import run_bass_kernel_spmd

F32 = mybir.dt.float32
BF16 = mybir.dt.bfloat16
AF = mybir.ActivationFunctionType
ALU = mybir.AluOpType
AX = mybir.AxisListType

NCORES = 8
BPC = 4
D = 1024
KC = 8
LC = 256
LL = 2048
T = LC + LL
DFF = 2816
EPS = 1e-6
BLOCKS = [(0, 256), (256, 768), (768, 1280), (1280, 1792), (1792, 2304)]
SEM_EPOCH = 50000


class Buf:
    def __init__(self, t, name):
        self.t = t
        self.name = name
        self.lw = []
        self.rd = []
        self.ds = None

    def __getitem__(self, idx):
        return self.t[idx]


class Eng:
    def __init__(self, name, e, is_pe=False):
        self.name = name
        self.e = e
        self.is_pe = is_pe
        self.sems = []
        self.count = 0
        self.epoch = 0
        self.seen = {}


class Kern:
    def __init__(self, nc):
        self.nc = nc
        self.es = contextlib.ExitStack()
        self.engs = {}
        for name, e, ispe in (("pe", nc.tensor, True), ("act", nc.scalar, False),
                              ("dve", nc.vector, False), ("pool", nc.gpsimd, False),
                              ("sp", nc.sync, False)):
            en = Eng(name, e, ispe)
            en.sems.append(self.es.enter_context(nc.semaphore("s_%s_0" % name)))
            self.engs[name] = en
        self.ndsem = 44
        self.dsem = [self.es.enter_context(nc.semaphore("s_d%d" % i)) for i in range(self.ndsem)]
        self.dtot = [0] * self.ndsem
        self.dranges = {"sp": (0, 26), "pool": (26, 44), "act": (0, 26), "dve": (0, 26), "pe": (0, 26)}
        self.drr = {k: 0 for k in self.dranges}
        self.nbuf = 0
        self.n_ops = 0
        self.eps_bufs = {}
        self.in_names = []

    def sb(self, shape, dtype, name=None, stack=None):
        self.nbuf += 1
        name = "%s_%d" % (name or "sb", self.nbuf)
        t = (stack or self.es).enter_context(self.nc.sbuf_tensor(name, list(shape), dtype))
        return Buf(t, name)

    def ps(self, name=None):
        self.nbuf += 1
        name = "%s_%d" % (name or "ps", self.nbuf)
        t = self.es.enter_context(self.nc.psum_tensor(name, [128, 512], F32))
        return Buf(t, name)

    def dram(self, name, shape, dtype, kind="Internal"):
        t = self.nc.dram_tensor(name, list(shape), dtype, kind=kind)
        if kind == "ExternalInput":
            self.in_names.append(name)
        return Buf(t, name)

    def _wait(self, eng, ev):
        if ev[0] == "E":
            src = self.engs[ev[1]]
            ep, n = ev[2], ev[3]
            if src is eng and eng.is_pe:
                return
            key = ("E", ev[1], ep)
            if eng.seen.get(key, 0) >= n:
                return
            eng.e.wait_ge(src.sems[ep], n)
            eng.seen[key] = n
        else:
            i = ev[1]
            tot = self.dtot[i]
            key = ("D", i)
            if eng.seen.get(key, 0) >= tot:
                return
            eng.e.wait_ge(self.dsem[i], tot)
            eng.seen[key] = tot

    def _deps(self, eng, reads, writes):
        for b in reads:
            for ev in b.lw:
                self._wait(eng, ev)
        for b in writes:
            for ev in b.lw:
                self._wait(eng, ev)
            for ev in b.rd:
                self._wait(eng, ev)

    def _record(self, ev, reads, writes):
        for b in reads:
            if ev[0] == "E":
                b.rd = [r for r in b.rd if not (r[0] == "E" and r[1] == ev[1])]
            else:
                b.rd = [r for r in b.rd if r != ev]
            b.rd.append(ev)
        for b in writes:
            b.lw = [ev]
            b.rd = []

    def op(self, engname, fn, reads=(), writes=()):
        eng = self.engs[engname]
        self._deps(eng, reads, writes)
        if eng.count >= SEM_EPOCH:
            eng.epoch += 1
            eng.count = 0
            eng.sems.append(self.es.enter_context(self.nc.semaphore("s_%s_%d" % (engname, eng.epoch))))
        ins = fn(eng.e)
        eng.count += 1
        ins.then_inc(eng.sems[eng.epoch], 1)
        ev = ("E", engname, eng.epoch, eng.count)
        self._record(ev, reads, writes)
        self.n_ops += 1

    def dma(self, engname, out, in_, reads=(), writes=(), **kw):
        eng = self.engs[engname]
        self._deps(eng, reads, writes)
        b = None
        for cand in list(writes) + list(reads):
            if cand.ds is not None and engname in cand.ds:
                b = cand
                break
        if b is None:
            b = (list(writes) + list(reads))[0]
            if b.ds is None:
                b.ds = {}
            lo, hi = self.dranges[engname]
            b.ds[engname] = lo + self.drr[engname]
            self.drr[engname] = (self.drr[engname] + 1) % (hi - lo)
        i = b.ds[engname]
        ins = eng.e.dma_start(out=out, in_=in_, **kw)
        ins.then_inc(self.dsem[i], 16)
        self.dtot[i] += 16
        ev = ("D", i)
        self._record(ev, reads, writes)
        self.n_ops += 1

    def barrier(self):
        for eng in self.engs.values():
            for other in self.engs.values():
                if other is eng:
                    continue
                if other.count > 0:
                    self._wait(eng, ("E", other.name, other.epoch, other.count))
            for i in range(self.ndsem):
                if self.dtot[i] > 0:
                    self._wait(eng, ("D", i))

    def mm(self, out, lhsT, rhs, start, stop, reads, writes):
        self.op("pe", lambda e: e.matmul(out, lhsT=lhsT, rhs=rhs, start=start, stop=stop), reads, writes)

    def transpose(self, out, in_, ident, reads, writes):
        self.op("pe", lambda e: e.transpose(out, in_, ident), reads, writes)

    def act(self, out, in_, func, reads, writes, bias=None, scale=None, eng="act"):
        kw = {}
        if bias is not None:
            kw["bias"] = bias
        if scale is not None:
            kw["scale"] = scale
        self.op(eng, lambda e: e.activation(out=out, in_=in_, func=func, **kw), reads, writes)

    def ts(self, eng, out, in0, s1, s2, op0, op1, reads, writes):
        if op1 is None:
            self.op(eng, lambda e: e.tensor_scalar(out=out, in0=in0, scalar1=s1, scalar2=None, op0=op0), reads, writes)
        else:
            self.op(eng, lambda e: e.tensor_scalar(out=out, in0=in0, scalar1=s1, scalar2=s2, op0=op0, op1=op1), reads, writes)

    def tt(self, eng, out, in0, in1, op, reads, writes):
        self.op(eng, lambda e: e.tensor_tensor(out=out, in0=in0, in1=in1, op=op), reads, writes)

    def stt(self, eng, out, in0, scalar, in1, op0, op1, reads, writes):
        self.op(eng, lambda e: e.scalar_tensor_tensor(out=out, in0=in0, scalar=scalar, in1=in1, op0=op0, op1=op1), reads, writes)

    def copy(self, eng, out, in_, reads, writes):
        if eng == "act":
            self.op(eng, lambda e: e.activation(out=out, in_=in_, func=AF.Copy), reads, writes)
        else:
            self.op(eng, lambda e: e.tensor_copy(out=out, in_=in_), reads, writes)

    def rsqrt(self, out, in_, scale, eps, reads, writes):
        self.op("act", lambda e: e.activation(out=out, in_=in_, func=AF.Sqrt, bias=self.eps_ap(eps), scale=scale), reads, writes)
        self.op("dve", lambda e: e.reciprocal(out=out, in_=out), writes, writes)

    def eps_ap(self, eps):
        if eps not in self.eps_bufs:
            b = self.sb([128, 1], F32, "eps")
            self.memset("dve", b[:, :], float(eps), [b])
            self.eps_bufs[eps] = b
        return self.eps_bufs[eps][:, 0:1]

    def memset(self, eng, ap, val, writes):
        self.op(eng, lambda e: e.memset(ap, val), (), writes)


def colvec(ap1d, n):
    return ap1d.rearrange("(c p) -> p c", p=128)


def stg_view(s, shape):
    n = 1
    for d_ in shape[1:]:
        n *= d_
    v = s[0:shape[0], 0:n]
    if len(shape) == 3:
        v = v.rearrange("p (a b) -> p a b", a=shape[1])
    return v


HD = 64
CD_Z, CD_XBC, CD_DT, CD_Q, CD_K, CD_V = 0, 1024, 2560, 2592, 3104, 3232
CD_IN = 3360


def build_program(cfg):
    nc = bass.Bass("TRN2", target_bir_lowering=False)
    K = Kern(nc)
    nb = cfg.get("nb", BPC)
    layers = cfg.get("layers", [0, 1])
    mixers = cfg.get("mixers", True)
    ES = contextlib.ExitStack

    def din(name, shape):
        return K.dram(name, shape, F32, kind="ExternalInput")

    x_d = din("x", [BPC, LL, D])
    c_d = din("c", [BPC, D])
    ctx_d = din("ctx", [BPC, LC, D])
    cctx_d = din("c_ctx", [D])
    ada_w_d = din("ada_w", [2, D, 6 * D])
    ada_b_d = din("ada_b", [2, 6 * D])
    nmix_d = din("norm_mix_g", [2, D])
    nffn_d = din("norm_ffn_g", [2, D])
    wup_d = din("ffn_w_up", [2, D, 2 * DFF])
    cw_d = din("ffn_conv_w", [2, 3, 2 * DFF])
    cb_d = din("ffn_conv_b", [2, 2 * DFF])
    wdn_d = din("ffn_w_down", [2, DFF, D])
    fng_d = din("final_norm_g", [D])
    if mixers and 1 in layers:
        cdin_d = din("cd_w_in", [1, D, CD_IN])
        cdout_d = din("cd_w_out", [1, 1536, D])
        scw_d = din("ssm_conv_w", [1, 5, 1536])
        scb_d = din("ssm_conv_b", [1, 1536])
        sdtb_d = din("ssm_dt_bias", [1, 2, 16])
        salog_d = din("ssm_a_log", [1, 2, 16])
        sd_d = din("ssm_d", [1, 16])
        sng_d = din("ssm_norm_g", [1, 1024])
        sink_d = din("swa_sink", [1, 8])
        swacos_d = din("swa_cos", [64, LL])
        swasin_d = din("swa_sin", [64, LL])
        triF_d = din("triF", [128, 128])
        triB_d = din("triB", [128, 128])
        strF_d = din("strF", [128, 128])
        strB_d = din("strB", [128, 128])
    if mixers and 0 in layers:
        abin_d = din("ab_w_in", [1, D, 2464])
        about_d = din("ab_w_out", [1, D, D])
        mqg_d = din("mla_q_norm_g", [1, 384])
        mqu_d = din("mla_w_q_up", [1, 384, 768])
        mkg_d = din("mla_kv_norm_g", [1, 256])
        mkvu_d = din("mla_w_kv_up", [1, 256, 1024])
        mlacos_d = din("mla_cos", [32, LL])
        mlasin_d = din("mla_sin", [32, LL])
        rmp_d = din("rwkv_mu_prev", [1, 1792])
        rmn_d = din("rwkv_mu_next", [1, 1792])
        rw0_d = din("rwkv_w0", [1, 2, 512])
        rw2_d = din("rwkv_w2", [1, 2, 64, 512])
        ra0_d = din("rwkv_a0", [1, 2, 512])
        ra2_d = din("rwkv_a2", [1, 2, 64, 512])
        rg2_d = din("rwkv_g2", [1, 128, 512])
        rkk_d = din("rwkv_k_k", [1, 512])
        rka_d = din("rwkv_k_a", [1, 512])
        rrk_d = din("rwkv_r_k", [1, 8, 64])
        rlg_d = din("rwkv_ln_g", [1, 512])
        rlb_d = din("rwkv_ln_b", [1, 512])
        rwlc_d = din("rw_lc", [2, 128, 128])
        rwlexc_d = din("rw_lexc", [2, 128, 128])
        rwmcol_d = din("rw_mcol", [2, 128, 2])
        rwm1_d = din("rw_m1", [2, 128, 128])
        rwm3_d = din("rw_m3", [2, 128, 384])
        rwm1t_d = din("rw_m1t", [2, 128, 128])
        blk64_d = din("blk64", [128, 128])
        gb_d = K.dram("gb_scr", [3, 4, 128, T], BF16)
        y_d = K.dram("y_scr", [18, 128, 512], F32)
        yf_d = K.dram("yf_scr", [18, 128, 512], F32)
    ident_d = din("ident", [128, 128])
    out_d = K.dram("out", [BPC, LL, D], F32, kind="ExternalOutput")
    if cfg.get("dbg_x", False):
        dbgx_d = K.dram("dbgx", [KC, 128, T], F32, kind="ExternalOutput")
    aT_d = K.dram("aT_scr", [DFF, T], BF16)
    xs_d = K.dram("x_scr", [KC, 128, T], F32)
    hT_d = K.dram("hT_scr", [KC, 128, T], BF16)
    sz_d = K.dram("sz_scr", [16, 128, 1024], BF16)
    hin_d = K.dram("hin_scr", [16, 128, 1024], BF16)
    xview = xs_d[:, :, :].rearrange("c p t -> p c t")
    hview = hT_d[:, :, :].rearrange("c p t -> p c t")
    xblk = [Buf(xs_d.t, "xblk%d" % j) for j in range(T // 256)]

    def xdeps(t0, t1):
        return xblk[t0 // 256:(t1 + 255) // 256]

    ident = K.sb([128, 128], F32, "ident")
    identb = K.sb([128, 128], BF16, "identb")
    ones_bf = K.sb([128, 128], BF16, "ones")
    ones_f = K.sb([128, 128], F32, "onesf")
    class LayerBuf(Buf):
        def __init__(self, b_):
            Buf.__init__(self, b_.t, b_.name)
            self.l = 0

        def __getitem__(self, idx):
            return self.t[(idx[0], self.l) + tuple(idx[1:])]

    modsT = LayerBuf(K.sb([128, 2, KC, 6, 5], F32, "modsT"))
    Amod = LayerBuf(K.sb([128, 2, KC, 2, 5], F32, "Amod"))
    gcols = K.sb([128, 5, KC], F32, "gcols")
    cwT = K.sb([128, 2, 3, 44], F32, "cwT")
    cbT = K.sb([128, 2, 44], F32, "cbT")
    PS = [K.ps("ps%d" % i) for i in range(8)]
    for e_ in (EPS, 64e-5, 1e-12, 1.0, 0.0):
        K.eps_ap(e_)

    K.dma("sp", ident[:, :], ident_d[:, :], reads=[ident_d], writes=[ident])
    K.copy("dve", identb[:, :], ident[:, :], [ident], [identb])
    K.memset("dve", ones_bf[:, :], 1.0, [ones_bf])
    K.memset("dve", ones_f[:, :], 1.0, [ones_f])
    for l in range(2):
        K.dma("sp", gcols[:, l, :], colvec(nmix_d[l, :], KC), reads=[nmix_d], writes=[gcols], allow_slow_non_contiguous=True)
        K.dma("sp", gcols[:, 2 + l, :], colvec(nffn_d[l, :], KC), reads=[nffn_d], writes=[gcols], allow_slow_non_contiguous=True)
        for tap in range(3):
            K.dma("sp", cwT[:, l, tap, :], colvec(cw_d[l, tap, :], 44), reads=[cw_d], writes=[cwT], allow_slow_non_contiguous=True)
        K.dma("sp", cbT[:, l, :], colvec(cb_d[l, :], 44), reads=[cb_d], writes=[cbT], allow_slow_non_contiguous=True)
    K.dma("sp", gcols[:, 4, :], colvec(fng_d[:], KC), reads=[fng_d], writes=[gcols], allow_slow_non_contiguous=True)

    def stage_mods(l):
        modsT.l = l
        Amod.l = l
        with ES() as st:
            condT = K.sb([128, KC, 5], F32, "condT", st)
            scond = K.sb([128, KC, 5], F32, "scond", st)
            abT = K.sb([128, 48], F32, "abT", st)
            wb = [K.sb([128, KC, 128], F32, "adaw", st) for _ in range(3)]
            for r in range(4):
                K.dma("sp", condT[:, :, r], colvec(c_d[r, :], KC), reads=[c_d], writes=[condT], allow_slow_non_contiguous=True)
            K.dma("sp", condT[:, :, 4], colvec(cctx_d[:], KC), reads=[cctx_d], writes=[condT], allow_slow_non_contiguous=True)
            K.dma("sp", abT[:, :], colvec(ada_b_d[l, :], 48), reads=[ada_b_d], writes=[abT], allow_slow_non_contiguous=True)
            K.act(scond[:, :, :], condT[:, :, :], AF.Silu, [condT], [scond])
            wview = ada_w_d[l, :, :].rearrange("(kc p) n -> p kc n", p=128)
            for j in range(48):
                w = wb[j % 3]
                K.dma("sp", w[:, :, :], wview[:, :, j * 128:(j + 1) * 128], reads=[ada_w_d], writes=[w])
                pb = PS[j % 2]
                for kc in range(KC):
                    K.mm(pb[:, 0:5], w[:, kc, :], scond[:, kc, :], kc == 0, kc == KC - 1, [w, scond], [pb])
                kind, cc = j // 8, j % 8
                K.ts("dve", modsT[:, cc, kind, :], pb[:, 0:5], abT[:, j:j + 1], None, ALU.add, None, [pb, abT], [modsT])
            for which, kind, gi in ((0, 1, l), (1, 4, 2 + l)):
                for cc in range(KC):
                    K.ts("dve", Amod[:, cc, which, :], modsT[:, cc, kind, :], 1.0, gcols[:, gi, cc:cc + 1],
                         ALU.add, ALU.mult, [modsT, gcols], [Amod])
        K.barrier()

    def stage_load(b):
        with ES() as st:
            xin = [K.sb([128, D], F32, "xin", st) for _ in range(3)]
            xo = [K.sb([128, KC, 128], F32, "xo", st) for _ in range(3)]
            for ti in range(T // 128):
                xb = xin[ti % 3]
                if ti < 2:
                    src, sb_ = ctx_d[b, ti * 128:(ti + 1) * 128, :], ctx_d
                else:
                    src, sb_ = x_d[b, (ti - 2) * 128:(ti - 1) * 128, :], x_d
                K.dma("sp", xb[:, :], src, reads=[sb_], writes=[xb])
                o = xo[ti % 3]
                for half in range(2):
                    pb = PS[(2 * ti + half) % 4]
                    for q in range(4):
                        cc = half * 4 + q
                        K.transpose(pb[:, q * 128:(q + 1) * 128], xb[:, cc * 128:(cc + 1) * 128], ident[:, :], [xb, ident], [pb])
                    K.copy("act" if half == 0 else "dve", o[:, half * 4:half * 4 + 4, :],
                           pb[:, :].rearrange("p (q t) -> p q t", q=4), [pb], [o])
                K.dma("pool", xview[:, :, ti * 128:(ti + 1) * 128], o[:, :, :], reads=[o], writes=xdeps(ti * 128, ti * 128 + 128))
        K.barrier()

    def stage_norm(b, which, hT, to_dram=False, lo=0):
        shift_kind = 0 if which == 0 else 3
        with ES() as st:
            xb = [K.sb([128, KC, 512], F32, "nxb", st) for _ in range(2)]
            sq = [K.sb([128, 512], BF16, "sq", st) for _ in range(3)]
            rstd = [K.sb([128, 512], F32, "rstd", st) for _ in range(2)]
            tmp = [K.sb([128, 512], F32, "ntmp", st) for _ in range(3)]
            for bi, (t0, t1) in enumerate(BLOCKS):
                if t1 <= lo:
                    continue
                n = t1 - t0
                row = 4 if bi == 0 else b
                x_ = xb[bi % 2]
                K.dma("sp", x_[:, :, 0:n], xview[:, :, t0:t1], reads=xdeps(t0, t1), writes=[x_])
                pb = PS[bi % 2]
                for cc in range(KC):
                    s = sq[cc % 3]
                    K.act(s[:, 0:n], x_[:, cc, 0:n], AF.Square, [x_], [s])
                    K.mm(pb[:, 0:n], ones_bf[:, :], s[:, 0:n], cc == 0, cc == KC - 1, [ones_bf, s], [pb])
                r = rstd[bi % 2]
                K.rsqrt(r[:, 0:n], pb[:, 0:n], 1.0 / D, EPS, [pb], [r])
                for cc in range(KC):
                    tm = tmp[cc % 3]
                    K.tt("dve", tm[:, 0:n], x_[:, cc, 0:n], r[:, 0:n], ALU.mult, [x_, r], [tm])
                    K.act(hT[:, cc, t0:t1], tm[:, 0:n], AF.Identity, [tm, Amod, modsT], [hT],
                          bias=modsT[:, cc, shift_kind, row:row + 1], scale=Amod[:, cc, which, row:row + 1])
            if to_dram:
                for cc in range(KC):
                    K.dma("pool", hT_d[cc, :, :], hT[:, cc, :], reads=[hT], writes=[hT_d])
        K.barrier()

    def load_hT(hT):
        for cc in range(KC):
            K.dma("sp", hT[:, cc, :], hT_d[cc, :, :], reads=[hT_d], writes=[hT])

    def apply_out(b, pieces, gate_kind, tok_lo):
        with ES() as st:
            xb = [K.sb([128, KC, 256], F32, "uxb", st) for _ in range(2)]
            for bi, t0 in enumerate(range(tok_lo, T, 256)):
                t1 = t0 + 256
                row = 4 if t0 < LC else b
                x_ = xb[bi % 2]
                K.dma("sp", x_[:, :, :], xview[:, :, t0:t1], reads=xdeps(t0, t1), writes=[x_])
                for oc in range(KC):
                    pb = PS[oc % 4]
                    for pi, (lf, rf, rd) in enumerate(pieces):
                        K.mm(pb[:, 0:256], lf(oc), rf(t0, t1), pi == 0, pi == len(pieces) - 1, rd, [pb])
                    K.stt("dve", x_[:, oc, :], pb[:, 0:256], modsT[:, oc, gate_kind, row:row + 1], x_[:, oc, :],
                          ALU.mult, ALU.add, [pb, modsT, x_], [x_])
                K.dma("pool", xview[:, :, t0:t1], x_[:, :, :], reads=[x_], writes=xdeps(t0, t1))

    cast_rr = [0]

    def load_w_bf16(dst_ap, src_ap, shape, src_buf, dst_buf, st_bufs, idx, eng=None):
        s = st_bufs[idx % len(st_bufs)]
        v = stg_view(s, shape)
        K.dma("sp", v, src_ap, reads=[src_buf], writes=[s])
        if eng is None:
            cast_rr[0] += 1
            eng = "dve" if cast_rr[0] % 2 == 0 else "act"
        K.copy(eng, dst_ap, v, [s], [dst_buf])

    def stage_ffn(b, l, do_ctx, hT):
        wupv = wup_d[l, :, :].rearrange("(kc p) n -> p kc n", p=128)
        segs = [(LC, LL)] + ([(0, LC)] if do_ctx else [])
        blocks = [bl for bl in BLOCKS if (do_ctx or bl[0] >= LC)]
        PADW = T + 4
        with ES() as st:
            wst = [K.sb([128, KC, 128], F32, "wst", st) for _ in range(4)]
            wbf = [K.sb([128, KC, 128], BF16, "wbf", st) for _ in range(4)]
            ug = [K.sb([128, PADW], F32, "ug", st) for _ in range(2)]
            uv = [K.sb([128, PADW], F32, "uv", st) for _ in range(2)]
            cg = [K.sb([128, T], F32, "cg", st) for _ in range(2)]
            cv = [K.sb([128, T], F32, "cv", st) for _ in range(2)]
            ao = [K.sb([128, T], BF16, "ao", st) for _ in range(2)]
            for u in ug + uv:
                K.memset("pool", u[:, :], 0.0, [u])

            def pad_off(t):
                return t + 1 if t < LC else t + 3

            def ffn_wload(fc):
                for half in range(2):
                    ws, wb_ = wst[2 * (fc % 2) + half], wbf[2 * (fc % 2) + half]
                    K.dma("sp", ws[:, :, :], wupv[:, :, half * DFF + fc * 128:half * DFF + (fc + 1) * 128], reads=[wup_d], writes=[ws])
                    K.copy("dve" if half == 0 else "act", wb_[:, :, :], ws[:, :, :], [ws], [wb_])

            def ffn_tail(fc):
                cg_, cv_, ao_ = cg[fc % 2], cv[fc % 2], ao[fc % 2]
                for (s0, ln) in segs:
                    K.act(cg_[:, s0:s0 + ln], cg_[:, s0:s0 + ln], AF.Silu, [cg_], [cg_])
                    K.tt("dve" if ln > 1024 else "pool", ao_[:, s0:s0 + ln], cg_[:, s0:s0 + ln], cv_[:, s0:s0 + ln], ALU.mult, [cg_, cv_], [ao_])
                lo = 0 if do_ctx else LC
                K.dma("pool", aT_d[fc * 128:(fc + 1) * 128, lo:T], ao_[:, lo:T], reads=[ao_], writes=[aT_d])

            ffn_wload(0)
            for fc in range(22):
                if fc + 1 < 22:
                    ffn_wload(fc + 1)
                g_, v_ = ug[fc % 2], uv[fc % 2]
                for bi, (t0, t1) in enumerate(blocks):
                    n = t1 - t0
                    for half, dst in ((0, g_), (1, v_)):
                        pb = PS[(bi * 2 + half) % 8]
                        wb_ = wbf[2 * (fc % 2) + half]
                        for kc in range(KC):
                            K.mm(pb[:, 0:n], wb_[:, kc, :], hT[:, kc, t0:t1],
                                 kc == 0, kc == KC - 1, [wb_, hT], [pb])
                        K.copy("act", dst[:, pad_off(t0):pad_off(t0) + n], pb[:, 0:n], [pb], [dst])
                cg_, cv_ = cg[fc % 2], cv[fc % 2]
                for half, src, dst, ch in ((0, g_, cg_, fc), (1, v_, cv_, 22 + fc)):
                    for (s0, ln) in segs:
                        p0 = pad_off(s0)
                        K.act(dst[:, s0:s0 + ln], src[:, p0:p0 + ln], AF.Identity, [src, cwT, cbT], [dst],
                              bias=cbT[:, l, ch:ch + 1], scale=cwT[:, l, 1, ch:ch + 1])
                        K.stt("dve", dst[:, s0:s0 + ln], src[:, p0 - 1:p0 - 1 + ln], cwT[:, l, 0, ch:ch + 1], dst[:, s0:s0 + ln],
                              ALU.mult, ALU.add, [src, cwT, dst], [dst])
                        K.stt("dve", dst[:, s0:s0 + ln], src[:, p0 + 1:p0 + 1 + ln], cwT[:, l, 2, ch:ch + 1], dst[:, s0:s0 + ln],
                              ALU.mult, ALU.add, [src, cwT, dst], [dst])
                if fc >= 1:
                    ffn_tail(fc - 1)
            ffn_tail(21)
        K.barrier()
        wdv = wdn_d[l, :, :].rearrange("(kc p) n -> p kc n", p=128)
        aTv = aT_d[:, :].rearrange("(kc p) t -> p kc t", p=128)
        with ES() as st:
            wd = K.sb([128, 22, D], BF16, "wd", st)
            wds = [K.sb([128, 2 * D], F32, "wds", st) for _ in range(2)]
            ab = [K.sb([128, 22, 256], BF16, "ab", st) for _ in range(2)]
            for j in range(11):
                load_w_bf16(wd[:, 2 * j:2 * j + 2, :], wdv[:, 2 * j:2 * j + 2, :], (128, 2, D), wdn_d, wd, wds, j)
            cnt = [0]

            def rhs_fn(t0, t1):
                return ab[cnt[0] % 2]

            lo = 0 if do_ctx else LC
            with ES() as st2:
                xb = [K.sb([128, KC, 256], F32, "uxb", st2) for _ in range(2)]
                for bi, t0 in enumerate(range(lo, T, 256)):
                    t1 = t0 + 256
                    row = 4 if t0 < LC else b
                    a_ = ab[bi % 2]
                    x_ = xb[bi % 2]
                    K.dma("sp", a_[:, :, :], aTv[:, :, t0:t1], reads=[aT_d], writes=[a_])
                    K.dma("sp", x_[:, :, :], xview[:, :, t0:t1], reads=xdeps(t0, t1), writes=[x_])
                    for oc in range(KC):
                        pb = PS[oc % 4]
                        for kc in range(22):
                            K.mm(pb[:, 0:256], wd[:, kc, oc * 128:(oc + 1) * 128], a_[:, kc, :], kc == 0, kc == 21, [wd, a_], [pb])
                        K.stt("dve", x_[:, oc, :], pb[:, 0:256], modsT[:, oc, 5, row:row + 1], x_[:, oc, :],
                              ALU.mult, ALU.add, [pb, modsT, x_], [x_])
                    K.dma("pool", xview[:, :, t0:t1], x_[:, :, :], reads=[x_], writes=xdeps(t0, t1))
        K.barrier()

    def stage_out(b):
        with ES() as st:
            xb = [K.sb([128, KC, 512], F32, "oxb", st) for _ in range(2)]
            sq = [K.sb([128, 512], BF16, "sq", st) for _ in range(3)]
            rstd = [K.sb([128, 512], F32, "rstd", st) for _ in range(2)]
            yT = [K.sb([128, KC, 512], F32, "yT", st) for _ in range(2)]
            ob = [K.sb([128, D], F32, "ob", st) for _ in range(3)]
            for bi, (t0, t1) in enumerate(BLOCKS[1:]):
                n = t1 - t0
                x_ = xb[bi % 2]
                K.dma("sp", x_[:, :, 0:n], xview[:, :, t0:t1], reads=xdeps(t0, t1), writes=[x_])
                pb = PS[bi % 2]
                for cc in range(KC):
                    s = sq[cc % 3]
                    K.act(s[:, 0:n], x_[:, cc, 0:n], AF.Square, [x_], [s])
                    K.mm(pb[:, 0:n], ones_bf[:, :], s[:, 0:n], cc == 0, cc == KC - 1, [ones_bf, s], [pb])
                r = rstd[bi % 2]
                K.rsqrt(r[:, 0:n], pb[:, 0:n], 1.0 / D, EPS, [pb], [r])
                y = yT[bi % 2]
                for cc in range(KC):
                    K.stt("dve", y[:, cc, 0:n], x_[:, cc, 0:n], gcols[:, 4, cc:cc + 1], r[:, 0:n],
                          ALU.mult, ALU.mult, [x_, gcols, r], [y])
                for ti in range(n // 128):
                    o = ob[ti % 3]
                    for half in range(2):
                        pb2 = PS[2 + (2 * ti + half) % 4]
                        for q in range(4):
                            cc = half * 4 + q
                            K.transpose(pb2[:, q * 128:(q + 1) * 128], y[:, cc, ti * 128:(ti + 1) * 128], ident[:, :], [y, ident], [pb2])
                        K.copy("act", o[:, half * 512:(half + 1) * 512], pb2[:, :], [pb2], [o])
                    tok = t0 - LC + ti * 128
                    K.dma("pool", out_d[b, tok:tok + 128, :], o[:, :], reads=[o], writes=[out_d])
        K.barrier()

    def stage_swa(b, hT):
        win = cdin_d[0, :, :].rearrange("(kc p) n -> p kc n", p=128)
        with ES() as st:
            attT = K.sb([64, 8, LL], BF16, "attT", st)
            wo = K.sb([64, 8, D], BF16, "wo_att", st)
            with ES() as st1:
                wq = K.sb([128, KC, 512], BF16, "wq", st1)
                wqs = K.sb([128, KC, 512], BF16, "wqs", st1)
                wk = K.sb([128, KC, 128], BF16, "wk", st1)
                wks = K.sb([128, KC, 128], BF16, "wks", st1)
                wv = K.sb([128, KC, 128], BF16, "wv", st1)
                cosT = K.sb([64, LL], F32, "cosT", st1)
                sinT = K.sb([64, LL], F32, "sinT", st1)
                qT = K.sb([64, 8, LL], BF16, "qT", st1)
                kT = K.sb([64, 2, T], BF16, "kT", st1)
                vtok = K.sb([128, 18, 2, 128], BF16, "vtok", st1)
                esink = K.sb([128, 8], F32, "esink", st1)
                maskP = K.sb([128, 128], F32, "maskP", st1)
                maskN = K.sb([128, 128], F32, "maskN", st1)
                t1b = [K.sb([64, 512], F32, "rt1", st1) for _ in range(2)]
                t2b = [K.sb([64, 512], F32, "rt2", st1) for _ in range(2)]
                pT = [K.sb([128, 512], BF16, "pT", st1) for _ in range(4)]
                dsum = [K.sb([128, 512], F32, "dsum", st1) for _ in range(2)]
                drec = [K.sb([64, 512], F32, "drec", st1) for _ in range(2)]
                K.memset("pool", vtok[:, :, :, :], 1.0, [vtok])
                stw = ES()
                wstg = [K.sb([128, 2048], F32, "wstg", stw)]
                K.dma("sp", cosT[:, :], swacos_d[:, :], reads=[swacos_d], writes=[cosT])
                K.dma("sp", sinT[:, :], swasin_d[:, :], reads=[swasin_d], writes=[sinT])
                K.dma("sp", maskP[:, :], triB_d[:, :], reads=[triB_d], writes=[maskP])
                K.dma("sp", maskN[:, :], triF_d[:, :], reads=[triF_d], writes=[maskN])
                K.dma("sp", esink[:, :], sink_d[0, :].partition_broadcast(128), reads=[sink_d], writes=[esink])
                K.act(esink[:, :], esink[:, :], AF.Exp, [esink], [esink])
                for j in range(2):
                    load_w_bf16(wq[:, :, j * 256:(j + 1) * 256], win[:, :, CD_Q + j * 256:CD_Q + (j + 1) * 256], (128, KC, 256), cdin_d, wq, wstg, 0)
                load_w_bf16(wk[:, :, :], win[:, :, CD_K:CD_K + 128], (128, KC, 128), cdin_d, wk, wstg, 0)
                load_w_bf16(wv[:, :, :], win[:, :, CD_V:CD_V + 128], (128, KC, 128), cdin_d, wv, wstg, 0)
                for (w_, ws_, nh) in ((wq, wqs, 64), (wk, wks, 16)):
                    wv4 = w_[:, :, :].rearrange("p k (h two d) -> p (k h) two d", two=2, d=32)
                    ws4 = ws_[:, :, :].rearrange("p k (h two d) -> p (k h) two d", two=2, d=32)
                    K.ts("pool", ws4[:, :, 0, :], wv4[:, :, 1, :], -1.0, None, ALU.mult, None, [w_], [ws_])
                    K.copy("pool", ws4[:, :, 1, :], wv4[:, :, 0, :], [w_], [ws_])
                wov = cdout_d[0, 1024:1536, :].rearrange("(h d) n -> d h n", d=64)
                for j in range(4):
                    load_w_bf16(wo[:, j * 2:(j + 1) * 2, :], wov[:, j * 2:(j + 1) * 2, :], (64, 2, D), cdout_d, wo, wstg, 0)
                K.barrier()
                stw.close()
                cnt = 0
                for h in range(8):
                    for j in range(4):
                        t0 = LC + j * 512
                        pa, pb = PS[(cnt * 2) % 4], PS[(cnt * 2 + 1) % 4]
                        for kc in range(KC):
                            K.mm(pa[0:64, :], wq[:, kc, h * 64:(h + 1) * 64], hT[:, kc, t0:t0 + 512], kc == 0, kc == KC - 1, [wq, hT], [pa])
                        for kc in range(KC):
                            K.mm(pb[0:64, :], wqs[:, kc, h * 64:(h + 1) * 64], hT[:, kc, t0:t0 + 512], kc == 0, kc == KC - 1, [wqs, hT], [pb])
                        a_, b_ = t1b[cnt % 2], t2b[cnt % 2]
                        K.tt("dve", a_[:, :], pa[0:64, :], cosT[:, j * 512:(j + 1) * 512], ALU.mult, [pa, cosT], [a_])
                        K.tt("dve", b_[:, :], pb[0:64, :], sinT[:, j * 512:(j + 1) * 512], ALU.mult, [pb, sinT], [b_])
                        K.tt("pool", qT[:, h, j * 512:(j + 1) * 512], a_[:, :], b_[:, :], ALU.add, [a_, b_], [qT])
                        cnt += 1
                for g in range(2):
                    pa = PS[cnt % 4]
                    for kc in range(KC):
                        K.mm(pa[0:64, 0:LC], wk[:, kc, g * 64:(g + 1) * 64], hT[:, kc, 0:LC], kc == 0, kc == KC - 1, [wk, hT], [pa])
                    K.copy("act", kT[:, g, 0:LC], pa[0:64, 0:LC], [pa], [kT])
                    cnt += 1
                    for j in range(4):
                        t0 = LC + j * 512
                        pa, pb = PS[(cnt * 2) % 4], PS[(cnt * 2 + 1) % 4]
                        for kc in range(KC):
                            K.mm(pa[0:64, :], wk[:, kc, g * 64:(g + 1) * 64], hT[:, kc, t0:t0 + 512], kc == 0, kc == KC - 1, [wk, hT], [pa])
                        for kc in range(KC):
                            K.mm(pb[0:64, :], wks[:, kc, g * 64:(g + 1) * 64], hT[:, kc, t0:t0 + 512], kc == 0, kc == KC - 1, [wks, hT], [pb])
                        a_, b_ = t1b[cnt % 2], t2b[cnt % 2]
                        K.tt("dve", a_[:, :], pa[0:64, :], cosT[:, j * 512:(j + 1) * 512], ALU.mult, [pa, cosT], [a_])
                        K.tt("dve", b_[:, :], pb[0:64, :], sinT[:, j * 512:(j + 1) * 512], ALU.mult, [pb, sinT], [b_])
                        K.tt("pool", kT[:, g, t0:t0 + 512], a_[:, :], b_[:, :], ALU.add, [a_, b_], [kT])
                        cnt += 1
                for ti in range(18):
                    pa = PS[ti % 4]
                    for kc in range(KC):
                        K.mm(pa[:, 0:128], hT[:, kc, ti * 128:(ti + 1) * 128], wv[:, kc, :], kc == 0, kc == KC - 1, [hT, wv], [pa])
                    K.copy("act", vtok[:, ti, :, 0:64], pa[:, 0:128].rearrange("p (g e) -> p g e", g=2), [pa], [vtok])
                units = []
                u = 0
                for i in range(16):
                    for g in range(2):
                        keys = [(0, None), (1, None)]
                        if i > 0:
                            keys.append((2 + i - 1, maskP))
                        keys.append((2 + i, None))
                        if i < 15:
                            keys.append((2 + i + 1, maskN))
                        for ki, (kt, mask) in enumerate(keys):
                            units.append((i, g, ki, kt, mask, len(keys), u))
                        u += 1

                def front(un, idx):
                    i, g, ki, kt, mask, nk, uu = un
                    psc = PS[idx % 4]
                    K.mm(psc[:, :].rearrange("p (h q) -> p h q", h=4), kT[:, g, kt * 128:(kt + 1) * 128],
                         qT[:, g * 4:(g + 1) * 4, i * 128:(i + 1) * 128], True, True, [kT, qT], [psc])

                def back(un, idx):
                    i, g, ki, kt, mask, nk, uu = un
                    psc = PS[idx % 4]
                    pacc = PS[4 + (uu % 2) * 2]
                    p_ = pT[idx % 4]
                    K.act(p_[:, :], psc[:, :], AF.Exp, [psc], [p_], scale=0.125)
                    if mask is not None:
                        K.tt("pool", p_[:, :].rearrange("p (h q) -> p h q", h=4), p_[:, :].rearrange("p (h q) -> p h q", h=4),
                             mask[:, :].unsqueeze(1).to_broadcast([128, 4, 128]), ALU.mult, [p_, mask], [p_])
                    K.mm(pacc[:, :], vtok[:, kt, g, :], p_[:, :], ki == 0, ki == nk - 1, [vtok, p_], [pacc])
                    if ki == nk - 1:
                        d_ = dsum[uu % 2]
                        r_ = drec[uu % 2]
                        K.tt("dve", d_[64:128, :].rearrange("p (h q) -> p h q", h=4), pacc[64:128, :].rearrange("p (h q) -> p h q", h=4),
                             esink[64:128, g * 4:(g + 1) * 4].unsqueeze(2).to_broadcast([64, 4, 128]), ALU.add, [pacc, esink], [d_])
                        K.op("dve", lambda e, d_=d_, r_=r_: e.reciprocal(out=r_[0:64, :], in_=d_[64:128, :]), [d_], [r_])
                        K.tt("dve", attT[:, g * 4:(g + 1) * 4, i * 128:(i + 1) * 128], pacc[0:64, :].rearrange("p (h q) -> p h q", h=4),
                             r_[:, :].rearrange("p (h q) -> p h q", h=4), ALU.mult, [pacc, r_], [attT])

                LA = 2
                for idx in range(min(LA, len(units))):
                    front(units[idx], idx)
                for idx, un in enumerate(units):
                    if idx + LA < len(units):
                        front(units[idx + LA], idx + LA)
                    back(un, idx)
            K.barrier()
            pieces = [((lambda oc, h=h: wo[:, h, oc * 128:(oc + 1) * 128]),
                       (lambda t0, t1, h=h: attT[:, h, t0 - LC:t1 - LC]), [wo, attT]) for h in range(8)]
            apply_out(b, pieces, 2, LC)
        K.barrier()

    def stage_ssd(b, hT_scope_fn):
        win = cdin_d[0, :, :].rearrange("(kc p) n -> p kc n", p=128)
        with ES() as st:
            uT = K.sb([128, 8, LL], BF16, "uT", st)
            with ES() as st1:
                xs_tok = K.sb([128, 18, 1024], BF16, "xs_tok", st1)
                B_tok = K.sb([128, 18, 256], BF16, "B_tok", st1)
                BCT = K.sb([128, 4, T], BF16, "BCT", st1)
                dtv = K.sb([128, 18, 32], F32, "dtv", st1)
                dtA = K.sb([128, 18, 32], F32, "dtA", st1)
                a_bc = K.sb([128, 32], F32, "a_bc", st1)
                dtb_bc = K.sb([128, 32], F32, "dtb_bc", st1)
                D_bc = K.sb([128, 16], F32, "D_bc", st1)
                sng = K.sb([128, 8], F32, "sng", st1)
                scw = K.sb([128, 5, 12], F32, "scw", st1)
                scb = K.sb([128, 12], F32, "scb", st1)
                triF = K.sb([128, 128], F32, "triF", st1)
                triB = K.sb([128, 128], F32, "triB", st1)
                strF = K.sb([128, 128], F32, "strF", st1)
                strB = K.sb([128, 128], F32, "strB", st1)
                for (dst, src) in ((triF, triF_d), (triB, triB_d), (strF, strF_d), (strB, strB_d)):
                    K.dma("sp", dst[:, :], src[:, :], reads=[src], writes=[dst])
                K.dma("sp", a_bc[:, :], salog_d[0, :, :].rearrange("a b -> (a b)").partition_broadcast(128), reads=[salog_d], writes=[a_bc])
                K.act(a_bc[:, :], a_bc[:, :], AF.Exp, [a_bc], [a_bc])
                K.ts("dve", a_bc[:, :], a_bc[:, :], -1.0, None, ALU.mult, None, [a_bc], [a_bc])
                K.dma("sp", dtb_bc[:, :], sdtb_d[0, :, :].rearrange("a b -> (a b)").partition_broadcast(128), reads=[sdtb_d], writes=[dtb_bc])
                K.dma("sp", D_bc[:, :], sd_d[0, :].partition_broadcast(128), reads=[sd_d], writes=[D_bc])
                K.dma("sp", sng[:, :], colvec(sng_d[0, :], 8), reads=[sng_d], writes=[sng], allow_slow_non_contiguous=True)
                for tap in range(5):
                    K.dma("sp", scw[:, tap, :], colvec(scw_d[0, tap, :], 12), reads=[scw_d], writes=[scw], allow_slow_non_contiguous=True)
                K.dma("sp", scb[:, :], colvec(scb_d[0, :], 12), reads=[scb_d], writes=[scb], allow_slow_non_contiguous=True)
                with ES() as st2:
                    hT = K.sb([128, KC, T], BF16, "hT", st2)
                    load_hT(hT)
                    wstg = [K.sb([128, 4096], F32, "wstg", st2)]
                    st3 = ES()
                    wch = [K.sb([128, KC, 128], BF16, "wch", st3) for _ in range(2)]
                    PW = T + 8
                    upad = [K.sb([128, PW], F32, "upad", st3) for _ in range(2)]
                    cvb = [K.sb([128, T], F32, "cvb", st3) for _ in range(1)]
                    xsT = [K.sb([128, T], BF16, "xsT", st3) for _ in range(2)]
                    for u_ in upad:
                        K.memset("pool", u_[:, :], 0.0, [u_])

                    def poff(t):
                        return t + 2 if t < LC else t + 6

                    def ssd_wload(fc):
                        w_ = wch[fc % 2]
                        load_w_bf16(w_[:, :, :], win[:, :, CD_XBC + fc * 128:CD_XBC + (fc + 1) * 128], (128, KC, 128), cdin_d, w_, wstg, fc)

                    ssd_wload(0)
                    for fc in range(12):
                        w_ = wch[fc % 2]
                        if fc + 1 < 12:
                            ssd_wload(fc + 1)
                        up = upad[fc % 2]
                        for bi, (t0, t1) in enumerate(BLOCKS):
                            n = t1 - t0
                            pb = PS[bi % 4]
                            for kc in range(KC):
                                K.mm(pb[:, 0:n], w_[:, kc, :], hT[:, kc, t0:t1], kc == 0, kc == KC - 1, [w_, hT], [pb])
                            K.copy("act", up[:, poff(t0):poff(t0) + n], pb[:, 0:n], [pb], [up])
                        cv_ = cvb[0]
                        for (s0, ln) in ((0, LC), (LC, LL)):
                            p0 = poff(s0)
                            K.act(cv_[:, s0:s0 + ln], up[:, p0:p0 + ln], AF.Identity, [up, scw, scb], [cv_],
                                  bias=scb[:, fc:fc + 1], scale=scw[:, 2, fc:fc + 1])
                            for tap in (0, 1, 3, 4):
                                K.stt("dve", cv_[:, s0:s0 + ln], up[:, p0 + tap - 2:p0 + tap - 2 + ln], scw[:, tap, fc:fc + 1], cv_[:, s0:s0 + ln],
                                      ALU.mult, ALU.add, [up, scw, cv_], [cv_])
                        if fc < 8:
                            dstT, dst_ap = xsT[fc % 2], xsT[fc % 2][:, :]
                        else:
                            dstT, dst_ap = BCT, BCT[:, fc - 8, :]
                        K.act(dst_ap, cv_[:, :], AF.Silu, [cv_], [dstT])
                        if fc < 10:
                            for grp in range(3):
                                tis = list(range(grp * 8, min(18, grp * 8 + 8)))
                                pb = PS[4 + grp % 2]
                                pbv = pb[:, :].bitcast(BF16)
                                for qi, ti in enumerate(tis):
                                    K.transpose(pbv[:, qi * 128:(qi + 1) * 128], dst_ap[:, ti * 128:(ti + 1) * 128], identb[:, :], [dstT, identb], [pb])
                                nt = len(tis)
                                src_v = pbv[:, 0:nt * 128].rearrange("p (a f) -> p a f", a=nt)
                                if fc < 8:
                                    K.copy("act", xs_tok[:, tis[0]:tis[0] + nt, fc * 128:(fc + 1) * 128], src_v, [pb], [xs_tok])
                                else:
                                    K.copy("act", B_tok[:, tis[0]:tis[0] + nt, (fc - 8) * 128:(fc - 7) * 128], src_v, [pb], [B_tok])
                    K.barrier()
                    st3.close()
                    wz = K.sb([128, KC, 1024], BF16, "wz", st2)
                    wdt = K.sb([128, KC, 32], BF16, "wdt", st2)
                    szb = [K.sb([128, 1024], BF16, "szb", st2) for _ in range(2)]
                    dtt = [K.sb([128, 32], F32, "dtt", st2) for _ in range(2)]
                    for j in range(2):
                        load_w_bf16(wz[:, :, j * 512:(j + 1) * 512], win[:, :, CD_Z + j * 512:CD_Z + (j + 1) * 512], (128, KC, 512), cdin_d, wz, wstg, j)
                    load_w_bf16(wdt[:, :, :], win[:, :, CD_DT:CD_DT + 32], (128, KC, 32), cdin_d, wdt, wstg, 0)
                    for ti in range(18):
                        pb = PS[ti % 4]
                        for kc in range(KC):
                            K.mm(pb[:, 0:32], hT[:, kc, ti * 128:(ti + 1) * 128], wdt[:, kc, :], kc == 0, kc == KC - 1, [hT, wdt], [pb])
                        d_ = dtt[ti % 2]
                        K.tt("dve", d_[:, :], pb[:, 0:32], dtb_bc[:, :], ALU.add, [pb, dtb_bc], [d_])
                        K.act(d_[:, :], d_[:, :], AF.Exp, [d_], [d_])
                        K.act(dtv[:, ti, :], d_[:, :], AF.Ln, [d_], [dtv], bias=K.eps_ap(1.0))
                    K.tt("dve", dtA[:, :, :], dtv[:, :, :], a_bc[:, :].unsqueeze(1).to_broadcast([128, 18, 32]), ALU.mult, [dtv, a_bc], [dtA])
                    for li in range(16):
                        ti = li + 2
                        s_ = szb[li % 2]
                        for j in range(2):
                            pb = PS[(li * 2 + j) % 4]
                            for kc in range(KC):
                                K.mm(pb[:, :], hT[:, kc, ti * 128:(ti + 1) * 128], wz[:, kc, j * 512:(j + 1) * 512], kc == 0, kc == KC - 1, [hT, wz], [pb])
                            K.act(s_[:, j * 512:(j + 1) * 512], pb[:, :], AF.Silu, [pb], [s_])
                        K.dma("pool", sz_d[li, :, :], s_[:, :], reads=[s_], writes=[sz_d])
                K.barrier()
                with ES() as st2:
                    Hf = K.sb([128, 2, 512], F32, "Hf", st2)
                    Hb = K.sb([128, 2, 512], F32, "Hb", st2)
                    hbf = [K.sb([128, 1024], BF16, "hbf", st2) for _ in range(2)]
                    hinf = [K.sb([128, 1024], BF16, "hinf", st2) for _ in range(2)]
                    szt = [K.sb([128, 1024], BF16, "szt", st2) for _ in range(2)]
                    prep = {}
                    for nm in ("acs0", "eac0", "dend0", "cdec0", "wgt0", "acs1", "eac1", "dend1", "cdec1", "wgt1"):
                        prep[nm] = K.sb([128, 16], F32, nm, st2)
                    xte = K.sb([128, 1024], BF16, "xte", st2)
                    rseg = K.sb([128, 16, 128], F32, "rseg", st2)
                    cbm = [K.sb([128, 2, 128], F32, "cbm", st2) for _ in range(2)]
                    eseg = [K.sb([128, 512], F32, "eseg", st2) for _ in range(2)]
                    Lt = [K.sb([128, 16, 128], BF16, "Lt", st2) for _ in range(2)]
                    xdt = [K.sb([128, 1024], BF16, "xdt", st2) for _ in range(2)]
                    yacc = K.sb([128, 1024], F32, "yacc", st2)
                    ytmp = K.sb([128, 512], F32, "ytmp", st2)
                    ub = K.sb([128, 1024], F32, "ub", st2)
                    ubf = K.sb([128, 1024], BF16, "ubf", st2)
                    ssq = K.sb([128, 2], F32, "ssq", st2)
                    junk = K.sb([128, 512], BF16, "junk", st2)
                    K.memset("dve", Hf[:, :, :], 0.0, [Hf])
                    K.memset("dve", Hb[:, :, :], 0.0, [Hb])
                    tri = (triF, triB)
                    strm = (strF, strB)

                    def do_prep(c, d):
                        pp = PS[0]
                        K.mm(pp[:, 0:16], tri[d][:, :], dtA[:, c, d * 16:(d + 1) * 16], True, True, [tri[d], dtA], [pp])
                        K.mm(pp[:, 16:32], ones_f[:, :], dtA[:, c, d * 16:(d + 1) * 16], True, True, [ones_f, dtA], [pp])
                        acs, eac, dend, cdec, wgt = (prep[n_ + str(d)] for n_ in ("acs", "eac", "dend", "cdec", "wgt"))
                        K.copy("act", acs[:, :], pp[:, 0:16], [pp], [acs])
                        K.act(eac[:, :], pp[:, 0:16], AF.Exp, [pp], [eac])
                        K.act(cdec[:, :], pp[:, 16:32], AF.Exp, [pp], [cdec])
                        K.tt("dve", dend[:, :], pp[:, 16:32], acs[:, :], ALU.subtract, [pp, acs], [dend])
                        K.act(dend[:, :], dend[:, :], AF.Exp, [dend], [dend])
                        K.tt("dve", wgt[:, :], dend[:, :], dtv[:, c, d * 16:(d + 1) * 16], ALU.mult, [dend, dtv], [wgt])

                    def state_update(c, d, H):
                        wgt, cdec = prep["wgt" + str(d)], prep["cdec" + str(d)]
                        K.tt("dve", xte[:, :].rearrange("p (h e) -> p h e", h=16), xs_tok[:, c, :].rearrange("p (h e) -> p h e", h=16),
                             wgt[:, :].unsqueeze(2).to_broadcast([128, 16, 64]), ALU.mult, [xs_tok, wgt], [xte])
                        for g in range(2):
                            pb = PS[6 + g]
                            K.mm(pb[:, :], B_tok[:, c, g * 128:(g + 1) * 128], xte[:, g * 512:(g + 1) * 512], True, True, [B_tok, xte], [pb])
                            hv = H[:, g, :].rearrange("p (h e) -> p h e", h=8)
                            K.tt("dve", hv, hv, cdec[:, g * 8:(g + 1) * 8].unsqueeze(2).to_broadcast([128, 8, 64]), ALU.mult, [H, cdec], [H])
                            K.tt("dve", H[:, g, :], H[:, g, :], pb[:, :], ALU.add, [H, pb], [H])

                    for c in range(18):
                        if c >= 2:
                            hb_ = hbf[c % 2]
                            K.copy("act", hb_[:, :], Hf[:, :, :].rearrange("p g e -> p (g e)"), [Hf], [hb_])
                            K.dma("pool", hin_d[c - 2, :, :], hb_[:, :], reads=[hb_], writes=[hin_d])
                        if c == 17:
                            break
                        do_prep(c, 0)
                        state_update(c, 0, Hf)
                    K.barrier()
                    for c in [1, 0] + list(range(17, 1, -1)):
                        do_prep(c, 1)
                        if c >= 2:
                            li = c - 2
                            do_prep(c, 0)
                            hi_ = hinf[li % 2]
                            sz_ = szt[li % 2]
                            K.dma("sp", hi_[:, :], hin_d[li, :, :], reads=[hin_d], writes=[hi_])
                            K.dma("sp", sz_[:, :], sz_d[li, :, :], reads=[sz_d], writes=[sz_])
                            hb_ = hbf[li % 2]
                            K.copy("act", hb_[:, :], Hb[:, :, :].rearrange("p g e -> p (g e)"), [Hb], [hb_])
                            tsl = slice(c * 128, (c + 1) * 128)
                            pcb = PS[1]
                            for g in range(2):
                                K.mm(pcb[:, g * 128:(g + 1) * 128], BCT[:, g, tsl], BCT[:, 2 + g, tsl], True, True, [BCT], [pcb])
                            for d in range(2):
                                K.tt("dve", cbm[d][:, :, :], pcb[:, 0:256].rearrange("p (g q) -> p g q", g=2),
                                     tri[d][:, :].unsqueeze(1).to_broadcast([128, 2, 128]), ALU.mult, [pcb, tri[d]], [cbm[d]])
                            for d in range(2):
                                K.tt("dve", rseg[:, :, :], tri[d][:, :].unsqueeze(1).to_broadcast([128, 16, 128]),
                                     dtA[:, c, d * 16:(d + 1) * 16].unsqueeze(2).to_broadcast([128, 16, 128]), ALU.mult, [tri[d], dtA], [rseg])
                                for hb4 in range(4):
                                    pseg = PS[2 + hb4 % 2]
                                    K.mm(pseg[:, :], strm[d][:, :], rseg[:, hb4 * 4:(hb4 + 1) * 4, :], True, True, [strm[d], rseg], [pseg])
                                    es_ = eseg[hb4 % 2]
                                    K.act(es_[:, :], pseg[:, :], AF.Exp, [pseg], [es_])
                                    g = hb4 // 2
                                    K.tt("dve", Lt[d][:, hb4 * 4:(hb4 + 1) * 4, :], es_[:, :].rearrange("p (h q) -> p h q", h=4),
                                         cbm[d][:, g, :].unsqueeze(1).to_broadcast([128, 4, 128]), ALU.mult, [es_, cbm[d]], [Lt[d]])
                                K.tt("dve", xdt[d][:, :].rearrange("p (h e) -> p h e", h=16), xs_tok[:, c, :].rearrange("p (h e) -> p h e", h=16),
                                     dtv[:, c, d * 16:(d + 1) * 16].unsqueeze(2).to_broadcast([128, 16, 64]), ALU.mult, [xs_tok, dtv], [xdt[d]])
                            for h in range(16):
                                py = PS[4 + h // 8]
                                col = (h % 8) * 64
                                for d in range(2):
                                    K.mm(py[:, col:col + 64], Lt[d][:, h, :], xdt[d][:, h * 64:(h + 1) * 64], d == 0, d == 1, [Lt[d], xdt[d]], [py])
                            K.tt("dve", yacc[:, :].rearrange("p (h e) -> p h e", h=16), xs_tok[:, c, :].rearrange("p (h e) -> p h e", h=16),
                                 D_bc[:, :].unsqueeze(2).to_broadcast([128, 16, 64]), ALU.mult, [xs_tok, D_bc], [yacc])
                            for g in range(2):
                                K.tt("dve", yacc[:, g * 512:(g + 1) * 512], yacc[:, g * 512:(g + 1) * 512], PS[4 + g][:, :], ALU.add, [yacc, PS[4 + g]], [yacc])
                            for d in range(2):
                                hsrc = hi_ if d == 0 else hb_
                                eac = prep["eac" + str(d)]
                                for g in range(2):
                                    po = PS[6 + g]
                                    K.mm(po[:, :], BCT[:, 2 + g, tsl], hsrc[:, g * 512:(g + 1) * 512], True, True, [BCT, hsrc], [po])
                                    K.tt("dve", ytmp[:, :].rearrange("p (h e) -> p h e", h=8), po[:, :].rearrange("p (h e) -> p h e", h=8),
                                         eac[:, g * 8:(g + 1) * 8].unsqueeze(2).to_broadcast([128, 8, 64]), ALU.mult, [po, eac], [ytmp])
                                    K.tt("dve", yacc[:, g * 512:(g + 1) * 512], yacc[:, g * 512:(g + 1) * 512], ytmp[:, :], ALU.add, [yacc, ytmp], [yacc])
                            K.tt("dve", ub[:, :], yacc[:, :], sz_[:, :], ALU.mult, [yacc, sz_], [ub])
                            K.memset("pool", ssq[:, :], 0.0, [ssq])
                            for g in range(2):
                                K.op("act", lambda e, g=g: e.activation(out=junk[:, :], in_=ub[:, g * 512:(g + 1) * 512], func=AF.Square,
                                                                         accum_out=ssq[:, g:g + 1]), [ub], [junk, ssq])
                            K.rsqrt(ssq[:, :], ssq[:, :], 1.0 / 512, EPS, [ssq], [ssq])
                            for g in range(2):
                                K.ts("dve", ubf[:, g * 512:(g + 1) * 512], ub[:, g * 512:(g + 1) * 512], ssq[:, g:g + 1], None, ALU.mult, None, [ub, ssq], [ubf])
                            pt = PS[1]
                            ptv = pt[:, :].bitcast(BF16)
                            for cc in range(8):
                                K.transpose(ptv[:, cc * 128:(cc + 1) * 128], ubf[:, cc * 128:(cc + 1) * 128], identb[:, :], [ubf, identb], [pt])
                            for cc in range(8):
                                K.act(uT[:, cc, li * 128:(li + 1) * 128], ptv[:, cc * 128:(cc + 1) * 128], AF.Identity, [pt, sng], [uT], scale=sng[:, cc:cc + 1])
                        if c != 2:
                            state_update(c, 1, Hb)
            K.barrier()
            wo = K.sb([128, 8, D], BF16, "wo_ssd", st)
            wostg = [K.sb([128, 4096], F32, "wostg", st)]
            for j in range(2):
                load_w_bf16(wo[:, j * 4:(j + 1) * 4, :], cdout_d[0, 0:1024, :].rearrange("(kc p) n -> p kc n", p=128)[:, j * 4:(j + 1) * 4, :],
                            (128, 4, D), cdout_d, wo, wostg, 0)
            pieces = [((lambda oc, cc=cc: wo[:, cc, oc * 128:(oc + 1) * 128]),
                       (lambda t0, t1, cc=cc: uT[:, cc, t0 - LC:t1 - LC]), [wo, uT]) for cc in range(8)]
            apply_out(b, pieces, 2, LC)
        K.barrier()


    AB_CQ, AB_CKV, AB_KR, AB_RW = 0, 384, 640, 672

    def stage_mla(b, hT):
        win = abin_d[0, :, :].rearrange("(kc p) n -> p kc n", p=128)
        sc = 96.0 ** -0.5
        with ES() as st:
            attT = K.sb([64, 8, T], BF16, "mattT", st)
            wo = K.sb([64, 8, D], BF16, "wo_mla", st)
            with ES() as st1:
                wcq = K.sb([128, KC, 384], BF16, "wcq", st1)
                wckv = K.sb([128, KC, 256], BF16, "wckv", st1)
                wkr = K.sb([128, KC, 96], BF16, "wkr", st1)
                wkrs = K.sb([128, KC, 96], BF16, "wkrs", st1)
                wqu = K.sb([128, 3, 768], BF16, "wqu", st1)
                wqus = K.sb([128, 24, 96], BF16, "wqus", st1)
                wkvu = K.sb([128, 2, 1024], BF16, "wkvu", st1)
                cqn = K.sb([128, 3, T], BF16, "cqn", st1)
                ckvn = K.sb([128, 2, T], BF16, "ckvn", st1)
                vo = K.sb([128, 18, 128], BF16, "mvo", st1)
                cosT = K.sb([96, LL], F32, "mcos", st1)
                sinT = K.sb([96, LL], F32, "msin", st1)
                gq = K.sb([128, 3], F32, "gq", st1)
                gkv = K.sb([128, 2], F32, "gkv", st1)
                qf = K.sb([96, T], BF16, "qf", st1)
                kf = K.sb([96, T], BF16, "kf", st1)
                sq = [K.sb([128, 512], BF16, "msq", st1) for _ in range(2)]
                rstd = [K.sb([128, 512], F32, "mrstd", st1) for _ in range(2)]
                t1b = [K.sb([96, 512], F32, "mt1", st1) for _ in range(1)]
                t2b = [K.sb([96, 512], F32, "mt2", st1) for _ in range(1)]
                pT = [K.sb([128, 512], BF16, "mpT", st1) for _ in range(4)]
                rec = [K.sb([64, 512], F32, "mrec", st1) for _ in range(2)]
                stw = ES()
                wstg = [K.sb([128, 4096], F32, "wstg", stw)]
                K.dma("sp", cosT[64:96, :], mlacos_d[:, :], reads=[mlacos_d], writes=[cosT])
                K.dma("sp", sinT[64:96, :], mlasin_d[:, :], reads=[mlasin_d], writes=[sinT])
                K.dma("sp", gq[:, :], colvec(mqg_d[0, :], 3), reads=[mqg_d], writes=[gq], allow_slow_non_contiguous=True)
                K.dma("sp", gkv[:, :], colvec(mkg_d[0, :], 2), reads=[mkg_d], writes=[gkv], allow_slow_non_contiguous=True)
                K.memset("pool", wkr[:, :, :], 0.0, [wkr])
                K.memset("pool", wkrs[:, :, :], 0.0, [wkrs])
                K.memset("pool", wqus[:, :, :], 0.0, [wqus])
                K.memset("pool", vo[:, :, :], 1.0, [vo])
                load_w_bf16(wcq[:, :, :], win[:, :, AB_CQ:AB_CQ + 384], (128, KC, 384), abin_d, wcq, wstg, 0)
                load_w_bf16(wckv[:, :, :], win[:, :, AB_CKV:AB_CKV + 256], (128, KC, 256), abin_d, wckv, wstg, 0)
                load_w_bf16(wkr[:, :, 64:96], win[:, :, AB_KR:AB_KR + 32], (128, KC, 32), abin_d, wkr, wstg, 0)
                load_w_bf16(wqu[:, :, :], mqu_d[0, :, :].rearrange("(c p) n -> p c n", p=128), (128, 3, 768), mqu_d, wqu, wstg, 0)
                load_w_bf16(wkvu[:, :, :], mkvu_d[0, :, :].rearrange("(c p) n -> p c n", p=128), (128, 2, 1024), mkvu_d, wkvu, wstg, 0)
                wov = about_d[0, 0:512, :].rearrange("(h d) n -> d h n", d=64)
                for j in range(2):
                    load_w_bf16(wo[:, j * 4:(j + 1) * 4, :], wov[:, j * 4:(j + 1) * 4, :], (64, 4, D), about_d, wo, wstg, 0)
                K.ts("pool", wkrs[:, :, 64:80], wkr[:, :, 80:96], -1.0, None, ALU.mult, None, [wkr], [wkrs])
                K.copy("pool", wkrs[:, :, 80:96], wkr[:, :, 64:80], [wkr], [wkrs])
                wq24 = wqu[:, :, :].rearrange("p c (h e) -> p (c h) e", e=96)
                K.ts("pool", wqus[:, :, 64:80], wq24[:, :, 80:96], -1.0, None, ALU.mult, None, [wqu], [wqus])
                K.copy("pool", wqus[:, :, 80:96], wq24[:, :, 64:80], [wqu], [wqus])
                K.barrier()
                stw.close()
                for bi, (t0, t1) in enumerate(BLOCKS):
                    n = t1 - t0
                    for (w_, nch, g_, dst, pbase) in ((wcq, 3, gq, cqn, 0), (wckv, 2, gkv, ckvn, 4)):
                        for c3 in range(nch):
                            pb = PS[pbase + c3]
                            for kc in range(KC):
                                K.mm(pb[:, 0:n], w_[:, kc, c3 * 128:(c3 + 1) * 128], hT[:, kc, t0:t1], kc == 0, kc == KC - 1, [w_, hT], [pb])
                        pst = PS[pbase + 3] if pbase == 0 else PS[pbase + 2]
                        for c3 in range(nch):
                            s_ = sq[c3 % 2]
                            K.act(s_[:, 0:n], PS[pbase + c3][:, 0:n], AF.Square, [PS[pbase + c3]], [s_])
                            K.mm(pst[:, 0:n], ones_bf[:, :], s_[:, 0:n], c3 == 0, c3 == nch - 1, [ones_bf, s_], [pst])
                        r_ = rstd[0 if pbase == 0 else 1]
                        K.rsqrt(r_[:, 0:n], pst[:, 0:n], 1.0 / (nch * 128), EPS, [pst], [r_])
                        for c3 in range(nch):
                            K.stt("dve", dst[:, c3, t0:t1], PS[pbase + c3][:, 0:n], g_[:, c3:c3 + 1], r_[:, 0:n], ALU.mult, ALU.mult,
                                  [PS[pbase + c3], g_, r_], [dst])
                R_ = slice(64, 96)
                for bi, (t0, t1) in enumerate(BLOCKS):
                    n = t1 - t0
                    pa, pb = PS[0 + 2 * (bi % 2)], PS[1 + 2 * (bi % 2)]
                    for kc in range(KC):
                        K.mm(pa[0:96, 0:n], wkr[:, kc, :], hT[:, kc, t0:t1], kc == 0, kc == KC - 1, [wkr, hT], [pa])
                    if bi == 0:
                        K.copy("dve", kf[R_, t0:t1], pa[R_, 0:n], [pa], [kf])
                    else:
                        for kc in range(KC):
                            K.mm(pb[0:96, 0:n], wkrs[:, kc, :], hT[:, kc, t0:t1], kc == 0, kc == KC - 1, [wkrs, hT], [pb])
                        a_, b_ = t1b[0], t2b[0]
                        K.tt("dve", a_[R_, 0:n], pa[R_, 0:n], cosT[R_, t0 - LC:t1 - LC], ALU.mult, [pa, cosT], [a_])
                        K.tt("dve", b_[R_, 0:n], pb[R_, 0:n], sinT[R_, t0 - LC:t1 - LC], ALU.mult, [pb, sinT], [b_])
                        K.tt("pool", kf[R_, t0:t1], a_[R_, 0:n], b_[R_, 0:n], ALU.add, [a_, b_], [kf])
                u = 0
                for h in range(8):
                    for ti in range(18):
                        pa = PS[4 + ti % 2]
                        for c3 in range(2):
                            K.mm(pa[:, 0:64], ckvn[:, c3, ti * 128:(ti + 1) * 128], wkvu[:, c3, h * 128 + 64:h * 128 + 128],
                                 c3 == 0, c3 == 1, [ckvn, wkvu], [pa])
                        K.copy("dve", vo[:, ti, 0:64], pa[:, 0:64], [pa], [vo])
                    for bi, (t0, t1) in enumerate(BLOCKS):
                        n = t1 - t0
                        pa, pc, pd = PS[0], PS[2], PS[3]
                        for c3 in range(3):
                            K.mm(pa[0:96, 0:n], wqu[:, c3, h * 96:h * 96 + 96], cqn[:, c3, t0:t1], c3 == 0, c3 == 2, [wqu, cqn], [pa])
                        K.copy("act", qf[0:64, t0:t1], pa[0:64, 0:n], [pa], [qf])
                        if bi == 0:
                            K.copy("dve", qf[R_, t0:t1], pa[R_, 0:n], [pa], [qf])
                        else:
                            for c3 in range(3):
                                K.mm(pc[0:96, 0:n], wqus[:, c3 * 8 + h, :], cqn[:, c3, t0:t1], c3 == 0, c3 == 2, [wqus, cqn], [pc])
                            a_, b_ = t1b[0], t2b[0]
                            K.tt("dve", a_[R_, 0:n], pa[R_, 0:n], cosT[R_, t0 - LC:t1 - LC], ALU.mult, [pa, cosT], [a_])
                            K.tt("dve", b_[R_, 0:n], pc[R_, 0:n], sinT[R_, t0 - LC:t1 - LC], ALU.mult, [pc, sinT], [b_])
                            K.tt("pool", qf[R_, t0:t1], a_[R_, 0:n], b_[R_, 0:n], ALU.add, [a_, b_], [qf])
                        for c3 in range(2):
                            K.mm(pd[0:64, 0:n], wkvu[:, c3, h * 128:h * 128 + 64], ckvn[:, c3, t0:t1], c3 == 0, c3 == 1, [wkvu, ckvn], [pd])
                        K.copy("act", kf[0:64, t0:t1], pd[0:64, 0:n], [pd], [kf])
                    units = []
                    for bi, (t0, t1) in enumerate(BLOCKS):
                        keys = [0, 1] if bi == 0 else list(range(18))
                        for ki, kt in enumerate(keys):
                            units.append((bi, t0, t1, ki, kt, len(keys), u))
                        u += 1

                    def front(un, idx):
                        bi, t0, t1, ki, kt, nk, uu = un
                        n = t1 - t0
                        psc = PS[idx % 4]
                        K.mm(psc[:, 0:n], kf[:, kt * 128:(kt + 1) * 128], qf[:, t0:t1], True, True, [kf, qf], [psc])

                    def back(un, idx, h=h):
                        bi, t0, t1, ki, kt, nk, uu = un
                        n = t1 - t0
                        psc = PS[idx % 4]
                        pacc = PS[4 + (uu % 2) * 2]
                        p_ = pT[idx % 4]
                        K.act(p_[:, 0:n], psc[:, 0:n], AF.Exp, [psc], [p_], scale=sc)
                        K.mm(pacc[:, 0:n], vo[:, kt, :], p_[:, 0:n], ki == 0, ki == nk - 1, [vo, p_], [pacc])
                        if ki == nk - 1:
                            r_ = rec[uu % 2]
                            K.op("dve", lambda e, r_=r_, pacc=pacc, n=n: e.reciprocal(out=r_[0:64, 0:n], in_=pacc[64:128, 0:n]), [pacc], [r_])
                            K.tt("dve", attT[:, h, t0:t1], pacc[0:64, 0:n], r_[:, 0:n], ALU.mult, [pacc, r_], [attT])

                    LA = 2
                    for idx in range(min(LA, len(units))):
                        front(units[idx], idx)
                    for idx, un in enumerate(units):
                        if idx + LA < len(units):
                            front(units[idx + LA], idx + LA)
                        back(un, idx)
            K.barrier()
            pieces = [((lambda oc, h=h: wo[:, h, oc * 128:(oc + 1) * 128]),
                       (lambda t0, t1, h=h: attT[:, h, t0:t1]), [wo, attT]) for h in range(8)]
            apply_out(b, pieces, 2, 0)
        K.barrier()

    CW = -math.exp(-0.5)

    def stage_rwkv(b):
        win = abin_d[0, :, :].rearrange("(kc p) n -> p kc n", p=128)
        with ES() as st:
            cols = K.sb([128, 10, 4], F32, "rcols", st)
            for i_, src in enumerate((rkk_d[0, :], rka_d[0, :], None, rrk_d[0, :, :].rearrange("a b -> (a b)"), rlg_d[0, :], rlb_d[0, :],
                                      None, ra0_d[0, 0, :], ra0_d[0, 1, :])):
                if src is not None:
                    K.dma("sp", cols[:, i_, :], colvec(src, 4), reads=[rkk_d], writes=[cols], allow_slow_non_contiguous=True)
            K.ts("dve", cols[:, 2, :], cols[:, 1, :], -1.0, 1.0, ALU.mult, ALU.add, [cols], [cols])
            with ES() as st1:
                rT = K.sb([128, 4, T], BF16, "rT", st1)
                kT = K.sb([128, 4, T], BF16, "kT", st1)
                kknT = K.sb([128, 4, T], BF16, "kknT", st1)
                vtok = K.sb([128, 18, 512], BF16, "rvtok", st1)
                xwaT = K.sb([128, T], BF16, "xwaT", st1)
                mu = K.sb([128, 3, 14], F32, "mu", st1)
                blk64 = K.sb([128, 128], BF16, "blk64", st1)
                blk64f = K.sb([128, 128], F32, "blk64f", st1)
                K.dma("sp", blk64f[:, :], blk64_d[:, :], reads=[blk64_d], writes=[blk64f])
                K.copy("dve", blk64[:, :], blk64f[:, :], [blk64f], [blk64])
                K.dma("sp", mu[:, 0, :], colvec(rmp_d[0, :], 14), reads=[rmp_d], writes=[mu], allow_slow_non_contiguous=True)
                K.dma("sp", mu[:, 1, :], colvec(rmn_d[0, :], 14), reads=[rmn_d], writes=[mu], allow_slow_non_contiguous=True)
                K.tt("dve", mu[:, 2, :], mu[:, 0, :], mu[:, 1, :], ALU.add, [mu], [mu])
                K.ts("dve", mu[:, 2, :], mu[:, 2, :], -1.0, 1.0, ALU.mult, ALU.add, [mu], [mu])
                with ES() as st2:
                    hT = K.sb([128, KC, T], BF16, "hT", st2)
                    load_hT(hT)
                    wstg = [K.sb([128, 4096], F32, "wstg", st2)]
                    wch = [K.sb([128, KC, 128], BF16, "rwch", st2) for _ in range(2)]
                    g2b = K.sb([128, 512], BF16, "g2b", st2)
                    upad = [K.sb([128, T + 4], F32, "rupad", st2) for _ in range(2)]
                    xs = [K.sb([128, T], F32, "rxs", st2) for _ in range(1)]
                    xsb = [K.sb([128, T], BF16, "rxsb", st2) for _ in range(1)]
                    t32 = [K.sb([128, 512], F32, "rt32", st2) for _ in range(2)]
                    t16 = [K.sb([128, 512], BF16, "rt16", st2) for _ in range(2)]
                    gbo = [K.sb([128, T], BF16, "gbo", st2) for _ in range(1)]
                    rk32 = [K.sb([128, 512], F32, "rk32", st2) for _ in range(2)]
                    for u_ in upad:
                        K.memset("pool", u_[:, :], 0.0, [u_])
                    load_w_bf16(g2b[:, :], rg2_d[0, :, :], (128, 512), rg2_d, g2b, wstg, 0)

                    def poff(t):
                        return t + 1 if t < LC else t + 3

                    order = [4, 5, 6, 7, 0, 1, 2, 3, 8, 9, 10, 11, 12, 13]
                    def rw_wload(oi):
                        fc_ = order[oi]
                        w_ = wch[oi % 2]
                        load_w_bf16(w_[:, :, :], win[:, :, AB_RW + fc_ * 128:AB_RW + (fc_ + 1) * 128], (128, KC, 128), abin_d, w_, wstg, 0)

                    rw_wload(0)
                    for oi, fc in enumerate(order):
                        w_ = wch[oi % 2]
                        if oi + 1 < len(order):
                            rw_wload(oi + 1)
                        up = upad[oi % 2]
                        for bi, (t0, t1) in enumerate(BLOCKS):
                            n = t1 - t0
                            pb = PS[bi % 4]
                            for kc in range(KC):
                                K.mm(pb[:, 0:n], w_[:, kc, :], hT[:, kc, t0:t1], kc == 0, kc == KC - 1, [w_, hT], [pb])
                            K.copy("act", up[:, poff(t0):poff(t0) + n], pb[:, 0:n], [pb], [up])
                        x_ = xs[0]
                        for (s0, ln) in ((0, LC), (LC, LL)):
                            p0 = poff(s0)
                            K.act(x_[:, s0:s0 + ln], up[:, p0:p0 + ln], AF.Identity, [up, mu], [x_], scale=mu[:, 2, fc:fc + 1])
                            K.stt("dve", x_[:, s0:s0 + ln], up[:, p0 - 1:p0 - 1 + ln], mu[:, 0, fc:fc + 1], x_[:, s0:s0 + ln], ALU.mult, ALU.add, [up, mu, x_], [x_])
                            K.stt("dve", x_[:, s0:s0 + ln], up[:, p0 + 1:p0 + 1 + ln], mu[:, 1, fc:fc + 1], x_[:, s0:s0 + ln], ALU.mult, ALU.add, [up, mu, x_], [x_])
                        if fc < 4:
                            c4 = fc
                            K.copy("act", rT[:, c4, :], x_[:, :], [x_], [rT])
                            for bi, (t0, t1) in enumerate(BLOCKS):
                                n = t1 - t0
                                a_, b_ = t32[bi % 2], t16[bi % 2]
                                K.stt("dve", b_[:, 0:n], x_[:, t0:t1], cols[:, 3, c4:c4 + 1], kT[:, c4, t0:t1], ALU.mult, ALU.mult, [x_, cols, kT], [b_])
                                pb = PS[4 + bi % 2]
                                K.mm(pb[:, 0:n], blk64[:, :], b_[:, 0:n], True, True, [blk64, b_], [pb])
                                K.copy("act", gbo[0][:, t0:t1], pb[:, 0:n], [pb], [gbo[0]])
                            K.dma("pool", gb_d[1, c4, :, :], gbo[0][:, :], reads=[gbo[0]], writes=[gb_d])
                        elif fc < 8:
                            c4 = fc - 4
                            K.copy("act", kT[:, c4, :], x_[:, :], [x_], [kT])
                            for bi, (t0, t1) in enumerate(BLOCKS):
                                n = t1 - t0
                                a_, b_ = t32[bi % 2], t16[bi % 2]
                                K.ts("dve", a_[:, 0:n], x_[:, t0:t1], cols[:, 0, c4:c4 + 1], None, ALU.mult, None, [x_, cols], [a_])
                                K.act(b_[:, 0:n], a_[:, 0:n], AF.Square, [a_], [b_])
                                pb = PS[4 + bi % 2]
                                K.mm(pb[:, 0:n], blk64[:, :], b_[:, 0:n], True, True, [blk64, b_], [pb])
                                r_ = rk32[bi % 2]
                                K.rsqrt(r_[:, 0:n], pb[:, 0:n], 1.0, 1e-12, [pb], [r_])
                                K.tt("pool", kknT[:, c4, t0:t1], a_[:, 0:n], r_[:, 0:n], ALU.mult, [a_, r_], [kknT])
                        elif fc < 12:
                            c4 = fc - 8
                            xb_ = xsb[0]
                            K.copy("act", xb_[:, :], x_[:, :], [x_], [xb_])
                            for grp in range(3):
                                tis = list(range(grp * 8, min(18, grp * 8 + 8)))
                                pb = PS[4 + grp % 2]
                                pbv = pb[:, :].bitcast(BF16)
                                for qi, ti in enumerate(tis):
                                    K.transpose(pbv[:, qi * 128:(qi + 1) * 128], xb_[:, ti * 128:(ti + 1) * 128], identb[:, :], [xb_, identb], [pb])
                                nt = len(tis)
                                K.copy("dve", vtok[:, tis[0]:tis[0] + nt, c4 * 128:(c4 + 1) * 128],
                                       pbv[:, 0:nt * 128].rearrange("p (a f) -> p a f", a=nt), [pb], [vtok])
                            K.dma("pool", gb_d[0, c4, :, :], xb_[:, :], reads=[xb_], writes=[gb_d])
                        elif fc == 12:
                            K.act(xwaT[0:64, :], x_[0:64, :], AF.Tanh, [x_], [xwaT])
                            K.copy("act", xwaT[64:128, :], x_[64:128, :], [x_], [xwaT])
                        else:
                            xb_ = xsb[0]
                            K.act(xb_[:, :], x_[:, :], AF.Sigmoid, [x_], [xb_])
                            for c4 in range(4):
                                for bi, (t0, t1) in enumerate(BLOCKS):
                                    n = t1 - t0
                                    pb = PS[bi % 4]
                                    K.mm(pb[:, 0:n], g2b[:, c4 * 128:(c4 + 1) * 128], xb_[:, t0:t1], True, True, [g2b, xb_], [pb])
                                    K.copy("act", gbo[0][:, t0:t1], pb[:, 0:n], [pb], [gbo[0]])
                                K.dma("pool", gb_d[2, c4, :, :], gbo[0][:, :], reads=[gbo[0]], writes=[gb_d])
                K.barrier()
                with ES() as st2:
                    w2b = K.sb([64, 2, 512], BF16, "w2b", st2)
                    a2b = K.sb([128, 2, 512], BF16, "a2b", st2)
                    w0bc = K.sb([128, 2, 512], F32, "w0bc", st2)
                    lcm = K.sb([128, 2, 128], F32, "lcm", st2)
                    lexcm = K.sb([128, 2, 128], F32, "lexcm", st2)
                    mcol = K.sb([128, 2, 2], F32, "mcol", st2)
                    m1 = K.sb([128, 2, 128], F32, "m1", st2)
                    m3 = K.sb([128, 2, 384], F32, "m3", st2)
                    m1t = K.sb([128, 2, 128], F32, "m1t", st2)
                    for dst, src in ((lcm, rwlc_d), (lexcm, rwlexc_d), (mcol, rwmcol_d), (m1, rwm1_d), (m3, rwm3_d), (m1t, rwm1t_d)):
                        for d in range(2):
                            K.dma("sp", dst[:, d, :], src[d, :, :], reads=[src], writes=[dst])
                    with ES() as stw:
                        wstg = [K.sb([128, 1024], F32, "wstg", stw)]
                        for d in range(2):
                            load_w_bf16(w2b[:, d, :], rw2_d[0, d, :, :], (64, 512), rw2_d, w2b, wstg, 0)
                            s_ = wstg[0]
                            K.dma("sp", s_[64:128, 0:512], ra2_d[0, d, :, :], reads=[ra2_d], writes=[s_])
                            K.copy("pool", a2b[64:128, d, :], s_[64:128, 0:512], [s_], [a2b])
                            K.dma("sp", w0bc[:, d, :], rw0_d[0, d, :].partition_broadcast(128), reads=[rw0_d], writes=[w0bc])
                        K.barrier()

                    def dir_stream(d):
                        B = PS[4 * d:4 * d + 4]
                        sg = K.sb([128, 512], F32, "sg", st2)
                        aT = K.sb([128, 128], F32, "aT", st2)
                        tmpa = K.sb([128, 128], F32, "tmpa", st2)
                        tmpb = K.sb([128, 128], F32, "tmpb", st2)
                        eL = K.sb([128, 128], F32, "eL", st2)
                        enL = K.sb([128, 128], F32, "enL", st2)
                        eLex = K.sb([128, 128], F32, "eLex", st2)
                        pm_sb = K.sb([128, 4, 2], F32, "pm_sb", st2)
                        gm = K.sb([128, 4, 2], F32, "gm", st2)
                        AR = K.sb([128, 4, 256], BF16, "AR", st2)
                        BH = K.sb([128, 4, 128], BF16, "BH", st2)
                        KH = K.sb([128, 4, 128], BF16, "KH", st2)
                        BKtok = K.sb([128, 2, 512], BF16, "BKtok", st2)
                        Q = [K.sb([128, 8, 128], F32, "Qa", st2), K.sb([128, 8, 128], F32, "Qb", st2)]
                        QT = [K.sb([128, 8, 128], F32, "QTa", st2), K.sb([128, 8, 128], F32, "QTb", st2)]
                        Nm = K.sb([128, 8, 128], F32, "Nm", st2)
                        S3 = K.sb([128, 8, 384], BF16, "S3", st2)
                        H = K.sb([128, 4, 64], F32, "H", st2)
                        H0 = K.sb([128, 4, 64], F32, "H0", st2)
                        H0b = K.sb([128, 4, 64], BF16, "H0b", st2)
                        W_sb = K.sb([128, 512], F32, "W_sb", st2)
                        U_sb = K.sb([128, 512], BF16, "U_sb", st2)
                        ybuf = K.sb([128, 512], F32, "ybuf", st2)
                        yold = K.sb([128, 512], F32, "yold", st2)
                        yield
                        K.memset("dve", H[:, :, :], 0.0, [H])
                        tiles = list(range(18)) if d == 0 else [1, 0] + list(range(17, 1, -1))
                        for ci, c in enumerate(tiles):
                            tsl = slice(c * 128, (c + 1) * 128)
                            pz = B[0]
                            K.mm(pz[:, :], xwaT[0:64, tsl], w2b[:, d, :], True, True, [xwaT, w2b], [pz])
                            K.tt("dve", sg[:, :], pz[:, :], w0bc[:, d, :], ALU.add, [pz, w0bc], [sg])
                            K.act(sg[:, :], sg[:, :], AF.Sigmoid, [sg], [sg])
                            for f4 in range(4):
                                fs = slice(f4 * 128, (f4 + 1) * 128)
                                pa = B[1]
                                K.mm(pa[:, 0:128], a2b[64:128, d, fs], xwaT[64:128, tsl], True, True, [a2b, xwaT], [pa])
                                K.act(aT[:, :], pa[:, 0:128], AF.Sigmoid, [pa, cols], [aT], bias=cols[:, 7 + d, f4:f4 + 1])
                                pl = B[2 + f4 % 2]
                                K.mm(pl[:, 0:128], sg[:, fs], lcm[:, d, :], True, True, [sg, lcm], [pl])
                                K.mm(pl[:, 128:256], sg[:, fs], lexcm[:, d, :], True, True, [sg, lexcm], [pl])
                                K.mm(pl[:, 256:258], sg[:, fs], mcol[:, d, :], True, True, [sg, mcol], [pl])
                                K.act(eL[:, :], pl[:, 0:128], AF.Exp, [pl], [eL], scale=CW)
                                K.act(enL[:, :], pl[:, 0:128], AF.Exp, [pl], [enL], scale=-CW)
                                K.act(eLex[:, :], pl[:, 128:256], AF.Exp, [pl], [eLex], scale=CW)
                                K.copy("act", pm_sb[:, f4, :], pl[:, 256:258], [pl], [pm_sb])
                                K.tt("dve", AR[:, f4, 128:256], rT[:, f4, tsl], eL[:, :], ALU.mult, [rT, eL], [AR])
                                K.stt("dve", AR[:, f4, 0:128], kknT[:, f4, tsl], -1.0, eLex[:, :], ALU.mult, ALU.mult, [kknT, eLex], [AR])
                                K.tt("pool", tmpa[:, :], kknT[:, f4, tsl], aT[:, :], ALU.mult, [kknT, aT], [tmpa])
                                K.tt("dve", BH[:, f4, :], tmpa[:, :], enL[:, :], ALU.mult, [tmpa, enL], [BH])
                                K.ts("dve", tmpb[:, :], aT[:, :], cols[:, 1, f4:f4 + 1], cols[:, 2, f4:f4 + 1], ALU.mult, ALU.add, [aT, cols], [tmpb])
                                K.tt("pool", tmpb[:, :], tmpb[:, :], kT[:, f4, tsl], ALU.mult, [tmpb, kT], [tmpb])
                                K.tt("dve", KH[:, f4, :], tmpb[:, :], enL[:, :], ALU.mult, [tmpb, enL], [KH])
                                yield
                            K.tt("dve", pm_sb[:, :, 1], pm_sb[:, :, 1], pm_sb[:, :, 0], ALU.subtract, [pm_sb], [pm_sb])
                            K.act(gm[:, :, :], pm_sb[:, :, :], AF.Exp, [pm_sb], [gm], scale=CW)
                            pt = B[1]
                            ptv = pt[:, :].bitcast(BF16)
                            for f4 in range(4):
                                K.transpose(ptv[:, f4 * 128:(f4 + 1) * 128], BH[:, f4, :], identb[:, :], [BH, identb], [pt])
                                K.transpose(ptv[:, 512 + f4 * 128:512 + (f4 + 1) * 128], KH[:, f4, :], identb[:, :], [KH, identb], [pt])
                            K.copy("act", BKtok[:, :, :], ptv[:, :].rearrange("p (a f) -> p a f", a=2), [pt], [BKtok])
                            yield
                            for h in range(8):
                                f4, hr = h // 2, slice((h % 2) * 64, (h % 2) * 64 + 64)
                                ps_ = B[2 * (h % 2)]
                                K.mm(ps_[:, 0:256], BH[hr, f4, :], AR[hr, f4, :], True, True, [BH, AR], [ps_])
                                K.mm(ps_[:, 256:512], KH[hr, f4, :], AR[hr, f4, :], True, True, [KH, AR], [ps_])
                                K.tt("dve", Q[0][:, h, :], ps_[:, 0:128], m1[:, d, :], ALU.mult, [ps_, m1], [Q[0]])
                                K.tt("dve", S3[:, h, :], ps_[:, 128:512], m3[:, d, :], ALU.mult, [ps_, m3], [S3])
                                pq = B[2 * (h % 2) + 1]
                                K.mm(pq[:, 0:128], AR[hr, f4, 0:128], BH[hr, f4, :], True, True, [AR, BH], [pq])
                                K.tt("dve", QT[0][:, h, :], pq[:, 0:128], m1t[:, d, :], ALU.mult, [pq, m1t], [QT[0]])
                                if h % 2 == 1:
                                    yield
                            K.tt("pool", Nm[:, :, :], Q[0][:, :, :], ident[:, :].unsqueeze(1).to_broadcast([128, 8, 128]), ALU.add, [Q[0], ident], [Nm])
                            cur = 0
                            for lev in range(1, 7):
                                nxt = 1 - cur
                                for half in range(2):
                                    pqa, pqb, pn = B[(3 * half) % 4], B[(3 * half + 1) % 4], B[(3 * half + 2) % 4]
                                    hs = slice(half * 4, half * 4 + 4)
                                    for hh in range(4):
                                        h = half * 4 + hh
                                        cs = slice(hh * 128, (hh + 1) * 128)
                                        K.mm(pqb[:, cs], Q[cur][:, h, :], QT[cur][:, h, :], True, True, [Q[cur], QT[cur]], [pqb])
                                        if lev < 6:
                                            K.mm(pqa[:, cs], QT[cur][:, h, :], Q[cur][:, h, :], True, True, [Q[cur], QT[cur]], [pqa])
                                    K.copy("act", QT[nxt][:, hs, :], pqb[:, :].rearrange("p (a f) -> p a f", a=4), [pqb], [QT[nxt]])
                                    if lev < 6:
                                        K.copy("dve", Q[nxt][:, hs, :], pqa[:, :].rearrange("p (a f) -> p a f", a=4), [pqa], [Q[nxt]])
                                    yield
                                    for hh in range(4):
                                        h = half * 4 + hh
                                        K.mm(pn[:, hh * 128:(hh + 1) * 128], QT[nxt][:, h, :], Nm[:, h, :], True, True, [QT[nxt], Nm], [pn])
                                    K.tt("dve", Nm[:, hs, :], Nm[:, hs, :], pn[:, :].rearrange("p (a f) -> p a f", a=4), ALU.add, [Nm, pn], [Nm])
                                    yield
                                cur = nxt
                            K.tt("dve", H0[:, :, :], H[:, :, :], gm[:, :, 0:1].to_broadcast([128, 4, 64]), ALU.mult, [H, gm], [H0])
                            K.copy("act", H0b[:, :, :], H0[:, :, :], [H0], [H0b])
                            pw = B[0]
                            for h in range(8):
                                f4, hr = h // 2, slice((h % 2) * 64, (h % 2) * 64 + 64)
                                cs = slice(h * 64, (h + 1) * 64)
                                K.mm(pw[:, cs], AR[hr, f4, 0:128], H0b[hr, f4, :], True, False, [AR, H0b], [pw])
                                K.mm(pw[:, cs], S3[:, h, 128:256], vtok[:, c, cs], False, True, [S3, vtok], [pw])
                            K.copy("act", W_sb[:, :], pw[:, :], [pw], [W_sb])
                            yield
                            pu = B[1]
                            for h in range(8):
                                cs = slice(h * 64, (h + 1) * 64)
                                K.mm(pu[:, cs], Nm[:, h, :], W_sb[:, cs], True, True, [Nm, W_sb], [pu])
                            K.copy("act", U_sb[:, :], pu[:, :], [pu], [U_sb])
                            yield
                            py = B[2]
                            for h in range(8):
                                f4, hr = h // 2, slice((h % 2) * 64, (h % 2) * 64 + 64)
                                cs = slice(h * 64, (h + 1) * 64)
                                K.mm(py[:, cs], AR[hr, f4, 128:256], H0b[hr, f4, :], True, False, [AR, H0b], [py])
                                K.mm(py[:, cs], S3[:, h, 0:128], U_sb[:, cs], False, False, [S3, U_sb], [py])
                                K.mm(py[:, cs], S3[:, h, 256:384], vtok[:, c, cs], False, True, [S3, vtok], [py])
                            if d == 0:
                                K.copy("dve", ybuf[:, :], py[:, :], [py], [ybuf])
                                K.dma("pool", yf_d[c, :, :], ybuf[:, :], reads=[ybuf], writes=[yf_t[c]])
                            else:
                                K.copy("dve", ybuf[:, :], py[:, :], [py], [ybuf])
                                K.dma("pool", y_d[c, :, :], ybuf[:, :], reads=[ybuf], writes=[yb_t[c]])
                            yield
                            if ci < len(tiles) - 1:
                                ph = B[3]
                                for f4 in range(4):
                                    fs = slice(f4 * 128, (f4 + 1) * 128)
                                    K.mm(ph[:, fs], BKtok[:, 0, fs], U_sb[:, fs], True, False, [BKtok, U_sb], [ph])
                                    K.mm(ph[:, fs], BKtok[:, 1, fs], vtok[:, c, fs], False, True, [BKtok, vtok], [ph])
                                phv = ph[:, :].rearrange("p (f x) -> p f x", f=4)
                                for e2_ in range(2):
                                    rs = slice(e2_ * 64, e2_ * 64 + 64)
                                    K.tt("dve", H[rs, :, :], H0[rs, :, :], phv[rs, :, e2_ * 64:(e2_ + 1) * 64], ALU.add, [H0, ph], [H])
                                    K.tt("dve", H[rs, :, :], H[rs, :, :], gm[rs, :, 1:2].to_broadcast([64, 4, 64]), ALU.mult, [H, gm], [H])
                                yield

                    yf_t = [Buf(yf_d.t, "yf%d" % i_) for i_ in range(18)]
                    yb_t = [Buf(y_d.t, "yb%d" % i_) for i_ in range(18)]
                    streams = [dir_stream(0), dir_stream(1)]
                    for s_ in streams:
                        next(s_)
                    for _ in range(cfg.get("rw_offset", 19)):
                        next(streams[1])
                    alive = list(streams)
                    while alive:
                        for s_ in list(alive):
                            try:
                                next(s_)
                            except StopIteration:
                                alive.remove(s_)
            K.barrier()
            rwo = K.sb([128, 4, T], BF16, "rwo", st)
            with ES() as st2:
                yt = [K.sb([128, 512], F32, "yt", st2) for _ in range(2)]
                ysq = K.sb([128, 512], F32, "ysq", st2)
                s1 = K.sb([128, 8], F32, "s1", st2)
                s2 = K.sb([128, 8], F32, "s2", st2)
                ynb = K.sb([128, 512], BF16, "ynb", st2)
                vT_ = [K.sb([128, 4, 128], BF16, "vT_", st2) for _ in range(2)]
                sc_ = [K.sb([128, 4, 128], BF16, "sc_", st2) for _ in range(2)]
                gg_ = [K.sb([128, 4, 128], BF16, "gg_", st2) for _ in range(2)]
                yn32 = K.sb([128, 4, 128], F32, "yn32", st2)
                bon = K.sb([128, 4, 128], F32, "bon", st2)
                gbv = gb_d[:, :, :, :].rearrange("a c p t -> a p c t")
                for c in range(18):
                    tsl = slice(c * 128, (c + 1) * 128)
                    y_ = yt[c % 2]
                    K.dma("sp", y_[:, :], y_d[c, :, :], reads=[y_d], writes=[y_])
                    K.dma("sp", ysq[:, :], yf_d[c, :, :], reads=[yf_d], writes=[ysq])
                    K.tt("dve", y_[:, :], y_[:, :], ysq[:, :], ALU.add, [y_, ysq], [y_])
                    K.dma("sp", vT_[c % 2][:, :, :], gbv[0, :, :, tsl], reads=[gb_d], writes=[vT_[c % 2]])
                    K.dma("sp", sc_[c % 2][:, :, :], gbv[1, :, :, tsl], reads=[gb_d], writes=[sc_[c % 2]])
                    K.dma("sp", gg_[c % 2][:, :, :], gbv[2, :, :, tsl], reads=[gb_d], writes=[gg_[c % 2]])
                    yv = y_[:, :].rearrange("p (h e) -> p h e", h=8)
                    K.op("dve", lambda e, yv=yv: e.tensor_reduce(out=s1[:, :], in_=yv, axis=AX.X, op=ALU.add), [y_], [s1])
                    K.act(ysq[:, :], y_[:, :], AF.Square, [y_], [ysq])
                    K.op("dve", lambda e: e.tensor_reduce(out=s2[:, :], in_=ysq[:, :].rearrange("p (h e) -> p h e", h=8), axis=AX.X, op=ALU.add), [ysq], [s2])
                    K.ts("dve", s1[:, :], s1[:, :], 1.0 / 64, None, ALU.mult, None, [s1], [s1])
                    K.tt("dve", ysq[:, 0:8], s1[:, :], s1[:, :], ALU.mult, [s1], [ysq])
                    K.stt("dve", s2[:, :], s2[:, :], 1.0 / 64, ysq[:, 0:8], ALU.mult, ALU.subtract, [s2, ysq], [s2])
                    K.rsqrt(s2[:, :], s2[:, :], 1.0, 64e-5, [s2], [s2])
                    K.tt("dve", yv, yv, s1[:, :].unsqueeze(2).to_broadcast([128, 8, 64]), ALU.subtract, [y_, s1], [y_])
                    K.tt("dve", ynb[:, :].rearrange("p (h e) -> p h e", h=8), yv, s2[:, :].unsqueeze(2).to_broadcast([128, 8, 64]), ALU.mult, [y_, s2], [ynb])
                    pt = PS[c % 2]
                    ptv = pt[:, :].bitcast(BF16)
                    for c4 in range(4):
                        K.transpose(ptv[:, c4 * 128:(c4 + 1) * 128], ynb[:, c4 * 128:(c4 + 1) * 128], identb[:, :], [ynb, identb], [pt])
                    for c4 in range(4):
                        K.act(yn32[:, c4, :], ptv[:, c4 * 128:(c4 + 1) * 128], AF.Identity, [pt, cols], [yn32],
                              bias=cols[:, 5, c4:c4 + 1], scale=cols[:, 4, c4:c4 + 1])
                    K.tt("pool", bon[:, :, :], vT_[c % 2][:, :, :], sc_[c % 2][:, :, :], ALU.mult, [vT_[c % 2], sc_[c % 2]], [bon])
                    K.tt("pool", yn32[:, :, :], yn32[:, :, :], bon[:, :, :], ALU.add, [yn32, bon], [yn32])
                    K.tt("dve", rwo[:, :, tsl], yn32[:, :, :], gg_[c % 2][:, :, :], ALU.mult, [yn32, gg_[c % 2]], [rwo])
            K.barrier()
            wo = K.sb([128, 4, D], BF16, "wo_rw", st)
            wostg = [K.sb([128, 4096], F32, "wostg", st)]
            load_w_bf16(wo[:, :, :], about_d[0, 512:1024, :].rearrange("(kc p) n -> p kc n", p=128), (128, 4, D), about_d, wo, wostg, 0)
            pieces = [((lambda oc, cc=cc: wo[:, cc, oc * 128:(oc + 1) * 128]),
                       (lambda t0, t1, cc=cc: rwo[:, cc, t0:t1]), [wo, rwo]) for cc in range(4)]
            apply_out(b, pieces, 2, 0)
        K.barrier()

    K.barrier()
    for l in layers:
        stage_mods(l)
    for b in range(nb):
        stage_load(b)
        for li, l in enumerate(layers):
            last = (li == len(layers) - 1) and not cfg.get("force_ctx", False)
            modsT.l = l
            Amod.l = l
            if mixers:
                if l == 1:
                    with ES() as sth:
                        hT = K.sb([128, KC, T], BF16, "hT", sth)
                        stage_norm(b, 0, hT, to_dram=True)
                        if cfg.get("swa", True):
                            stage_swa(b, hT)
                    if cfg.get("ssd", True):
                        stage_ssd(b, None)
                else:
                    with ES() as sth:
                        hT = K.sb([128, KC, T], BF16, "hT", sth)
                        stage_norm(b, 0, hT, to_dram=True)
                        if cfg.get("mla", True):
                            stage_mla(b, hT)
                    if cfg.get("rwkv", True):
                        stage_rwkv(b)
            with ES() as sth:
                hT = K.sb([128, KC, T], BF16, "hT", sth)
                stage_norm(b, 1, hT, lo=0 if not last else LC)
                stage_ffn(b, l, do_ctx=not last, hT=hT)
        stage_out(b)
        if cfg.get("dbg_x", False) and b == 0:
            for cc in range(KC):
                K.dma("sp", dbgx_d[cc, :, :], xs_d[cc, :, :], reads=xblk, writes=[dbgx_d])
    K.barrier()
    K.es.close()
    return nc, K


CONST_INPUTS = None


def _rope_tables(rot_dim):
    n_freq = rot_dim // 4
    rows = np.arange(LL, dtype=np.float32) // 64
    cols = np.arange(LL, dtype=np.float32) % 64
    inv = (np.float32(10000.0) ** (-np.arange(n_freq, dtype=np.float32) / np.float32(n_freq))).astype(np.float32)
    ang = np.concatenate([rows[:, None] * inv[None, :], cols[:, None] * inv[None, :]], axis=-1).astype(np.float32)
    cos, sin = np.cos(ang).astype(np.float32), np.sin(ang).astype(np.float32)
    cosT = np.concatenate([cos.T, cos.T], axis=0)
    sinT = np.concatenate([sin.T, sin.T], axis=0)
    return np.ascontiguousarray(cosT), np.ascontiguousarray(sinT)


def const_inputs():
    global CONST_INPUTS
    if CONST_INPUTS is None:
        s = np.arange(128)
        triF = (s[:, None] <= s[None, :]).astype(np.float32)
        c = {"ident": np.eye(128, dtype=np.float32), "triF": triF, "triB": np.ascontiguousarray(triF.T),
             "strF": (s[:, None] > s[None, :]).astype(np.float32), "strB": (s[:, None] < s[None, :]).astype(np.float32)}
        c["swa_cos"], c["swa_sin"] = _rope_tables(64)
        c["mla_cos"], c["mla_sin"] = _rope_tables(32)
        triB = triF.T
        incl = [triF, triB]
        strict = [c["strB"], c["strF"]]
        m = 63
        c["rw_lc"] = np.stack([incl[d] - incl[d][:, m:m + 1] for d in range(2)]).astype(np.float32)
        c["rw_lexc"] = np.stack([strict[d] - incl[d][:, m:m + 1] for d in range(2)]).astype(np.float32)
        c["rw_mcol"] = np.stack([np.stack([incl[d][:, m], np.ones(128, np.float32)], axis=1) for d in range(2)]).astype(np.float32)
        c["rw_m1"] = np.stack([strict[d] for d in range(2)]).astype(np.float32)
        c["rw_m3"] = np.stack([np.concatenate([incl[d], strict[d], incl[d]], axis=1) for d in range(2)]).astype(np.float32)
        c["rw_m1t"] = np.stack([np.ascontiguousarray(strict[d].T) for d in range(2)]).astype(np.float32)
        blk = np.zeros((128, 128), np.float32)
        blk[:64, :64] = 1.0
        blk[64:, 64:] = 1.0
        c["blk64"] = blk
        CONST_INPUTS = c
    return CONST_INPUTS


def make_in_maps(nc_names, inputs, ncores=NCORES):
    consts = const_inputs()
    in_maps = []
    for core in range(ncores):
        m = {}
        sl = slice(core * BPC, (core + 1) * BPC)
        for k in nc_names:
            if k in consts:
                m[k] = consts[k]
            else:
                v = np.asarray(inputs[k])
                m[k] = np.ascontiguousarray(v[sl] if k in ("x", "c", "ctx") else v)
        in_maps.append(m)
    return in_maps


def kernel(**inputs):
    cfg = {}
    nc, K = build_program(cfg)
    in_maps = make_in_maps(K.in_names, inputs)
    res = run_bass_kernel_spmd(nc, in_maps, core_ids=list(range(NCORES)))
    return np.concatenate([r["out"] for r in res.results], axis=0)
```

```python
import contextlib
import math
import numpy as np
import concourse.bass as bass
import concourse.mybir as mybir
from concourse.bass_utils import run_bass_kernel_spmd

F32 = mybir.dt.float32
BF16 = mybir.dt.bfloat16
AF = mybir.ActivationFunctionType
ALU = mybir.AluOpType
AX = mybir.AxisListType

NCORES = 8
BPC = 4
D = 1024
KC = 8
LC = 256
LL = 2048
T = LC + LL
DFF = 2816
EPS = 1e-6
BLOCKS = [(0, 256), (256, 768), (768, 1280), (1280, 1792), (1792, 2304)]
SEM_EPOCH = 50000


class Buf:
    def __init__(self, t, name):
        self.t = t
        self.name = name
        self.lw = []
        self.rd = []
        self.ds = None

    def __getitem__(self, idx):
        return self.t[idx]


class Eng:
    def __init__(self, name, e, is_pe=False):
        self.name = name
        self.e = e
        self.is_pe = is_pe
        self.sems = []
        self.count = 0
        self.epoch = 0
        self.seen = {}


class Kern:
    def __init__(self, nc):
        self.nc = nc
        self.es = contextlib.ExitStack()
        self.engs = {}
        for name, e, ispe in (("pe", nc.tensor, True), ("act", nc.scalar, False),
                              ("dve", nc.vector, False), ("pool", nc.gpsimd, False),
                              ("sp", nc.sync, False)):
            en = Eng(name, e, ispe)
            en.sems.append(self.es.enter_context(nc.semaphore("s_%s_0" % name)))
            self.engs[name] = en
        self.ndsem = 44
        self.dsem = [self.es.enter_context(nc.semaphore("s_d%d" % i)) for i in range(self.ndsem)]
        self.dtot = [0] * self.ndsem
        self.dranges = {"sp": (0, 26), "pool": (26, 44), "act": (0, 26), "dve": (0, 26), "pe": (0, 26)}
        self.drr = {k: 0 for k in self.dranges}
        self.nbuf = 0
        self.n_ops = 0
        self.eps_bufs = {}
        self.in_names = []

    def sb(self, shape, dtype, name=None, stack=None):
        self.nbuf += 1
        name = "%s_%d" % (name or "sb", self.nbuf)
        t = (stack or self.es).enter_context(self.nc.sbuf_tensor(name, list(shape), dtype))
        return Buf(t, name)

    def ps(self, name=None):
        self.nbuf += 1
        name = "%s_%d" % (name or "ps", self.nbuf)
        t = self.es.enter_context(self.nc.psum_tensor(name, [128, 512], F32))
        return Buf(t, name)

    def dram(self, name, shape, dtype, kind="Internal"):
        t = self.nc.dram_tensor(name, list(shape), dtype, kind=kind)
        if kind == "ExternalInput":
            self.in_names.append(name)
        return Buf(t, name)

    def _wait(self, eng, ev):
        if ev[0] == "E":
            src = self.engs[ev[1]]
            ep, n = ev[2], ev[3]
            if src is eng and eng.is_pe:
                return
            key = ("E", ev[1], ep)
            if eng.seen.get(key, 0) >= n:
                return
            eng.e.wait_ge(src.sems[ep], n)
            eng.seen[key] = n
        else:
            i = ev[1]
            tot = self.dtot[i]
            key = ("D", i)
            if eng.seen.get(key, 0) >= tot:
                return
            eng.e.wait_ge(self.dsem[i], tot)
            eng.seen[key] = tot

    def _deps(self, eng, reads, writes):
        for b in reads:
            for ev in b.lw:
                self._wait(eng, ev)
        for b in writes:
            for ev in b.lw:
                self._wait(eng, ev)
            for ev in b.rd:
                self._wait(eng, ev)

    def _record(self, ev, reads, writes):
        for b in reads:
            if ev[0] == "E":
                b.rd = [r for r in b.rd if not (r[0] == "E" and r[1] == ev[1])]
            else:
                b.rd = [r for r in b.rd if r != ev]
            b.rd.append(ev)
        for b in writes:
            b.lw = [ev]
            b.rd = []

    def op(self, engname, fn, reads=(), writes=()):
        eng = self.engs[engname]
        self._deps(eng, reads, writes)
        if eng.count >= SEM_EPOCH:
            eng.epoch += 1
            eng.count = 0
            eng.sems.append(self.es.enter_context(self.nc.semaphore("s_%s_%d" % (engname, eng.epoch))))
        ins = fn(eng.e)
        eng.count += 1
        ins.then_inc(eng.sems[eng.epoch], 1)
        ev = ("E", engname, eng.epoch, eng.count)
        self._record(ev, reads, writes)
        self.n_ops += 1

    def dma(self, engname, out, in_, reads=(), writes=(), **kw):
        eng = self.engs[engname]
        self._deps(eng, reads, writes)
        b = None
        for cand in list(writes) + list(reads):
            if cand.ds is not None and engname in cand.ds:
                b = cand
                break
        if b is None:
            b = (list(writes) + list(reads))[0]
            if b.ds is None:
                b.ds = {}
            lo, hi = self.dranges[engname]
            b.ds[engname] = lo + self.drr[engname]
            self.drr[engname] = (self.drr[engname] + 1) % (hi - lo)
        i = b.ds[engname]
        ins = eng.e.dma_start(out=out, in_=in_, **kw)
        ins.then_inc(self.dsem[i], 16)
        self.dtot[i] += 16
        ev = ("D", i)
        self._record(ev, reads, writes)
        self.n_ops += 1

    def barrier(self):
        for eng in self.engs.values():
            for other in self.engs.values():
                if other is eng:
                    continue
                if other.count > 0:
                    self._wait(eng, ("E", other.name, other.epoch, other.count))
            for i in range(self.ndsem):
                if self.dtot[i] > 0:
                    self._wait(eng, ("D", i))

    def mm(self, out, lhsT, rhs, start, stop, reads, writes):
        self.op("pe", lambda e: e.matmul(out, lhsT=lhsT, rhs=rhs, start=start, stop=stop), reads, writes)

    def transpose(self, out, in_, ident, reads, writes):
        self.op("pe", lambda e: e.transpose(out, in_, ident), reads, writes)

    def act(self, out, in_, func, reads, writes, bias=None, scale=None, eng="act"):
        kw = {}
        if bias is not None:
            kw["bias"] = bias
        if scale is not None:
            kw["scale"] = scale
        self.op(eng, lambda e: e.activation(out=out, in_=in_, func=func, **kw), reads, writes)

    def ts(self, eng, out, in0, s1, s2, op0, op1, reads, writes):
        if op1 is None:
            self.op(eng, lambda e: e.tensor_scalar(out=out, in0=in0, scalar1=s1, scalar2=None, op0=op0), reads, writes)
        else:
            self.op(eng, lambda e: e.tensor_scalar(out=out, in0=in0, scalar1=s1, scalar2=s2, op0=op0, op1=op1), reads, writes)

    def tt(self, eng, out, in0, in1, op, reads, writes):
        self.op(eng, lambda e: e.tensor_tensor(out=out, in0=in0, in1=in1, op=op), reads, writes)

    def stt(self, eng, out, in0, scalar, in1, op0, op1, reads, writes):
        self.op(eng, lambda e: e.scalar_tensor_tensor(out=out, in0=in0, scalar=scalar, in1=in1, op0=op0, op1=op1), reads, writes)

    def copy(self, eng, out, in_, reads, writes):
        if eng == "act":
            self.op(eng, lambda e: e.activation(out=out, in_=in_, func=AF.Copy), reads, writes)
        else:
            self.op(eng, lambda e: e.tensor_copy(out=out, in_=in_), reads, writes)

    def rsqrt(self, out, in_, scale, eps, reads, writes):
        self.op("act", lambda e: e.activation(out=out, in_=in_, func=AF.Sqrt, bias=self.eps_ap(eps), scale=scale), reads, writes)
        self.op("dve", lambda e: e.reciprocal(out=out, in_=out), writes, writes)

    def eps_ap(self, eps):
        if eps not in self.eps_bufs:
            b = self.sb([128, 1], F32, "eps")
            self.memset("dve", b[:, :], float(eps), [b])
            self.eps_bufs[eps] = b
        return self.eps_bufs[eps][:, 0:1]

    def memset(self, eng, ap, val, writes):
        self.op(eng, lambda e: e.memset(ap, val), (), writes)


def colvec(ap1d, n):
    return ap1d.rearrange("(c p) -> p c", p=128)


def stg_view(s, shape):
    n = 1
    for d_ in shape[1:]:
        n *= d_
    v = s[0:shape[0], 0:n]
    if len(shape) == 3:
        v = v.rearrange("p (a b) -> p a b", a=shape[1])
    return v


HD = 64
CD_Z, CD_XBC, CD_DT, CD_Q, CD_K, CD_V = 0, 1024, 2560, 2592, 3104, 3232
CD_IN = 3360


def build_program(cfg):
    nc = bass.Bass("TRN2", target_bir_lowering=False)
    K = Kern(nc)
    nb = cfg.get("nb", BPC)
    layers = cfg.get("layers", [0, 1])
    mixers = cfg.get("mixers", True)
    ES = contextlib.ExitStack

    def din(name, shape):
        return K.dram(name, shape, F32, kind="ExternalInput")

    x_d = din("x", [BPC, LL, D])
    c_d = din("c", [BPC, D])
    ctx_d = din("ctx", [BPC, LC, D])
    cctx_d = din("c_ctx", [D])
    ada_w_d = din("ada_w", [2, D, 6 * D])
    ada_b_d = din("ada_b", [2, 6 * D])
    nmix_d = din("norm_mix_g", [2, D])
    nffn_d = din("norm_ffn_g", [2, D])
    wup_d = din("ffn_w_up", [2, D, 2 * DFF])
    cw_d = din("ffn_conv_w", [2, 3, 2 * DFF])
    cb_d = din("ffn_conv_b", [2, 2 * DFF])
    wdn_d = din("ffn_w_down", [2, DFF, D])
    fng_d = din("final_norm_g", [D])
    if mixers and 1 in layers:
        cdin_d = din("cd_w_in", [1, D, CD_IN])
        cdout_d = din("cd_w_out", [1, 1536, D])
        scw_d = din("ssm_conv_w", [1, 5, 1536])
        scb_d = din("ssm_conv_b", [1, 1536])
        sdtb_d = din("ssm_dt_bias", [1, 2, 16])
        salog_d = din("ssm_a_log", [1, 2, 16])
        sd_d = din("ssm_d", [1, 16])
        sng_d = din("ssm_norm_g", [1, 1024])
        sink_d = din("swa_sink", [1, 8])
        swacos_d = din("swa_cos", [64, LL])
        swasin_d = din("swa_sin", [64, LL])
        triF_d = din("triF", [128, 128])
        triB_d = din("triB", [128, 128])
        strF_d = din("strF", [128, 128])
        strB_d = din("strB", [128, 128])
    if mixers and 0 in layers:
        abin_d = din("ab_w_in", [1, D, 2464])
        about_d = din("ab_w_out", [1, D, D])
        mqg_d = din("mla_q_norm_g", [1, 384])
        mqu_d = din("mla_w_q_up", [1, 384, 768])
        mkg_d = din("mla_kv_norm_g", [1, 256])
        mkvu_d = din("mla_w_kv_up", [1, 256, 1024])
        mlacos_d = din("mla_cos", [32, LL])
        mlasin_d = din("mla_sin", [32, LL])
        rmp_d = din("rwkv_mu_prev", [1, 1792])
        rmn_d = din("rwkv_mu_next", [1, 1792])
        rw0_d = din("rwkv_w0", [1, 2, 512])
        rw2_d = din("rwkv_w2", [1, 2, 64, 512])
        ra0_d = din("rwkv_a0", [1, 2, 512])
        ra2_d = din("rwkv_a2", [1, 2, 64, 512])
        rg2_d = din("rwkv_g2", [1, 128, 512])
        rkk_d = din("rwkv_k_k", [1, 512])
        rka_d = din("rwkv_k_a", [1, 512])
        rrk_d = din("rwkv_r_k", [1, 8, 64])
        rlg_d = din("rwkv_ln_g", [1, 512])
        rlb_d = din("rwkv_ln_b", [1, 512])
        rwlc_d = din("rw_lc", [2, 128, 128])
        rwlexc_d = din("rw_lexc", [2, 128, 128])
        rwmcol_d = din("rw_mcol", [2, 128, 2])
        rwm1_d = din("rw_m1", [2, 128, 128])
        rwm3_d = din("rw_m3", [2, 128, 384])
        rwm1t_d = din("rw_m1t", [2, 128, 128])
        blk64_d = din("blk64", [128, 128])
        gb_d = K.dram("gb_scr", [3, 4, 128, T], BF16)
        y_d = K.dram("y_scr", [18, 128, 512], F32)
        yf_d = K.dram("yf_scr", [18, 128, 512], F32)
    ident_d = din("ident", [128, 128])
    out_d = K.dram("out", [BPC, LL, D], F32, kind="ExternalOutput")
    if cfg.get("dbg_x", False):
        dbgx_d = K.dram("dbgx", [KC, 128, T], F32, kind="ExternalOutput")
    aT_d = K.dram("aT_scr", [DFF, T], BF16)
    xs_d = K.dram("x_scr", [KC, 128, T], F32)
    hT_d = K.dram("hT_scr", [KC, 128, T], BF16)
    sz_d = K.dram("sz_scr", [16, 128, 1024], BF16)
    hin_d = K.dram("hin_scr", [16, 128, 1024], BF16)
    xview = xs_d[:, :, :].rearrange("c p t -> p c t")
    hview = hT_d[:, :, :].rearrange("c p t -> p c t")
    xblk = [Buf(xs_d.t, "xblk%d" % j) for j in range(T // 256)]

    def xdeps(t0, t1):
        return xblk[t0 // 256:(t1 + 255) // 256]

    ident = K.sb([128, 128], F32, "ident")
    identb = K.sb([128, 128], BF16, "identb")
    ones_bf = K.sb([128, 128], BF16, "ones")
    ones_f = K.sb([128, 128], F32, "onesf")
    class LayerBuf(Buf):
        def __init__(self, b_):
            Buf.__init__(self, b_.t, b_.name)
            self.l = 0

        def __getitem__(self, idx):
            return self.t[(idx[0], self.l) + tuple(idx[1:])]

    modsT = LayerBuf(K.sb([128, 2, KC, 6, 5], F32, "modsT"))
    Amod = LayerBuf(K.sb([128, 2, KC, 2, 5], F32, "Amod"))
    gcols = K.sb([128, 5, KC], F32, "gcols")
    cwT = K.sb([128, 2, 3, 44], F32, "cwT")
    cbT = K.sb([128, 2, 44], F32, "cbT")
    PS = [K.ps("ps%d" % i) for i in range(8)]
    for e_ in (EPS, 64e-5, 1e-12, 1.0, 0.0):
        K.eps_ap(e_)

    K.dma("sp", ident[:, :], ident_d[:, :], reads=[ident_d], writes=[ident])
    K.copy("dve", identb[:, :], ident[:, :], [ident], [identb])
    K.memset("dve", ones_bf[:, :], 1.0, [ones_bf])
    K.memset("dve", ones_f[:, :], 1.0, [ones_f])
    for l in range(2):
        K.dma("sp", gcols[:, l, :], colvec(nmix_d[l, :], KC), reads=[nmix_d], writes=[gcols], allow_slow_non_contiguous=True)
        K.dma("sp", gcols[:, 2 + l, :], colvec(nffn_d[l, :], KC), reads=[nffn_d], writes=[gcols], allow_slow_non_contiguous=True)
        for tap in range(3):
            K.dma("sp", cwT[:, l, tap, :], colvec(cw_d[l, tap, :], 44), reads=[cw_d], writes=[cwT], allow_slow_non_contiguous=True)
        K.dma("sp", cbT[:, l, :], colvec(cb_d[l, :], 44), reads=[cb_d], writes=[cbT], allow_slow_non_contiguous=True)
    K.dma("sp", gcols[:, 4, :], colvec(fng_d[:], KC), reads=[fng_d], writes=[gcols], allow_slow_non_contiguous=True)

    def stage_mods(l):
        modsT.l = l
        Amod.l = l
        with ES() as st:
            condT = K.sb([128, KC, 5], F32, "condT", st)
            scond = K.sb([128, KC, 5], F32, "scond", st)
            abT = K.sb([128, 48], F32, "abT", st)
            wb = [K.sb([128, KC, 128], F32, "adaw", st) for _ in range(3)]
            for r in range(4):
                K.dma("sp", condT[:, :, r], colvec(c_d[r, :], KC), reads=[c_d], writes=[condT], allow_slow_non_contiguous=True)
            K.dma("sp", condT[:, :, 4], colvec(cctx_d[:], KC), reads=[cctx_d], writes=[condT], allow_slow_non_contiguous=True)
            K.dma("sp", abT[:, :], colvec(ada_b_d[l, :], 48), reads=[ada_b_d], writes=[abT], allow_slow_non_contiguous=True)
            K.act(scond[:, :, :], condT[:, :, :], AF.Silu, [condT], [scond])
            wview = ada_w_d[l, :, :].rearrange("(kc p) n -> p kc n", p=128)
            for j in range(48):
                w = wb[j % 3]
                K.dma("sp", w[:, :, :], wview[:, :, j * 128:(j + 1) * 128], reads=[ada_w_d], writes=[w])
                pb = PS[j % 2]
                for kc in range(KC):
                    K.mm(pb[:, 0:5], w[:, kc, :], scond[:, kc, :], kc == 0, kc == KC - 1, [w, scond], [pb])
                kind, cc = j // 8, j % 8
                K.ts("dve", modsT[:, cc, kind, :], pb[:, 0:5], abT[:, j:j + 1], None, ALU.add, None, [pb, abT], [modsT])
            for which, kind, gi in ((0, 1, l), (1, 4, 2 + l)):
                for cc in range(KC):
                    K.ts("dve", Amod[:, cc, which, :], modsT[:, cc, kind, :], 1.0, gcols[:, gi, cc:cc + 1],
                         ALU.add, ALU.mult, [modsT, gcols], [Amod])
        K.barrier()

    def stage_load(b):
        with ES() as st:
            xin = [K.sb([128, D], F32, "xin", st) for _ in range(3)]
            xo = [K.sb([128, KC, 128], F32, "xo", st) for _ in range(3)]
            for ti in range(T // 128):
                xb = xin[ti % 3]
                if ti < 2:
                    src, sb_ = ctx_d[b, ti * 128:(ti + 1) * 128, :], ctx_d
                else:
                    src, sb_ = x_d[b, (ti - 2) * 128:(ti - 1) * 128, :], x_d
                K.dma("sp", xb[:, :], src, reads=[sb_], writes=[xb])
                o = xo[ti % 3]
                for half in range(2):
                    pb = PS[(2 * ti + half) % 4]
                    for q in range(4):
                        cc = half * 4 + q
                        K.transpose(pb[:, q * 128:(q + 1) * 128], xb[:, cc * 128:(cc + 1) * 128], ident[:, :], [xb, ident], [pb])
                    K.copy("act" if half == 0 else "dve", o[:, half * 4:half * 4 + 4, :],
                           pb[:, :].rearrange("p (q t) -> p q t", q=4), [pb], [o])
                K.dma("pool", xview[:, :, ti * 128:(ti + 1) * 128], o[:, :, :], reads=[o], writes=xdeps(ti * 128, ti * 128 + 128))
        K.barrier()

    def stage_norm(b, which, hT, to_dram=False, lo=0):
        shift_kind = 0 if which == 0 else 3
        with ES() as st:
            xb = [K.sb([128, KC, 512], F32, "nxb", st) for _ in range(2)]
            sq = [K.sb([128, 512], BF16, "sq", st) for _ in range(3)]
            rstd = [K.sb([128, 512], F32, "rstd", st) for _ in range(2)]
            tmp = [K.sb([128, 512], F32, "ntmp", st) for _ in range(3)]
            for bi, (t0, t1) in enumerate(BLOCKS):
                if t1 <= lo:
                    continue
                n = t1 - t0
                row = 4 if bi == 0 else b
                x_ = xb[bi % 2]
                K.dma("sp", x_[:, :, 0:n], xview[:, :, t0:t1], reads=xdeps(t0, t1), writes=[x_])
                pb = PS[bi % 2]
                for cc in range(KC):
                    s = sq[cc % 3]
                    K.act(s[:, 0:n], x_[:, cc, 0:n], AF.Square, [x_], [s])
                    K.mm(pb[:, 0:n], ones_bf[:, :], s[:, 0:n], cc == 0, cc == KC - 1, [ones_bf, s], [pb])
                r = rstd[bi % 2]
                K.rsqrt(r[:, 0:n], pb[:, 0:n], 1.0 / D, EPS, [pb], [r])
                for cc in range(KC):
                    tm = tmp[cc % 3]
                    K.tt("dve", tm[:, 0:n], x_[:, cc, 0:n], r[:, 0:n], ALU.mult, [x_, r], [tm])
                    K.act(hT[:, cc, t0:t1], tm[:, 0:n], AF.Identity, [tm, Amod, modsT], [hT],
                          bias=modsT[:, cc, shift_kind, row:row + 1], scale=Amod[:, cc, which, row:row + 1])
            if to_dram:
                for cc in range(KC):
                    K.dma("pool", hT_d[cc, :, :], hT[:, cc, :], reads=[hT], writes=[hT_d])
        K.barrier()

    def load_hT(hT):
        for cc in range(KC):
            K.dma("sp", hT[:, cc, :], hT_d[cc, :, :], reads=[hT_d], writes=[hT])

    def apply_out(b, pieces, gate_kind, tok_lo):
        with ES() as st:
            xb = [K.sb([128, KC, 256], F32, "uxb", st) for _ in range(2)]
            for bi, t0 in enumerate(range(tok_lo, T, 256)):
                t1 = t0 + 256
                row = 4 if t0 < LC else b
                x_ = xb[bi % 2]
                K.dma("sp", x_[:, :, :], xview[:, :, t0:t1], reads=xdeps(t0, t1), writes=[x_])
                for oc in range(KC):
                    pb = PS[oc % 4]
                    for pi, (lf, rf, rd) in enumerate(pieces):
                        K.mm(pb[:, 0:256], lf(oc), rf(t0, t1), pi == 0, pi == len(pieces) - 1, rd, [pb])
                    K.stt("dve", x_[:, oc, :], pb[:, 0:256], modsT[:, oc, gate_kind, row:row + 1], x_[:, oc, :],
                          ALU.mult, ALU.add, [pb, modsT, x_], [x_])
                K.dma("pool", xview[:, :, t0:t1], x_[:, :, :], reads=[x_], writes=xdeps(t0, t1))

    cast_rr = [0]

    def load_w_bf16(dst_ap, src_ap, shape, src_buf, dst_buf, st_bufs, idx, eng=None):
        s = st_bufs[idx % len(st_bufs)]
        v = stg_view(s, shape)
        K.dma("sp", v, src_ap, reads=[src_buf], writes=[s])
        if eng is None:
            cast_rr[0] += 1
            eng = "dve" if cast_rr[0] % 2 == 0 else "act"
        K.copy(eng, dst_ap, v, [s], [dst_buf])

    def stage_ffn(b, l, do_ctx, hT):
        wupv = wup_d[l, :, :].rearrange("(kc p) n -> p kc n", p=128)
        segs = [(LC, LL)] + ([(0, LC)] if do_ctx else [])
        blocks = [bl for bl in BLOCKS if (do_ctx or bl[0] >= LC)]
        PADW = T + 4
        with ES() as st:
            wst = [K.sb([128, KC, 128], F32, "wst", st) for _ in range(4)]
            wbf = [K.sb([128, KC, 128], BF16, "wbf", st) for _ in range(4)]
            ug = [K.sb([128, PADW], F32, "ug", st) for _ in range(2)]
            uv = [K.sb([128, PADW], F32, "uv", st) for _ in range(2)]
            cg = [K.sb([128, T], F32, "cg", st) for _ in range(2)]
            cv = [K.sb([128, T], F32, "cv", st) for _ in range(2)]
            ao = [K.sb([128, T], BF16, "ao", st) for _ in range(2)]
            for u in ug + uv:
                K.memset("pool", u[:, :], 0.0, [u])

            def pad_off(t):
                return t + 1 if t < LC else t + 3

            def ffn_wload(fc):
                for half in range(2):
                    ws, wb_ = wst[2 * (fc % 2) + half], wbf[2 * (fc % 2) + half]
                    K.dma("sp", ws[:, :, :], wupv[:, :, half * DFF + fc * 128:half * DFF + (fc + 1) * 128], reads=[wup_d], writes=[ws])
                    K.copy("dve" if half == 0 else "act", wb_[:, :, :], ws[:, :, :], [ws], [wb_])

            def ffn_tail(fc):
                cg_, cv_, ao_ = cg[fc % 2], cv[fc % 2], ao[fc % 2]
                for (s0, ln) in segs:
                    K.act(cg_[:, s0:s0 + ln], cg_[:, s0:s0 + ln], AF.Silu, [cg_], [cg_])
                    K.tt("dve" if ln > 1024 else "pool", ao_[:, s0:s0 + ln], cg_[:, s0:s0 + ln], cv_[:, s0:s0 + ln], ALU.mult, [cg_, cv_], [ao_])
                lo = 0 if do_ctx else LC
                K.dma("pool", aT_d[fc * 128:(fc + 1) * 128, lo:T], ao_[:, lo:T], reads=[ao_], writes=[aT_d])

            ffn_wload(0)
            for fc in range(22):
                if fc + 1 < 22:
                    ffn_wload(fc + 1)
                g_, v_ = ug[fc % 2], uv[fc % 2]
                for bi, (t0, t1) in enumerate(blocks):
                    n = t1 - t0
                    for half, dst in ((0, g_), (1, v_)):
                        pb = PS[(bi * 2 + half) % 8]
                        wb_ = wbf[2 * (fc % 2) + half]
                        for kc in range(KC):
                            K.mm(pb[:, 0:n], wb_[:, kc, :], hT[:, kc, t0:t1],
                                 kc == 0, kc == KC - 1, [wb_, hT], [pb])
                        K.copy("act", dst[:, pad_off(t0):pad_off(t0) + n], pb[:, 0:n], [pb], [dst])
                cg_, cv_ = cg[fc % 2], cv[fc % 2]
                for half, src, dst, ch in ((0, g_, cg_, fc), (1, v_, cv_, 22 + fc)):
                    for (s0, ln) in segs:
                        p0 = pad_off(s0)
                        K.act(dst[:, s0:s0 + ln], src[:, p0:p0 + ln], AF.Identity, [src, cwT, cbT], [dst],
                              bias=cbT[:, l, ch:ch + 1], scale=cwT[:, l, 1, ch:ch + 1])
                        K.stt("dve", dst[:, s0:s0 + ln], src[:, p0 - 1:p0 - 1 + ln], cwT[:, l, 0, ch:ch + 1], dst[:, s0:s0 + ln],
                              ALU.mult, ALU.add, [src, cwT, dst], [dst])
                        K.stt("dve", dst[:, s0:s0 + ln], src[:, p0 + 1:p0 + 1 + ln], cwT[:, l, 2, ch:ch + 1], dst[:, s0:s0 + ln],
                              ALU.mult, ALU.add, [src, cwT, dst], [dst])
                if fc >= 1:
                    ffn_tail(fc - 1)
            ffn_tail(21)
        K.barrier()
        wdv = wdn_d[l, :, :].rearrange("(kc p) n -> p kc n", p=128)
        aTv = aT_d[:, :].rearrange("(kc p) t -> p kc t", p=128)
        with ES() as st:
            wd = K.sb([128, 22, D], BF16, "wd", st)
            wds = [K.sb([128, 2 * D], F32, "wds", st) for _ in range(2)]
            ab = [K.sb([128, 22, 256], BF16, "ab", st) for _ in range(2)]
            for j in range(11):
                load_w_bf16(wd[:, 2 * j:2 * j + 2, :], wdv[:, 2 * j:2 * j + 2, :], (128, 2, D), wdn_d, wd, wds, j)
            cnt = [0]

            def rhs_fn(t0, t1):
                return ab[cnt[0] % 2]

            lo = 0 if do_ctx else LC
            with ES() as st2:
                xb = [K.sb([128, KC, 256], F32, "uxb", st2) for _ in range(2)]
                for bi, t0 in enumerate(range(lo, T, 256)):
                    t1 = t0 + 256
                    row = 4 if t0 < LC else b
                    a_ = ab[bi % 2]
                    x_ = xb[bi % 2]
                    K.dma("sp", a_[:, :, :], aTv[:, :, t0:t1], reads=[aT_d], writes=[a_])
                    K.dma("sp", x_[:, :, :], xview[:, :, t0:t1], reads=xdeps(t0, t1), writes=[x_])
                    for oc in range(KC):
                        pb = PS[oc % 4]
                        for kc in range(22):
                            K.mm(pb[:, 0:256], wd[:, kc, oc * 128:(oc + 1) * 128], a_[:, kc, :], kc == 0, kc == 21, [wd, a_], [pb])
                        K.stt("dve", x_[:, oc, :], pb[:, 0:256], modsT[:, oc, 5, row:row + 1], x_[:, oc, :],
                              ALU.mult, ALU.add, [pb, modsT, x_], [x_])
                    K.dma("pool", xview[:, :, t0:t1], x_[:, :, :], reads=[x_], writes=xdeps(t0, t1))
        K.barrier()

    def stage_out(b):
        with ES() as st:
            xb = [K.sb([128, KC, 512], F32, "oxb", st) for _ in range(2)]
            sq = [K.sb([128, 512], BF16, "sq", st) for _ in range(3)]
            rstd = [K.sb([128, 512], F32, "rstd", st) for _ in range(2)]
            yT = [K.sb([128, KC, 512], F32, "yT", st) for _ in range(2)]
            ob = [K.sb([128, D], F32, "ob", st) for _ in range(3)]
            for bi, (t0, t1) in enumerate(BLOCKS[1:]):
                n = t1 - t0
                x_ = xb[bi % 2]
                K.dma("sp", x_[:, :, 0:n], xview[:, :, t0:t1], reads=xdeps(t0, t1), writes=[x_])
                pb = PS[bi % 2]
                for cc in range(KC):
                    s = sq[cc % 3]
                    K.act(s[:, 0:n], x_[:, cc, 0:n], AF.Square, [x_], [s])
                    K.mm(pb[:, 0:n], ones_bf[:, :], s[:, 0:n], cc == 0, cc == KC - 1, [ones_bf, s], [pb])
                r = rstd[bi % 2]
                K.rsqrt(r[:, 0:n], pb[:, 0:n], 1.0 / D, EPS, [pb], [r])
                y = yT[bi % 2]
                for cc in range(KC):
                    K.stt("dve", y[:, cc, 0:n], x_[:, cc, 0:n], gcols[:, 4, cc:cc + 1], r[:, 0:n],
                          ALU.mult, ALU.mult, [x_, gcols, r], [y])
                for ti in range(n // 128):
                    o = ob[ti % 3]
                    for half in range(2):
                        pb2 = PS[2 + (2 * ti + half) % 4]
                        for q in range(4):
                            cc = half * 4 + q
                            K.transpose(pb2[:, q * 128:(q + 1) * 128], y[:, cc, ti * 128:(ti + 1) * 128], ident[:, :], [y, ident], [pb2])
                        K.copy("act", o[:, half * 512:(half + 1) * 512], pb2[:, :], [pb2], [o])
                    tok = t0 - LC + ti * 128
                    K.dma("pool", out_d[b, tok:tok + 128, :], o[:, :], reads=[o], writes=[out_d])
        K.barrier()

    def stage_swa(b, hT):
        win = cdin_d[0, :, :].rearrange("(kc p) n -> p kc n", p=128)
        with ES() as st:
            attT = K.sb([64, 8, LL], BF16, "attT", st)
            wo = K.sb([64, 8, D], BF16, "wo_att", st)
            with ES() as st1:
                wq = K.sb([128, KC, 512], BF16, "wq", st1)
                wqs = K.sb([128, KC, 512], BF16, "wqs", st1)
                wk = K.sb([128, KC, 128], BF16, "wk", st1)
                wks = K.sb([128, KC, 128], BF16, "wks", st1)
                wv = K.sb([128, KC, 128], BF16, "wv", st1)
                cosT = K.sb([64, LL], F32, "cosT", st1)
                sinT = K.sb([64, LL], F32, "sinT", st1)
                qT = K.sb([64, 8, LL], BF16, "qT", st1)
                kT = K.sb([64, 2, T], BF16, "kT", st1)
                vtok = K.sb([128, 18, 2, 128], BF16, "vtok", st1)
                esink = K.sb([128, 8], F32, "esink", st1)
                maskP = K.sb([128, 128], F32, "maskP", st1)
                maskN = K.sb([128, 128], F32, "maskN", st1)
                t1b = [K.sb([64, 512], F32, "rt1", st1) for _ in range(2)]
                t2b = [K.sb([64, 512], F32, "rt2", st1) for _ in range(2)]
                pT = [K.sb([128, 512], BF16, "pT", st1) for _ in range(4)]
                dsum = [K.sb([128, 512], F32, "dsum", st1) for _ in range(2)]
                drec = [K.sb([64, 512], F32, "drec", st1) for _ in range(2)]
                K.memset("pool", vtok[:, :, :, :], 1.0, [vtok])
                stw = ES()
                wstg = [K.sb([128, 2048], F32, "wstg", stw)]
                K.dma("sp", cosT[:, :], swacos_d[:, :], reads=[swacos_d], writes=[cosT])
                K.dma("sp", sinT[:, :], swasin_d[:, :], reads=[swasin_d], writes=[sinT])
                K.dma("sp", maskP[:, :], triB_d[:, :], reads=[triB_d], writes=[maskP])
                K.dma("sp", maskN[:, :], triF_d[:, :], reads=[triF_d], writes=[maskN])
                K.dma("sp", esink[:, :], sink_d[0, :].partition_broadcast(128), reads=[sink_d], writes=[esink])
                K.act(esink[:, :], esink[:, :], AF.Exp, [esink], [esink])
                for j in range(2):
                    load_w_bf16(wq[:, :, j * 256:(j + 1) * 256], win[:, :, CD_Q + j * 256:CD_Q + (j + 1) * 256], (128, KC, 256), cdin_d, wq, wstg, 0)
                load_w_bf16(wk[:, :, :], win[:, :, CD_K:CD_K + 128], (128, KC, 128), cdin_d, wk, wstg, 0)
                load_w_bf16(wv[:, :, :], win[:, :, CD_V:CD_V + 128], (128, KC, 128), cdin_d, wv, wstg, 0)
                for (w_, ws_, nh) in ((wq, wqs, 64), (wk, wks, 16)):
                    wv4 = w_[:, :, :].rearrange("p k (h two d) -> p (k h) two d", two=2, d=32)
                    ws4 = ws_[:, :, :].rearrange("p k (h two d) -> p (k h) two d", two=2, d=32)
                    K.ts("pool", ws4[:, :, 0, :], wv4[:, :, 1, :], -1.0, None, ALU.mult, None, [w_], [ws_])
                    K.copy("pool", ws4[:, :, 1, :], wv4[:, :, 0, :], [w_], [ws_])
                wov = cdout_d[0, 1024:1536, :].rearrange("(h d) n -> d h n", d=64)
                for j in range(4):
                    load_w_bf16(wo[:, j * 2:(j + 1) * 2, :], wov[:, j * 2:(j + 1) * 2, :], (64, 2, D), cdout_d, wo, wstg, 0)
                K.barrier()
                stw.close()
                cnt = 0
                for h in range(8):
                    for j in range(4):
                        t0 = LC + j * 512
                        pa, pb = PS[(cnt * 2) % 4], PS[(cnt * 2 + 1) % 4]
                        for kc in range(KC):
                            K.mm(pa[0:64, :], wq[:, kc, h * 64:(h + 1) * 64], hT[:, kc, t0:t0 + 512], kc == 0, kc == KC - 1, [wq, hT], [pa])
                        for kc in range(KC):
                            K.mm(pb[0:64, :], wqs[:, kc, h * 64:(h + 1) * 64], hT[:, kc, t0:t0 + 512], kc == 0, kc == KC - 1, [wqs, hT], [pb])
                        a_, b_ = t1b[cnt % 2], t2b[cnt % 2]
                        K.tt("dve", a_[:, :], pa[0:64, :], cosT[:, j * 512:(j + 1) * 512], ALU.mult, [pa, cosT], [a_])
                        K.tt("dve", b_[:, :], pb[0:64, :], sinT[:, j * 512:(j + 1) * 512], ALU.mult, [pb, sinT], [b_])
                        K.tt("pool", qT[:, h, j * 512:(j + 1) * 512], a_[:, :], b_[:, :], ALU.add, [a_, b_], [qT])
                        cnt += 1
                for g in range(2):
                    pa = PS[cnt % 4]
                    for kc in range(KC):
                        K.mm(pa[0:64, 0:LC], wk[:, kc, g * 64:(g + 1) * 64], hT[:, kc, 0:LC], kc == 0, kc == KC - 1, [wk, hT], [pa])
                    K.copy("act", kT[:, g, 0:LC], pa[0:64, 0:LC], [pa], [kT])
                    cnt += 1
                    for j in range(4):
                        t0 = LC + j * 512
                        pa, pb = PS[(cnt * 2) % 4], PS[(cnt * 2 + 1) % 4]
                        for kc in range(KC):
                            K.mm(pa[0:64, :], wk[:, kc, g * 64:(g + 1) * 64], hT[:, kc, t0:t0 + 512], kc == 0, kc == KC - 1, [wk, hT], [pa])
                        for kc in range(KC):
                            K.mm(pb[0:64, :], wks[:, kc, g * 64:(g + 1) * 64], hT[:, kc, t0:t0 + 512], kc == 0, kc == KC - 1, [wks, hT], [pb])
                        a_, b_ = t1b[cnt % 2], t2b[cnt % 2]
                        K.tt("dve", a_[:, :], pa[0:64, :], cosT[:, j * 512:(j + 1) * 512], ALU.mult, [pa, cosT], [a_])
                        K.tt("dve", b_[:, :], pb[0:64, :], sinT[:, j * 512:(j + 1) * 512], ALU.mult, [pb, sinT], [b_])
                        K.tt("pool", kT[:, g, t0:t0 + 512], a_[:, :], b_[:, :], ALU.add, [a_, b_], [kT])
                        cnt += 1
                for ti in range(18):
                    pa = PS[ti % 4]
                    for kc in range(KC):
                        K.mm(pa[:, 0:128], hT[:, kc, ti * 128:(ti + 1) * 128], wv[:, kc, :], kc == 0, kc == KC - 1, [hT, wv], [pa])
                    K.copy("act", vtok[:, ti, :, 0:64], pa[:, 0:128].rearrange("p (g e) -> p g e", g=2), [pa], [vtok])
                units = []
                u = 0
                for i in range(16):
                    for g in range(2):
                        keys = [(0, None), (1, None)]
                        if i > 0:
                            keys.append((2 + i - 1, maskP))
                        keys.append((2 + i, None))
                        if i < 15:
                            keys.append((2 + i + 1, maskN))
                        for ki, (kt, mask) in enumerate(keys):
                            units.append((i, g, ki, kt, mask, len(keys), u))
                        u += 1

                def front(un, idx):
                    i, g, ki, kt, mask, nk, uu = un
                    psc = PS[idx % 4]
                    K.mm(psc[:, :].rearrange("p (h q) -> p h q", h=4), kT[:, g, kt * 128:(kt + 1) * 128],
                         qT[:, g * 4:(g + 1) * 4, i * 128:(i + 1) * 128], True, True, [kT, qT], [psc])

                def back(un, idx):
                    i, g, ki, kt, mask, nk, uu = un
                    psc = PS[idx % 4]
                    pacc = PS[4 + (uu % 2) * 2]
                    p_ = pT[idx % 4]
                    K.act(p_[:, :], psc[:, :], AF.Exp, [psc], [p_], scale=0.125)
                    if mask is not None:
                        K.tt("dve", p_[:, :].rearrange("p (h q) -> p h q", h=4), p_[:, :].rearrange("p (h q) -> p h q", h=4),
                             mask[:, :].unsqueeze(1).to_broadcast([128, 4, 128]), ALU.mult, [p_, mask], [p_])
                    K.mm(pacc[:, :], vtok[:, kt, g, :], p_[:, :], ki == 0, ki == nk - 1, [vtok, p_], [pacc])
                    if ki == nk - 1:
                        d_ = dsum[uu % 2]
                        r_ = drec[uu % 2]
                        K.tt("dve", d_[64:128, :].rearrange("p (h q) -> p h q", h=4), pacc[64:128, :].rearrange("p (h q) -> p h q", h=4),
                             esink[64:128, g * 4:(g + 1) * 4].unsqueeze(2).to_broadcast([64, 4, 128]), ALU.add, [pacc, esink], [d_])
                        K.op("dve", lambda e, d_=d_, r_=r_: e.reciprocal(out=r_[0:64, :], in_=d_[64:128, :]), [d_], [r_])
                        K.tt("dve", attT[:, g * 4:(g + 1) * 4, i * 128:(i + 1) * 128], pacc[0:64, :].rearrange("p (h q) -> p h q", h=4),
                             r_[:, :].rearrange("p (h q) -> p h q", h=4), ALU.mult, [pacc, r_], [attT])

                LA = 2
                for idx in range(min(LA, len(units))):
                    front(units[idx], idx)
                for idx, un in enumerate(units):
                    if idx + LA < len(units):
                        front(units[idx + LA], idx + LA)
                    back(un, idx)
            K.barrier()
            pieces = [((lambda oc, h=h: wo[:, h, oc * 128:(oc + 1) * 128]),
                       (lambda t0, t1, h=h: attT[:, h, t0 - LC:t1 - LC]), [wo, attT]) for h in range(8)]
            apply_out(b, pieces, 2, LC)
        K.barrier()

    def stage_ssd(b, hT_scope_fn):
        win = cdin_d[0, :, :].rearrange("(kc p) n -> p kc n", p=128)
        with ES() as st:
            uT = K.sb([128, 8, LL], BF16, "uT", st)
            with ES() as st1:
                xs_tok = K.sb([128, 18, 1024], BF16, "xs_tok", st1)
                B_tok = K.sb([128, 18, 256], BF16, "B_tok", st1)
                BCT = K.sb([128, 4, T], BF16, "BCT", st1)
                dtv = K.sb([128, 18, 32], F32, "dtv", st1)
                dtA = K.sb([128, 18, 32], F32, "dtA", st1)
                a_bc = K.sb([128, 32], F32, "a_bc", st1)
                dtb_bc = K.sb([128, 32], F32, "dtb_bc", st1)
                D_bc = K.sb([128, 16], F32, "D_bc", st1)
                sng = K.sb([128, 8], F32, "sng", st1)
                scw = K.sb([128, 5, 12], F32, "scw", st1)
                scb = K.sb([128, 12], F32, "scb", st1)
                triF = K.sb([128, 128], F32, "triF", st1)
                triB = K.sb([128, 128], F32, "triB", st1)
                strF = K.sb([128, 128], F32, "strF", st1)
                strB = K.sb([128, 128], F32, "strB", st1)
                for (dst, src) in ((triF, triF_d), (triB, triB_d), (strF, strF_d), (strB, strB_d)):
                    K.dma("sp", dst[:, :], src[:, :], reads=[src], writes=[dst])
                K.dma("sp", a_bc[:, :], salog_d[0, :, :].rearrange("a b -> (a b)").partition_broadcast(128), reads=[salog_d], writes=[a_bc])
                K.act(a_bc[:, :], a_bc[:, :], AF.Exp, [a_bc], [a_bc])
                K.ts("dve", a_bc[:, :], a_bc[:, :], -1.0, None, ALU.mult, None, [a_bc], [a_bc])
                K.dma("sp", dtb_bc[:, :], sdtb_d[0, :, :].rearrange("a b -> (a b)").partition_broadcast(128), reads=[sdtb_d], writes=[dtb_bc])
                K.dma("sp", D_bc[:, :], sd_d[0, :].partition_broadcast(128), reads=[sd_d], writes=[D_bc])
                K.dma("sp", sng[:, :], colvec(sng_d[0, :], 8), reads=[sng_d], writes=[sng], allow_slow_non_contiguous=True)
                for tap in range(5):
                    K.dma("sp", scw[:, tap, :], colvec(scw_d[0, tap, :], 12), reads=[scw_d], writes=[scw], allow_slow_non_contiguous=True)
                K.dma("sp", scb[:, :], colvec(scb_d[0, :], 12), reads=[scb_d], writes=[scb], allow_slow_non_contiguous=True)
                with ES() as st2:
                    hT = K.sb([128, KC, T], BF16, "hT", st2)
                    load_hT(hT)
                    wstg = [K.sb([128, 4096], F32, "wstg", st2)]
                    st3 = ES()
                    wch = [K.sb([128, KC, 128], BF16, "wch", st3) for _ in range(2)]
                    PW = T + 8
                    upad = [K.sb([128, PW], F32, "upad", st3) for _ in range(2)]
                    cvb = [K.sb([128, T], F32, "cvb", st3) for _ in range(1)]
                    xsT = [K.sb([128, T], BF16, "xsT", st3) for _ in range(2)]
                    for u_ in upad:
                        K.memset("pool", u_[:, :], 0.0, [u_])

                    def poff(t):
                        return t + 2 if t < LC else t + 6

                    def ssd_wload(fc):
                        w_ = wch[fc % 2]
                        load_w_bf16(w_[:, :, :], win[:, :, CD_XBC + fc * 128:CD_XBC + (fc + 1) * 128], (128, KC, 128), cdin_d, w_, wstg, fc)

                    ssd_wload(0)
                    for fc in range(12):
                        w_ = wch[fc % 2]
                        if fc + 1 < 12:
                            ssd_wload(fc + 1)
                        up = upad[fc % 2]
                        for bi, (t0, t1) in enumerate(BLOCKS):
                            n = t1 - t0
                            pb = PS[bi % 4]
                            for kc in range(KC):
                                K.mm(pb[:, 0:n], w_[:, kc, :], hT[:, kc, t0:t1], kc == 0, kc == KC - 1, [w_, hT], [pb])
                            K.copy("act", up[:, poff(t0):poff(t0) + n], pb[:, 0:n], [pb], [up])
                        cv_ = cvb[0]
                        for (s0, ln) in ((0, LC), (LC, LL)):
                            p0 = poff(s0)
                            K.act(cv_[:, s0:s0 + ln], up[:, p0:p0 + ln], AF.Identity, [up, scw, scb], [cv_],
                                  bias=scb[:, fc:fc + 1], scale=scw[:, 2, fc:fc + 1])
                            for tap in (0, 1, 3, 4):
                                K.stt("dve", cv_[:, s0:s0 + ln], up[:, p0 + tap - 2:p0 + tap - 2 + ln], scw[:, tap, fc:fc + 1], cv_[:, s0:s0 + ln],
                                      ALU.mult, ALU.add, [up, scw, cv_], [cv_])
                        if fc < 8:
                            dstT, dst_ap = xsT[fc % 2], xsT[fc % 2][:, :]
                        else:
                            dstT, dst_ap = BCT, BCT[:, fc - 8, :]
                        K.act(dst_ap, cv_[:, :], AF.Silu, [cv_], [dstT])
                        if fc < 10:
                            for grp in range(3):
                                tis = list(range(grp * 8, min(18, grp * 8 + 8)))
                                pb = PS[4 + grp % 2]
                                pbv = pb[:, :].bitcast(BF16)
                                for qi, ti in enumerate(tis):
                                    K.transpose(pbv[:, qi * 128:(qi + 1) * 128], dst_ap[:, ti * 128:(ti + 1) * 128], identb[:, :], [dstT, identb], [pb])
                                nt = len(tis)
                                src_v = pbv[:, 0:nt * 128].rearrange("p (a f) -> p a f", a=nt)
                                if fc < 8:
                                    K.copy("act", xs_tok[:, tis[0]:tis[0] + nt, fc * 128:(fc + 1) * 128], src_v, [pb], [xs_tok])
                                else:
                                    K.copy("act", B_tok[:, tis[0]:tis[0] + nt, (fc - 8) * 128:(fc - 7) * 128], src_v, [pb], [B_tok])
                    K.barrier()
                    st3.close()
                    wz = K.sb([128, KC, 1024], BF16, "wz", st2)
                    wdt = K.sb([128, KC, 32], BF16, "wdt", st2)
                    szb = [K.sb([128, 1024], BF16, "szb", st2) for _ in range(2)]
                    dtt = [K.sb([128, 32], F32, "dtt", st2) for _ in range(2)]
                    for j in range(2):
                        load_w_bf16(wz[:, :, j * 512:(j + 1) * 512], win[:, :, CD_Z + j * 512:CD_Z + (j + 1) * 512], (128, KC, 512), cdin_d, wz, wstg, j)
                    load_w_bf16(wdt[:, :, :], win[:, :, CD_DT:CD_DT + 32], (128, KC, 32), cdin_d, wdt, wstg, 0)
                    for ti in range(18):
                        pb = PS[ti % 4]
                        for kc in range(KC):
                            K.mm(pb[:, 0:32], hT[:, kc, ti * 128:(ti + 1) * 128], wdt[:, kc, :], kc == 0, kc == KC - 1, [hT, wdt], [pb])
                        d_ = dtt[ti % 2]
                        K.tt("dve", d_[:, :], pb[:, 0:32], dtb_bc[:, :], ALU.add, [pb, dtb_bc], [d_])
                        K.act(d_[:, :], d_[:, :], AF.Exp, [d_], [d_])
                        K.act(dtv[:, ti, :], d_[:, :], AF.Ln, [d_], [dtv], bias=K.eps_ap(1.0))
                    K.tt("dve", dtA[:, :, :], dtv[:, :, :], a_bc[:, :].unsqueeze(1).to_broadcast([128, 18, 32]), ALU.mult, [dtv, a_bc], [dtA])
                    for li in range(16):
                        ti = li + 2
                        s_ = szb[li % 2]
                        for j in range(2):
                            pb = PS[(li * 2 + j) % 4]
                            for kc in range(KC):
                                K.mm(pb[:, :], hT[:, kc, ti * 128:(ti + 1) * 128], wz[:, kc, j * 512:(j + 1) * 512], kc == 0, kc == KC - 1, [hT, wz], [pb])
                            K.act(s_[:, j * 512:(j + 1) * 512], pb[:, :], AF.Silu, [pb], [s_])
                        K.dma("pool", sz_d[li, :, :], s_[:, :], reads=[s_], writes=[sz_d])
                K.barrier()
                with ES() as st2:
                    Hf = K.sb([128, 2, 512], F32, "Hf", st2)
                    Hb = K.sb([128, 2, 512], F32, "Hb", st2)
                    hbf = [K.sb([128, 1024], BF16, "hbf", st2) for _ in range(2)]
                    hinf = [K.sb([128, 1024], BF16, "hinf", st2) for _ in range(2)]
                    szt = [K.sb([128, 1024], BF16, "szt", st2) for _ in range(2)]
                    prep = {}
                    for nm in ("acs0", "eac0", "dend0", "cdec0", "wgt0", "acs1", "eac1", "dend1", "cdec1", "wgt1"):
                        prep[nm] = K.sb([128, 16], F32, nm, st2)
                    xte = K.sb([128, 1024], BF16, "xte", st2)
                    rseg = K.sb([128, 16, 128], F32, "rseg", st2)
                    cbm = [K.sb([128, 2, 128], F32, "cbm", st2) for _ in range(2)]
                    eseg = [K.sb([128, 512], F32, "eseg", st2) for _ in range(2)]
                    Lt = [K.sb([128, 16, 128], BF16, "Lt", st2) for _ in range(2)]
                    xdt = [K.sb([128, 1024], BF16, "xdt", st2) for _ in range(2)]
                    yacc = K.sb([128, 1024], F32, "yacc", st2)
                    ytmp = K.sb([128, 512], F32, "ytmp", st2)
                    ub = K.sb([128, 1024], F32, "ub", st2)
                    ubf = K.sb([128, 1024], BF16, "ubf", st2)
                    ssq = K.sb([128, 2], F32, "ssq", st2)
                    junk = K.sb([128, 512], BF16, "junk", st2)
                    K.memset("dve", Hf[:, :, :], 0.0, [Hf])
                    K.memset("dve", Hb[:, :, :], 0.0, [Hb])
                    tri = (triF, triB)
                    strm = (strF, strB)

                    def do_prep(c, d):
                        pp = PS[0]
                        K.mm(pp[:, 0:16], tri[d][:, :], dtA[:, c, d * 16:(d + 1) * 16], True, True, [tri[d], dtA], [pp])
                        K.mm(pp[:, 16:32], ones_f[:, :], dtA[:, c, d * 16:(d + 1) * 16], True, True, [ones_f, dtA], [pp])
                        acs, eac, dend, cdec, wgt = (prep[n_ + str(d)] for n_ in ("acs", "eac", "dend", "cdec", "wgt"))
                        K.copy("act", acs[:, :], pp[:, 0:16], [pp], [acs])
                        K.act(eac[:, :], pp[:, 0:16], AF.Exp, [pp], [eac])
                        K.act(cdec[:, :], pp[:, 16:32], AF.Exp, [pp], [cdec])
                        K.tt("dve", dend[:, :], pp[:, 16:32], acs[:, :], ALU.subtract, [pp, acs], [dend])
                        K.act(dend[:, :], dend[:, :], AF.Exp, [dend], [dend])
                        K.tt("dve", wgt[:, :], dend[:, :], dtv[:, c, d * 16:(d + 1) * 16], ALU.mult, [dend, dtv], [wgt])

                    def state_update(c, d, H):
                        wgt, cdec = prep["wgt" + str(d)], prep["cdec" + str(d)]
                        K.tt("dve", xte[:, :].rearrange("p (h e) -> p h e", h=16), xs_tok[:, c, :].rearrange("p (h e) -> p h e", h=16),
                             wgt[:, :].unsqueeze(2).to_broadcast([128, 16, 64]), ALU.mult, [xs_tok, wgt], [xte])
                        for g in range(2):
                            pb = PS[6 + g]
                            K.mm(pb[:, :], B_tok[:, c, g * 128:(g + 1) * 128], xte[:, g * 512:(g + 1) * 512], True, True, [B_tok, xte], [pb])
                            hv = H[:, g, :].rearrange("p (h e) -> p h e", h=8)
                            K.tt("dve", hv, hv, cdec[:, g * 8:(g + 1) * 8].unsqueeze(2).to_broadcast([128, 8, 64]), ALU.mult, [H, cdec], [H])
                            K.tt("dve", H[:, g, :], H[:, g, :], pb[:, :], ALU.add, [H, pb], [H])

                    for c in range(18):
                        if c >= 2:
                            hb_ = hbf[c % 2]
                            K.copy("act", hb_[:, :], Hf[:, :, :].rearrange("p g e -> p (g e)"), [Hf], [hb_])
                            K.dma("pool", hin_d[c - 2, :, :], hb_[:, :], reads=[hb_], writes=[hin_d])
                        if c == 17:
                            break
                        do_prep(c, 0)
                        state_update(c, 0, Hf)
                    K.barrier()
                    for c in [1, 0] + list(range(17, 1, -1)):
                        do_prep(c, 1)
                        if c >= 2:
                            li = c - 2
                            do_prep(c, 0)
                            hi_ = hinf[li % 2]
                            sz_ = szt[li % 2]
                            K.dma("sp", hi_[:, :], hin_d[li, :, :], reads=[hin_d], writes=[hi_])
                            K.dma("sp", sz_[:, :], sz_d[li, :, :], reads=[sz_d], writes=[sz_])
                            hb_ = hbf[li % 2]
                            K.copy("act", hb_[:, :], Hb[:, :, :].rearrange("p g e -> p (g e)"), [Hb], [hb_])
                            tsl = slice(c * 128, (c + 1) * 128)
                            pcb = PS[1]
                            for g in range(2):
                                K.mm(pcb[:, g * 128:(g + 1) * 128], BCT[:, g, tsl], BCT[:, 2 + g, tsl], True, True, [BCT], [pcb])
                            for d in range(2):
                                K.tt("dve", cbm[d][:, :, :], pcb[:, 0:256].rearrange("p (g q) -> p g q", g=2),
                                     tri[d][:, :].unsqueeze(1).to_broadcast([128, 2, 128]), ALU.mult, [pcb, tri[d]], [cbm[d]])
                            for d in range(2):
                                K.tt("dve", rseg[:, :, :], tri[d][:, :].unsqueeze(1).to_broadcast([128, 16, 128]),
                                     dtA[:, c, d * 16:(d + 1) * 16].unsqueeze(2).to_broadcast([128, 16, 128]), ALU.mult, [tri[d], dtA], [rseg])
                                for hb4 in range(4):
                                    pseg = PS[2 + hb4 % 2]
                                    K.mm(pseg[:, :], strm[d][:, :], rseg[:, hb4 * 4:(hb4 + 1) * 4, :], True, True, [strm[d], rseg], [pseg])
                                    es_ = eseg[hb4 % 2]
                                    K.act(es_[:, :], pseg[:, :], AF.Exp, [pseg], [es_])
                                    g = hb4 // 2
                                    K.tt("dve", Lt[d][:, hb4 * 4:(hb4 + 1) * 4, :], es_[:, :].rearrange("p (h q) -> p h q", h=4),
                                         cbm[d][:, g, :].unsqueeze(1).to_broadcast([128, 4, 128]), ALU.mult, [es_, cbm[d]], [Lt[d]])
                                K.tt("dve", xdt[d][:, :].rearrange("p (h e) -> p h e", h=16), xs_tok[:, c, :].rearrange("p (h e) -> p h e", h=16),
                                     dtv[:, c, d * 16:(d + 1) * 16].unsqueeze(2).to_broadcast([128, 16, 64]), ALU.mult, [xs_tok, dtv], [xdt[d]])
                            for h in range(16):
                                py = PS[4 + h // 8]
                                col = (h % 8) * 64
                                for d in range(2):
                                    K.mm(py[:, col:col + 64], Lt[d][:, h, :], xdt[d][:, h * 64:(h + 1) * 64], d == 0, d == 1, [Lt[d], xdt[d]], [py])
                            K.tt("dve", yacc[:, :].rearrange("p (h e) -> p h e", h=16), xs_tok[:, c, :].rearrange("p (h e) -> p h e", h=16),
                                 D_bc[:, :].unsqueeze(2).to_broadcast([128, 16, 64]), ALU.mult, [xs_tok, D_bc], [yacc])
                            for g in range(2):
                                K.tt("dve", yacc[:, g * 512:(g + 1) * 512], yacc[:, g * 512:(g + 1) * 512], PS[4 + g][:, :], ALU.add, [yacc, PS[4 + g]], [yacc])
                            for d in range(2):
                                hsrc = hi_ if d == 0 else hb_
                                eac = prep["eac" + str(d)]
                                for g in range(2):
                                    po = PS[6 + g]
                                    K.mm(po[:, :], BCT[:, 2 + g, tsl], hsrc[:, g * 512:(g + 1) * 512], True, True, [BCT, hsrc], [po])
                                    K.tt("dve", ytmp[:, :].rearrange("p (h e) -> p h e", h=8), po[:, :].rearrange("p (h e) -> p h e", h=8),
                                         eac[:, g * 8:(g + 1) * 8].unsqueeze(2).to_broadcast([128, 8, 64]), ALU.mult, [po, eac], [ytmp])
                                    K.tt("dve", yacc[:, g * 512:(g + 1) * 512], yacc[:, g * 512:(g + 1) * 512], ytmp[:, :], ALU.add, [yacc, ytmp], [yacc])
                            K.tt("dve", ub[:, :], yacc[:, :], sz_[:, :], ALU.mult, [yacc, sz_], [ub])
                            K.memset("pool", ssq[:, :], 0.0, [ssq])
                            for g in range(2):
                                K.op("act", lambda e, g=g: e.activation(out=junk[:, :], in_=ub[:, g * 512:(g + 1) * 512], func=AF.Square,
                                                                         accum_out=ssq[:, g:g + 1]), [ub], [junk, ssq])
                            K.rsqrt(ssq[:, :], ssq[:, :], 1.0 / 512, EPS, [ssq], [ssq])
                            for g in range(2):
                                K.ts("dve", ubf[:, g * 512:(g + 1) * 512], ub[:, g * 512:(g + 1) * 512], ssq[:, g:g + 1], None, ALU.mult, None, [ub, ssq], [ubf])
                            pt = PS[1]
                            ptv = pt[:, :].bitcast(BF16)
                            for cc in range(8):
                                K.transpose(ptv[:, cc * 128:(cc + 1) * 128], ubf[:, cc * 128:(cc + 1) * 128], identb[:, :], [ubf, identb], [pt])
                            for cc in range(8):
                                K.act(uT[:, cc, li * 128:(li + 1) * 128], ptv[:, cc * 128:(cc + 1) * 128], AF.Identity, [pt, sng], [uT], scale=sng[:, cc:cc + 1])
                        if c != 2:
                            state_update(c, 1, Hb)
            K.barrier()
            wo = K.sb([128, 8, D], BF16, "wo_ssd", st)
            wostg = [K.sb([128, 4096], F32, "wostg", st)]
            for j in range(2):
                load_w_bf16(wo[:, j * 4:(j + 1) * 4, :], cdout_d[0, 0:1024, :].rearrange("(kc p) n -> p kc n", p=128)[:, j * 4:(j + 1) * 4, :],
                            (128, 4, D), cdout_d, wo, wostg, 0)
            pieces = [((lambda oc, cc=cc: wo[:, cc, oc * 128:(oc + 1) * 128]),
                       (lambda t0, t1, cc=cc: uT[:, cc, t0 - LC:t1 - LC]), [wo, uT]) for cc in range(8)]
            apply_out(b, pieces, 2, LC)
        K.barrier()


    AB_CQ, AB_CKV, AB_KR, AB_RW = 0, 384, 640, 672

    def stage_mla(b, hT):
        win = abin_d[0, :, :].rearrange("(kc p) n -> p kc n", p=128)
        sc = 96.0 ** -0.5
        with ES() as st:
            attT = K.sb([64, 8, T], BF16, "mattT", st)
            wo = K.sb([64, 8, D], BF16, "wo_mla", st)
            with ES() as st1:
                wcq = K.sb([128, KC, 384], BF16, "wcq", st1)
                wckv = K.sb([128, KC, 256], BF16, "wckv", st1)
                wkr = K.sb([128, KC, 96], BF16, "wkr", st1)
                wkrs = K.sb([128, KC, 96], BF16, "wkrs", st1)
                wqu = K.sb([128, 3, 768], BF16, "wqu", st1)
                wqus = K.sb([128, 24, 96], BF16, "wqus", st1)
                wkvu = K.sb([128, 2, 1024], BF16, "wkvu", st1)
                cqn = K.sb([128, 3, T], BF16, "cqn", st1)
                ckvn = K.sb([128, 2, T], BF16, "ckvn", st1)
                vo = K.sb([128, 18, 128], BF16, "mvo", st1)
                cosT = K.sb([96, LL], F32, "mcos", st1)
                sinT = K.sb([96, LL], F32, "msin", st1)
                gq = K.sb([128, 3], F32, "gq", st1)
                gkv = K.sb([128, 2], F32, "gkv", st1)
                qf = K.sb([96, T], BF16, "qf", st1)
                kf = K.sb([96, T], BF16, "kf", st1)
                sq = [K.sb([128, 512], BF16, "msq", st1) for _ in range(2)]
                rstd = [K.sb([128, 512], F32, "mrstd", st1) for _ in range(2)]
                t1b = [K.sb([96, 512], F32, "mt1", st1) for _ in range(1)]
                t2b = [K.sb([96, 512], F32, "mt2", st1) for _ in range(1)]
                pT = [K.sb([128, 512], BF16, "mpT", st1) for _ in range(4)]
                rec = [K.sb([64, 512], F32, "mrec", st1) for _ in range(2)]
                stw = ES()
                wstg = [K.sb([128, 4096], F32, "wstg", stw)]
                K.dma("sp", cosT[64:96, :], mlacos_d[:, :], reads=[mlacos_d], writes=[cosT])
                K.dma("sp", sinT[64:96, :], mlasin_d[:, :], reads=[mlasin_d], writes=[sinT])
                K.dma("sp", gq[:, :], colvec(mqg_d[0, :], 3), reads=[mqg_d], writes=[gq], allow_slow_non_contiguous=True)
                K.dma("sp", gkv[:, :], colvec(mkg_d[0, :], 2), reads=[mkg_d], writes=[gkv], allow_slow_non_contiguous=True)
                K.memset("pool", wkr[:, :, :], 0.0, [wkr])
                K.memset("pool", wkrs[:, :, :], 0.0, [wkrs])
                K.memset("pool", wqus[:, :, :], 0.0, [wqus])
                K.memset("pool", vo[:, :, :], 1.0, [vo])
                load_w_bf16(wcq[:, :, :], win[:, :, AB_CQ:AB_CQ + 384], (128, KC, 384), abin_d, wcq, wstg, 0)
                load_w_bf16(wckv[:, :, :], win[:, :, AB_CKV:AB_CKV + 256], (128, KC, 256), abin_d, wckv, wstg, 0)
                load_w_bf16(wkr[:, :, 64:96], win[:, :, AB_KR:AB_KR + 32], (128, KC, 32), abin_d, wkr, wstg, 0)
                load_w_bf16(wqu[:, :, :], mqu_d[0, :, :].rearrange("(c p) n -> p c n", p=128), (128, 3, 768), mqu_d, wqu, wstg, 0)
                load_w_bf16(wkvu[:, :, :], mkvu_d[0, :, :].rearrange("(c p) n -> p c n", p=128), (128, 2, 1024), mkvu_d, wkvu, wstg, 0)
                wov = about_d[0, 0:512, :].rearrange("(h d) n -> d h n", d=64)
                for j in range(2):
                    load_w_bf16(wo[:, j * 4:(j + 1) * 4, :], wov[:, j * 4:(j + 1) * 4, :], (64, 4, D), about_d, wo, wstg, 0)
                K.ts("pool", wkrs[:, :, 64:80], wkr[:, :, 80:96], -1.0, None, ALU.mult, None, [wkr], [wkrs])
                K.copy("pool", wkrs[:, :, 80:96], wkr[:, :, 64:80], [wkr], [wkrs])
                wq24 = wqu[:, :, :].rearrange("p c (h e) -> p (c h) e", e=96)
                K.ts("pool", wqus[:, :, 64:80], wq24[:, :, 80:96], -1.0, None, ALU.mult, None, [wqu], [wqus])
                K.copy("pool", wqus[:, :, 80:96], wq24[:, :, 64:80], [wqu], [wqus])
                K.barrier()
                stw.close()
                for bi, (t0, t1) in enumerate(BLOCKS):
                    n = t1 - t0
                    for (w_, nch, g_, dst, pbase) in ((wcq, 3, gq, cqn, 0), (wckv, 2, gkv, ckvn, 4)):
                        for c3 in range(nch):
                            pb = PS[pbase + c3]
                            for kc in range(KC):
                                K.mm(pb[:, 0:n], w_[:, kc, c3 * 128:(c3 + 1) * 128], hT[:, kc, t0:t1], kc == 0, kc == KC - 1, [w_, hT], [pb])
                        pst = PS[pbase + 3] if pbase == 0 else PS[pbase + 2]
                        for c3 in range(nch):
                            s_ = sq[c3 % 2]
                            K.act(s_[:, 0:n], PS[pbase + c3][:, 0:n], AF.Square, [PS[pbase + c3]], [s_])
                            K.mm(pst[:, 0:n], ones_bf[:, :], s_[:, 0:n], c3 == 0, c3 == nch - 1, [ones_bf, s_], [pst])
                        r_ = rstd[0 if pbase == 0 else 1]
                        K.rsqrt(r_[:, 0:n], pst[:, 0:n], 1.0 / (nch * 128), EPS, [pst], [r_])
                        for c3 in range(nch):
                            K.stt("dve", dst[:, c3, t0:t1], PS[pbase + c3][:, 0:n], g_[:, c3:c3 + 1], r_[:, 0:n], ALU.mult, ALU.mult,
                                  [PS[pbase + c3], g_, r_], [dst])
                R_ = slice(64, 96)
                for bi, (t0, t1) in enumerate(BLOCKS):
                    n = t1 - t0
                    pa, pb = PS[0 + 2 * (bi % 2)], PS[1 + 2 * (bi % 2)]
                    for kc in range(KC):
                        K.mm(pa[0:96, 0:n], wkr[:, kc, :], hT[:, kc, t0:t1], kc == 0, kc == KC - 1, [wkr, hT], [pa])
                    if bi == 0:
                        K.copy("dve", kf[R_, t0:t1], pa[R_, 0:n], [pa], [kf])
                    else:
                        for kc in range(KC):
                            K.mm(pb[0:96, 0:n], wkrs[:, kc, :], hT[:, kc, t0:t1], kc == 0, kc == KC - 1, [wkrs, hT], [pb])
                        a_, b_ = t1b[0], t2b[0]
                        K.tt("dve", a_[R_, 0:n], pa[R_, 0:n], cosT[R_, t0 - LC:t1 - LC], ALU.mult, [pa, cosT], [a_])
                        K.tt("dve", b_[R_, 0:n], pb[R_, 0:n], sinT[R_, t0 - LC:t1 - LC], ALU.mult, [pb, sinT], [b_])
                        K.tt("pool", kf[R_, t0:t1], a_[R_, 0:n], b_[R_, 0:n], ALU.add, [a_, b_], [kf])
                u = 0
                for h in range(8):
                    for ti in range(18):
                        pa = PS[4 + ti % 2]
                        for c3 in range(2):
                            K.mm(pa[:, 0:64], ckvn[:, c3, ti * 128:(ti + 1) * 128], wkvu[:, c3, h * 128 + 64:h * 128 + 128],
                                 c3 == 0, c3 == 1, [ckvn, wkvu], [pa])
                        K.copy("dve", vo[:, ti, 0:64], pa[:, 0:64], [pa], [vo])
                    for bi, (t0, t1) in enumerate(BLOCKS):
                        n = t1 - t0
                        pa, pc, pd = PS[0], PS[2], PS[3]
                        for c3 in range(3):
                            K.mm(pa[0:96, 0:n], wqu[:, c3, h * 96:h * 96 + 96], cqn[:, c3, t0:t1], c3 == 0, c3 == 2, [wqu, cqn], [pa])
                        K.copy("act", qf[0:64, t0:t1], pa[0:64, 0:n], [pa], [qf])
                        if bi == 0:
                            K.copy("dve", qf[R_, t0:t1], pa[R_, 0:n], [pa], [qf])
                        else:
                            for c3 in range(3):
                                K.mm(pc[0:96, 0:n], wqus[:, c3 * 8 + h, :], cqn[:, c3, t0:t1], c3 == 0, c3 == 2, [wqus, cqn], [pc])
                            a_, b_ = t1b[0], t2b[0]
                            K.tt("dve", a_[R_, 0:n], pa[R_, 0:n], cosT[R_, t0 - LC:t1 - LC], ALU.mult, [pa, cosT], [a_])
                            K.tt("dve", b_[R_, 0:n], pc[R_, 0:n], sinT[R_, t0 - LC:t1 - LC], ALU.mult, [pc, sinT], [b_])
                            K.tt("pool", qf[R_, t0:t1], a_[R_, 0:n], b_[R_, 0:n], ALU.add, [a_, b_], [qf])
                        for c3 in range(2):
                            K.mm(pd[0:64, 0:n], wkvu[:, c3, h * 128:h * 128 + 64], ckvn[:, c3, t0:t1], c3 == 0, c3 == 1, [wkvu, ckvn], [pd])
                        K.copy("act", kf[0:64, t0:t1], pd[0:64, 0:n], [pd], [kf])
                    units = []
                    for bi, (t0, t1) in enumerate(BLOCKS):
                        keys = [0, 1] if bi == 0 else list(range(18))
                        for ki, kt in enumerate(keys):
                            units.append((bi, t0, t1, ki, kt, len(keys), u))
                        u += 1

                    def front(un, idx):
                        bi, t0, t1, ki, kt, nk, uu = un
                        n = t1 - t0
                        psc = PS[idx % 4]
                        K.mm(psc[:, 0:n], kf[:, kt * 128:(kt + 1) * 128], qf[:, t0:t1], True, True, [kf, qf], [psc])

                    def back(un, idx, h=h):
                        bi, t0, t1, ki, kt, nk, uu = un
                        n = t1 - t0
                        psc = PS[idx % 4]
                        pacc = PS[4 + (uu % 2) * 2]
                        p_ = pT[idx % 4]
                        K.act(p_[:, 0:n], psc[:, 0:n], AF.Exp, [psc], [p_], scale=sc)
                        K.mm(pacc[:, 0:n], vo[:, kt, :], p_[:, 0:n], ki == 0, ki == nk - 1, [vo, p_], [pacc])
                        if ki == nk - 1:
                            r_ = rec[uu % 2]
                            K.op("dve", lambda e, r_=r_, pacc=pacc, n=n: e.reciprocal(out=r_[0:64, 0:n], in_=pacc[64:128, 0:n]), [pacc], [r_])
                            K.tt("dve", attT[:, h, t0:t1], pacc[0:64, 0:n], r_[:, 0:n], ALU.mult, [pacc, r_], [attT])

                    LA = 2
                    for idx in range(min(LA, len(units))):
                        front(units[idx], idx)
                    for idx, un in enumerate(units):
                        if idx + LA < len(units):
                            front(units[idx + LA], idx + LA)
                        back(un, idx)
            K.barrier()
            pieces = [((lambda oc, h=h: wo[:, h, oc * 128:(oc + 1) * 128]),
                       (lambda t0, t1, h=h: attT[:, h, t0:t1]), [wo, attT]) for h in range(8)]
            apply_out(b, pieces, 2, 0)
        K.barrier()

    CW = -math.exp(-0.5)

    def stage_rwkv(b):
        win = abin_d[0, :, :].rearrange("(kc p) n -> p kc n", p=128)
        with ES() as st:
            cols = K.sb([128, 10, 4], F32, "rcols", st)
            for i_, src in enumerate((rkk_d[0, :], rka_d[0, :], None, rrk_d[0, :, :].rearrange("a b -> (a b)"), rlg_d[0, :], rlb_d[0, :],
                                      None, ra0_d[0, 0, :], ra0_d[0, 1, :])):
                if src is not None:
                    K.dma("sp", cols[:, i_, :], colvec(src, 4), reads=[rkk_d], writes=[cols], allow_slow_non_contiguous=True)
            K.ts("dve", cols[:, 2, :], cols[:, 1, :], -1.0, 1.0, ALU.mult, ALU.add, [cols], [cols])
            with ES() as st1:
                rT = K.sb([128, 4, T], BF16, "rT", st1)
                kT = K.sb([128, 4, T], BF16, "kT", st1)
                kknT = K.sb([128, 4, T], BF16, "kknT", st1)
                vtok = K.sb([128, 18, 512], BF16, "rvtok", st1)
                xwaT = K.sb([128, T], BF16, "xwaT", st1)
                mu = K.sb([128, 3, 14], F32, "mu", st1)
                blk64 = K.sb([128, 128], BF16, "blk64", st1)
                blk64f = K.sb([128, 128], F32, "blk64f", st1)
                K.dma("sp", blk64f[:, :], blk64_d[:, :], reads=[blk64_d], writes=[blk64f])
                K.copy("dve", blk64[:, :], blk64f[:, :], [blk64f], [blk64])
                K.dma("sp", mu[:, 0, :], colvec(rmp_d[0, :], 14), reads=[rmp_d], writes=[mu], allow_slow_non_contiguous=True)
                K.dma("sp", mu[:, 1, :], colvec(rmn_d[0, :], 14), reads=[rmn_d], writes=[mu], allow_slow_non_contiguous=True)
                K.tt("dve", mu[:, 2, :], mu[:, 0, :], mu[:, 1, :], ALU.add, [mu], [mu])
                K.ts("dve", mu[:, 2, :], mu[:, 2, :], -1.0, 1.0, ALU.mult, ALU.add, [mu], [mu])
                with ES() as st2:
                    hT = K.sb([128, KC, T], BF16, "hT", st2)
                    load_hT(hT)
                    wstg = [K.sb([128, 4096], F32, "wstg", st2)]
                    wch = [K.sb([128, KC, 128], BF16, "rwch", st2) for _ in range(2)]
                    g2b = K.sb([128, 512], BF16, "g2b", st2)
                    upad = [K.sb([128, T + 4], F32, "rupad", st2) for _ in range(2)]
                    xs = [K.sb([128, T], F32, "rxs", st2) for _ in range(1)]
                    xsb = [K.sb([128, T], BF16, "rxsb", st2) for _ in range(1)]
                    t32 = [K.sb([128, 512], F32, "rt32", st2) for _ in range(2)]
                    t16 = [K.sb([128, 512], BF16, "rt16", st2) for _ in range(2)]
                    gbo = [K.sb([128, T], BF16, "gbo", st2) for _ in range(1)]
                    rk32 = [K.sb([128, 512], F32, "rk32", st2) for _ in range(2)]
                    for u_ in upad:
                        K.memset("pool", u_[:, :], 0.0, [u_])
                    load_w_bf16(g2b[:, :], rg2_d[0, :, :], (128, 512), rg2_d, g2b, wstg, 0)

                    def poff(t):
                        return t + 1 if t < LC else t + 3

                    order = [4, 5, 6, 7, 0, 1, 2, 3, 8, 9, 10, 11, 12, 13]
                    def rw_wload(oi):
                        fc_ = order[oi]
                        w_ = wch[oi % 2]
                        load_w_bf16(w_[:, :, :], win[:, :, AB_RW + fc_ * 128:AB_RW + (fc_ + 1) * 128], (128, KC, 128), abin_d, w_, wstg, 0)

                    rw_wload(0)
                    for oi, fc in enumerate(order):
                        w_ = wch[oi % 2]
                        if oi + 1 < len(order):
                            rw_wload(oi + 1)
                        up = upad[oi % 2]
                        for bi, (t0, t1) in enumerate(BLOCKS):
                            n = t1 - t0
                            pb = PS[bi % 4]
                            for kc in range(KC):
                                K.mm(pb[:, 0:n], w_[:, kc, :], hT[:, kc, t0:t1], kc == 0, kc == KC - 1, [w_, hT], [pb])
                            K.copy("act", up[:, poff(t0):poff(t0) + n], pb[:, 0:n], [pb], [up])
                        x_ = xs[0]
                        for (s0, ln) in ((0, LC), (LC, LL)):
                            p0 = poff(s0)
                            K.act(x_[:, s0:s0 + ln], up[:, p0:p0 + ln], AF.Identity, [up, mu], [x_], scale=mu[:, 2, fc:fc + 1])
                            K.stt("dve", x_[:, s0:s0 + ln], up[:, p0 - 1:p0 - 1 + ln], mu[:, 0, fc:fc + 1], x_[:, s0:s0 + ln], ALU.mult, ALU.add, [up, mu, x_], [x_])
                            K.stt("dve", x_[:, s0:s0 + ln], up[:, p0 + 1:p0 + 1 + ln], mu[:, 1, fc:fc + 1], x_[:, s0:s0 + ln], ALU.mult, ALU.add, [up, mu, x_], [x_])
                        if fc < 4:
                            c4 = fc
                            K.copy("act", rT[:, c4, :], x_[:, :], [x_], [rT])
                            for bi, (t0, t1) in enumerate(BLOCKS):
                                n = t1 - t0
                                a_, b_ = t32[bi % 2], t16[bi % 2]
                                K.stt("dve", b_[:, 0:n], x_[:, t0:t1], cols[:, 3, c4:c4 + 1], kT[:, c4, t0:t1], ALU.mult, ALU.mult, [x_, cols, kT], [b_])
                                pb = PS[4 + bi % 2]
                                K.mm(pb[:, 0:n], blk64[:, :], b_[:, 0:n], True, True, [blk64, b_], [pb])
                                K.copy("act", gbo[0][:, t0:t1], pb[:, 0:n], [pb], [gbo[0]])
                            K.dma("pool", gb_d[1, c4, :, :], gbo[0][:, :], reads=[gbo[0]], writes=[gb_d])
                        elif fc < 8:
                            c4 = fc - 4
                            K.copy("act", kT[:, c4, :], x_[:, :], [x_], [kT])
                            for bi, (t0, t1) in enumerate(BLOCKS):
                                n = t1 - t0
                                a_, b_ = t32[bi % 2], t16[bi % 2]
                                K.ts("dve", a_[:, 0:n], x_[:, t0:t1], cols[:, 0, c4:c4 + 1], None, ALU.mult, None, [x_, cols], [a_])
                                K.act(b_[:, 0:n], a_[:, 0:n], AF.Square, [a_], [b_])
                                pb = PS[4 + bi % 2]
                                K.mm(pb[:, 0:n], blk64[:, :], b_[:, 0:n], True, True, [blk64, b_], [pb])
                                r_ = rk32[bi % 2]
                                K.rsqrt(r_[:, 0:n], pb[:, 0:n], 1.0, 1e-12, [pb], [r_])
                                K.tt("pool", kknT[:, c4, t0:t1], a_[:, 0:n], r_[:, 0:n], ALU.mult, [a_, r_], [kknT])
                        elif fc < 12:
                            c4 = fc - 8
                            xb_ = xsb[0]
                            K.copy("act", xb_[:, :], x_[:, :], [x_], [xb_])
                            for grp in range(3):
                                tis = list(range(grp * 8, min(18, grp * 8 + 8)))
                                pb = PS[4 + grp % 2]
                                pbv = pb[:, :].bitcast(BF16)
                                for qi, ti in enumerate(tis):
                                    K.transpose(pbv[:, qi * 128:(qi + 1) * 128], xb_[:, ti * 128:(ti + 1) * 128], identb[:, :], [xb_, identb], [pb])
                                nt = len(tis)
                                K.copy("dve", vtok[:, tis[0]:tis[0] + nt, c4 * 128:(c4 + 1) * 128],
                                       pbv[:, 0:nt * 128].rearrange("p (a f) -> p a f", a=nt), [pb], [vtok])
                            K.dma("pool", gb_d[0, c4, :, :], xb_[:, :], reads=[xb_], writes=[gb_d])
                        elif fc == 12:
                            K.act(xwaT[0:64, :], x_[0:64, :], AF.Tanh, [x_], [xwaT])
                            K.copy("act", xwaT[64:128, :], x_[64:128, :], [x_], [xwaT])
                        else:
                            xb_ = xsb[0]
                            K.act(xb_[:, :], x_[:, :], AF.Sigmoid, [x_], [xb_])
                            for c4 in range(4):
                                for bi, (t0, t1) in enumerate(BLOCKS):
                                    n = t1 - t0
                                    pb = PS[bi % 4]
                                    K.mm(pb[:, 0:n], g2b[:, c4 * 128:(c4 + 1) * 128], xb_[:, t0:t1], True, True, [g2b, xb_], [pb])
                                    K.copy("act", gbo[0][:, t0:t1], pb[:, 0:n], [pb], [gbo[0]])
                                K.dma("pool", gb_d[2, c4, :, :], gbo[0][:, :], reads=[gbo[0]], writes=[gb_d])
                K.barrier()
                with ES() as st2:
                    w2b = K.sb([64, 2, 512], BF16, "w2b", st2)
                    a2b = K.sb([128, 2, 512], BF16, "a2b", st2)
                    w0bc = K.sb([128, 2, 512], F32, "w0bc", st2)
                    lcm = K.sb([128, 2, 128], F32, "lcm", st2)
                    lexcm = K.sb([128, 2, 128], F32, "lexcm", st2)
                    mcol = K.sb([128, 2, 2], F32, "mcol", st2)
                    m1 = K.sb([128, 2, 128], F32, "m1", st2)
                    m3 = K.sb([128, 2, 384], F32, "m3", st2)
                    m1t = K.sb([128, 2, 128], F32, "m1t", st2)
                    for dst, src in ((lcm, rwlc_d), (lexcm, rwlexc_d), (mcol, rwmcol_d), (m1, rwm1_d), (m3, rwm3_d), (m1t, rwm1t_d)):
                        for d in range(2):
                            K.dma("sp", dst[:, d, :], src[d, :, :], reads=[src], writes=[dst])
                    with ES() as stw:
                        wstg = [K.sb([128, 1024], F32, "wstg", stw)]
                        for d in range(2):
                            load_w_bf16(w2b[:, d, :], rw2_d[0, d, :, :], (64, 512), rw2_d, w2b, wstg, 0)
                            s_ = wstg[0]
                            K.dma("sp", s_[64:128, 0:512], ra2_d[0, d, :, :], reads=[ra2_d], writes=[s_])
                            K.copy("pool", a2b[64:128, d, :], s_[64:128, 0:512], [s_], [a2b])
                            K.dma("sp", w0bc[:, d, :], rw0_d[0, d, :].partition_broadcast(128), reads=[rw0_d], writes=[w0bc])
                        K.barrier()

                    def dir_stream(d):
                        B = PS[4 * d:4 * d + 4]
                        sg = K.sb([128, 512], F32, "sg", st2)
                        aTs = K.sb([128, 128], F32, "aT", st2)
                        tmpa = K.sb([128, 128], F32, "tmpa", st2)
                        tmpb = K.sb([128, 128], F32, "tmpb", st2)
                        eL = K.sb([128, 128], F32, "eL", st2)
                        enL = K.sb([128, 128], F32, "enL", st2)
                        eLex = K.sb([128, 128], F32, "eLex", st2)
                        pm_sb = K.sb([128, 4, 2], F32, "pm_sb", st2)
                        gm = K.sb([128, 4, 2], F32, "gm", st2)
                        AR = K.sb([128, 4, 256], BF16, "AR", st2)
                        BH = K.sb([128, 4, 128], BF16, "BH", st2)
                        KH = K.sb([128, 4, 128], BF16, "KH", st2)
                        BKtok = K.sb([128, 2, 512], BF16, "BKtok", st2)
                        Q = [K.sb([128, 8, 128], F32, "Qa", st2), K.sb([128, 8, 128], F32, "Qb", st2)]
                        QT = [K.sb([128, 8, 128], F32, "QTa", st2), K.sb([128, 8, 128], F32, "QTb", st2)]
                        Nm = K.sb([128, 8, 128], F32, "Nm", st2)
                        S3 = K.sb([128, 8, 384], BF16, "S3", st2)
                        H = K.sb([128, 4, 64], F32, "H", st2)
                        H0 = K.sb([128, 4, 64], F32, "H0", st2)
                        H0b = K.sb([128, 4, 64], BF16, "H0b", st2)
                        W_sb = K.sb([128, 512], F32, "W_sb", st2)
                        U_sb = K.sb([128, 512], BF16, "U_sb", st2)
                        ybuf = W_sb if cfg.get("bcd", True) else K.sb([128, 512], F32, "ybuf", st2)
                        aT4 = K.sb([128, 4, 128], F32, "aT4", st2)
                        yield
                        K.memset("dve", H[:, :, :], 0.0, [H])
                        tiles = list(range(18)) if d == 0 else [1, 0] + list(range(17, 1, -1))
                        for ci, c in enumerate(tiles):
                            tsl = slice(c * 128, (c + 1) * 128)
                            pz = B[0]
                            K.mm(pz[:, :], xwaT[0:64, tsl], w2b[:, d, :], True, True, [xwaT, w2b], [pz])
                            K.tt("dve", sg[:, :], pz[:, :], w0bc[:, d, :], ALU.add, [pz, w0bc], [sg])
                            K.act(sg[:, :], sg[:, :], AF.Sigmoid, [sg], [sg])
                            if cfg.get("bcd", True):
                                pa = B[1]
                                for f4 in range(4):
                                    fs = slice(f4 * 128, (f4 + 1) * 128)
                                    K.mm(pa[:, fs], a2b[64:128, d, fs], xwaT[64:128, tsl], True, True, [a2b, xwaT], [pa])
                                for f4 in range(4):
                                    fs = slice(f4 * 128, (f4 + 1) * 128)
                                    K.act(aT4[:, f4, :], pa[:, fs], AF.Sigmoid, [pa, cols], [aT4], bias=cols[:, 7 + d, f4:f4 + 1])
                            for f4 in range(4):
                                fs = slice(f4 * 128, (f4 + 1) * 128)
                                if cfg.get("bcd", True):
                                    aT = Buf(aT4.t, "aT4v")
                                    aT_ap = aT4[:, f4, :]
                                else:
                                    pa = B[1]
                                    K.mm(pa[:, 0:128], a2b[64:128, d, fs], xwaT[64:128, tsl], True, True, [a2b, xwaT], [pa])
                                    K.act(aTs[:, :], pa[:, 0:128], AF.Sigmoid, [pa, cols], [aTs], bias=cols[:, 7 + d, f4:f4 + 1])
                                    aT = aTs
                                    aT_ap = aTs[:, :]
                                pl = B[2 + f4 % 2]
                                K.mm(pl[:, 0:128], sg[:, fs], lcm[:, d, :], True, True, [sg, lcm], [pl])
                                K.mm(pl[:, 128:256], sg[:, fs], lexcm[:, d, :], True, True, [sg, lexcm], [pl])
                                K.mm(pl[:, 256:258], sg[:, fs], mcol[:, d, :], True, True, [sg, mcol], [pl])
                                K.act(eL[:, :], pl[:, 0:128], AF.Exp, [pl], [eL], scale=CW)
                                K.act(enL[:, :], pl[:, 0:128], AF.Exp, [pl], [enL], scale=-CW)
                                K.act(eLex[:, :], pl[:, 128:256], AF.Exp, [pl], [eLex], scale=CW)
                                K.copy("act", pm_sb[:, f4, :], pl[:, 256:258], [pl], [pm_sb])
                                K.tt("dve", AR[:, f4, 128:256], rT[:, f4, tsl], eL[:, :], ALU.mult, [rT, eL], [AR])
                                K.stt("dve", AR[:, f4, 0:128], kknT[:, f4, tsl], -1.0, eLex[:, :], ALU.mult, ALU.mult, [kknT, eLex], [AR])
                                K.tt("pool", tmpa[:, :], kknT[:, f4, tsl], aT_ap, ALU.mult, [kknT, aT4 if cfg.get("bcd", True) else aT], [tmpa])
                                K.tt("dve", BH[:, f4, :], tmpa[:, :], enL[:, :], ALU.mult, [tmpa, enL], [BH])
                                K.ts("dve", tmpb[:, :], aT_ap, cols[:, 1, f4:f4 + 1], cols[:, 2, f4:f4 + 1], ALU.mult, ALU.add, [aT4 if cfg.get("bcd", True) else aT, cols], [tmpb])
                                K.tt("pool", tmpb[:, :], tmpb[:, :], kT[:, f4, tsl], ALU.mult, [tmpb, kT], [tmpb])
                                K.tt("dve", KH[:, f4, :], tmpb[:, :], enL[:, :], ALU.mult, [tmpb, enL], [KH])
                                yield
                            K.tt("dve", pm_sb[:, :, 1], pm_sb[:, :, 1], pm_sb[:, :, 0], ALU.subtract, [pm_sb], [pm_sb])
                            K.act(gm[:, :, :], pm_sb[:, :, :], AF.Exp, [pm_sb], [gm], scale=CW)
                            pt = B[1]
                            ptv = pt[:, :].bitcast(BF16)
                            for f4 in range(4):
                                K.transpose(ptv[:, f4 * 128:(f4 + 1) * 128], BH[:, f4, :], identb[:, :], [BH, identb], [pt])
                                K.transpose(ptv[:, 512 + f4 * 128:512 + (f4 + 1) * 128], KH[:, f4, :], identb[:, :], [KH, identb], [pt])
                            K.copy("act", BKtok[:, :, :], ptv[:, :].rearrange("p (a f) -> p a f", a=2), [pt], [BKtok])
                            yield
                            for h in range(8):
                                f4, hr = h // 2, slice((h % 2) * 64, (h % 2) * 64 + 64)
                                ps_ = B[2 * (h % 2)]
                                K.mm(ps_[:, 0:256], BH[hr, f4, :], AR[hr, f4, :], True, True, [BH, AR], [ps_])
                                K.mm(ps_[:, 256:512], KH[hr, f4, :], AR[hr, f4, :], True, True, [KH, AR], [ps_])
                                K.tt("dve", Q[0][:, h, :], ps_[:, 0:128], m1[:, d, :], ALU.mult, [ps_, m1], [Q[0]])
                                K.tt("dve", S3[:, h, :], ps_[:, 128:512], m3[:, d, :], ALU.mult, [ps_, m3], [S3])
                                pq = B[2 * (h % 2) + 1]
                                K.mm(pq[:, 0:128], AR[hr, f4, 0:128], BH[hr, f4, :], True, True, [AR, BH], [pq])
                                K.tt("dve", QT[0][:, h, :], pq[:, 0:128], m1t[:, d, :], ALU.mult, [pq, m1t], [QT[0]])
                                if h % 2 == 1:
                                    yield
                            K.tt("dve" if cfg.get("bcd", True) else "pool", Nm[:, :, :], Q[0][:, :, :], ident[:, :].unsqueeze(1).to_broadcast([128, 8, 128]), ALU.add, [Q[0], ident], [Nm])
                            cur = 0
                            for lev in range(1, 7):
                                nxt = cur if cfg.get("inplace", False) else 1 - cur
                                for half in range(2):
                                    pqa, pqb, pn = B[(3 * half) % 4], B[(3 * half + 1) % 4], B[(3 * half + 2) % 4]
                                    hs = slice(half * 4, half * 4 + 4)
                                    for hh in range(4):
                                        h = half * 4 + hh
                                        cs = slice(hh * 128, (hh + 1) * 128)
                                        K.mm(pqb[:, cs], Q[cur][:, h, :], QT[cur][:, h, :], True, True, [Q[cur], QT[cur]], [pqb])
                                        if lev < 6:
                                            K.mm(pqa[:, cs], QT[cur][:, h, :], Q[cur][:, h, :], True, True, [Q[cur], QT[cur]], [pqa])
                                    K.copy("act", QT[nxt][:, hs, :], pqb[:, :].rearrange("p (a f) -> p a f", a=4), [pqb], [QT[nxt]])
                                    if lev < 6:
                                        K.copy("dve", Q[nxt][:, hs, :], pqa[:, :].rearrange("p (a f) -> p a f", a=4), [pqa], [Q[nxt]])
                                    yield
                                    for hh in range(4):
                                        h = half * 4 + hh
                                        K.mm(pn[:, hh * 128:(hh + 1) * 128], QT[nxt][:, h, :], Nm[:, h, :], True, True, [QT[nxt], Nm], [pn])
                                    K.tt("dve", Nm[:, hs, :], Nm[:, hs, :], pn[:, :].rearrange("p (a f) -> p a f", a=4), ALU.add, [Nm, pn], [Nm])
                                    yield
                                cur = nxt
                            K.tt("dve", H0[:, :, :], H[:, :, :], gm[:, :, 0:1].to_broadcast([128, 4, 64]), ALU.mult, [H, gm], [H0])
                            K.copy("act", H0b[:, :, :], H0[:, :, :], [H0], [H0b])
                            pw = B[0]
                            for h in range(8):
                                f4, hr = h // 2, slice((h % 2) * 64, (h % 2) * 64 + 64)
                                cs = slice(h * 64, (h + 1) * 64)
                                K.mm(pw[:, cs], AR[hr, f4, 0:128], H0b[hr, f4, :], True, False, [AR, H0b], [pw])
                                K.mm(pw[:, cs], S3[:, h, 128:256], vtok[:, c, cs], False, True, [S3, vtok], [pw])
                            K.copy("act", W_sb[:, :], pw[:, :], [pw], [W_sb])
                            yield
                            pu = B[1]
                            for h in range(8):
                                cs = slice(h * 64, (h + 1) * 64)
                                K.mm(pu[:, cs], Nm[:, h, :], W_sb[:, cs], True, True, [Nm, W_sb], [pu])
                            K.copy("act", U_sb[:, :], pu[:, :], [pu], [U_sb])
                            yield
                            py = B[2]
                            for h in range(8):
                                f4, hr = h // 2, slice((h % 2) * 64, (h % 2) * 64 + 64)
                                cs = slice(h * 64, (h + 1) * 64)
                                K.mm(py[:, cs], AR[hr, f4, 128:256], H0b[hr, f4, :], True, False, [AR, H0b], [py])
                                K.mm(py[:, cs], S3[:, h, 0:128], U_sb[:, cs], False, False, [S3, U_sb], [py])
                                K.mm(py[:, cs], S3[:, h, 256:384], vtok[:, c, cs], False, True, [S3, vtok], [py])
                            if d == 0:
                                K.copy("dve", ybuf[:, :], py[:, :], [py], [ybuf])
                                K.dma("pool", yf_d[c, :, :], ybuf[:, :], reads=[ybuf], writes=[yf_t[c]])
                            else:
                                K.copy("dve", ybuf[:, :], py[:, :], [py], [ybuf])
                                K.dma("pool", y_d[c, :, :], ybuf[:, :], reads=[ybuf], writes=[yb_t[c]])
                            yield
                            if ci < len(tiles) - 1:
                                ph = B[3]
                                for f4 in range(4):
                                    fs = slice(f4 * 128, (f4 + 1) * 128)
                                    K.mm(ph[:, fs], BKtok[:, 0, fs], U_sb[:, fs], True, False, [BKtok, U_sb], [ph])
                                    K.mm(ph[:, fs], BKtok[:, 1, fs], vtok[:, c, fs], False, True, [BKtok, vtok], [ph])
                                phv = ph[:, :].rearrange("p (f x) -> p f x", f=4)
                                for e2_ in range(2):
                                    rs = slice(e2_ * 64, e2_ * 64 + 64)
                                    K.tt("dve", H[rs, :, :], H0[rs, :, :], phv[rs, :, e2_ * 64:(e2_ + 1) * 64], ALU.add, [H0, ph], [H])
                                    K.tt("dve", H[rs, :, :], H[rs, :, :], gm[rs, :, 1:2].to_broadcast([64, 4, 64]), ALU.mult, [H, gm], [H])
                                yield

                    yf_t = [Buf(yf_d.t, "yf%d" % i_) for i_ in range(18)]
                    yb_t = [Buf(y_d.t, "yb%d" % i_) for i_ in range(18)]
                    streams = [dir_stream(0), dir_stream(1)]
                    for s_ in streams:
                        next(s_)
                    for _ in range(cfg.get("rw_offset", 19)):
                        next(streams[1])
                    alive = list(streams)
                    while alive:
                        for s_ in list(alive):
                            try:
                                next(s_)
                            except StopIteration:
                                alive.remove(s_)
            K.barrier()
            rwo = K.sb([128, 4, T], BF16, "rwo", st)
            with ES() as st2:
                yt = [K.sb([128, 512], F32, "yt", st2) for _ in range(2)]
                ysq = K.sb([128, 512], F32, "ysq", st2)
                s1 = K.sb([128, 8], F32, "s1", st2)
                s2 = K.sb([128, 8], F32, "s2", st2)
                ynb = K.sb([128, 512], BF16, "ynb", st2)
                vT_ = [K.sb([128, 4, 128], BF16, "vT_", st2) for _ in range(2)]
                sc_ = [K.sb([128, 4, 128], BF16, "sc_", st2) for _ in range(2)]
                gg_ = [K.sb([128, 4, 128], BF16, "gg_", st2) for _ in range(2)]
                yn32 = K.sb([128, 4, 128], F32, "yn32", st2)
                bon = K.sb([128, 4, 128], F32, "bon", st2)
                gbv = gb_d[:, :, :, :].rearrange("a c p t -> a p c t")
                for c in range(18):
                    tsl = slice(c * 128, (c + 1) * 128)
                    y_ = yt[c % 2]
                    K.dma("sp", y_[:, :], y_d[c, :, :], reads=[y_d], writes=[y_])
                    K.dma("sp", ysq[:, :], yf_d[c, :, :], reads=[yf_d], writes=[ysq])
                    K.tt("dve", y_[:, :], y_[:, :], ysq[:, :], ALU.add, [y_, ysq], [y_])
                    K.dma("sp", vT_[c % 2][:, :, :], gbv[0, :, :, tsl], reads=[gb_d], writes=[vT_[c % 2]])
                    K.dma("sp", sc_[c % 2][:, :, :], gbv[1, :, :, tsl], reads=[gb_d], writes=[sc_[c % 2]])
                    K.dma("sp", gg_[c % 2][:, :, :], gbv[2, :, :, tsl], reads=[gb_d], writes=[gg_[c % 2]])
                    yv = y_[:, :].rearrange("p (h e) -> p h e", h=8)
                    K.op("dve", lambda e, yv=yv: e.tensor_reduce(out=s1[:, :], in_=yv, axis=AX.X, op=ALU.add), [y_], [s1])
                    K.act(ysq[:, :], y_[:, :], AF.Square, [y_], [ysq])
                    K.op("dve", lambda e: e.tensor_reduce(out=s2[:, :], in_=ysq[:, :].rearrange("p (h e) -> p h e", h=8), axis=AX.X, op=ALU.add), [ysq], [s2])
                    K.ts("dve", s1[:, :], s1[:, :], 1.0 / 64, None, ALU.mult, None, [s1], [s1])
                    K.tt("dve", ysq[:, 0:8], s1[:, :], s1[:, :], ALU.mult, [s1], [ysq])
                    K.stt("dve", s2[:, :], s2[:, :], 1.0 / 64, ysq[:, 0:8], ALU.mult, ALU.subtract, [s2, ysq], [s2])
                    K.rsqrt(s2[:, :], s2[:, :], 1.0, 64e-5, [s2], [s2])
                    K.tt("dve", yv, yv, s1[:, :].unsqueeze(2).to_broadcast([128, 8, 64]), ALU.subtract, [y_, s1], [y_])
                    K.tt("dve", ynb[:, :].rearrange("p (h e) -> p h e", h=8), yv, s2[:, :].unsqueeze(2).to_broadcast([128, 8, 64]), ALU.mult, [y_, s2], [ynb])
                    pt = PS[c % 2]
                    ptv = pt[:, :].bitcast(BF16)
                    for c4 in range(4):
                        K.transpose(ptv[:, c4 * 128:(c4 + 1) * 128], ynb[:, c4 * 128:(c4 + 1) * 128], identb[:, :], [ynb, identb], [pt])
                    for c4 in range(4):
                        K.act(yn32[:, c4, :], ptv[:, c4 * 128:(c4 + 1) * 128], AF.Identity, [pt, cols], [yn32],
                              bias=cols[:, 5, c4:c4 + 1], scale=cols[:, 4, c4:c4 + 1])
                    K.tt("pool", bon[:, :, :], vT_[c % 2][:, :, :], sc_[c % 2][:, :, :], ALU.mult, [vT_[c % 2], sc_[c % 2]], [bon])
                    K.tt("pool", yn32[:, :, :], yn32[:, :, :], bon[:, :, :], ALU.add, [yn32, bon], [yn32])
                    K.tt("dve", rwo[:, :, tsl], yn32[:, :, :], gg_[c % 2][:, :, :], ALU.mult, [yn32, gg_[c % 2]], [rwo])
            K.barrier()
            wo = K.sb([128, 4, D], BF16, "wo_rw", st)
            wostg = [K.sb([128, 4096], F32, "wostg", st)]
            load_w_bf16(wo[:, :, :], about_d[0, 512:1024, :].rearrange("(kc p) n -> p kc n", p=128), (128, 4, D), about_d, wo, wostg, 0)
            pieces = [((lambda oc, cc=cc: wo[:, cc, oc * 128:(oc + 1) * 128]),
                       (lambda t0, t1, cc=cc: rwo[:, cc, t0:t1]), [wo, rwo]) for cc in range(4)]
            apply_out(b, pieces, 2, 0)
        K.barrier()

    K.barrier()
    for l in layers:
        stage_mods(l)
    for b in range(nb):
        stage_load(b)
        for li, l in enumerate(layers):
            last = (li == len(layers) - 1) and not cfg.get("force_ctx", False)
            modsT.l = l
            Amod.l = l
            if mixers:
                if l == 1:
                    with ES() as sth:
                        hT = K.sb([128, KC, T], BF16, "hT", sth)
                        stage_norm(b, 0, hT, to_dram=True)
                        if cfg.get("swa", True):
                            stage_swa(b, hT)
                    if cfg.get("ssd", True):
                        stage_ssd(b, None)
                else:
                    with ES() as sth:
                        hT = K.sb([128, KC, T], BF16, "hT", sth)
                        stage_norm(b, 0, hT, to_dram=True)
                        if cfg.get("mla", True):
                            stage_mla(b, hT)
                    if cfg.get("rwkv", True):
                        stage_rwkv(b)
            with ES() as sth:
                hT = K.sb([128, KC, T], BF16, "hT", sth)
                stage_norm(b, 1, hT, lo=0 if not last else LC)
                stage_ffn(b, l, do_ctx=not last, hT=hT)
        stage_out(b)
        if cfg.get("dbg_x", False) and b == 0:
            for cc in range(KC):
                K.dma("sp", dbgx_d[cc, :, :], xs_d[cc, :, :], reads=xblk, writes=[dbgx_d])
    K.barrier()
    K.es.close()
    return nc, K


CONST_INPUTS = None


def _rope_tables(rot_dim):
    n_freq = rot_dim // 4
    rows = np.arange(LL, dtype=np.float32) // 64
    cols = np.arange(LL, dtype=np.float32) % 64
    inv = (np.float32(10000.0) ** (-np.arange(n_freq, dtype=np.float32) / np.float32(n_freq))).astype(np.float32)
    ang = np.concatenate([rows[:, None] * inv[None, :], cols[:, None] * inv[None, :]], axis=-1).astype(np.float32)
    cos, sin = np.cos(ang).astype(np.float32), np.sin(ang).astype(np.float32)
    cosT = np.concatenate([cos.T, cos.T], axis=0)
    sinT = np.concatenate([sin.T, sin.T], axis=0)
    return np.ascontiguousarray(cosT), np.ascontiguousarray(sinT)


def const_inputs():
    global CONST_INPUTS
    if CONST_INPUTS is None:
        s = np.arange(128)
        triF = (s[:, None] <= s[None, :]).astype(np.float32)
        c = {"ident": np.eye(128, dtype=np.float32), "triF": triF, "triB": np.ascontiguousarray(triF.T),
             "strF": (s[:, None] > s[None, :]).astype(np.float32), "strB": (s[:, None] < s[None, :]).astype(np.float32)}
        c["swa_cos"], c["swa_sin"] = _rope_tables(64)
        c["mla_cos"], c["mla_sin"] = _rope_tables(32)
        triB = triF.T
        incl = [triF, triB]
        strict = [c["strB"], c["strF"]]
        m = 63
        c["rw_lc"] = np.stack([incl[d] - incl[d][:, m:m + 1] for d in range(2)]).astype(np.float32)
        c["rw_lexc"] = np.stack([strict[d] - incl[d][:, m:m + 1] for d in range(2)]).astype(np.float32)
        c["rw_mcol"] = np.stack([np.stack([incl[d][:, m], np.ones(128, np.float32)], axis=1) for d in range(2)]).astype(np.float32)
        c["rw_m1"] = np.stack([strict[d] for d in range(2)]).astype(np.float32)
        c["rw_m3"] = np.stack([np.concatenate([incl[d], strict[d], incl[d]], axis=1) for d in range(2)]).astype(np.float32)
        c["rw_m1t"] = np.stack([np.ascontiguousarray(strict[d].T) for d in range(2)]).astype(np.float32)
        blk = np.zeros((128, 128), np.float32)
        blk[:64, :64] = 1.0
        blk[64:, 64:] = 1.0
        c["blk64"] = blk
        CONST_INPUTS = c
    return CONST_INPUTS


def make_in_maps(nc_names, inputs, ncores=NCORES):
    consts = const_inputs()
    in_maps = []
    for core in range(ncores):
        m = {}
        sl = slice(core * BPC, (core + 1) * BPC)
        for k in nc_names:
            if k in consts:
                m[k] = consts[k]
            else:
                v = np.asarray(inputs[k])
                m[k] = np.ascontiguousarray(v[sl] if k in ("x", "c", "ctx") else v)
        in_maps.append(m)
    return in_maps


def kernel(**inputs):
    cfg = {}
    nc, K = build_program(cfg)
    in_maps = make_in_maps(K.in_names, inputs)
    res = run_bass_kernel_spmd(nc, in_maps, core_ids=list(range(NCORES)))
    return np.concatenate([r["out"] for r in res.results], axis=0)
```
